# Optimizing a Trainium2 kernel written in Bass

```python
import math
import jax, jax.numpy as jnp
from jax import lax
import numpy as np

D_MODEL = 1024
BATCH = 2
SEQ = 16384
DEPTH = 4

N_ATTN_HEADS = 8
HEAD_DIM = D_MODEL // 16
ATTN_DIM = N_ATTN_HEADS * HEAD_DIM
Q_BLOCK = 128
FORGET_BIAS_MEAN = 3.0
POOL_WINDOWS = (2, 4, 8, 16)
N_POOL_GROUPS = len(POOL_WINDOWS)
POOL_GROUP_DIM = D_MODEL // 8
POOL_DIM = N_POOL_GROUPS * POOL_GROUP_DIM
EVEN_IN_DIM = 3 * ATTN_DIM + N_ATTN_HEADS + POOL_DIM
EVEN_MIX_DIM = ATTN_DIM + POOL_DIM
CONV_DIM = D_MODEL
CONV_WIDTH = 31
D_FF = 2816
FFN_CONV_WIDTH = 3
EPS = 1e-6

N_EVEN = (DEPTH + 1) // 2
N_ODD = DEPTH // 2

kernel_name = "fox_pool_conformer_convffn_hybrid"


def rms_norm(x, g):
    xf = x.astype(jnp.float32)
    y = xf * lax.rsqrt(jnp.mean(xf * xf, axis=-1, keepdims=True) + EPS)
    return (y * g.astype(jnp.float32)).astype(x.dtype)


def layer_norm(x, g, b):
    xf = x.astype(jnp.float32)
    mu = jnp.mean(xf, axis=-1, keepdims=True)
    xc = xf - mu
    y = xc * lax.rsqrt(jnp.mean(xc * xc, axis=-1, keepdims=True) + EPS)
    return (y * g.astype(jnp.float32) + b.astype(jnp.float32)).astype(x.dtype)


def causal_dwconv(x, w, b):
    k_width, ch = w.shape
    y = lax.conv_general_dilated(
        x, w[:, None, :].astype(x.dtype), window_strides=(1,), padding=[(k_width - 1, 0)],
        dimension_numbers=("NWC", "WIO", "NWC"), feature_group_count=ch)
    return y + b.astype(x.dtype)


def forgetting_attention(q, k, v, f_logit, q_g, k_g):
    bsz, seq, nh, dh = q.shape
    nb = seq // Q_BLOCK
    q = rms_norm(q, q_g)
    k = rms_norm(k, k_g)
    scale = 1.0 / math.sqrt(dh)
    c = jnp.cumsum(jax.nn.log_sigmoid(f_logit.astype(jnp.float32)), axis=1)
    c_k = jnp.transpose(c, (0, 2, 1))
    k_h = jnp.transpose(k, (0, 2, 1, 3))
    v_h = jnp.transpose(v, (0, 2, 1, 3))
    q_blocks = q.reshape(bsz, nb, Q_BLOCK, nh, dh).transpose(1, 0, 3, 2, 4)
    c_blocks = c.reshape(bsz, nb, Q_BLOCK, nh).transpose(1, 0, 3, 2)
    k_pos = jnp.arange(seq)

    def one_block(args):
        q_blk, cq_blk, blk = args
        q_pos = blk * Q_BLOCK + jnp.arange(Q_BLOCK)
        s = jnp.einsum("bhqd,bhkd->bhqk", q_blk, k_h).astype(jnp.float32) * scale
        s = s + cq_blk[..., :, None] - c_k[:, :, None, :]
        mask = k_pos[None, :] <= q_pos[:, None]
        s = jnp.where(mask[None, None], s, -jnp.inf)
        p = jax.nn.softmax(s, axis=-1)
        return jnp.einsum("bhqk,bhkd->bhqd", p.astype(v_h.dtype), v_h)

    out = lax.map(one_block, (q_blocks, c_blocks, jnp.arange(nb)))
    return out.transpose(1, 0, 3, 2, 4).reshape(bsz, seq, nh * dh)


def multiscale_pool(u, w_pool, pool_scale):
    bsz, seq, _ = u.shape
    ug = u.reshape(bsz, seq, N_POOL_GROUPS, POOL_GROUP_DIM)
    csz = jnp.pad(jnp.cumsum(ug.astype(jnp.float32), axis=1), ((0, 0), (1, 0), (0, 0), (0, 0)))
    pos = jnp.arange(1, seq + 1, dtype=jnp.float32)
    outs = []
    for g, w in enumerate(POOL_WINDOWS):
        upper = csz[:, 1:, g]
        lower = jnp.pad(csz[:, : seq + 1 - w, g], ((0, 0), (w - 1, 0), (0, 0)))
        count = jnp.minimum(pos, float(w))[None, :, None]
        outs.append((upper - lower) / count)
    pooled = jnp.stack(outs, axis=2).astype(u.dtype)
    mixed = pooled - ug
    y = jnp.einsum("bsgc,gcd->bsgd", mixed, w_pool).reshape(bsz, seq, POOL_DIM)
    return y * pool_scale


def even_mixer(h, w_in, b_f, q_g, k_g, w_pool, pool_scale, w_out):
    bsz, seq, _ = h.shape
    z = h @ w_in
    o = 0
    q = z[..., o:o + ATTN_DIM].reshape(bsz, seq, N_ATTN_HEADS, HEAD_DIM); o += ATTN_DIM
    k = z[..., o:o + ATTN_DIM].reshape(bsz, seq, N_ATTN_HEADS, HEAD_DIM); o += ATTN_DIM
    v = z[..., o:o + ATTN_DIM].reshape(bsz, seq, N_ATTN_HEADS, HEAD_DIM); o += ATTN_DIM
    f_logit = z[..., o:o + N_ATTN_HEADS] + b_f; o += N_ATTN_HEADS
    u = z[..., o:o + POOL_DIM]
    a_out = forgetting_attention(q, k, v, f_logit, q_g, k_g)
    p_out = multiscale_pool(u, w_pool, pool_scale)
    return jnp.concatenate([a_out, p_out], axis=-1) @ w_out


def conformer_conv(h, w_pw1, dw_w, dw_b, ln_g, ln_b, w_pw2):
    a, g = jnp.split(h @ w_pw1, 2, axis=-1)
    u = a * jax.nn.sigmoid(g)
    u = causal_dwconv(u, dw_w, dw_b)
    u = jax.nn.silu(layer_norm(u, ln_g, ln_b))
    return u @ w_pw2


def conv_ffn(h, w_up, conv_w, conv_b, w_down):
    u = causal_dwconv(h @ w_up, conv_w, conv_b)
    gate, val = jnp.split(u, 2, axis=-1)
    return (jax.nn.silu(gate) * val) @ w_down


def setup_inputs(seed: int = 0) -> dict:
    key = jax.random.key(seed)
    ks = jax.random.split(key, 24)
    nrm = lambda k, shape, s: jax.random.normal(k, shape, jnp.float32) * s
    d = D_MODEL
    return {
        "x": nrm(ks[0], (BATCH, SEQ, d), 1.0),
        "even_norm_g": 1.0 + nrm(ks[1], (N_EVEN, d), 0.05),
        "even_w_in": nrm(ks[2], (N_EVEN, d, EVEN_IN_DIM), d ** -0.5),
        "even_b_f": FORGET_BIAS_MEAN + nrm(ks[3], (N_EVEN, N_ATTN_HEADS), 0.5),
        "even_q_norm_g": 1.0 + nrm(ks[4], (N_EVEN, HEAD_DIM), 0.05),
        "even_k_norm_g": 1.0 + nrm(ks[5], (N_EVEN, HEAD_DIM), 0.05),
        "even_w_pool": nrm(ks[6], (N_EVEN, N_POOL_GROUPS, POOL_GROUP_DIM, POOL_GROUP_DIM), POOL_GROUP_DIM ** -0.5),
        "even_pool_scale": 1.0 + nrm(ks[7], (N_EVEN, POOL_DIM), 0.1),
        "even_w_out": nrm(ks[8], (N_EVEN, EVEN_MIX_DIM, d), EVEN_MIX_DIM ** -0.5),
        "odd_norm_g": 1.0 + nrm(ks[9], (N_ODD, d), 0.05),
        "odd_w_pw1": nrm(ks[10], (N_ODD, d, 2 * CONV_DIM), d ** -0.5),
        "odd_dw_w": nrm(ks[11], (N_ODD, CONV_WIDTH, CONV_DIM), CONV_WIDTH ** -0.5),
        "odd_dw_b": nrm(ks[12], (N_ODD, CONV_DIM), 0.02),
        "odd_ln_g": 1.0 + nrm(ks[13], (N_ODD, CONV_DIM), 0.05),
        "odd_ln_b": nrm(ks[14], (N_ODD, CONV_DIM), 0.02),
        "odd_w_pw2": nrm(ks[15], (N_ODD, CONV_DIM, d), CONV_DIM ** -0.5),
        "ffn_norm_g": 1.0 + nrm(ks[16], (DEPTH, d), 0.05),
        "ffn_w_up": nrm(ks[17], (DEPTH, d, 2 * D_FF), d ** -0.5),
        "ffn_conv_w": nrm(ks[18], (DEPTH, FFN_CONV_WIDTH, 2 * D_FF), FFN_CONV_WIDTH ** -0.5),
        "ffn_conv_b": nrm(ks[19], (DEPTH, 2 * D_FF), 0.02),
        "ffn_w_down": nrm(ks[20], (DEPTH, D_FF, d), D_FF ** -0.5),
    }


def reference(x, even_norm_g, even_w_in, even_b_f, even_q_norm_g, even_k_norm_g, even_w_pool,
              even_pool_scale, even_w_out, odd_norm_g, odd_w_pw1, odd_dw_w, odd_dw_b, odd_ln_g,
              odd_ln_b, odd_w_pw2, ffn_norm_g, ffn_w_up, ffn_conv_w, ffn_conv_b, ffn_w_down):
    for layer in range(DEPTH):
        i = layer // 2
        if layer % 2 == 0:
            h = rms_norm(x, even_norm_g[i])
            x = x + even_mixer(h, even_w_in[i], even_b_f[i], even_q_norm_g[i], even_k_norm_g[i],
                               even_w_pool[i], even_pool_scale[i], even_w_out[i])
        else:
            h = rms_norm(x, odd_norm_g[i])
            x = x + conformer_conv(h, odd_w_pw1[i], odd_dw_w[i], odd_dw_b[i], odd_ln_g[i],
                                   odd_ln_b[i], odd_w_pw2[i])
        h = rms_norm(x, ffn_norm_g[layer])
        x = x + conv_ffn(h, ffn_w_up[layer], ffn_conv_w[layer], ffn_conv_b[layer], ffn_w_down[layer])
    return x
```

```python
import contextlib
import numpy as np
import concourse.bass as bass
import concourse.mybir as mybir
from concourse.bass_utils import run_bass_kernel_spmd

F32 = mybir.dt.float32
BF16 = mybir.dt.bfloat16
AF = mybir.ActivationFunctionType
ALU = mybir.AluOpType

D = 1024
KC = 8
DFF = 2816
NJ = 22
NCORES = 8
SEQ = 16384
HALO = 128
CH = 2048
SEG = CH + HALO
T = 2 * SEG
EPS = 1e-6

SAME_ENGINE_SYNC = True


class Op:
    __slots__ = ("eng", "fn", "deps", "is_dma", "sem", "val", "need_inc", "key", "phase", "inc")

    def __init__(self, eng, fn, is_dma, key=None):
        self.eng = eng
        self.fn = fn
        self.deps = []
        self.is_dma = is_dma
        self.sem = None
        self.val = None
        self.need_inc = False
        self.key = key
        self.inc = 16


class Prog:
    ENGS = ("pe", "act", "dve", "pool", "sp")

    def __init__(self, nc, stack):
        self.nc = nc
        self.stack = stack
        self.ops = {e: [] for e in self.ENGS}
        self.last_w = {}
        self.readers = {}
        self.dma_sems = {}
        self.n_ops = 0
        self.phase = 0

    def op(self, eng, fn, reads=(), writes=(), dma_key=None, inc=16):
        o = Op(eng, fn, dma_key is not None, dma_key)
        o.inc = inc
        o.phase = self.phase
        deps = []
        for r in reads:
            w = self.last_w.get(r)
            if w is not None:
                deps.append(w)
        for r in writes:
            w = self.last_w.get(r)
            if w is not None:
                deps.append(w)
            deps.extend(self.readers.get(r, ()))
        seen = set()
        for d in deps:
            if id(d) in seen or d is o:
                continue
            seen.add(id(d))
            if (not d.is_dma) and d.eng == eng and (eng == "pe" or not SAME_ENGINE_SYNC):
                continue
            if d.is_dma and o.is_dma and d.key == o.key and isinstance(o.key, tuple) and o.key[0] == "G":
                continue
            o.deps.append(d)
            d.need_inc = True
        for r in reads:
            self.readers.setdefault(r, []).append(o)
        for r in writes:
            self.last_w[r] = o
            self.readers[r] = []
        self.ops[eng].append(o)
        self.n_ops += 1
        return o

    def barrier(self):
        deps = []
        for e in self.ENGS:
            for o in reversed(self.ops[e]):
                if not o.is_dma:
                    deps.append(o)
                    break
        last_dma = {}
        for e in self.ENGS:
            for o in self.ops[e]:
                if o.is_dma:
                    last_dma[o.key] = o
        deps.extend(last_dma.values())
        for e in self.ENGS:
            o = Op(e, lambda en: en.nop(), False)
            o.phase = self.phase
            for d in deps:
                if (not d.is_dma) and d.eng == e:
                    continue
                o.deps.append(d)
                d.need_inc = True
            self.ops[e].append(o)
        self.last_w = {}
        self.readers = {}
        self.phase += 1

    def emit(self, final_waits=()):
        nc = self.nc
        stack = self.stack
        eng_sem = {}
        ecnt = {}
        gtot = {}
        for e in self.ENGS:
            for o in self.ops[e]:
                if o.is_dma:
                    if o.key not in self.dma_sems:
                        self.dma_sems[o.key] = [
                            stack.enter_context(nc.semaphore("d%d" % len(self.dma_sems))), 0]
                    ent = self.dma_sems[o.key]
                    ent[1] += o.inc
                    o.sem, o.val = ent[0], ent[1]
                    if isinstance(o.key, tuple) and o.key[0] == "G":
                        gtot[(o.key, o.phase)] = ent[1]
                elif o.need_inc:
                    k_ = (e, o.phase % 4)
                    if k_ not in eng_sem:
                        eng_sem[k_] = stack.enter_context(nc.semaphore("s_%s_%d" % k_))
                        ecnt[k_] = 0
                    ecnt[k_] += 1
                    o.sem, o.val = eng_sem[k_], ecnt[k_]
        for e in self.ENGS:
            for o in self.ops[e]:
                if o.is_dma and (o.key, o.phase) in gtot:
                    o.val = gtot[(o.key, o.phase)]
        final = list(final_waits)
        block = stack.enter_context(nc.Block())
        handles = {"pe": "tensor", "act": "scalar", "dve": "vector", "pool": "gpsimd", "sp": "sync"}

        def run(ename, e):
            waited = {}
            for o in self.ops[ename]:
                for d in o.deps:
                    k = id(d.sem)
                    if waited.get(k, 0) >= d.val:
                        continue
                    e.wait_ge(d.sem, d.val)
                    waited[k] = d.val
                ins = o.fn(e)
                if o.is_dma:
                    ins.then_inc(o.sem, o.inc)
                elif o.need_inc:
                    ins.then_inc(o.sem, 1)
            if ename == "pool":
                for ent in self.dma_sems.values():
                    e.wait_ge(ent[0], ent[1])

        for ename in self.ENGS:
            dec = getattr(block, handles[ename])

            def body(e, _n=ename):
                run(_n, e)
            dec(body)


_PID = {}


def pid4(e):
    k = id(e)
    if k not in _PID:
        _PID[k] = e.partition_id() % 4
    return _PID[k]


class Ctx:
    def __init__(self, nc, stack):
        self.nc = nc
        self.stack = stack
        self.P = Prog(nc, stack)
        self._n = 0

    def sb(self, shape, dtype, name=None):
        self._n += 1
        return self.stack.enter_context(self.nc.sbuf_tensor("%s_%d" % (name or "sb", self._n), list(shape), dtype))

    def ps(self, shape, dtype=F32, name=None):
        self._n += 1
        return self.stack.enter_context(self.nc.psum_tensor("%s_%d" % (name or "ps", self._n), list(shape), dtype))


class Cfg:
    def __init__(self, CH=4096):
        self.CH = CH
        self.T = CH + HALO
        self.SEQ = 4 * CH
        self.NKB = self.SEQ // 128
        self.NQB = self.SEQ // 512


def make_groups(T_, first_n, step, halo):
    out = [(0, min(first_n, T_), 0)]
    pos = out[0][1]
    while pos < T_:
        n = min(step, T_ - pos)
        out.append((pos, n, halo))
        pos += n
    return out


class Phase:
    def __init__(self, cx):
        self.cx = cx

    def __enter__(self):
        self.st = contextlib.ExitStack()
        self.saved = self.cx.stack
        self.cx.stack = self.st
        return self

    def __exit__(self, *a):
        self.cx.P.barrier()
        self.cx.stack = self.saved
        self.st.close()
        return False


def setup_consts(cx):
    P = cx.P
    cx.ones_f = cx.sb([128, 128], F32, "ones_f")
    cx.ones2_f = cx.sb([128, 128], F32, "ones2_f")
    cx.eps_t = cx.sb([128, 1], F32, "eps")
    cx.one_t = cx.sb([128, 1], F32, "one")
    cx.ident_b = cx.sb([128, 128], BF16, "ident_b")
    cx.maskb = cx.sb([128, 128], BF16, "maskb")
    cx.flag = cx.sb([128, 1], F32, "flag")
    P.op("dve", lambda e: e.memset(cx.ones_f[:, :], 1.0), writes=[("c", "ones_f")])
    P.op("dve", lambda e: e.memset(cx.ones2_f[:, :], 1.0), writes=[("c", "ones2_f")])
    P.op("dve", lambda e: e.memset(cx.ones2_f[0:64, 64:128], 0.0), writes=[("c", "ones2_f")])
    P.op("dve", lambda e: e.memset(cx.ones2_f[64:128, 0:64], 0.0), writes=[("c", "ones2_f")])
    P.op("dve", lambda e: e.memset(cx.eps_t[:, :], EPS), writes=[("c", "eps")])
    P.op("dve", lambda e: e.memset(cx.one_t[:, :], 1.0), writes=[("c", "one")])
    P.op("dve", lambda e: e.memset(cx.ident_b[:, :], 0.0), writes=[("c", "ident")])
    P.op("pool", lambda e: e.affine_select(out=cx.ident_b[:, :], in_=cx.ident_b[:, :], pattern=[[1, 128]],
                                           compare_op=ALU.not_equal, fill=1.0, base=0, channel_multiplier=-1),
         reads=[("c", "ident")], writes=[("c", "ident")])
    P.op("dve", lambda e: e.memset(cx.maskb[:, :], 0.0), writes=[("c", "maskb")])
    P.op("pool", lambda e: e.affine_select(out=cx.maskb[:, :], in_=cx.maskb[:, :], pattern=[[1, 128]],
                                           compare_op=ALU.is_ge, fill=-30000.0, base=0, channel_multiplier=-1),
         reads=[("c", "maskb")], writes=[("c", "maskb")])
    P.op("sp", lambda e: e.dma_start(out=cx.flag[:, :], in_=cx.dram["flag"]), writes=[("c", "flag")], dma_key=("G", "small"))


def emit_rmsnorm(cx, xt, nu, g_t, H, res, sq, red, ps_stat, std, rstd):
    P = cx.P
    P.op("act", lambda e: e.activation(out=sq[:, :, :nu], in_=xt[:, :, :nu], func=AF.Square),
         reads=[res["xt"]], writes=[res["sq"]])
    P.op("pool", lambda e: e.tensor_tensor(out=red[:, :nu], in0=sq[:, 0, :nu], in1=sq[:, 1, :nu], op=ALU.add),
         reads=[res["sq"]], writes=[res["red"]])
    for kc in range(2, KC):
        P.op("pool", lambda e, kc=kc: e.tensor_tensor(out=red[:, :nu], in0=red[:, :nu], in1=sq[:, kc, :nu], op=ALU.add),
             reads=[res["sq"], res["red"]], writes=[res["red"]])
    P.op("pe", lambda e: e.matmul(ps_stat[:, :nu], lhsT=cx.ones_f[:, :], rhs=red[:, :nu], start=True, stop=True),
         reads=[res["red"], ("c", "ones_f")], writes=[res["ps_stat"]])
    P.op("act", lambda e: e.activation(out=std[:, :nu], in_=ps_stat[:, :nu], func=AF.Sqrt, scale=1.0 / D, bias=cx.eps_t[:, 0:1]),
         reads=[res["ps_stat"], ("c", "eps")], writes=[res["std"]])
    P.op("dve", lambda e: e.reciprocal(out=rstd[:, :nu], in_=std[:, :nu]),
         reads=[res["std"]], writes=[res["rstd"]])
    for kc in range(KC):
        P.op("dve", lambda e, kc=kc: e.scalar_tensor_tensor(
            out=H[:, kc, :nu], in0=xt[:, kc, :nu], scalar=g_t[:, kc:kc + 1], in1=rstd[:, :nu],
            op0=ALU.mult, op1=ALU.mult),
            reads=[res["xt"], res["rstd"], res["g"]], writes=[res["H"]])


def load_small(cx, tag, name, shape, src):
    t = cx.sb(shape, F32, name)
    r = (tag, name)
    cx.P.op("sp", lambda e: e.dma_start(out=t[:, :], in_=src), writes=[r], dma_key=("G", "small"))
    return t, r


def store_group(cx, tag, xo, rxo, Xo, o0, n, first):
    P = cx.P
    if first:
        P.op("dve", lambda e: e.tensor_scalar(out=xo[:, :, 0:HALO], in0=xo[:, :, 0:HALO], scalar1=cx.flag[:, 0:1],
                                              scalar2=None, op0=ALU.mult),
             reads=[rxo, ("c", "flag")], writes=[rxo])
    P.op("pool", lambda e: e.dma_start(out=Xo[:, :, o0:o0 + n], in_=xo[:, :, :n]),
         reads=[rxo], writes=[(tag, "Xout")], dma_key=("st", 0))


def ffn_phase(cx, cfg, X_in, X_out, wup_bf, wdn_bf, g_dram, cw_dram, cb_dram, tag, out_off=0):
    P = cx.P
    groups = make_groups(cfg.T, 512, 510, 2)
    Xi = X_in.rearrange("(kc p) t -> p kc t", p=128)
    Xo = X_out.rearrange("(kc p) t -> p kc t", p=128)
    with Phase(cx):
        g_t, rg = load_small(cx, tag, "g", [128, KC], g_dram)
        cw_t, rcw = load_small(cx, tag, "cw", [128, 44 * 3], cw_dram)
        cb_t, rcb = load_small(cx, tag, "cb", [128, 44], cb_dram)
        NX = 2
        xt = [cx.sb([128, KC, 512], F32, "xt") for _ in range(NX)]
        sq = cx.sb([128, KC, 512], F32, "sq")
        red = cx.sb([128, 512], F32, "red")
        std = cx.sb([128, 512], F32, "std")
        rstd = cx.sb([128, 512], F32, "rstd")
        Hs = [cx.sb([128, KC, 512], BF16, "H") for _ in range(2)]
        NW = 3
        wup = [cx.sb([128, 2, KC, 128], BF16, "wup") for _ in range(NW)]
        acc_g = [cx.sb([128, 512], F32, "accg") for _ in range(2)]
        acc_v = [cx.sb([128, 512], F32, "accv") for _ in range(2)]
        sg = [cx.sb([128, 512], F32, "sg") for _ in range(2)]
        Gs = [cx.sb([128, NJ, 512], BF16, "G") for _ in range(2)]
        ND = 2
        wdn = [cx.sb([128, NJ, 128], BF16, "wdn") for _ in range(ND)]
        xo = cx.sb([128, KC, 512], F32, "xo")
        ps_stat = cx.ps([128, 512], F32, "psst")
        ps_g = [cx.ps([128, 512], F32, "psg") for _ in range(2)]
        ps_v = [cx.ps([128, 512], F32, "psv") for _ in range(2)]
        ps_o = [cx.ps([128, 512], F32, "pso") for _ in range(2)]
        cnt = {"w": 0, "d": 0, "a": 0, "o": 0}

        def pre(gi):
            o0, n, h = groups[gi]
            nu = n + h
            s = gi % NX
            rx = (tag, "xt", s)
            P.op("sp", lambda e: e.dma_start(out=xt[s][:, :, :nu], in_=Xi[:, :, o0 - h:o0 + n]),
                 reads=[(tag, "Xin")], writes=[rx], dma_key=("xt", s))
            res = {"xt": rx, "sq": (tag, "sq"), "red": (tag, "red"), "ps_stat": (tag, "psst"), "std": (tag, "std"),
                   "rstd": (tag, "rstd"), "g": rg, "H": (tag, "H", gi % 2)}
            emit_rmsnorm(cx, xt[s], nu, g_t, Hs[gi % 2], res, sq, red, ps_stat, std, rstd)

        def up(gi):
            o0, n, h = groups[gi]
            nu = n + h
            H = Hs[gi % 2]
            G = Gs[gi % 2]
            rH = (tag, "H", gi % 2)
            rG = (tag, "G", gi % 2)
            for j in range(NJ):
                ws = cnt["w"] % NW
                cnt["w"] += 1
                rw = (tag, "wup", ws)
                P.op("sp", lambda e, j=j, ws=ws: e.dma_start(
                    out=wup[ws][:, :, :, :].rearrange("p a k o -> p (a k o)"), in_=wup_bf[j]),
                    writes=[rw], dma_key=("wup", ws))
                a = cnt["a"] % 2
                cnt["a"] += 1
                rpg, rpv = (tag, "psg", a), (tag, "psv", a)
                for gv, psb, rp in ((0, ps_g[a], rpg), (1, ps_v[a], rpv)):
                    for kc in range(KC):
                        P.op("pe", lambda e, gv=gv, kc=kc, psb=psb, ws=ws: e.matmul(
                            psb[:, :nu], lhsT=wup[ws][:, gv, kc, :], rhs=H[:, kc, :nu],
                            start=(kc == 0), stop=(kc == KC - 1)),
                            reads=[rw, rH], writes=[rp])
                ag, av, sgt = acc_g[a], acc_v[a], sg[a]
                rag, rav, rsg = (tag, "accg", a), (tag, "accv", a), (tag, "sg", a)
                for (acc, racc, psb, rp, ch) in ((ag, rag, ps_g[a], rpg, j), (av, rav, ps_v[a], rpv, NJ + j)):
                    w0 = cw_t[:, ch * 3 + 0:ch * 3 + 1]
                    w1 = cw_t[:, ch * 3 + 1:ch * 3 + 2]
                    w2 = cw_t[:, ch * 3 + 2:ch * 3 + 3]
                    bb = cb_t[:, ch:ch + 1]
                    P.op("act", lambda e, acc=acc, psb=psb, w2=w2, bb=bb: e.activation(
                        out=acc[:, :n], in_=psb[:, h:h + n], func=AF.Identity, scale=w2, bias=bb),
                        reads=[rp, rcw, rcb], writes=[racc])
                    for sh, wk in ((1, w1), (2, w0)):
                        if h >= sh:
                            P.op("dve", lambda e, acc=acc, psb=psb, wk=wk, sh=sh: e.scalar_tensor_tensor(
                                out=acc[:, :n], in0=psb[:, h - sh:h - sh + n], scalar=wk, in1=acc[:, :n],
                                op0=ALU.mult, op1=ALU.add),
                                reads=[rp, rcw, racc], writes=[racc])
                        else:
                            P.op("dve", lambda e, acc=acc, psb=psb, wk=wk, sh=sh: e.scalar_tensor_tensor(
                                out=acc[:, sh:n], in0=psb[:, 0:n - sh], scalar=wk, in1=acc[:, sh:n],
                                op0=ALU.mult, op1=ALU.add),
                                reads=[rp, rcw, racc], writes=[racc])
                P.op("act", lambda e, ag=ag, sgt=sgt: e.activation(out=sgt[:, :n], in_=ag[:, :n], func=AF.Silu),
                     reads=[rag], writes=[rsg])
                P.op("pool", lambda e, j=j, sgt=sgt, av=av: e.tensor_tensor(
                    out=G[:, j, :n], in0=sgt[:, :n], in1=av[:, :n], op=ALU.mult),
                    reads=[rsg, rav], writes=[rG])

        def down(gi):
            o0, n, h = groups[gi]
            G = Gs[gi % 2]
            rG = (tag, "G", gi % 2)
            s = gi % NX
            rx = (tag, "xt", s)
            rxo = (tag, "xo")
            for oc in range(KC):
                ds_ = cnt["d"] % ND
                cnt["d"] += 1
                rw = (tag, "wdn", ds_)
                P.op("sp", lambda e, oc=oc, ds_=ds_: e.dma_start(
                    out=wdn[ds_][:, :, :].rearrange("p j o -> p (j o)"), in_=wdn_bf[oc]),
                    writes=[rw], dma_key=("wdn", ds_))
                b = cnt["o"] % 2
                cnt["o"] += 1
                rp = (tag, "pso", b)
                for j in range(NJ):
                    P.op("pe", lambda e, j=j, b=b, ds_=ds_: e.matmul(
                        ps_o[b][:, :n], lhsT=wdn[ds_][:, j, :], rhs=G[:, j, :n],
                        start=(j == 0), stop=(j == NJ - 1)),
                        reads=[rw, rG], writes=[rp])
                P.op("dve", lambda e, oc=oc, b=b: e.tensor_tensor(
                    out=xo[:, oc, :n], in0=ps_o[b][:, :n], in1=xt[s][:, oc, h:h + n], op=ALU.add),
                    reads=[rp, rx], writes=[rxo])
            if out_off == 0:
                store_group(cx, tag, xo, rxo, Xo, o0, n, gi == 0)
            else:
                lo = max(o0, out_off)
                if lo < o0 + n:
                    P.op("pool", lambda e: e.dma_start(out=Xo[:, :, lo - out_off:o0 + n - out_off], in_=xo[:, :, lo - o0:n]),
                         reads=[rxo], writes=[(tag, "Xout")], dma_key=("st", 0))

        ng = len(groups)
        pre(0)
        up(0)
        for gi in range(ng):
            if gi + 1 < ng:
                pre(gi + 1)
                up(gi + 1)
            down(gi)


CW31 = 31
HO = 30


def odd_phase(cx, cfg, X_in, X_out, w1_bf, w2_bf, dg_bf, vec_dram, tag):
    P = cx.P
    groups = make_groups(cfg.T, 482, 482, HO)
    Xi = X_in.rearrange("(kc p) t -> p kc t", p=128)
    Xo = X_out.rearrange("(kc p) t -> p kc t", p=128)
    with Phase(cx):
        vec, rvec = load_small(cx, tag, "vec", [128, 32], vec_dram)
        g_t = vec[:, 0:8]
        xt = [cx.sb([128, KC, 512], F32, "xt") for _ in range(2)]
        sqk = [cx.sb([128, 512], F32, "sqk") for _ in range(2)]
        red = cx.sb([128, 512], F32, "red")
        lr1 = cx.sb([128, 512], F32, "lr1")
        lr2 = cx.sb([128, 512], F32, "lr2")
        std = cx.sb([128, 512], F32, "std")
        rstd = cx.sb([128, 512], F32, "rstd")
        mean = cx.sb([128, 512], F32, "mean")
        m2 = cx.sb([128, 512], F32, "m2")
        t1 = [cx.sb([128, 512], F32, "t1") for _ in range(2)]
        Hs = [cx.sb([128, KC, 512], BF16, "H") for _ in range(2)]
        w1 = [cx.sb([128, 2, KC, 128], BF16, "w1") for _ in range(3)]
        dg = [cx.sb([128, CW31, 128], BF16, "dg") for _ in range(2)]
        Us = [cx.sb([128, KC, 512 + HO], BF16, "U") for _ in range(2)]
        sig = [cx.sb([128, 512], F32, "sig") for _ in range(2)]
        Vs = [cx.sb([128, KC, 512], F32, "V") for _ in range(2)]
        Ss = [cx.sb([128, KC, 512], BF16, "S") for _ in range(2)]
        w2 = [cx.sb([128, KC, 128], BF16, "w2") for _ in range(2)]
        xo = cx.sb([128, KC, 512], F32, "xo")
        ps = [cx.ps([128, 512], F32, "ps") for _ in range(8)]
        rps = [(tag, "ps", i) for i in range(8)]
        cnt = {"w1": 0, "ag": 0, "dg": 0, "c": 0, "w2": 0, "o": 0}

        def pre(gi):
            o0, n, h = groups[gi]
            nu = n + h
            s = gi % 2
            rx = (tag, "xt", s)
            P.op("sp", lambda e: e.dma_start(out=xt[s][:, :, :nu], in_=Xi[:, :, o0 - h:o0 + n]),
                 reads=[(tag, "Xin")], writes=[rx], dma_key=("xt", s))
            rH = (tag, "H", s)
            x_ = xt[s]
            P.op("act", lambda e: e.activation(out=red[:, :nu], in_=x_[:, 0, :nu], func=AF.Square),
                 reads=[rx], writes=[(tag, "red")])
            for kc in range(1, KC):
                q = sqk[kc % 2]
                rq = (tag, "sqk", kc % 2)
                P.op("act", lambda e, kc=kc, q=q: e.activation(out=q[:, :nu], in_=x_[:, kc, :nu], func=AF.Square),
                     reads=[rx], writes=[rq])
                P.op("pool", lambda e, q=q: e.tensor_tensor(out=red[:, :nu], in0=red[:, :nu], in1=q[:, :nu], op=ALU.add),
                     reads=[rq, (tag, "red")], writes=[(tag, "red")])
            P.op("pe", lambda e: e.matmul(ps[0][:, :nu], lhsT=cx.ones_f[:, :], rhs=red[:, :nu], start=True, stop=True),
                 reads=[(tag, "red"), ("c", "ones_f")], writes=[rps[0]])
            P.op("act", lambda e: e.activation(out=std[:, :nu], in_=ps[0][:, :nu], func=AF.Sqrt, scale=1.0 / D, bias=cx.eps_t[:, 0:1]),
                 reads=[rps[0], ("c", "eps")], writes=[(tag, "std")])
            P.op("dve", lambda e: e.reciprocal(out=rstd[:, :nu], in_=std[:, :nu]),
                 reads=[(tag, "std")], writes=[(tag, "rstd")])
            for kc in range(KC):
                P.op("dve", lambda e, kc=kc: e.scalar_tensor_tensor(
                    out=Hs[s][:, kc, :nu], in0=x_[:, kc, :nu], scalar=g_t[:, kc:kc + 1], in1=rstd[:, :nu],
                    op0=ALU.mult, op1=ALU.mult),
                    reads=[rx, (tag, "rstd"), rvec], writes=[rH])

        def glu(gi):
            o0, n, h = groups[gi]
            nu = n + h
            s = gi % 2
            H = Hs[s]
            rH = (tag, "H", s)
            U = Us[s]
            rU = (tag, "U", s)
            cs = HO - h
            if h == 0:
                P.op("pool", lambda e: e.memset(U[:, :, 0:HO], 0.0), writes=[rU])
            for j in range(KC):
                ws = cnt["w1"] % 3
                cnt["w1"] += 1
                rw = (tag, "w1", ws)
                P.op("sp", lambda e, j=j, ws=ws: e.dma_start(
                    out=w1[ws][:, :, :, :].rearrange("p a k o -> p (a k o)"), in_=w1_bf[j]),
                    writes=[rw], dma_key=("w1", ws))
                a = cnt["ag"] % 2
                cnt["ag"] += 1
                pa, pg = ps[1 + 2 * a], ps[2 + 2 * a]
                rpa, rpg = rps[1 + 2 * a], rps[2 + 2 * a]
                for gv, psb, rp in ((0, pa, rpa), (1, pg, rpg)):
                    for kc in range(KC):
                        P.op("pe", lambda e, gv=gv, kc=kc, psb=psb, ws=ws: e.matmul(
                            psb[:, :nu], lhsT=w1[ws][:, gv, kc, :], rhs=H[:, kc, :nu],
                            start=(kc == 0), stop=(kc == KC - 1)),
                            reads=[rw, rH], writes=[rp])
                sg_ = sig[a]
                rsg = (tag, "sig", a)
                P.op("act", lambda e, sg_=sg_, pg=pg: e.activation(out=sg_[:, :nu], in_=pg[:, :nu], func=AF.Sigmoid),
                     reads=[rpg], writes=[rsg])
                P.op("dve", lambda e, j=j, sg_=sg_, pa=pa: e.tensor_tensor(
                    out=U[:, j, cs:cs + nu], in0=pa[:, :nu], in1=sg_[:, :nu], op=ALU.mult),
                    reads=[rpa, rsg], writes=[rU])

        def conv(gi):
            o0, n, h = groups[gi]
            s = gi % 2
            U = Us[s]
            rU = (tag, "U", s)
            V = Vs[s]
            rV = (tag, "V", s)
            for j in range(KC):
                d_ = cnt["dg"] % 2
                cnt["dg"] += 1
                rd = (tag, "dg", d_)
                P.op("sp", lambda e, j=j, d_=d_: e.dma_start(
                    out=dg[d_][:, :, :].rearrange("p k o -> p (k o)"), in_=dg_bf[j]),
                    writes=[rd], dma_key=("dg", d_))
                b = cnt["c"] % 2
                cnt["c"] += 1
                pc, rpc = ps[5 + b], rps[5 + b]
                for k in range(CW31):
                    P.op("pe", lambda e, j=j, k=k, d_=d_, pc=pc: e.matmul(
                        pc[:, :n], lhsT=dg[d_][:, k, :], rhs=U[:, j, k:k + n],
                        start=(k == 0), stop=(k == CW31 - 1)),
                        reads=[rd, rU], writes=[rpc])
                P.op("act", lambda e, j=j, pc=pc: e.activation(
                    out=V[:, j, :n], in_=pc[:, :n], func=AF.Identity, bias=vec[:, 8 + j:9 + j], scale=1.0),
                    reads=[rpc, rvec], writes=[rV])

        def ln_a(gi):
            o0, n, h = groups[gi]
            s = gi % 2
            V = Vs[s]
            rV = (tag, "V", s)
            P.op("pool", lambda e: e.tensor_tensor(out=lr1[:, :n], in0=V[:, 0, :n], in1=V[:, 1, :n], op=ALU.add),
                 reads=[rV], writes=[(tag, "lr1")])
            for kc in range(2, KC):
                P.op("pool", lambda e, kc=kc: e.tensor_tensor(out=lr1[:, :n], in0=lr1[:, :n], in1=V[:, kc, :n], op=ALU.add),
                     reads=[rV, (tag, "lr1")], writes=[(tag, "lr1")])
            P.op("act", lambda e: e.activation(out=lr2[:, :n], in_=V[:, 0, :n], func=AF.Square),
                 reads=[rV], writes=[(tag, "lr2")])
            for kc in range(1, KC):
                q = sqk[kc % 2]
                rq = (tag, "sqk", kc % 2)
                P.op("act", lambda e, kc=kc, q=q: e.activation(out=q[:, :n], in_=V[:, kc, :n], func=AF.Square),
                     reads=[rV], writes=[rq])
                P.op("pool", lambda e, q=q: e.tensor_tensor(out=lr2[:, :n], in0=lr2[:, :n], in1=q[:, :n], op=ALU.add),
                     reads=[rq, (tag, "lr2")], writes=[(tag, "lr2")])

        def ln_b(gi):
            o0, n, h = groups[gi]
            s = gi % 2
            V = Vs[s]
            rV = (tag, "V", s)
            S = Ss[s]
            rS = (tag, "S", s)
            P.op("pe", lambda e: e.matmul(ps[0][:, :n], lhsT=cx.ones_f[:, :], rhs=lr1[:, :n], start=True, stop=True),
                 reads=[(tag, "lr1"), ("c", "ones_f")], writes=[rps[0]])
            P.op("pe", lambda e: e.matmul(ps[7][:, :n], lhsT=cx.ones_f[:, :], rhs=lr2[:, :n], start=True, stop=True),
                 reads=[(tag, "lr2"), ("c", "ones_f")], writes=[rps[7]])
            P.op("dve", lambda e: e.tensor_scalar(out=mean[:, :n], in0=ps[0][:, :n], scalar1=1.0 / D, scalar2=None, op0=ALU.mult),
                 reads=[rps[0]], writes=[(tag, "mean")])
            P.op("dve", lambda e: e.tensor_tensor(out=m2[:, :n], in0=mean[:, :n], in1=mean[:, :n], op=ALU.mult),
                 reads=[(tag, "mean")], writes=[(tag, "m2")])
            P.op("dve", lambda e: e.scalar_tensor_tensor(out=m2[:, :n], in0=ps[7][:, :n], scalar=1.0 / D, in1=m2[:, :n],
                                                         op0=ALU.mult, op1=ALU.subtract),
                 reads=[rps[7], (tag, "m2")], writes=[(tag, "m2")])
            P.op("act", lambda e: e.activation(out=std[:, :n], in_=m2[:, :n], func=AF.Sqrt, scale=1.0, bias=cx.eps_t[:, 0:1]),
                 reads=[(tag, "m2"), ("c", "eps")], writes=[(tag, "std")])
            P.op("dve", lambda e: e.reciprocal(out=rstd[:, :n], in_=std[:, :n]),
                 reads=[(tag, "std")], writes=[(tag, "rstd")])
            for kc in range(KC):
                tt = t1[kc % 2]
                rt = (tag, "t1", kc % 2)
                P.op("dve", lambda e, kc=kc, tt=tt: e.tensor_tensor(out=tt[:, :n], in0=V[:, kc, :n], in1=mean[:, :n], op=ALU.subtract),
                     reads=[rV, (tag, "mean")], writes=[rt])
                P.op("dve", lambda e, tt=tt: e.tensor_tensor(out=tt[:, :n], in0=tt[:, :n], in1=rstd[:, :n], op=ALU.mult),
                     reads=[rt, (tag, "rstd")], writes=[rt])
                P.op("act", lambda e, kc=kc, tt=tt: e.activation(
                    out=S[:, kc, :n], in_=tt[:, :n], func=AF.Silu, scale=vec[:, 16 + kc:17 + kc], bias=vec[:, 24 + kc:25 + kc]),
                    reads=[rt, rvec], writes=[rS])

        def pw2(gi):
            o0, n, h = groups[gi]
            s = gi % 2
            S = Ss[s]
            rS = (tag, "S", s)
            rx = (tag, "xt", s)
            rxo = (tag, "xo")
            for oc in range(KC):
                w_ = cnt["w2"] % 2
                cnt["w2"] += 1
                rw = (tag, "w2", w_)
                P.op("sp", lambda e, oc=oc, w_=w_: e.dma_start(
                    out=w2[w_][:, :, :].rearrange("p k o -> p (k o)"), in_=w2_bf[oc]),
                    writes=[rw], dma_key=("w2", w_))
                b = cnt["o"] % 2
                cnt["o"] += 1
                po, rpo = ps[1 + b], rps[1 + b]
                for kc in range(KC):
                    P.op("pe", lambda e, kc=kc, w_=w_, po=po: e.matmul(
                        po[:, :n], lhsT=w2[w_][:, kc, :], rhs=S[:, kc, :n],
                        start=(kc == 0), stop=(kc == KC - 1)),
                        reads=[rw, rS], writes=[rpo])
                P.op("dve", lambda e, oc=oc, po=po: e.tensor_tensor(
                    out=xo[:, oc, :n], in0=po[:, :n], in1=xt[s][:, oc, h:h + n], op=ALU.add),
                    reads=[rpo, rx], writes=[rxo])
            store_group(cx, tag, xo, rxo, Xo, o0, n, gi == 0)

        ng = len(groups)
        pre(0)
        glu(0)
        conv(0)
        for gi in range(ng):
            ln_a(gi)
            if gi + 1 < ng:
                pre(gi + 1)
                glu(gi + 1)
            ln_b(gi)
            if gi + 1 < ng:
                conv(gi + 1)
            pw2(gi)


HP = 15
POOL_W = (2, 4, 8, 16)


def even_pre_phase(cx, cfg, X_in, wqk_bf, wf_bf, wv_bf, wp_bf, vec_dram, bf_dram, tag):
    P = cx.P
    dr = cx.dram
    groups = make_groups(cfg.T, 512, 497, HP)
    Xi = X_in.rearrange("(kc p) t -> p kc t", p=128)
    with Phase(cx):
        vec, rvec = load_small(cx, tag, "vec", [128, 16], vec_dram)
        invc, rinvc = load_small(cx, tag, "invc", [128, 4 * 32], dr["invc"])
        bft = cx.sb([8, 1], F32, "bft")
        nbf = cx.sb([8, 1], F32, "nbf")
        gq8 = cx.sb([128, 1], F32, "gq8")
        P.op("sp", lambda e: e.dma_start(out=bft[:, :], in_=bf_dram), writes=[(tag, "bft")], dma_key=("G", "small"))
        P.op("dve", lambda e: e.tensor_scalar(out=nbf[:, :], in0=bft[:, :], scalar1=-1.0, scalar2=None, op0=ALU.mult),
             reads=[(tag, "bft")], writes=[(tag, "nbf")])
        P.op("dve", lambda e: e.tensor_scalar(out=gq8[:, :], in0=vec[:, 8:9], scalar1=0.125, scalar2=None, op0=ALU.mult),
             reads=[rvec], writes=[(tag, "gq8")])
        g_t = vec[:, 0:8]
        wv = cx.sb([128, KC, 512], BF16, "wv")
        wf = cx.sb([128, KC, 8], BF16, "wf")
        wp = cx.sb([128, 4, 128], BF16, "wp")
        P.op("sp", lambda e: e.dma_start(out=wv[:, :, :].rearrange("p k o -> p (k o)"), in_=wv_bf), writes=[(tag, "wv")], dma_key=("G", "small"))
        P.op("sp", lambda e: e.dma_start(out=wf[:, :, :].rearrange("p k o -> p (k o)"), in_=wf_bf), writes=[(tag, "wf")], dma_key=("G", "small"))
        P.op("sp", lambda e: e.dma_start(out=wp[:, :, :].rearrange("p k o -> p (k o)"), in_=wp_bf), writes=[(tag, "wp")], dma_key=("G", "small"))
        xt = [cx.sb([128, KC, 512], F32, "xt") for _ in range(2)]
        sqk = [cx.sb([128, 512], F32, "sqk") for _ in range(2)]
        red = cx.sb([128, 512], F32, "red")
        std = cx.sb([128, 512], F32, "std")
        rstd = cx.sb([128, 512], F32, "rstd")
        Hs = [cx.sb([128, KC, 512], BF16, "H") for _ in range(2)]
        wq = [cx.sb([128, KC, 128], BF16, "wq") for _ in range(3)]
        qsq = [cx.sb([128, 512], F32, "qsq") for _ in range(2)]
        qstd = [cx.sb([128, 512], F32, "qstd") for _ in range(2)]
        qo = [cx.sb([128, 512], BF16, "qo") for _ in range(2)]
        vo = [cx.sb([128, 512], BF16, "vo") for _ in range(2)]
        fe = cx.sb([8, 512], F32, "fe")
        fo = [cx.sb([8, 512], F32, "fo") for _ in range(2)]
        ub = [cx.sb([128, 512], F32, "ub") for _ in range(2)]
        sa = cx.sb([128, 512], F32, "sa")
        sb_ = cx.sb([128, 512], F32, "sb")
        mx = [cx.sb([128, 512], BF16, "mx") for _ in range(2)]
        po = [cx.sb([128, 512], BF16, "po") for _ in range(2)]
        ps = [cx.ps([128, 512], F32, "ps") for _ in range(8)]
        rps = [(tag, "ps", i) for i in range(8)]
        cnt = {"w": 0, "pj": 0, "q": 0, "v": 0, "u": 0, "f": 0}

        def pre(gi):
            o0, n, h = groups[gi]
            nu = n + h
            s = gi % 2
            rx = (tag, "xt", s)
            x_ = xt[s]
            P.op("sp", lambda e: e.dma_start(out=x_[:, :, :nu], in_=Xi[:, :, o0 - h:o0 + n]),
                 reads=[(tag, "Xin")], writes=[rx], dma_key=("xt", s))
            P.op("act", lambda e: e.activation(out=red[:, :nu], in_=x_[:, 0, :nu], func=AF.Square),
                 reads=[rx], writes=[(tag, "red")])
            for kc in range(1, KC):
                q = sqk[kc % 2]
                rq = (tag, "sqk", kc % 2)
                P.op("act", lambda e, kc=kc, q=q: e.activation(out=q[:, :nu], in_=x_[:, kc, :nu], func=AF.Square),
                     reads=[rx], writes=[rq])
                P.op("pool", lambda e, q=q: e.tensor_tensor(out=red[:, :nu], in0=red[:, :nu], in1=q[:, :nu], op=ALU.add),
                     reads=[rq, (tag, "red")], writes=[(tag, "red")])
            P.op("pe", lambda e: e.matmul(ps[0][:, :nu], lhsT=cx.ones_f[:, :], rhs=red[:, :nu], start=True, stop=True),
                 reads=[(tag, "red"), ("c", "ones_f")], writes=[rps[0]])
            P.op("act", lambda e: e.activation(out=std[:, :nu], in_=ps[0][:, :nu], func=AF.Sqrt, scale=1.0 / D, bias=cx.eps_t[:, 0:1]),
                 reads=[rps[0], ("c", "eps")], writes=[(tag, "std")])
            P.op("dve", lambda e: e.reciprocal(out=rstd[:, :nu], in_=std[:, :nu]),
                 reads=[(tag, "std")], writes=[(tag, "rstd")])
            for kc in range(KC):
                P.op("dve", lambda e, kc=kc: e.scalar_tensor_tensor(
                    out=Hs[s][:, kc, :nu], in0=x_[:, kc, :nu], scalar=g_t[:, kc:kc + 1], in1=rstd[:, :nu],
                    op0=ALU.mult, op1=ALU.mult),
                    reads=[rx, (tag, "rstd"), rvec], writes=[(tag, "H", s)])

        def proj(gi):
            o0, n, h = groups[gi]
            nu = n + h
            s = gi % 2
            H = Hs[s]
            rH = (tag, "H", s)
            t0 = o0 - h
            lo = max(o0, HALO)
            hi = o0 + n
            own = lo < hi
            c_lo, c_hi = lo - t0, hi - t0
            tk = lo - HALO
            for c in range(12):
                if c < 8 and not own:
                    continue
                ws = cnt["w"] % 3
                cnt["w"] += 1
                rw = (tag, "wq", ws)
                P.op("sp", lambda e, c=c, ws=ws: e.dma_start(
                    out=wq[ws][:, :, :].rearrange("p k o -> p (k o)"), in_=wqk_bf[c]),
                    writes=[rw], dma_key=("wq", ws))
                b = cnt["pj"] % 2
                cnt["pj"] += 1
                pj, rpj = ps[1 + b], rps[1 + b]
                for kc in range(KC):
                    P.op("pe", lambda e, kc=kc, ws=ws, pj=pj: e.matmul(
                        pj[:, :nu], lhsT=wq[ws][:, kc, :], rhs=H[:, kc, :nu],
                        start=(kc == 0), stop=(kc == KC - 1)),
                        reads=[rw, rH], writes=[rpj])
                if c < 8:
                    a = cnt["q"] % 2
                    cnt["q"] += 1
                    rqs, rqd, rqo = (tag, "qsq", a), (tag, "qstd", a), (tag, "qo", a)
                    P.op("act", lambda e, a=a, pj=pj: e.activation(out=qsq[a][:, :nu], in_=pj[:, :nu], func=AF.Square),
                         reads=[rpj], writes=[rqs])
                    P.op("pe", lambda e, a=a: e.matmul(ps[3][:, :nu], lhsT=cx.ones2_f[:, :], rhs=qsq[a][:, :nu], start=True, stop=True),
                         reads=[rqs, ("c", "ones2_f")], writes=[rps[3]])
                    P.op("act", lambda e, a=a: e.activation(out=qstd[a][:, :nu], in_=ps[3][:, :nu], func=AF.Sqrt, scale=1.0 / 64, bias=cx.eps_t[:, 0:1]),
                         reads=[rps[3], ("c", "eps")], writes=[rqd])
                    P.op("dve", lambda e, a=a: e.reciprocal(out=qstd[a][:, :nu], in_=qstd[a][:, :nu]),
                         reads=[rqd], writes=[rqd])
                    gsc = gq8[:, 0:1] if c < 4 else vec[:, 9:10]
                    P.op("dve", lambda e, a=a, pj=pj, gsc=gsc: e.scalar_tensor_tensor(
                        out=qo[a][:, :nu], in0=pj[:, :nu], scalar=gsc, in1=qstd[a][:, :nu], op0=ALU.mult, op1=ALU.mult),
                        reads=[rpj, rqd, rvec, (tag, "gq8")], writes=[rqo])
                    dst = dr["Qc"] if c < 4 else dr["Kc"]
                    cc = c % 4
                    P.op("pool", lambda e, a=a, dst=dst, cc=cc: e.dma_start(
                        out=dst[cc * 128:(cc + 1) * 128, tk:tk + (hi - lo)], in_=qo[a][:, c_lo:c_hi]),
                        reads=[rqo], writes=[(tag, "QKc")], dma_key=("qo", a))
                else:
                    g = c - 8
                    a = cnt["u"] % 2
                    cnt["u"] += 1
                    u_ = ub[a]
                    ru = (tag, "ub", a)
                    P.op("act", lambda e, u_=u_, pj=pj: e.activation(out=u_[:, :nu], in_=pj[:, :nu], func=AF.Identity),
                         reads=[rpj], writes=[ru])
                    src, rsrc = u_, ru
                    bufs = [(sa, (tag, "sa")), (sb_, (tag, "sb"))]
                    for st in range(g + 1):
                        sh = 1 << st
                        dstt, rdst = bufs[st % 2]
                        P.op("dve", lambda e, src=src, dstt=dstt, sh=sh: e.tensor_tensor(
                            out=dstt[:, sh:nu], in0=src[:, sh:nu], in1=src[:, 0:nu - sh], op=ALU.add),
                            reads=[rsrc], writes=[rdst])
                        P.op("pool", lambda e, src=src, dstt=dstt, sh=sh: e.tensor_copy(out=dstt[:, 0:sh], in_=src[:, 0:sh]),
                             reads=[rsrc], writes=[rdst])
                        src, rsrc = dstt, rdst
                    w_ = POOL_W[g]
                    m_ = mx[a]
                    rm = (tag, "mx", a)
                    P.op("dve", lambda e, src=src, u_=u_, m_=m_, w_=w_: e.scalar_tensor_tensor(
                        out=m_[:, :n], in0=src[:, h:nu], scalar=1.0 / w_, in1=u_[:, h:nu], op0=ALU.mult, op1=ALU.subtract),
                        reads=[rsrc, ru], writes=[rm])
                    if gi == 0:
                        tt = sqk[0]
                        rt = (tag, "sqk", 0)
                        P.op("dve", lambda e, src=src, g=g, tt=tt: e.tensor_tensor(
                            out=tt[:, 0:32], in0=src[:, HALO:HALO + 32], in1=invc[:, g * 32:(g + 1) * 32], op=ALU.mult),
                            reads=[rsrc, rinvc], writes=[rt])
                        P.op("dve", lambda e, u_=u_, m_=m_, tt=tt: e.tensor_tensor(
                            out=m_[:, HALO:HALO + 32], in0=tt[:, 0:32], in1=u_[:, HALO:HALO + 32], op=ALU.subtract),
                            reads=[rt, ru, rm], writes=[rm])
                    P.op("pe", lambda e, g=g, m_=m_: e.matmul(ps[7][:, :n], lhsT=wp[:, g, :], rhs=m_[:, :n], start=True, stop=True),
                         reads=[rm, (tag, "wp")], writes=[rps[7]])
                    p_ = po[a]
                    rp_ = (tag, "po", a)
                    P.op("act", lambda e, g=g, p_=p_: e.activation(out=p_[:, :n], in_=ps[7][:, :n], func=AF.Identity,
                                                                   scale=vec[:, 10 + g:11 + g]),
                         reads=[rps[7], rvec], writes=[rp_])
                    P.op("pool", lambda e, g=g, p_=p_: e.dma_start(out=dr["Pout"][g * 128:(g + 1) * 128, o0:o0 + n], in_=p_[:, :n]),
                         reads=[rp_], writes=[(tag, "Pout")], dma_key=("po", a))
            if not own:
                return
            for kc in range(KC):
                P.op("pe", lambda e, kc=kc: e.matmul(ps[6][0:8, :nu], lhsT=wf[:, kc, :], rhs=H[:, kc, :nu],
                                                     start=(kc == 0), stop=(kc == KC - 1)),
                     reads=[(tag, "wf"), rH], writes=[rps[6]])
            a = cnt["f"] % 2
            cnt["f"] += 1
            P.op("act", lambda e: e.activation(out=fe[:, :nu], in_=ps[6][0:8, :nu], func=AF.Exp, scale=-1.0, bias=nbf[:, 0:1]),
                 reads=[rps[6], (tag, "nbf")], writes=[(tag, "fe")])
            P.op("act", lambda e, a=a: e.activation(out=fo[a][:, :nu], in_=fe[:, :nu], func=AF.Ln, scale=1.0, bias=cx.one_t[0:8, 0:1]),
                 reads=[(tag, "fe"), ("c", "one")], writes=[(tag, "fo", a)])
            P.op("pool", lambda e, a=a: e.dma_start(out=dr["Fc"][:, tk:tk + (hi - lo)], in_=fo[a][:, c_lo:c_hi]),
                 reads=[(tag, "fo", a)], writes=[(tag, "Fc")], dma_key=("fo", a))
            c0 = c_lo
            while c0 < c_hi:
                m = min(128, c_hi - c0)
                b = cnt["v"] % 2
                cnt["v"] += 1
                pv, rpv = ps[4 + b], rps[4 + b]
                for kc in range(KC):
                    P.op("pe", lambda e, kc=kc, c0=c0, m=m, pv=pv: e.matmul(
                        pv[:m, :], lhsT=H[:, kc, c0:c0 + m], rhs=wv[:, kc, :], start=(kc == 0), stop=(kc == KC - 1)),
                        reads=[rH, (tag, "wv")], writes=[rpv])
                v_ = vo[b]
                rv = (tag, "vo", b)
                P.op("act", lambda e, m=m, pv=pv, v_=v_: e.activation(out=v_[:m, :], in_=pv[:m, :], func=AF.Copy),
                     reads=[rpv], writes=[rv])
                tok = tk + (c0 - c_lo)
                P.op("pool", lambda e, m=m, v_=v_, tok=tok: e.dma_start(out=dr["Vc"][tok:tok + m, :], in_=v_[:m, :]),
                     reads=[rv], writes=[(tag, "Vc")], dma_key=("vo", b))
                c0 += m

        ng = len(groups)
        pre(0)
        for gi in range(ng):
            if gi + 1 < ng:
                pre(gi + 1)
            proj(gi)


GROUPS4 = [[0, 1, 2, 3], [4, 5, 6, 7]]
import os
ADBG = int(os.environ.get("ATT_DBG", "5"))


def attn_phase(cx, cfg, tag):
    P = cx.P
    dr = cx.dram
    CH_, SEQ_, NKB, NQB = cfg.CH, cfg.SEQ, cfg.NKB, cfg.NQB
    segl = SEQ_ // 64
    nbs = CH_ // 128
    q4 = CH_ // 4
    for cc in range(4):
        for nm in ("Q", "K"):
            P.op("pool", lambda e, nm=nm, cc=cc: e.collective_compute(
                "AllGather", ALU.bypass, replica_groups=GROUPS4,
                ins=[dr[nm + "c"][cc * 128:(cc + 1) * 128, :].opt()], outs=[dr[nm + "g"][cc * 512:(cc + 1) * 512, :].opt()]),
                reads=[(tag, nm + "c")], writes=[(tag, nm + "g")], dma_key=("G", "cc" + nm), inc=1)
        P.op("pool", lambda e, cc=cc: e.collective_compute(
            "AllGather", ALU.bypass, replica_groups=GROUPS4,
            ins=[dr["Vc"][cc * q4:(cc + 1) * q4, :].opt()], outs=[dr["Vg"][cc * CH_:(cc + 1) * CH_, :].opt()]),
            reads=[(tag, "Vc")], writes=[(tag, "Vg")], dma_key=("G", "ccV"), inc=1)
    P.op("pool", lambda e: e.collective_compute(
        "AllGather", ALU.bypass, replica_groups=GROUPS4, ins=[dr["Fc"].opt()], outs=[dr["Fg"].opt()]),
        reads=[(tag, "Fc")], writes=[(tag, "Fg")], dma_key=("G", "ccF"), inc=1)
    cx.chk("attn_ag")
    with Phase(cx):
        KT = [cx.sb([128, SEQ_], BF16, "KT") for _ in range(2)]
        VT = [cx.sb([128, NKB, 128], BF16, "VT") for _ in range(2)]
        Lm = cx.sb([128, 128], F32, "Lm")
        sp = cx.sb([128, segl], F32, "sp")
        onesl = cx.sb([128, segl], F32, "onesl")
        cl = cx.sb([128, segl], F32, "cl")
        offs = cx.sb([128, 1], F32, "offs")
        r1 = cx.sb([128, segl], F32, "r1")
        hi = cx.sb([128, segl], BF16, "hi")
        mid = cx.sb([128, segl], BF16, "mid")
        lo = cx.sb([128, segl], BF16, "lo")
        nhi = cx.sb([128, segl], BF16, "nhi")
        nmid = cx.sb([128, segl], BF16, "nmid")
        nlo = cx.sb([128, segl], BF16, "nlo")
        onesb = cx.sb([128, segl], BF16, "onesb")
        zt = cx.sb([128, 4, HALO], BF16, "zt")
        NQT = 3
        qt = [cx.sb([128, 512], BF16, "qt") for _ in range(NQT)]
        NPT = 4
        pt = [cx.sb([128, 512], BF16, "pt") for _ in range(NPT)]
        rec = [cx.sb([64, 512], F32, "rec") for _ in range(2)]
        ost = [cx.sb([64, 512], BF16, "ost") for _ in range(2)]
        NS = 4
        ps_s = [cx.ps([128, 512], F32, "pss") for _ in range(NS)]
        ps_o = [cx.ps([128, 512], F32, "pso") for _ in range(2)]
        ps_m = cx.ps([128, 512], F32, "psm")

        P.op("dve", lambda e: e.memset(Lm[:, :], 1.0), writes=[(tag, "Lm")])
        P.op("pool", lambda e: e.affine_select(out=Lm[:, :], in_=Lm[:, :], pattern=[[1, 128]], compare_op=ALU.is_gt,
                                               fill=0.0, base=0, channel_multiplier=-1),
             reads=[(tag, "Lm")], writes=[(tag, "Lm")])
        P.op("dve", lambda e: e.memset(Lm[0:64, 64:128], 0.0), reads=[(tag, "Lm")], writes=[(tag, "Lm")])
        P.op("dve", lambda e: e.memset(onesl[:, :], 1.0), writes=[(tag, "onesl")])
        P.op("dve", lambda e: e.memset(onesb[:, :], 1.0), writes=[(tag, "onesb")])
        P.op("dve", lambda e: e.memset(zt[:, :, :], 0.0), writes=[(tag, "zt")])
        P.op("sp", lambda e: e.dma_start(out=dr["Og"][0:512, CH_ - HALO:CH_].rearrange("(a p) t -> p a t", p=128), in_=zt[:, :, :]),
             reads=[(tag, "zt")], writes=[(tag, "Ogpad")], dma_key=("G", "small"))
        for h in range(2):
            P.op("dve", lambda e, h=h: e.memset(VT[h][:, :, 64:128], 1.0), writes=[(tag, "VTo", h)])

        def loc_k(e, nm):
            i4 = pid4(e)
            src = dr[nm + "g"].rearrange("(c s r) t -> c s r t", c=4, s=4)[i4, :, :, :]
            return e.dma_start(out=dr[nm + "l"].rearrange("r (s t) -> s r t", s=4), in_=src)

        def loc_v(e, j):
            i4 = pid4(e)
            src = dr["Vg"][j * CH_:(j + 1) * CH_, :].rearrange("(s n) (i c) -> s n i c", s=4, i=4)[:, :, i4, :]
            return e.dma_start(out=dr["Vl"].rearrange("(s j n) c -> j s n c", s=4, j=4)[j], in_=src)

        def loc_f(e):
            i4 = pid4(e)
            src = dr["Fg"].rearrange("(s i h) t -> s i h t", s=4, i=4)[:, i4, :, :]
            return e.dma_start(out=dr["Fl"].rearrange("(s h) t -> s h t", s=4), in_=src)

        P.op("sp", loc_f, reads=[(tag, "Fg")], writes=[(tag, "Fl")], dma_key=("G", "locf"))
        P.op("sp", lambda e: loc_k(e, "K"), reads=[(tag, "Kg")], writes=[(tag, "Kl")], dma_key=("G", "lock"))
        P.op("sp", lambda e: loc_k(e, "Q"), reads=[(tag, "Qg")], writes=[(tag, "Ql")], dma_key=("G", "locq"))
        for j in range(4):
            P.op("sp", lambda e, j=j: loc_v(e, j), reads=[(tag, "Vg")], writes=[(tag, "Vl")], dma_key=("G", "locv"))

        cx.chk("attn_loc")

        def ld_k(e, h, s):
            return e.dma_start(out=KT[h][0:64, s * CH_:(s + 1) * CH_], in_=dr["Kl"][h * 64:(h + 1) * 64, s * CH_:(s + 1) * CH_])

        def ld_v(e, h, s):
            src = dr["Vl"][s * CH_:(s + 1) * CH_, h * 64:(h + 1) * 64].rearrange("(b p) c -> p b c", p=128)
            return e.dma_start(out=VT[h][:, s * nbs:(s + 1) * nbs, 0:64], in_=src)

        def ld_f(e, h, s):
            src = dr["Fl"][s * 2 + h:s * 2 + h + 1, :].rearrange("o (j t) -> (o j) t", t=segl)
            return e.dma_start(out=sp[h * 64 + s * 16:h * 64 + (s + 1) * 16, :], in_=src)

        for h in range(2):
            for s in range(4):
                P.op("sp", lambda e, h=h, s=s: ld_f(e, h, s), reads=[(tag, "Fl")], writes=[(tag, "sp", h, s)],
                     dma_key=("G", "ldf"))
        for h in range(2):
            for s in range(4):
                P.op("sp", lambda e, h=h, s=s: ld_k(e, h, s), reads=[(tag, "Kl")], writes=[(tag, "KT", h, s)],
                     dma_key=("G", "ldk"))
        rsp = [(tag, "sp", h, s) for h in range(2) for s in range(4)]
        P.op("dve", lambda e: e.tensor_tensor_scan(out=cl[:, :], data0=onesl[:, :], data1=sp[:, :], initial=0.0,
                                                   op0=ALU.mult, op1=ALU.add),
             reads=rsp + [(tag, "onesl")], writes=[(tag, "cl")])
        P.op("pe", lambda e: e.matmul(ps_m[:, 0:2], lhsT=Lm[:, :], rhs=cl[:, segl - 2:segl], start=True, stop=True),
             reads=[(tag, "cl"), (tag, "Lm")], writes=[(tag, "psm")])
        P.op("dve", lambda e: e.tensor_copy(out=offs[:, :], in_=ps_m[:, 1:2]), reads=[(tag, "psm")], writes=[(tag, "offs")])
        P.op("dve", lambda e: e.tensor_scalar(out=cl[:, :], in0=cl[:, :], scalar1=offs[:, 0:1], scalar2=None, op0=ALU.add),
             reads=[(tag, "cl"), (tag, "offs")], writes=[(tag, "cl")])
        P.op("dve", lambda e: e.tensor_copy(out=hi[:, :], in_=cl[:, :]), reads=[(tag, "cl")], writes=[(tag, "hi")])
        P.op("dve", lambda e: e.tensor_tensor(out=r1[:, :], in0=cl[:, :], in1=hi[:, :], op=ALU.subtract),
             reads=[(tag, "cl"), (tag, "hi")], writes=[(tag, "r1")])
        P.op("dve", lambda e: e.tensor_copy(out=mid[:, :], in_=r1[:, :]), reads=[(tag, "r1")], writes=[(tag, "mid")])
        P.op("dve", lambda e: e.tensor_tensor(out=r1[:, :], in0=r1[:, :], in1=mid[:, :], op=ALU.subtract),
             reads=[(tag, "r1"), (tag, "mid")], writes=[(tag, "r1")])
        P.op("dve", lambda e: e.tensor_copy(out=lo[:, :], in_=r1[:, :]), reads=[(tag, "r1")], writes=[(tag, "lo")])
        for src_, dst_, nm in ((hi, nhi, "nhi"), (mid, nmid, "nmid"), (lo, nlo, "nlo")):
            P.op("dve", lambda e, src_=src_, dst_=dst_: e.tensor_scalar(out=dst_[:, :], in0=src_[:, :], scalar1=-1.0, scalar2=None, op0=ALU.mult),
                 reads=[(tag, "hi"), (tag, "mid"), (tag, "lo")], writes=[(tag, nm)])
        cx.chk("attn_cs")
        k = 0
        for h in range(2):
            for r, (tk_, tq_) in enumerate(((hi, onesb), (mid, onesb), (lo, onesb), (onesb, nhi), (onesb, nmid), (onesb, nlo))):
                for dst_nm, t_ in (("AugK", tk_), ("AugQ", tq_)):
                    P.op("sp", lambda e, h=h, r=r, dst_nm=dst_nm, t_=t_: e.dma_start(
                        out=dr[dst_nm][h * 6 + r:h * 6 + r + 1, :].rearrange("o (j t) -> (o j) t", t=segl), in_=t_[h * 64:(h + 1) * 64, :]),
                        reads=[(tag, "hi"), (tag, "mid"), (tag, "lo"), (tag, "nhi"), (tag, "nmid"), (tag, "nlo"), (tag, "onesb")],
                        writes=[(tag, dst_nm, h)], dma_key=("G", "aug"))
                    k += 1
        cx.chk("attn_aug")
        for h in range(2):
            P.op("sp", lambda e, h=h: e.dma_start(out=KT[h][64:70, :], in_=dr["AugK"][h * 6:(h + 1) * 6, :]),
                 reads=[(tag, "AugK", h)], writes=[(tag, "KTa", h)], dma_key=("G", "ldka"))
        cx.chk("attn_ka")
        for h in range(2):
            for s in range(4):
                P.op("sp", lambda e, h=h, s=s: ld_v(e, h, s), reads=[(tag, "Vl")], writes=[(tag, "VT", h, s)],
                     dma_key=("G", "ldv"))

        cx.chk("attn_ld")
        steps = []
        DEPTH = 2
        state = {"qi": -1, "oi": -1}
        qinfo = {}

        def load_q(h, qb):
            state["qi"] += 1
            qs = state["qi"] % NQT
            rq = (tag, "qt", qs)
            Q0 = qb * 512
            s = Q0 // CH_
            c0 = Q0 % CH_

            P.op("sp", lambda e: e.dma_start(out=qt[qs][0:64, :], in_=dr["Ql"][h * 64:(h + 1) * 64, Q0:Q0 + 512]),
                 reads=[(tag, "Ql")], writes=[rq], dma_key=("ldq", qs))
            P.op("sp", lambda e: e.dma_start(out=qt[qs][64:70, :], in_=dr["AugQ"][h * 6:(h + 1) * 6, Q0:Q0 + 512]),
                 reads=[(tag, "AugQ", h)], writes=[(tag, "qta", qs)], dma_key=("ldqa", qs))
            qinfo[(h, qb)] = qs

        def qk(i):
            h, qb, kb, nk = steps[i]
            if kb == 0:
                load_q(h, qb)
            qs = qinfo[(h, qb)]
            sb_i = i % NS
            j = kb - 4 * qb
            a = 128 * j if j > 0 else 0
            diag = j >= 0
            s_src = (kb * 128) // CH_
            rk = [(tag, "KT", h, s_src), (tag, "KTa", h)]
            P.op("pe", lambda e: e.matmul(ps_s[sb_i][:, a:512], lhsT=KT[h][0:70, kb * 128:(kb + 1) * 128],
                                          rhs=qt[qs][0:70, a:512], start=True, stop=not diag),
                 reads=rk + [(tag, "qt", qs), (tag, "qta", qs)], writes=[(tag, "pss", sb_i)])
            if diag and ADBG >= 2:
                P.op("pe", lambda e: e.matmul(ps_s[sb_i][:, a:a + 128], lhsT=cx.ident_b[:, :], rhs=cx.maskb[:, :],
                                              start=False, stop=True),
                     reads=[("c", "ident"), ("c", "maskb")], writes=[(tag, "pss", sb_i)])
            pi = i % NPT
            if ADBG < 3:
                return
            P.op("act", lambda e: e.activation(out=pt[pi][:, a:512], in_=ps_s[sb_i][:, a:512], func=AF.Exp),
                 reads=[(tag, "pss", sb_i)], writes=[(tag, "pt", pi)])

        def pv(i):
            if ADBG < 4:
                return
            h, qb, kb, nk = steps[i]
            j = kb - 4 * qb
            a = 128 * j if j > 0 else 0
            pi = i % NPT
            s_src = (kb * 128) // CH_
            if kb == 0:
                state["oi"] += 1
            ob = state["oi"] % 2
            P.op("pe", lambda e: e.matmul(ps_o[ob][:, a:512], lhsT=VT[h][:, kb, :], rhs=pt[pi][:, a:512],
                                          start=(kb == 0), stop=(kb == nk - 1)),
                 reads=[(tag, "VT", h, s_src), (tag, "VTo", h), (tag, "pt", pi)], writes=[(tag, "pso", ob)])
            if kb == nk - 1 and ADBG >= 5:
                Q0 = qb * 512
                P.op("dve", lambda e: e.reciprocal(out=rec[ob][:, :], in_=ps_o[ob][64:128, :]),
                     reads=[(tag, "pso", ob)], writes=[(tag, "rec", ob)])
                P.op("dve", lambda e: e.tensor_tensor(out=ost[ob][:, :], in0=ps_o[ob][0:64, :], in1=rec[ob][:, :], op=ALU.mult),
                     reads=[(tag, "pso", ob), (tag, "rec", ob)], writes=[(tag, "ost", ob)])
                jc, c0_ = Q0 // CH_, Q0 % CH_
                P.op("pool", lambda e: e.dma_start(out=dr["Oc"][jc * 128 + h * 64:jc * 128 + (h + 1) * 64, c0_:c0_ + 512], in_=ost[ob][:, :]),
                     reads=[(tag, "ost", ob)], writes=[(tag, "Oc")], dma_key=("ost", ob))

        for hh in range(2):
            base = len(steps)
            for qb in range(NQB):
                nk = 4 * qb + 4
                for kb in range(nk):
                    steps.append((hh, qb, kb, nk))
            n = len(steps)
            for i in range(base, n + DEPTH):
                if i < n:
                    qk(i)
                if i - DEPTH >= base:
                    pv(i - DEPTH)
            if hh == 0:
                P.barrier()
    for jc in range(4):
        P.op("pool", lambda e, jc=jc: e.collective_compute(
            "AllGather", ALU.bypass, replica_groups=GROUPS4,
            ins=[dr["Oc"][jc * 128:(jc + 1) * 128, :].opt()], outs=[dr["Og"][(jc + 1) * 512:(jc + 2) * 512, :].opt()]),
            reads=[(tag, "Oc")], writes=[(tag, "Og")], dma_key=("G", "ccO"), inc=1)


def even_post_phase(cx, cfg, X_in, X_out, wo_bf, tag):
    P = cx.P
    dr = cx.dram
    groups = make_groups(cfg.T, 512, 512, 0)
    Xi = X_in.rearrange("(kc p) t -> p kc t", p=128)
    Xo = X_out.rearrange("(kc p) t -> p kc t", p=128)
    Alv = dr["Al"].rearrange("(kc p) t -> p kc t", p=128)
    Pov = dr["Pout"].rearrange("(kc p) t -> p kc t", p=128)

    Og3 = dr["Og"].rearrange("(j r) t -> j r t", r=512)

    def loc_a(e):
        i4 = pid4(e)
        return e.dma_start(out=dr["Al"][:, HALO:], in_=Og3[i4 + 1, :, :])

    def loc_h(e):
        i4 = pid4(e)
        return e.dma_start(out=dr["Al"][:, 0:HALO], in_=Og3[i4, :, cfg.CH - HALO:cfg.CH])
    P.op("sp", loc_a, reads=[(tag, "Og")], writes=[(tag, "Al")], dma_key=("G", "loca"))
    P.op("sp", loc_h, reads=[(tag, "Og")], writes=[(tag, "Alh")], dma_key=("G", "loca"))
    with Phase(cx):
        xt = [cx.sb([128, KC, 512], F32, "xt") for _ in range(2)]
        At = [cx.sb([128, KC, 512], BF16, "At") for _ in range(2)]
        wo = [cx.sb([128, KC, 128], BF16, "wo") for _ in range(3)]
        xo = [cx.sb([128, KC, 512], F32, "xo") for _ in range(2)]
        ps = [cx.ps([128, 512], F32, "ps") for _ in range(2)]
        cnt = {"w": 0, "o": 0}
        for gi, (o0, n, h) in enumerate(groups):
            s = gi % 2
            rx, rA, rxo = (tag, "xt", s), (tag, "At", s), (tag, "xo", s)
            P.op("sp", lambda e, s=s, o0=o0, n=n: e.dma_start(out=xt[s][:, :, :n], in_=Xi[:, :, o0:o0 + n]),
                 reads=[(tag, "Xin")], writes=[rx], dma_key=("xt", s))

            P.op("sp", lambda e, s=s, o0=o0, n=n: e.dma_start(out=At[s][:, 0:4, :n], in_=Alv[:, :, o0:o0 + n]),
                 reads=[(tag, "Al"), (tag, "Alh")], writes=[(tag, "Ata", s)], dma_key=("Ata", s))
            P.op("sp", lambda e, s=s, o0=o0, n=n: e.dma_start(out=At[s][:, 4:8, :n], in_=Pov[:, :, o0:o0 + n]),
                 reads=[(tag, "Pout")], writes=[(tag, "Atp", s)], dma_key=("Atp", s))
            for oc in range(KC):
                ws = cnt["w"] % 3
                cnt["w"] += 1
                rw = (tag, "wo", ws)
                P.op("sp", lambda e, oc=oc, ws=ws: e.dma_start(out=wo[ws][:, :, :].rearrange("p k o -> p (k o)"), in_=wo_bf[oc]),
                     writes=[rw], dma_key=("wo", ws))
                b = cnt["o"] % 2
                cnt["o"] += 1
                rp = (tag, "ps", b)
                for kc in range(KC):
                    P.op("pe", lambda e, kc=kc, ws=ws, b=b, s=s, n=n: e.matmul(
                        ps[b][:, :n], lhsT=wo[ws][:, kc, :], rhs=At[s][:, kc, :n], start=(kc == 0), stop=(kc == KC - 1)),
                        reads=[rw, (tag, "Ata", s), (tag, "Atp", s)], writes=[rp])
                P.op("dve", lambda e, oc=oc, b=b, s=s, n=n: e.tensor_tensor(
                    out=xo[s][:, oc, :n], in0=ps[b][:, :n], in1=xt[s][:, oc, :n], op=ALU.add),
                    reads=[rp, rx], writes=[rxo])
            store_group(cx, tag, xo[s], rxo, Xo, o0, n, gi == 0)


def cast_all(cx, items):
    P = cx.P
    CWD = 4096
    with Phase(cx):
        st_f = [cx.sb([128, CWD], F32, "cst_f") for _ in range(3)]
        st_b = [cx.sb([128, CWD], BF16, "cst_b") for _ in range(3)]
        i = 0
        for src, dst, rows, cols in items:
            for r0 in range(0, rows, 128):
                for c0 in range(0, cols, CWD):
                    cw_ = min(CWD, cols - c0)
                    s = i % 3
                    rf, rb = ("cf", s), ("cb", s)
                    P.op("sp", lambda e, src=src, r0=r0, c0=c0, cw_=cw_, s=s: e.dma_start(
                        out=st_f[s][:, :cw_], in_=src[r0:r0 + 128, c0:c0 + cw_]), writes=[rf], dma_key=rf)
                    if i % 2 == 0:
                        P.op("dve", lambda e, cw_=cw_, s=s: e.tensor_copy(out=st_b[s][:, :cw_], in_=st_f[s][:, :cw_]),
                             reads=[rf], writes=[rb])
                    else:
                        P.op("act", lambda e, cw_=cw_, s=s: e.activation(out=st_b[s][:, :cw_], in_=st_f[s][:, :cw_], func=AF.Copy),
                             reads=[rf], writes=[rb])
                    P.op("pool", lambda e, dst=dst, r0=r0, c0=c0, cw_=cw_, s=s: e.dma_start(
                        out=dst[r0:r0 + 128, c0:c0 + cw_], in_=st_b[s][:, :cw_]),
                        reads=[rb], writes=[("cdst",)], dma_key=("cst", s))
                    i += 1


def build_diag(cx, dww_dram, dg_bf, tag):
    P = cx.P
    with Phase(cx):
        dww, rdw = load_small(cx, tag, "dww", [128, KC * CW31], dww_dram)
        dst = [cx.sb([128, CW31, 128], BF16, "dgst") for _ in range(2)]
        for j in range(KC):
            s = j % 2
            rs = (tag, "dgst", s)
            for k in range(CW31):
                P.op("pool", lambda e, j=j, k=k, s=s: e.tensor_scalar(
                    out=dst[s][:, k, :], in0=cx.ident_b[:, :], scalar1=dww[:, j * CW31 + k:j * CW31 + k + 1], scalar2=0.0,
                    op0=ALU.mult, op1=ALU.add),
                    reads=[rdw, ("c", "ident")], writes=[rs])
            P.op("sp", lambda e, j=j, s=s: e.dma_start(out=dg_bf[j], in_=dst[s][:, :, :].rearrange("p k o -> p (k o)")),
                 reads=[rs], writes=[(tag, "dgbf")], dma_key=("dgst", s))


WSPEC_E = (("wqk", 12 * 128, 1024), ("wf", 128, 64), ("wv", 128, 4096), ("wp", 128, 512), ("wo", 8 * 128, 1024))
WSPEC_O = (("w1", 8 * 128, 2048), ("w2", 8 * 128, 1024))
WSPEC_F = (("wup", 22 * 128, 2048), ("wdn", 8 * 128, 2816))


def build_program(cfg, n_layers=4, do_ffn=True, stop=None):
    nc = bass.Bass("TRN2", target_bir_lowering=False)
    T_, CH_, SEQ_ = cfg.T, cfg.CH, cfg.SEQ

    def din(name, shape, dt=F32):
        return nc.dram_tensor(name, list(shape), dt, kind="ExternalInput").ap()

    def dsc(name, shape, dt):
        return nc.dram_tensor(name, list(shape), dt).ap()

    dr = {}
    dr["x"] = din("x", [D, T_])
    dr["flag"] = din("flag", [128, 1])
    dr["invc"] = din("invc", [128, 4 * 32])
    Y = nc.dram_tensor("y", [D, CH_], F32, kind="ExternalOutput").ap()
    XA = dsc("XA", [D, T_], F32)
    XB = dsc("XB", [D, T_], F32)
    W = {}
    casts = []
    n_even = (n_layers + 1) // 2
    n_odd = n_layers // 2
    for e_ in range(n_even):
        for nm, r, c in WSPEC_E:
            f = din("e%d_%s" % (e_, nm), [r, c])
            b = dsc("e%d_%s_b" % (e_, nm), [r, c], BF16)
            W[("e", e_, nm)] = b
            casts.append((f, b, r, c))
        W[("e", e_, "vec")] = din("e%d_vec" % e_, [128, 16])
        W[("e", e_, "bf")] = din("e%d_bf" % e_, [8, 1])
    for o_ in range(n_odd):
        for nm, r, c in WSPEC_O:
            f = din("o%d_%s" % (o_, nm), [r, c])
            b = dsc("o%d_%s_b" % (o_, nm), [r, c], BF16)
            W[("o", o_, nm)] = b
            casts.append((f, b, r, c))
        W[("o", o_, "vec")] = din("o%d_vec" % o_, [128, 32])
        W[("o", o_, "dww")] = din("o%d_dww" % o_, [128, KC * CW31])
        W[("o", o_, "dg")] = dsc("o%d_dg" % o_, [KC * 128, CW31 * 128], BF16)
    if do_ffn:
        for l in range(n_layers):
            for nm, r, c in WSPEC_F:
                f = din("f%d_%s" % (l, nm), [r, c])
                b = dsc("f%d_%s_b" % (l, nm), [r, c], BF16)
                W[("f", l, nm)] = b
                casts.append((f, b, r, c))
            W[("f", l, "g")] = din("f%d_g" % l, [128, 8])
            W[("f", l, "cw")] = din("f%d_cw" % l, [128, 132])
            W[("f", l, "cb")] = din("f%d_cb" % l, [128, 44])
    dr["Qc"] = dsc("Qc", [512, CH_], BF16)
    dr["Kc"] = dsc("Kc", [512, CH_], BF16)
    dr["Vc"] = dsc("Vc", [CH_, 512], BF16)
    dr["Fc"] = dsc("Fc", [8, CH_], F32)
    dr["Qg"] = dsc("Qg", [4 * 512, CH_], BF16)
    dr["Kg"] = dsc("Kg", [4 * 512, CH_], BF16)
    dr["Vg"] = dsc("Vg", [4 * CH_, 512], BF16)
    dr["Fg"] = dsc("Fg", [4 * 8, CH_], F32)
    dr["Ql"] = dsc("Ql", [128, SEQ_], BF16)
    dr["Kl"] = dsc("Kl", [128, SEQ_], BF16)
    dr["Vl"] = dsc("Vl", [SEQ_, 128], BF16)
    dr["Fl"] = dsc("Fl", [8, CH_], F32)
    dr["Al"] = dsc("Al", [512, T_], BF16)
    dr["AugK"] = dsc("AugK", [12, SEQ_], BF16)
    dr["AugQ"] = dsc("AugQ", [12, SEQ_], BF16)
    dr["Oc"] = dsc("Oc", [4 * 128, CH_], BF16)
    dr["Og"] = dsc("Og", [5 * 512, CH_], BF16)
    dr["Pout"] = dsc("Pout", [512, T_], BF16)

    def slabs(ap, p=128):
        return ap.rearrange("(j p) c -> j p c", p=p)

    with contextlib.ExitStack() as stack:
        cx = Ctx(nc, stack)
        cx.dram = dr
        setup_consts(cx)
        cx.P.barrier()
        class _Stop(Exception):
            pass

        def chk(name):
            if stop == name:
                raise _Stop()
        cx.chk = chk
        try:
            chk("consts")
            cast_all(cx, casts)
            chk("cast")
            for o_ in range(n_odd):
                build_diag(cx, W[("o", o_, "dww")], slabs(W[("o", o_, "dg")]), "dg%d" % o_)
            chk("diag")
            _build_layers(cx, cfg, dr, W, XA, XB, Y, n_layers, do_ffn, slabs, chk)
        except _Stop:
            pass
        cx.P.emit()
    return nc, None


def _build_layers(cx, cfg, dr, W, XA, XB, Y, n_layers, do_ffn, slabs, chk):
    if True:
        cur = dr["x"]
        bufs = [XA, XB]
        bi = 0
        n_sub = n_layers * (2 if do_ffn else 1)
        sub = 0

        def nxt():
            nonlocal bi
            b = bufs[bi]
            bi ^= 1
            return b
        for l in range(n_layers):
            i2 = l // 2
            if l % 2 == 0:
                tag = "E%d" % l
                even_pre_phase(cx, cfg, cur, slabs(W[("e", i2, "wqk")]), W[("e", i2, "wf")], W[("e", i2, "wv")],
                               W[("e", i2, "wp")], W[("e", i2, "vec")], W[("e", i2, "bf")], tag + "a")
                chk("E1")
                attn_phase(cx, cfg, tag + "b")
                cx.P.barrier()
                chk("attn")
                sub += 1
                last = (sub == n_sub)
                pass
                dst = nxt()
                even_post_phase(cx, cfg, cur, dst, slabs(W[("e", i2, "wo")]), tag + "c")
                cur = dst
                chk("E3")
            else:
                tag = "O%d" % l
                sub += 1
                dst = nxt()
                odd_phase(cx, cfg, cur, dst, slabs(W[("o", i2, "w1")]), slabs(W[("o", i2, "w2")]),
                          slabs(W[("o", i2, "dg")]), W[("o", i2, "vec")], tag)
                cur = dst
            if do_ffn:
                sub += 1
                last = (sub == n_sub)
                if last:
                    ffn_phase(cx, cfg, cur, Y, slabs(W[("f", l, "wup")]), slabs(W[("f", l, "wdn")]),
                              W[("f", l, "g")], W[("f", l, "cw")], W[("f", l, "cb")], "F%d" % l, out_off=HALO)
                else:
                    dst = nxt()
                    ffn_phase(cx, cfg, cur, dst, slabs(W[("f", l, "wup")]), slabs(W[("f", l, "wdn")]),
                              W[("f", l, "g")], W[("f", l, "cw")], W[("f", l, "cb")], "F%d" % l)
                    cur = dst
        if not do_ffn:
            cx.P.op("sp", lambda e: e.dma_start(out=Y, in_=cur[:, HALO:]), dma_key=("G", "fin"))


def lay_slabs(w):
    n = w.shape[1] // 128
    a = w.reshape(KC, 128, n, 128).transpose(2, 1, 0, 3)
    return np.ascontiguousarray(a).reshape(n * 128, KC * 128)


def lay_pairs(w, half):
    n = half // 128
    a = w.reshape(KC, 128, 2, n, 128).transpose(3, 1, 2, 0, 4)
    return np.ascontiguousarray(a).reshape(n * 128, 2 * KC * 128)


def lay_kmajor(w):
    c = w.shape[1]
    return np.ascontiguousarray(w.reshape(KC, 128, c).transpose(1, 0, 2)).reshape(128, KC * c)


def lay_wdn(w_down):
    a = w_down.reshape(NJ, 128, KC, 128).transpose(2, 1, 0, 3)
    return np.ascontiguousarray(a).reshape(KC * 128, NJ * 128)


def lay_vec(v, nch):
    return np.ascontiguousarray(v.reshape(nch, 128).T)


def lay_cw(cw):
    k = cw.shape[0]
    n = cw.shape[1] // 128
    return np.ascontiguousarray(cw.reshape(k, n, 128).transpose(2, 1, 0)).reshape(128, n * k)


def host_inputs(cfg, inp, n_layers=4, do_ffn=True):
    f32 = np.float32
    shared = {}
    n_even = (n_layers + 1) // 2
    n_odd = n_layers // 2
    for e_ in range(n_even):
        w_in = np.asarray(inp["even_w_in"][e_], f32)
        q, k, v = w_in[:, 0:512], w_in[:, 512:1024], w_in[:, 1024:1536]
        f, u = w_in[:, 1536:1544], w_in[:, 1544:2056]
        shared["e%d_wqk" % e_] = lay_slabs(np.concatenate([q, k, u], axis=1))
        shared["e%d_wf" % e_] = lay_kmajor(f)
        shared["e%d_wv" % e_] = lay_kmajor(v)
        wp = np.asarray(inp["even_w_pool"][e_], f32)
        shared["e%d_wp" % e_] = np.ascontiguousarray(wp.transpose(1, 0, 2)).reshape(128, 512)
        shared["e%d_wo" % e_] = lay_slabs(np.asarray(inp["even_w_out"][e_], f32))
        vec = np.zeros((128, 16), f32)
        vec[:, 0:8] = lay_vec(np.asarray(inp["even_norm_g"][e_], f32), 8)
        vec[:, 8] = np.tile(np.asarray(inp["even_q_norm_g"][e_], f32), 2)
        vec[:, 9] = np.tile(np.asarray(inp["even_k_norm_g"][e_], f32), 2)
        vec[:, 10:14] = lay_vec(np.asarray(inp["even_pool_scale"][e_], f32), 4)
        shared["e%d_vec" % e_] = vec
        shared["e%d_bf" % e_] = np.asarray(inp["even_b_f"][e_], f32).reshape(8, 1).copy()
    for o_ in range(n_odd):
        shared["o%d_w1" % o_] = lay_pairs(np.asarray(inp["odd_w_pw1"][o_], f32), 1024)
        shared["o%d_w2" % o_] = lay_slabs(np.asarray(inp["odd_w_pw2"][o_], f32))
        vec = np.zeros((128, 32), f32)
        vec[:, 0:8] = lay_vec(np.asarray(inp["odd_norm_g"][o_], f32), 8)
        vec[:, 8:16] = lay_vec(np.asarray(inp["odd_dw_b"][o_], f32), 8)
        vec[:, 16:24] = lay_vec(np.asarray(inp["odd_ln_g"][o_], f32), 8)
        vec[:, 24:32] = lay_vec(np.asarray(inp["odd_ln_b"][o_], f32), 8)
        shared["o%d_vec" % o_] = vec
        shared["o%d_dww" % o_] = lay_cw(np.asarray(inp["odd_dw_w"][o_], f32))
    if do_ffn:
        for l in range(n_layers):
            shared["f%d_wup" % l] = lay_pairs(np.asarray(inp["ffn_w_up"][l], f32), DFF)
            shared["f%d_wdn" % l] = lay_wdn(np.asarray(inp["ffn_w_down"][l], f32))
            shared["f%d_g" % l] = lay_vec(np.asarray(inp["ffn_norm_g"][l], f32), 8)
            shared["f%d_cw" % l] = lay_cw(np.asarray(inp["ffn_conv_w"][l], f32))
            shared["f%d_cb" % l] = lay_vec(np.asarray(inp["ffn_conv_b"][l], f32), 44)
    x = np.asarray(inp["x"], f32)
    maps = []
    for c in range(NCORES):
        b, i = c // 4, c % 4
        xt = np.zeros((D, cfg.T), f32)
        lo = i * cfg.CH - HALO
        if i == 0:
            xt[:, HALO:] = x[b, 0:cfg.CH].T
        else:
            xt[:, :] = x[b, lo:lo + cfg.T].T
        flag = np.full((128, 1), 0.0 if i == 0 else 1.0, f32)
        invc = np.zeros((128, 4 * 32), f32)
        pos = np.arange(1, 33, dtype=f32)
        for g, w in enumerate(POOL_W):
            cntv = np.minimum(pos, float(w)) if i == 0 else np.full(32, float(w), f32)
            invc[:, g * 32:(g + 1) * 32] = (1.0 / cntv)[None, :]
        m = dict(shared)
        m["x"] = xt
        m["flag"] = flag
        m["invc"] = invc
        maps.append(m)
    return maps


_CACHE = {}
N_SPLIT = 1


def _sub_inputs(inputs, l0, nl):
    out = {}
    for k, v in inputs.items():
        if k == "x":
            out[k] = v
        elif k.startswith("even_"):
            out[k] = v[(l0 + 1) // 2:]
        elif k.startswith("odd_"):
            out[k] = v[l0 // 2:]
        else:
            out[k] = v[l0:]
    return out


def kernel(**inputs):
    cfg = Cfg(4096)
    nl = 4 // N_SPLIT
    if "nc" not in _CACHE:
        _CACHE["nc"] = build_program(cfg, n_layers=nl)[0]
    nc = _CACHE["nc"]
    inp = {k: np.asarray(v) for k, v in inputs.items()}
    x = np.asarray(inp["x"], np.float32)
    for part in range(N_SPLIT):
        sub = _sub_inputs(inp, part * nl, nl)
        sub["x"] = x
        maps = host_inputs(cfg, sub, n_layers=nl)
        res = run_bass_kernel_spmd(nc, maps, core_ids=list(range(NCORES)))
        out = np.empty((2, SEQ, D), np.float32)
        for c in range(NCORES):
            b, i = c // 4, c % 4
            out[b, i * cfg.CH:(i + 1) * cfg.CH, :] = res.results[c]["y"].T
        x = out
    return x
```

```python
import contextlib
import numpy as np
import concourse.bass as bass
import concourse.mybir as mybir
from concourse.bass_utils import run_bass_kernel_spmd

F32 = mybir.dt.float32
BF16 = mybir.dt.bfloat16
AF = mybir.ActivationFunctionType
ALU = mybir.AluOpType

D = 1024
KC = 8
DFF = 2816
NJ = 22
NCORES = 8
SEQ = 16384
HALO = 128
CH = 2048
SEG = CH + HALO
T = 2 * SEG
EPS = 1e-6

SAME_ENGINE_SYNC = True
NO_SELF_SYNC = ("pe",)


class Op:
    __slots__ = ("eng", "fn", "deps", "is_dma", "sem", "val", "need_inc", "key", "phase", "inc")

    def __init__(self, eng, fn, is_dma, key=None):
        self.eng = eng
        self.fn = fn
        self.deps = []
        self.is_dma = is_dma
        self.sem = None
        self.val = None
        self.need_inc = False
        self.key = key
        self.inc = 16


class Prog:
    ENGS = ("pe", "act", "dve", "pool", "sp")

    def __init__(self, nc, stack):
        self.nc = nc
        self.stack = stack
        self.ops = {e: [] for e in self.ENGS}
        self.last_w = {}
        self.readers = {}
        self.dma_sems = {}
        self.n_ops = 0
        self.phase = 0

    def op(self, eng, fn, reads=(), writes=(), dma_key=None, inc=16):
        o = Op(eng, fn, dma_key is not None, dma_key)
        o.inc = inc
        o.phase = self.phase
        deps = []
        for r in reads:
            w = self.last_w.get(r)
            if w is not None:
                deps.append(w)
        for r in writes:
            w = self.last_w.get(r)
            if w is not None:
                deps.append(w)
            deps.extend(self.readers.get(r, ()))
        seen = set()
        for d in deps:
            if id(d) in seen or d is o:
                continue
            seen.add(id(d))
            if (not d.is_dma) and d.eng == eng and (eng in NO_SELF_SYNC or not SAME_ENGINE_SYNC):
                continue
            if d.is_dma and o.is_dma and d.key == o.key and isinstance(o.key, tuple) and o.key[0] == "G":
                continue
            o.deps.append(d)
            d.need_inc = True
        for r in reads:
            self.readers.setdefault(r, []).append(o)
        for r in writes:
            self.last_w[r] = o
            self.readers[r] = []
        self.ops[eng].append(o)
        self.n_ops += 1
        return o

    def barrier(self):
        deps = []
        for e in self.ENGS:
            for o in reversed(self.ops[e]):
                if not o.is_dma:
                    deps.append(o)
                    break
        last_dma = {}
        for e in self.ENGS:
            for o in self.ops[e]:
                if o.is_dma:
                    last_dma[o.key] = o
        deps.extend(last_dma.values())
        for e in self.ENGS:
            o = Op(e, lambda en: en.nop(), False)
            o.phase = self.phase
            for d in deps:
                if (not d.is_dma) and d.eng == e:
                    continue
                o.deps.append(d)
                d.need_inc = True
            self.ops[e].append(o)
        self.last_w = {}
        self.readers = {}
        self.phase += 1

    def emit(self, final_waits=()):
        nc = self.nc
        stack = self.stack
        eng_sem = {}
        ecnt = {}
        gtot = {}
        for e in self.ENGS:
            for o in self.ops[e]:
                if o.is_dma:
                    if o.key not in self.dma_sems:
                        self.dma_sems[o.key] = [
                            stack.enter_context(nc.semaphore("d%d" % len(self.dma_sems))), 0]
                    ent = self.dma_sems[o.key]
                    ent[1] += o.inc
                    o.sem, o.val = ent[0], ent[1]
                    if isinstance(o.key, tuple) and o.key[0] == "G":
                        gtot[(o.key, o.phase)] = ent[1]
                elif o.need_inc:
                    k_ = (e, o.phase % 4)
                    if k_ not in eng_sem:
                        eng_sem[k_] = stack.enter_context(nc.semaphore("s_%s_%d" % k_))
                        ecnt[k_] = 0
                    ecnt[k_] += 1
                    o.sem, o.val = eng_sem[k_], ecnt[k_]
        for e in self.ENGS:
            for o in self.ops[e]:
                if o.is_dma and (o.key, o.phase) in gtot:
                    o.val = gtot[(o.key, o.phase)]
        final = list(final_waits)
        block = stack.enter_context(nc.Block())
        handles = {"pe": "tensor", "act": "scalar", "dve": "vector", "pool": "gpsimd", "sp": "sync"}

        def run(ename, e):
            waited = {}
            for o in self.ops[ename]:
                for d in o.deps:
                    k = id(d.sem)
                    if waited.get(k, 0) >= d.val:
                        continue
                    e.wait_ge(d.sem, d.val)
                    waited[k] = d.val
                ins = o.fn(e)
                if o.is_dma:
                    ins.then_inc(o.sem, o.inc)
                elif o.need_inc:
                    ins.then_inc(o.sem, 1)
            if ename == "pool":
                for ent in self.dma_sems.values():
                    e.wait_ge(ent[0], ent[1])

        for ename in self.ENGS:
            dec = getattr(block, handles[ename])

            def body(e, _n=ename):
                run(_n, e)
            dec(body)


_PID = {}


def pid4(e):
    k = id(e)
    if k not in _PID:
        _PID[k] = e.partition_id() % 4
    return _PID[k]


class Ctx:
    def __init__(self, nc, stack):
        self.nc = nc
        self.stack = stack
        self.P = Prog(nc, stack)
        self._n = 0

    def sb(self, shape, dtype, name=None):
        self._n += 1
        return self.stack.enter_context(self.nc.sbuf_tensor("%s_%d" % (name or "sb", self._n), list(shape), dtype))

    def ps(self, shape, dtype=F32, name=None):
        self._n += 1
        return self.stack.enter_context(self.nc.psum_tensor("%s_%d" % (name or "ps", self._n), list(shape), dtype))


class Cfg:
    def __init__(self, CH=4096):
        self.CH = CH
        self.T = CH + HALO
        self.SEQ = 4 * CH
        self.NKB = self.SEQ // 128
        self.NQB = self.SEQ // 512


def make_groups(T_, first_n, step, halo):
    out = [(0, min(first_n, T_), 0)]
    pos = out[0][1]
    while pos < T_:
        n = min(step, T_ - pos)
        out.append((pos, n, halo))
        pos += n
    return out


class Phase:
    def __init__(self, cx):
        self.cx = cx

    def __enter__(self):
        self.st = contextlib.ExitStack()
        self.saved = self.cx.stack
        self.cx.stack = self.st
        return self

    def __exit__(self, *a):
        self.cx.P.barrier()
        self.cx.stack = self.saved
        self.st.close()
        return False


def setup_consts(cx):
    P = cx.P
    cx.ones_f = cx.sb([128, 128], F32, "ones_f")
    cx.ones2_f = cx.sb([128, 128], F32, "ones2_f")
    cx.eps_t = cx.sb([128, 1], F32, "eps")
    cx.one_t = cx.sb([128, 1], F32, "one")
    cx.ident_b = cx.sb([128, 128], BF16, "ident_b")
    cx.maskb = cx.sb([128, 128], BF16, "maskb")
    cx.flag = cx.sb([128, 1], F32, "flag")
    P.op("dve", lambda e: e.memset(cx.ones_f[:, :], 1.0), writes=[("c", "ones_f")])
    P.op("dve", lambda e: e.memset(cx.ones2_f[:, :], 1.0), writes=[("c", "ones2_f")])
    P.op("dve", lambda e: e.memset(cx.ones2_f[0:64, 64:128], 0.0), writes=[("c", "ones2_f")])
    P.op("dve", lambda e: e.memset(cx.ones2_f[64:128, 0:64], 0.0), writes=[("c", "ones2_f")])
    P.op("dve", lambda e: e.memset(cx.eps_t[:, :], EPS), writes=[("c", "eps")])
    P.op("dve", lambda e: e.memset(cx.one_t[:, :], 1.0), writes=[("c", "one")])
    P.op("dve", lambda e: e.memset(cx.ident_b[:, :], 0.0), writes=[("c", "ident")])
    P.op("pool", lambda e: e.affine_select(out=cx.ident_b[:, :], in_=cx.ident_b[:, :], pattern=[[1, 128]],
                                           compare_op=ALU.not_equal, fill=1.0, base=0, channel_multiplier=-1),
         reads=[("c", "ident")], writes=[("c", "ident")])
    P.op("dve", lambda e: e.memset(cx.maskb[:, :], 0.0), writes=[("c", "maskb")])
    P.op("pool", lambda e: e.affine_select(out=cx.maskb[:, :], in_=cx.maskb[:, :], pattern=[[1, 128]],
                                           compare_op=ALU.is_ge, fill=-30000.0, base=0, channel_multiplier=-1),
         reads=[("c", "maskb")], writes=[("c", "maskb")])
    P.op("sp", lambda e: e.dma_start(out=cx.flag[:, :], in_=cx.dram["flag"]), writes=[("c", "flag")], dma_key=("G", "small"))


def emit_rmsnorm(cx, xt, nu, g_t, H, res, sq, red, ps_stat, std, rstd):
    P = cx.P
    P.op("act", lambda e: e.activation(out=sq[:, :, :nu], in_=xt[:, :, :nu], func=AF.Square),
         reads=[res["xt"]], writes=[res["sq"]])
    P.op("pool", lambda e: e.tensor_tensor(out=red[:, :nu], in0=sq[:, 0, :nu], in1=sq[:, 1, :nu], op=ALU.add),
         reads=[res["sq"]], writes=[res["red"]])
    for kc in range(2, KC):
        P.op("pool", lambda e, kc=kc: e.tensor_tensor(out=red[:, :nu], in0=red[:, :nu], in1=sq[:, kc, :nu], op=ALU.add),
             reads=[res["sq"], res["red"]], writes=[res["red"]])
    P.op("pe", lambda e: e.matmul(ps_stat[:, :nu], lhsT=cx.ones_f[:, :], rhs=red[:, :nu], start=True, stop=True),
         reads=[res["red"], ("c", "ones_f")], writes=[res["ps_stat"]])
    P.op("act", lambda e: e.activation(out=std[:, :nu], in_=ps_stat[:, :nu], func=AF.Sqrt, scale=1.0 / D, bias=cx.eps_t[:, 0:1]),
         reads=[res["ps_stat"], ("c", "eps")], writes=[res["std"]])
    P.op("dve", lambda e: e.reciprocal(out=rstd[:, :nu], in_=std[:, :nu]),
         reads=[res["std"]], writes=[res["rstd"]])
    for kc in range(KC):
        P.op("dve", lambda e, kc=kc: e.scalar_tensor_tensor(
            out=H[:, kc, :nu], in0=xt[:, kc, :nu], scalar=g_t[:, kc:kc + 1], in1=rstd[:, :nu],
            op0=ALU.mult, op1=ALU.mult),
            reads=[res["xt"], res["rstd"], res["g"]], writes=[res["H"]])


def load_small(cx, tag, name, shape, src):
    t = cx.sb(shape, F32, name)
    r = (tag, name)
    cx.P.op("sp", lambda e: e.dma_start(out=t[:, :], in_=src), writes=[r], dma_key=("G", "small"))
    return t, r


def store_group(cx, tag, xo, rxo, Xo, o0, n, first):
    P = cx.P
    if first:
        P.op("dve", lambda e: e.tensor_scalar(out=xo[:, :, 0:HALO], in0=xo[:, :, 0:HALO], scalar1=cx.flag[:, 0:1],
                                              scalar2=None, op0=ALU.mult),
             reads=[rxo, ("c", "flag")], writes=[rxo])
    P.op("pool", lambda e: e.dma_start(out=Xo[:, :, o0:o0 + n], in_=xo[:, :, :n]),
         reads=[rxo], writes=[(tag, "Xout")], dma_key=("st", 0))


def ffn_phase(cx, cfg, X_in, X_out, wup_bf, wdn_bf, g_dram, cw_dram, cb_dram, tag, out_off=0):
    P = cx.P
    groups = make_groups(cfg.T, 512, 510, 2)
    Xi = X_in.rearrange("(kc p) t -> p kc t", p=128)
    Xo = X_out.rearrange("(kc p) t -> p kc t", p=128)
    with Phase(cx):
        g_t, rg = load_small(cx, tag, "g", [128, KC], g_dram)
        cw_t, rcw = load_small(cx, tag, "cw", [128, 44 * 3], cw_dram)
        cb_t, rcb = load_small(cx, tag, "cb", [128, 44], cb_dram)
        NX = 2
        xt = [cx.sb([128, KC, 512], F32, "xt") for _ in range(NX)]
        sq = cx.sb([128, KC, 512], F32, "sq")
        red = cx.sb([128, 512], F32, "red")
        std = cx.sb([128, 512], F32, "std")
        rstd = cx.sb([128, 512], F32, "rstd")
        Hs = [cx.sb([128, KC, 512], BF16, "H") for _ in range(2)]
        NW = 5
        wup = [cx.sb([128, 2, KC, 128], BF16, "wup") for _ in range(NW)]
        acc_g = [cx.sb([128, 512], F32, "accg") for _ in range(2)]
        acc_v = [cx.sb([128, 512], F32, "accv") for _ in range(2)]
        sg = [cx.sb([128, 512], F32, "sg") for _ in range(2)]
        Gs = [cx.sb([128, NJ, 512], BF16, "G") for _ in range(2)]
        ND = 3
        wdn = [cx.sb([128, NJ, 128], BF16, "wdn") for _ in range(ND)]
        xo = cx.sb([128, KC, 512], F32, "xo")
        ps_stat = cx.ps([128, 512], F32, "psst")
        ps_g = [cx.ps([128, 512], F32, "psg") for _ in range(2)]
        ps_v = [cx.ps([128, 512], F32, "psv") for _ in range(2)]
        ps_o = [cx.ps([128, 512], F32, "pso") for _ in range(2)]
        cnt = {"w": 0, "d": 0, "a": 0, "o": 0}

        def pre(gi):
            o0, n, h = groups[gi]
            nu = n + h
            s = gi % NX
            rx = (tag, "xt", s)
            P.op("sp", lambda e: e.dma_start(out=xt[s][:, :, :nu], in_=Xi[:, :, o0 - h:o0 + n]),
                 reads=[(tag, "Xin")], writes=[rx], dma_key=("xt", s))
            res = {"xt": rx, "sq": (tag, "sq"), "red": (tag, "red"), "ps_stat": (tag, "psst"), "std": (tag, "std"),
                   "rstd": (tag, "rstd"), "g": rg, "H": (tag, "H", gi % 2)}
            emit_rmsnorm(cx, xt[s], nu, g_t, Hs[gi % 2], res, sq, red, ps_stat, std, rstd)

        def up(gi):
            o0, n, h = groups[gi]
            nu = n + h
            H = Hs[gi % 2]
            G = Gs[gi % 2]
            rH = (tag, "H", gi % 2)
            rG = (tag, "G", gi % 2)
            for j in range(NJ):
                ws = cnt["w"] % NW
                cnt["w"] += 1
                rw = (tag, "wup", ws)
                P.op("sp", lambda e, j=j, ws=ws: e.dma_start(
                    out=wup[ws][:, :, :, :].rearrange("p a k o -> p (a k o)"), in_=wup_bf[j]),
                    writes=[rw], dma_key=("wup", ws))
                a = cnt["a"] % 2
                cnt["a"] += 1
                rpg, rpv = (tag, "psg", a), (tag, "psv", a)
                for gv, psb, rp in ((0, ps_g[a], rpg), (1, ps_v[a], rpv)):
                    for kc in range(KC):
                        P.op("pe", lambda e, gv=gv, kc=kc, psb=psb, ws=ws: e.matmul(
                            psb[:, :nu], lhsT=wup[ws][:, gv, kc, :], rhs=H[:, kc, :nu],
                            start=(kc == 0), stop=(kc == KC - 1)),
                            reads=[rw, rH], writes=[rp])
                ag, av, sgt = acc_g[a], acc_v[a], sg[a]
                rag, rav, rsg = (tag, "accg", a), (tag, "accv", a), (tag, "sg", a)
                for (acc, racc, psb, rp, ch) in ((ag, rag, ps_g[a], rpg, j), (av, rav, ps_v[a], rpv, NJ + j)):
                    w0 = cw_t[:, ch * 3 + 0:ch * 3 + 1]
                    w1 = cw_t[:, ch * 3 + 1:ch * 3 + 2]
                    w2 = cw_t[:, ch * 3 + 2:ch * 3 + 3]
                    bb = cb_t[:, ch:ch + 1]
                    P.op("act", lambda e, acc=acc, psb=psb, w2=w2, bb=bb: e.activation(
                        out=acc[:, :n], in_=psb[:, h:h + n], func=AF.Identity, scale=w2, bias=bb),
                        reads=[rp, rcw, rcb], writes=[racc])
                    for sh, wk in ((1, w1), (2, w0)):
                        if h >= sh:
                            P.op("dve", lambda e, acc=acc, psb=psb, wk=wk, sh=sh: e.scalar_tensor_tensor(
                                out=acc[:, :n], in0=psb[:, h - sh:h - sh + n], scalar=wk, in1=acc[:, :n],
                                op0=ALU.mult, op1=ALU.add),
                                reads=[rp, rcw, racc], writes=[racc])
                        else:
                            P.op("dve", lambda e, acc=acc, psb=psb, wk=wk, sh=sh: e.scalar_tensor_tensor(
                                out=acc[:, sh:n], in0=psb[:, 0:n - sh], scalar=wk, in1=acc[:, sh:n],
                                op0=ALU.mult, op1=ALU.add),
                                reads=[rp, rcw, racc], writes=[racc])
                P.op("act", lambda e, ag=ag, sgt=sgt: e.activation(out=sgt[:, :n], in_=ag[:, :n], func=AF.Silu),
                     reads=[rag], writes=[rsg])
                P.op("pool", lambda e, j=j, sgt=sgt, av=av: e.tensor_tensor(
                    out=G[:, j, :n], in0=sgt[:, :n], in1=av[:, :n], op=ALU.mult),
                    reads=[rsg, rav], writes=[rG])

        def down(gi):
            o0, n, h = groups[gi]
            G = Gs[gi % 2]
            rG = (tag, "G", gi % 2)
            s = gi % NX
            rx = (tag, "xt", s)
            rxo = (tag, "xo")
            for oc in range(KC):
                ds_ = cnt["d"] % ND
                cnt["d"] += 1
                rw = (tag, "wdn", ds_)
                P.op("sp", lambda e, oc=oc, ds_=ds_: e.dma_start(
                    out=wdn[ds_][:, :, :].rearrange("p j o -> p (j o)"), in_=wdn_bf[oc]),
                    writes=[rw], dma_key=("wdn", ds_))
                b = cnt["o"] % 2
                cnt["o"] += 1
                rp = (tag, "pso", b)
                for j in range(NJ):
                    P.op("pe", lambda e, j=j, b=b, ds_=ds_: e.matmul(
                        ps_o[b][:, :n], lhsT=wdn[ds_][:, j, :], rhs=G[:, j, :n],
                        start=(j == 0), stop=(j == NJ - 1)),
                        reads=[rw, rG], writes=[rp])
                P.op("dve", lambda e, oc=oc, b=b: e.tensor_tensor(
                    out=xo[:, oc, :n], in0=ps_o[b][:, :n], in1=xt[s][:, oc, h:h + n], op=ALU.add),
                    reads=[rp, rx], writes=[rxo])
            if out_off == 0:
                store_group(cx, tag, xo, rxo, Xo, o0, n, gi == 0)
            else:
                lo = max(o0, out_off)
                if lo < o0 + n:
                    P.op("pool", lambda e: e.dma_start(out=Xo[:, :, lo - out_off:o0 + n - out_off], in_=xo[:, :, lo - o0:n]),
                         reads=[rxo], writes=[(tag, "Xout")], dma_key=("st", 0))

        ng = len(groups)
        pre(0)
        up(0)
        for gi in range(ng):
            if gi + 1 < ng:
                pre(gi + 1)
                up(gi + 1)
            down(gi)


CW31 = 31
HO = 30


def odd_phase(cx, cfg, X_in, X_out, w1_bf, w2_bf, dg_bf, vec_dram, tag):
    P = cx.P
    groups = make_groups(cfg.T, 482, 482, HO)
    Xi = X_in.rearrange("(kc p) t -> p kc t", p=128)
    Xo = X_out.rearrange("(kc p) t -> p kc t", p=128)
    with Phase(cx):
        vec, rvec = load_small(cx, tag, "vec", [128, 32], vec_dram)
        g_t = vec[:, 0:8]
        xt = [cx.sb([128, KC, 512], F32, "xt") for _ in range(2)]
        sqk = [cx.sb([128, 512], F32, "sqk") for _ in range(2)]
        red = cx.sb([128, 512], F32, "red")
        lr1 = cx.sb([128, 512], F32, "lr1")
        lr2 = cx.sb([128, 512], F32, "lr2")
        std = cx.sb([128, 512], F32, "std")
        rstd = cx.sb([128, 512], F32, "rstd")
        mean = cx.sb([128, 512], F32, "mean")
        m2 = cx.sb([128, 512], F32, "m2")
        t1 = [cx.sb([128, 512], F32, "t1") for _ in range(2)]
        Hs = [cx.sb([128, KC, 512], BF16, "H") for _ in range(2)]
        w1 = [cx.sb([128, 2, KC, 128], BF16, "w1") for _ in range(3)]
        dg = [cx.sb([128, CW31, 128], BF16, "dg") for _ in range(2)]
        Us = [cx.sb([128, KC, 512 + HO], BF16, "U") for _ in range(2)]
        sig = [cx.sb([128, 512], F32, "sig") for _ in range(2)]
        Vs = [cx.sb([128, KC, 512], F32, "V") for _ in range(2)]
        Ss = [cx.sb([128, KC, 512], BF16, "S") for _ in range(2)]
        w2 = [cx.sb([128, KC, 128], BF16, "w2") for _ in range(2)]
        xo = cx.sb([128, KC, 512], F32, "xo")
        ps = [cx.ps([128, 512], F32, "ps") for _ in range(8)]
        rps = [(tag, "ps", i) for i in range(8)]
        cnt = {"w1": 0, "ag": 0, "dg": 0, "c": 0, "w2": 0, "o": 0}

        def pre(gi):
            o0, n, h = groups[gi]
            nu = n + h
            s = gi % 2
            rx = (tag, "xt", s)
            P.op("sp", lambda e: e.dma_start(out=xt[s][:, :, :nu], in_=Xi[:, :, o0 - h:o0 + n]),
                 reads=[(tag, "Xin")], writes=[rx], dma_key=("xt", s))
            rH = (tag, "H", s)
            x_ = xt[s]
            P.op("act", lambda e: e.activation(out=red[:, :nu], in_=x_[:, 0, :nu], func=AF.Square),
                 reads=[rx], writes=[(tag, "red")])
            for kc in range(1, KC):
                q = sqk[kc % 2]
                rq = (tag, "sqk", kc % 2)
                P.op("act", lambda e, kc=kc, q=q: e.activation(out=q[:, :nu], in_=x_[:, kc, :nu], func=AF.Square),
                     reads=[rx], writes=[rq])
                P.op("pool", lambda e, q=q: e.tensor_tensor(out=red[:, :nu], in0=red[:, :nu], in1=q[:, :nu], op=ALU.add),
                     reads=[rq, (tag, "red")], writes=[(tag, "red")])
            P.op("pe", lambda e: e.matmul(ps[0][:, :nu], lhsT=cx.ones_f[:, :], rhs=red[:, :nu], start=True, stop=True),
                 reads=[(tag, "red"), ("c", "ones_f")], writes=[rps[0]])
            P.op("act", lambda e: e.activation(out=std[:, :nu], in_=ps[0][:, :nu], func=AF.Sqrt, scale=1.0 / D, bias=cx.eps_t[:, 0:1]),
                 reads=[rps[0], ("c", "eps")], writes=[(tag, "std")])
            P.op("dve", lambda e: e.reciprocal(out=rstd[:, :nu], in_=std[:, :nu]),
                 reads=[(tag, "std")], writes=[(tag, "rstd")])
            for kc in range(KC):
                P.op("dve", lambda e, kc=kc: e.scalar_tensor_tensor(
                    out=Hs[s][:, kc, :nu], in0=x_[:, kc, :nu], scalar=g_t[:, kc:kc + 1], in1=rstd[:, :nu],
                    op0=ALU.mult, op1=ALU.mult),
                    reads=[rx, (tag, "rstd"), rvec], writes=[rH])

        def glu(gi):
            o0, n, h = groups[gi]
            nu = n + h
            s = gi % 2
            H = Hs[s]
            rH = (tag, "H", s)
            U = Us[s]
            rU = (tag, "U", s)
            cs = HO - h
            if h == 0:
                P.op("pool", lambda e: e.memset(U[:, :, 0:HO], 0.0), writes=[rU])
            for j in range(KC):
                ws = cnt["w1"] % 3
                cnt["w1"] += 1
                rw = (tag, "w1", ws)
                P.op("sp", lambda e, j=j, ws=ws: e.dma_start(
                    out=w1[ws][:, :, :, :].rearrange("p a k o -> p (a k o)"), in_=w1_bf[j]),
                    writes=[rw], dma_key=("w1", ws))
                a = cnt["ag"] % 2
                cnt["ag"] += 1
                pa, pg = ps[1 + 2 * a], ps[2 + 2 * a]
                rpa, rpg = rps[1 + 2 * a], rps[2 + 2 * a]
                for gv, psb, rp in ((0, pa, rpa), (1, pg, rpg)):
                    for kc in range(KC):
                        P.op("pe", lambda e, gv=gv, kc=kc, psb=psb, ws=ws: e.matmul(
                            psb[:, :nu], lhsT=w1[ws][:, gv, kc, :], rhs=H[:, kc, :nu],
                            start=(kc == 0), stop=(kc == KC - 1)),
                            reads=[rw, rH], writes=[rp])
                sg_ = sig[a]
                rsg = (tag, "sig", a)
                P.op("act", lambda e, sg_=sg_, pg=pg: e.activation(out=sg_[:, :nu], in_=pg[:, :nu], func=AF.Sigmoid),
                     reads=[rpg], writes=[rsg])
                P.op("dve", lambda e, j=j, sg_=sg_, pa=pa: e.tensor_tensor(
                    out=U[:, j, cs:cs + nu], in0=pa[:, :nu], in1=sg_[:, :nu], op=ALU.mult),
                    reads=[rpa, rsg], writes=[rU])

        def conv(gi):
            o0, n, h = groups[gi]
            s = gi % 2
            U = Us[s]
            rU = (tag, "U", s)
            V = Vs[s]
            rV = (tag, "V", s)
            for j in range(KC):
                d_ = cnt["dg"] % 2
                cnt["dg"] += 1
                rd = (tag, "dg", d_)
                P.op("sp", lambda e, j=j, d_=d_: e.dma_start(
                    out=dg[d_][:, :, :].rearrange("p k o -> p (k o)"), in_=dg_bf[j]),
                    writes=[rd], dma_key=("dg", d_))
                b = cnt["c"] % 2
                cnt["c"] += 1
                pc, rpc = ps[5 + b], rps[5 + b]
                for k in range(CW31):
                    P.op("pe", lambda e, j=j, k=k, d_=d_, pc=pc: e.matmul(
                        pc[:, :n], lhsT=dg[d_][:, k, :], rhs=U[:, j, k:k + n],
                        start=(k == 0), stop=(k == CW31 - 1)),
                        reads=[rd, rU], writes=[rpc])
                P.op("act", lambda e, j=j, pc=pc: e.activation(
                    out=V[:, j, :n], in_=pc[:, :n], func=AF.Identity, bias=vec[:, 8 + j:9 + j], scale=1.0),
                    reads=[rpc, rvec], writes=[rV])

        def ln_a(gi):
            o0, n, h = groups[gi]
            s = gi % 2
            V = Vs[s]
            rV = (tag, "V", s)
            P.op("pool", lambda e: e.tensor_tensor(out=lr1[:, :n], in0=V[:, 0, :n], in1=V[:, 1, :n], op=ALU.add),
                 reads=[rV], writes=[(tag, "lr1")])
            for kc in range(2, KC):
                P.op("pool", lambda e, kc=kc: e.tensor_tensor(out=lr1[:, :n], in0=lr1[:, :n], in1=V[:, kc, :n], op=ALU.add),
                     reads=[rV, (tag, "lr1")], writes=[(tag, "lr1")])
            P.op("act", lambda e: e.activation(out=lr2[:, :n], in_=V[:, 0, :n], func=AF.Square),
                 reads=[rV], writes=[(tag, "lr2")])
            for kc in range(1, KC):
                q = sqk[kc % 2]
                rq = (tag, "sqk", kc % 2)
                P.op("act", lambda e, kc=kc, q=q: e.activation(out=q[:, :n], in_=V[:, kc, :n], func=AF.Square),
                     reads=[rV], writes=[rq])
                P.op("pool", lambda e, q=q: e.tensor_tensor(out=lr2[:, :n], in0=lr2[:, :n], in1=q[:, :n], op=ALU.add),
                     reads=[rq, (tag, "lr2")], writes=[(tag, "lr2")])

        def ln_b(gi):
            o0, n, h = groups[gi]
            s = gi % 2
            V = Vs[s]
            rV = (tag, "V", s)
            S = Ss[s]
            rS = (tag, "S", s)
            P.op("pe", lambda e: e.matmul(ps[0][:, :n], lhsT=cx.ones_f[:, :], rhs=lr1[:, :n], start=True, stop=True),
                 reads=[(tag, "lr1"), ("c", "ones_f")], writes=[rps[0]])
            P.op("pe", lambda e: e.matmul(ps[7][:, :n], lhsT=cx.ones_f[:, :], rhs=lr2[:, :n], start=True, stop=True),
                 reads=[(tag, "lr2"), ("c", "ones_f")], writes=[rps[7]])
            P.op("dve", lambda e: e.tensor_scalar(out=mean[:, :n], in0=ps[0][:, :n], scalar1=1.0 / D, scalar2=None, op0=ALU.mult),
                 reads=[rps[0]], writes=[(tag, "mean")])
            P.op("dve", lambda e: e.tensor_tensor(out=m2[:, :n], in0=mean[:, :n], in1=mean[:, :n], op=ALU.mult),
                 reads=[(tag, "mean")], writes=[(tag, "m2")])
            P.op("dve", lambda e: e.scalar_tensor_tensor(out=m2[:, :n], in0=ps[7][:, :n], scalar=1.0 / D, in1=m2[:, :n],
                                                         op0=ALU.mult, op1=ALU.subtract),
                 reads=[rps[7], (tag, "m2")], writes=[(tag, "m2")])
            P.op("act", lambda e: e.activation(out=std[:, :n], in_=m2[:, :n], func=AF.Sqrt, scale=1.0, bias=cx.eps_t[:, 0:1]),
                 reads=[(tag, "m2"), ("c", "eps")], writes=[(tag, "std")])
            P.op("dve", lambda e: e.reciprocal(out=rstd[:, :n], in_=std[:, :n]),
                 reads=[(tag, "std")], writes=[(tag, "rstd")])
            for kc in range(KC):
                tt = t1[kc % 2]
                rt = (tag, "t1", kc % 2)
                P.op("dve", lambda e, kc=kc, tt=tt: e.tensor_tensor(out=tt[:, :n], in0=V[:, kc, :n], in1=mean[:, :n], op=ALU.subtract),
                     reads=[rV, (tag, "mean")], writes=[rt])
                P.op("dve", lambda e, tt=tt: e.tensor_tensor(out=tt[:, :n], in0=tt[:, :n], in1=rstd[:, :n], op=ALU.mult),
                     reads=[rt, (tag, "rstd")], writes=[rt])
                P.op("act", lambda e, kc=kc, tt=tt: e.activation(
                    out=S[:, kc, :n], in_=tt[:, :n], func=AF.Silu, scale=vec[:, 16 + kc:17 + kc], bias=vec[:, 24 + kc:25 + kc]),
                    reads=[rt, rvec], writes=[rS])

        def pw2(gi):
            o0, n, h = groups[gi]
            s = gi % 2
            S = Ss[s]
            rS = (tag, "S", s)
            rx = (tag, "xt", s)
            rxo = (tag, "xo")
            for oc in range(KC):
                w_ = cnt["w2"] % 2
                cnt["w2"] += 1
                rw = (tag, "w2", w_)
                P.op("sp", lambda e, oc=oc, w_=w_: e.dma_start(
                    out=w2[w_][:, :, :].rearrange("p k o -> p (k o)"), in_=w2_bf[oc]),
                    writes=[rw], dma_key=("w2", w_))
                b = cnt["o"] % 2
                cnt["o"] += 1
                po, rpo = ps[1 + b], rps[1 + b]
                for kc in range(KC):
                    P.op("pe", lambda e, kc=kc, w_=w_, po=po: e.matmul(
                        po[:, :n], lhsT=w2[w_][:, kc, :], rhs=S[:, kc, :n],
                        start=(kc == 0), stop=(kc == KC - 1)),
                        reads=[rw, rS], writes=[rpo])
                P.op("dve", lambda e, oc=oc, po=po: e.tensor_tensor(
                    out=xo[:, oc, :n], in0=po[:, :n], in1=xt[s][:, oc, h:h + n], op=ALU.add),
                    reads=[rpo, rx], writes=[rxo])
            store_group(cx, tag, xo, rxo, Xo, o0, n, gi == 0)

        ng = len(groups)
        pre(0)
        glu(0)
        conv(0)
        for gi in range(ng):
            ln_a(gi)
            if gi + 1 < ng:
                pre(gi + 1)
                glu(gi + 1)
            ln_b(gi)
            if gi + 1 < ng:
                conv(gi + 1)
            pw2(gi)


HP = 15
POOL_W = (2, 4, 8, 16)


def even_pre_phase(cx, cfg, X_in, wqk_bf, wf_bf, wv_bf, wp_bf, vec_dram, bf_dram, tag):
    P = cx.P
    dr = cx.dram
    groups = make_groups(cfg.T, 512, 497, HP)
    Xi = X_in.rearrange("(kc p) t -> p kc t", p=128)
    with Phase(cx):
        vec, rvec = load_small(cx, tag, "vec", [128, 16], vec_dram)
        invc, rinvc = load_small(cx, tag, "invc", [128, 4 * 32], dr["invc"])
        bft = cx.sb([8, 1], F32, "bft")
        nbf = cx.sb([8, 1], F32, "nbf")
        gq8 = cx.sb([128, 1], F32, "gq8")
        P.op("sp", lambda e: e.dma_start(out=bft[:, :], in_=bf_dram), writes=[(tag, "bft")], dma_key=("G", "small"))
        P.op("dve", lambda e: e.tensor_scalar(out=nbf[:, :], in0=bft[:, :], scalar1=-1.0, scalar2=None, op0=ALU.mult),
             reads=[(tag, "bft")], writes=[(tag, "nbf")])
        P.op("dve", lambda e: e.tensor_scalar(out=gq8[:, :], in0=vec[:, 8:9], scalar1=0.125, scalar2=None, op0=ALU.mult),
             reads=[rvec], writes=[(tag, "gq8")])
        g_t = vec[:, 0:8]
        wv = cx.sb([128, KC, 512], BF16, "wv")
        wf = cx.sb([128, KC, 8], BF16, "wf")
        wp = cx.sb([128, 4, 128], BF16, "wp")
        P.op("sp", lambda e: e.dma_start(out=wv[:, :, :].rearrange("p k o -> p (k o)"), in_=wv_bf), writes=[(tag, "wv")], dma_key=("G", "small"))
        P.op("sp", lambda e: e.dma_start(out=wf[:, :, :].rearrange("p k o -> p (k o)"), in_=wf_bf), writes=[(tag, "wf")], dma_key=("G", "small"))
        P.op("sp", lambda e: e.dma_start(out=wp[:, :, :].rearrange("p k o -> p (k o)"), in_=wp_bf), writes=[(tag, "wp")], dma_key=("G", "small"))
        xt = [cx.sb([128, KC, 512], F32, "xt") for _ in range(2)]
        sqk = [cx.sb([128, 512], F32, "sqk") for _ in range(2)]
        red = cx.sb([128, 512], F32, "red")
        std = cx.sb([128, 512], F32, "std")
        rstd = cx.sb([128, 512], F32, "rstd")
        Hs = [cx.sb([128, KC, 512], BF16, "H") for _ in range(2)]
        wq = [cx.sb([128, KC, 128], BF16, "wq") for _ in range(3)]
        qsq = [cx.sb([128, 512], F32, "qsq") for _ in range(2)]
        qstd = [cx.sb([128, 512], F32, "qstd") for _ in range(2)]
        qo = [cx.sb([128, 512], BF16, "qo") for _ in range(2)]
        vo = [cx.sb([128, 512], BF16, "vo") for _ in range(2)]
        fe = cx.sb([8, 512], F32, "fe")
        fo = [cx.sb([8, 512], F32, "fo") for _ in range(2)]
        ub = [cx.sb([128, 512], F32, "ub") for _ in range(2)]
        sa = cx.sb([128, 512], F32, "sa")
        sb_ = cx.sb([128, 512], F32, "sb")
        mx = [cx.sb([128, 512], BF16, "mx") for _ in range(2)]
        po = [cx.sb([128, 512], BF16, "po") for _ in range(2)]
        ps = [cx.ps([128, 512], F32, "ps") for _ in range(8)]
        rps = [(tag, "ps", i) for i in range(8)]
        cnt = {"w": 0, "pj": 0, "q": 0, "v": 0, "u": 0, "f": 0}

        def pre(gi):
            o0, n, h = groups[gi]
            nu = n + h
            s = gi % 2
            rx = (tag, "xt", s)
            x_ = xt[s]
            P.op("sp", lambda e: e.dma_start(out=x_[:, :, :nu], in_=Xi[:, :, o0 - h:o0 + n]),
                 reads=[(tag, "Xin")], writes=[rx], dma_key=("xt", s))
            P.op("act", lambda e: e.activation(out=red[:, :nu], in_=x_[:, 0, :nu], func=AF.Square),
                 reads=[rx], writes=[(tag, "red")])
            for kc in range(1, KC):
                q = sqk[kc % 2]
                rq = (tag, "sqk", kc % 2)
                P.op("act", lambda e, kc=kc, q=q: e.activation(out=q[:, :nu], in_=x_[:, kc, :nu], func=AF.Square),
                     reads=[rx], writes=[rq])
                P.op("pool", lambda e, q=q: e.tensor_tensor(out=red[:, :nu], in0=red[:, :nu], in1=q[:, :nu], op=ALU.add),
                     reads=[rq, (tag, "red")], writes=[(tag, "red")])
            P.op("pe", lambda e: e.matmul(ps[0][:, :nu], lhsT=cx.ones_f[:, :], rhs=red[:, :nu], start=True, stop=True),
                 reads=[(tag, "red"), ("c", "ones_f")], writes=[rps[0]])
            P.op("act", lambda e: e.activation(out=std[:, :nu], in_=ps[0][:, :nu], func=AF.Sqrt, scale=1.0 / D, bias=cx.eps_t[:, 0:1]),
                 reads=[rps[0], ("c", "eps")], writes=[(tag, "std")])
            P.op("dve", lambda e: e.reciprocal(out=rstd[:, :nu], in_=std[:, :nu]),
                 reads=[(tag, "std")], writes=[(tag, "rstd")])
            for kc in range(KC):
                P.op("dve", lambda e, kc=kc: e.scalar_tensor_tensor(
                    out=Hs[s][:, kc, :nu], in0=x_[:, kc, :nu], scalar=g_t[:, kc:kc + 1], in1=rstd[:, :nu],
                    op0=ALU.mult, op1=ALU.mult),
                    reads=[rx, (tag, "rstd"), rvec], writes=[(tag, "H", s)])

        def proj(gi):
            o0, n, h = groups[gi]
            nu = n + h
            s = gi % 2
            H = Hs[s]
            rH = (tag, "H", s)
            t0 = o0 - h
            lo = max(o0, HALO)
            hi = o0 + n
            own = lo < hi
            c_lo, c_hi = lo - t0, hi - t0
            tk = lo - HALO
            for c in range(12):
                if c < 8 and not own:
                    continue
                ws = cnt["w"] % 3
                cnt["w"] += 1
                rw = (tag, "wq", ws)
                P.op("sp", lambda e, c=c, ws=ws: e.dma_start(
                    out=wq[ws][:, :, :].rearrange("p k o -> p (k o)"), in_=wqk_bf[c]),
                    writes=[rw], dma_key=("wq", ws))
                b = cnt["pj"] % 2
                cnt["pj"] += 1
                pj, rpj = ps[1 + b], rps[1 + b]
                for kc in range(KC):
                    P.op("pe", lambda e, kc=kc, ws=ws, pj=pj: e.matmul(
                        pj[:, :nu], lhsT=wq[ws][:, kc, :], rhs=H[:, kc, :nu],
                        start=(kc == 0), stop=(kc == KC - 1)),
                        reads=[rw, rH], writes=[rpj])
                if c < 8:
                    a = cnt["q"] % 2
                    cnt["q"] += 1
                    rqs, rqd, rqo = (tag, "qsq", a), (tag, "qstd", a), (tag, "qo", a)
                    P.op("act", lambda e, a=a, pj=pj: e.activation(out=qsq[a][:, :nu], in_=pj[:, :nu], func=AF.Square),
                         reads=[rpj], writes=[rqs])
                    P.op("pe", lambda e, a=a: e.matmul(ps[3][:, :nu], lhsT=cx.ones2_f[:, :], rhs=qsq[a][:, :nu], start=True, stop=True),
                         reads=[rqs, ("c", "ones2_f")], writes=[rps[3]])
                    P.op("act", lambda e, a=a: e.activation(out=qstd[a][:, :nu], in_=ps[3][:, :nu], func=AF.Sqrt, scale=1.0 / 64, bias=cx.eps_t[:, 0:1]),
                         reads=[rps[3], ("c", "eps")], writes=[rqd])
                    P.op("dve", lambda e, a=a: e.reciprocal(out=qstd[a][:, :nu], in_=qstd[a][:, :nu]),
                         reads=[rqd], writes=[rqd])
                    gsc = gq8[:, 0:1] if c < 4 else vec[:, 9:10]
                    P.op("dve", lambda e, a=a, pj=pj, gsc=gsc: e.scalar_tensor_tensor(
                        out=qo[a][:, :nu], in0=pj[:, :nu], scalar=gsc, in1=qstd[a][:, :nu], op0=ALU.mult, op1=ALU.mult),
                        reads=[rpj, rqd, rvec, (tag, "gq8")], writes=[rqo])
                    dst = dr["Qc"] if c < 4 else dr["Kc"]
                    cc = c % 4
                    P.op("pool", lambda e, a=a, dst=dst, cc=cc: e.dma_start(
                        out=dst[cc * 128:(cc + 1) * 128, tk:tk + (hi - lo)], in_=qo[a][:, c_lo:c_hi]),
                        reads=[rqo], writes=[(tag, "QKc")], dma_key=("qo", a))
                else:
                    g = c - 8
                    a = cnt["u"] % 2
                    cnt["u"] += 1
                    u_ = ub[a]
                    ru = (tag, "ub", a)
                    P.op("act", lambda e, u_=u_, pj=pj: e.activation(out=u_[:, :nu], in_=pj[:, :nu], func=AF.Identity),
                         reads=[rpj], writes=[ru])
                    src, rsrc = u_, ru
                    bufs = [(sa, (tag, "sa")), (sb_, (tag, "sb"))]
                    for st in range(g + 1):
                        sh = 1 << st
                        dstt, rdst = bufs[st % 2]
                        P.op("dve", lambda e, src=src, dstt=dstt, sh=sh: e.tensor_tensor(
                            out=dstt[:, sh:nu], in0=src[:, sh:nu], in1=src[:, 0:nu - sh], op=ALU.add),
                            reads=[rsrc], writes=[rdst])
                        P.op("pool", lambda e, src=src, dstt=dstt, sh=sh: e.tensor_copy(out=dstt[:, 0:sh], in_=src[:, 0:sh]),
                             reads=[rsrc], writes=[rdst])
                        src, rsrc = dstt, rdst
                    w_ = POOL_W[g]
                    m_ = mx[a]
                    rm = (tag, "mx", a)
                    P.op("dve", lambda e, src=src, u_=u_, m_=m_, w_=w_: e.scalar_tensor_tensor(
                        out=m_[:, :n], in0=src[:, h:nu], scalar=1.0 / w_, in1=u_[:, h:nu], op0=ALU.mult, op1=ALU.subtract),
                        reads=[rsrc, ru], writes=[rm])
                    if gi == 0:
                        tt = sqk[0]
                        rt = (tag, "sqk", 0)
                        P.op("dve", lambda e, src=src, g=g, tt=tt: e.tensor_tensor(
                            out=tt[:, 0:32], in0=src[:, HALO:HALO + 32], in1=invc[:, g * 32:(g + 1) * 32], op=ALU.mult),
                            reads=[rsrc, rinvc], writes=[rt])
                        P.op("dve", lambda e, u_=u_, m_=m_, tt=tt: e.tensor_tensor(
                            out=m_[:, HALO:HALO + 32], in0=tt[:, 0:32], in1=u_[:, HALO:HALO + 32], op=ALU.subtract),
                            reads=[rt, ru, rm], writes=[rm])
                    P.op("pe", lambda e, g=g, m_=m_: e.matmul(ps[7][:, :n], lhsT=wp[:, g, :], rhs=m_[:, :n], start=True, stop=True),
                         reads=[rm, (tag, "wp")], writes=[rps[7]])
                    p_ = po[a]
                    rp_ = (tag, "po", a)
                    P.op("act", lambda e, g=g, p_=p_: e.activation(out=p_[:, :n], in_=ps[7][:, :n], func=AF.Identity,
                                                                   scale=vec[:, 10 + g:11 + g]),
                         reads=[rps[7], rvec], writes=[rp_])
                    P.op("pool", lambda e, g=g, p_=p_: e.dma_start(out=dr["Pout"][g * 128:(g + 1) * 128, o0:o0 + n], in_=p_[:, :n]),
                         reads=[rp_], writes=[(tag, "Pout")], dma_key=("po", a))
            if not own:
                return
            for kc in range(KC):
                P.op("pe", lambda e, kc=kc: e.matmul(ps[6][0:8, :nu], lhsT=wf[:, kc, :], rhs=H[:, kc, :nu],
                                                     start=(kc == 0), stop=(kc == KC - 1)),
                     reads=[(tag, "wf"), rH], writes=[rps[6]])
            a = cnt["f"] % 2
            cnt["f"] += 1
            P.op("act", lambda e: e.activation(out=fe[:, :nu], in_=ps[6][0:8, :nu], func=AF.Exp, scale=-1.0, bias=nbf[:, 0:1]),
                 reads=[rps[6], (tag, "nbf")], writes=[(tag, "fe")])
            P.op("act", lambda e, a=a: e.activation(out=fo[a][:, :nu], in_=fe[:, :nu], func=AF.Ln, scale=1.0, bias=cx.one_t[0:8, 0:1]),
                 reads=[(tag, "fe"), ("c", "one")], writes=[(tag, "fo", a)])
            P.op("pool", lambda e, a=a: e.dma_start(out=dr["Fc"][:, tk:tk + (hi - lo)], in_=fo[a][:, c_lo:c_hi]),
                 reads=[(tag, "fo", a)], writes=[(tag, "Fc")], dma_key=("fo", a))
            c0 = c_lo
            while c0 < c_hi:
                m = min(128, c_hi - c0)
                b = cnt["v"] % 2
                cnt["v"] += 1
                pv, rpv = ps[4 + b], rps[4 + b]
                for kc in range(KC):
                    P.op("pe", lambda e, kc=kc, c0=c0, m=m, pv=pv: e.matmul(
                        pv[:m, :], lhsT=H[:, kc, c0:c0 + m], rhs=wv[:, kc, :], start=(kc == 0), stop=(kc == KC - 1)),
                        reads=[rH, (tag, "wv")], writes=[rpv])
                v_ = vo[b]
                rv = (tag, "vo", b)
                P.op("act", lambda e, m=m, pv=pv, v_=v_: e.activation(out=v_[:m, :], in_=pv[:m, :], func=AF.Copy),
                     reads=[rpv], writes=[rv])
                tok = tk + (c0 - c_lo)
                P.op("pool", lambda e, m=m, v_=v_, tok=tok: e.dma_start(out=dr["Vc"][tok:tok + m, :], in_=v_[:m, :]),
                     reads=[rv], writes=[(tag, "Vc")], dma_key=("vo", b))
                c0 += m

        ng = len(groups)
        pre(0)
        for gi in range(ng):
            if gi + 1 < ng:
                pre(gi + 1)
            proj(gi)


GROUPS4 = [[0, 1, 2, 3], [4, 5, 6, 7]]
import os
ADBG = int(os.environ.get("ATT_DBG", "5"))


def attn_phase(cx, cfg, tag):
    P = cx.P
    dr = cx.dram
    CH_, SEQ_, NKB, NQB = cfg.CH, cfg.SEQ, cfg.NKB, cfg.NQB
    segl = SEQ_ // 64
    nbs = CH_ // 128
    q4 = CH_ // 4
    for cc in range(4):
        for nm in ("Q", "K"):
            P.op("pool", lambda e, nm=nm, cc=cc: e.collective_compute(
                "AllGather", ALU.bypass, replica_groups=GROUPS4,
                ins=[dr[nm + "c"][cc * 128:(cc + 1) * 128, :].opt()], outs=[dr[nm + "g"][cc * 512:(cc + 1) * 512, :].opt()]),
                reads=[(tag, nm + "c")], writes=[(tag, nm + "g")], dma_key=("G", "cc" + nm), inc=1)
        P.op("pool", lambda e, cc=cc: e.collective_compute(
            "AllGather", ALU.bypass, replica_groups=GROUPS4,
            ins=[dr["Vc"][cc * q4:(cc + 1) * q4, :].opt()], outs=[dr["Vg"][cc * CH_:(cc + 1) * CH_, :].opt()]),
            reads=[(tag, "Vc")], writes=[(tag, "Vg")], dma_key=("G", "ccV"), inc=1)
    P.op("pool", lambda e: e.collective_compute(
        "AllGather", ALU.bypass, replica_groups=GROUPS4, ins=[dr["Fc"].opt()], outs=[dr["Fg"].opt()]),
        reads=[(tag, "Fc")], writes=[(tag, "Fg")], dma_key=("G", "ccF"), inc=1)
    cx.chk("attn_ag")
    with Phase(cx):
        KT = [cx.sb([128, SEQ_], BF16, "KT") for _ in range(2)]
        VT = [cx.sb([128, NKB, 128], BF16, "VT") for _ in range(2)]
        Lm = cx.sb([128, 128], F32, "Lm")
        sp = cx.sb([128, segl], F32, "sp")
        onesl = cx.sb([128, segl], F32, "onesl")
        cl = cx.sb([128, segl], F32, "cl")
        offs = cx.sb([128, 1], F32, "offs")
        r1 = cx.sb([128, segl], F32, "r1")
        hi = cx.sb([128, segl], BF16, "hi")
        mid = cx.sb([128, segl], BF16, "mid")
        lo = cx.sb([128, segl], BF16, "lo")
        nhi = cx.sb([128, segl], BF16, "nhi")
        nmid = cx.sb([128, segl], BF16, "nmid")
        nlo = cx.sb([128, segl], BF16, "nlo")
        onesb = cx.sb([128, segl], BF16, "onesb")
        zt = cx.sb([128, 4, HALO], BF16, "zt")
        NQT = 3
        qt = [cx.sb([128, 512], BF16, "qt") for _ in range(NQT)]
        NPT = 4
        pt = [cx.sb([128, 512], BF16, "pt") for _ in range(NPT)]
        rec = [cx.sb([64, 512], F32, "rec") for _ in range(2)]
        ost = [cx.sb([64, 512], BF16, "ost") for _ in range(2)]
        NS = 4
        ps_s = [cx.ps([128, 512], F32, "pss") for _ in range(NS)]
        ps_o = [cx.ps([128, 512], F32, "pso") for _ in range(2)]
        ps_m = cx.ps([128, 512], F32, "psm")

        P.op("dve", lambda e: e.memset(Lm[:, :], 1.0), writes=[(tag, "Lm")])
        P.op("pool", lambda e: e.affine_select(out=Lm[:, :], in_=Lm[:, :], pattern=[[1, 128]], compare_op=ALU.is_gt,
                                               fill=0.0, base=0, channel_multiplier=-1),
             reads=[(tag, "Lm")], writes=[(tag, "Lm")])
        P.op("dve", lambda e: e.memset(Lm[0:64, 64:128], 0.0), reads=[(tag, "Lm")], writes=[(tag, "Lm")])
        P.op("dve", lambda e: e.memset(onesl[:, :], 1.0), writes=[(tag, "onesl")])
        P.op("dve", lambda e: e.memset(onesb[:, :], 1.0), writes=[(tag, "onesb")])
        P.op("dve", lambda e: e.memset(zt[:, :, :], 0.0), writes=[(tag, "zt")])
        P.op("sp", lambda e: e.dma_start(out=dr["Og"][0:512, CH_ - HALO:CH_].rearrange("(a p) t -> p a t", p=128), in_=zt[:, :, :]),
             reads=[(tag, "zt")], writes=[(tag, "Ogpad")], dma_key=("G", "small"))
        for h in range(2):
            P.op("dve", lambda e, h=h: e.memset(VT[h][:, :, 64:128], 1.0), writes=[(tag, "VTo", h)])

        def loc_k(e, nm):
            i4 = pid4(e)
            src = dr[nm + "g"].rearrange("(c s r) t -> c s r t", c=4, s=4)[i4, :, :, :]
            return e.dma_start(out=dr[nm + "l"].rearrange("r (s t) -> s r t", s=4), in_=src)

        def loc_v(e, j):
            i4 = pid4(e)
            src = dr["Vg"][j * CH_:(j + 1) * CH_, :].rearrange("(s n) (i c) -> s n i c", s=4, i=4)[:, :, i4, :]
            return e.dma_start(out=dr["Vl"].rearrange("(s j n) c -> j s n c", s=4, j=4)[j], in_=src)

        def loc_f(e):
            i4 = pid4(e)
            src = dr["Fg"].rearrange("(s i h) t -> s i h t", s=4, i=4)[:, i4, :, :]
            return e.dma_start(out=dr["Fl"].rearrange("(s h) t -> s h t", s=4), in_=src)

        P.op("sp", loc_f, reads=[(tag, "Fg")], writes=[(tag, "Fl")], dma_key=("G", "locf"))
        P.op("sp", lambda e: loc_k(e, "K"), reads=[(tag, "Kg")], writes=[(tag, "Kl")], dma_key=("G", "lock"))
        P.op("sp", lambda e: loc_k(e, "Q"), reads=[(tag, "Qg")], writes=[(tag, "Ql")], dma_key=("G", "locq"))
        for j in range(4):
            P.op("sp", lambda e, j=j: loc_v(e, j), reads=[(tag, "Vg")], writes=[(tag, "Vl")], dma_key=("G", "locv"))

        cx.chk("attn_loc")

        def ld_k(e, h, s):
            return e.dma_start(out=KT[h][0:64, s * CH_:(s + 1) * CH_], in_=dr["Kl"][h * 64:(h + 1) * 64, s * CH_:(s + 1) * CH_])

        def ld_v(e, h, s):
            src = dr["Vl"][s * CH_:(s + 1) * CH_, h * 64:(h + 1) * 64].rearrange("(b p) c -> p b c", p=128)
            return e.dma_start(out=VT[h][:, s * nbs:(s + 1) * nbs, 0:64], in_=src)

        def ld_f(e, h, s):
            src = dr["Fl"][s * 2 + h:s * 2 + h + 1, :].rearrange("o (j t) -> (o j) t", t=segl)
            return e.dma_start(out=sp[h * 64 + s * 16:h * 64 + (s + 1) * 16, :], in_=src)

        for h in range(2):
            for s in range(4):
                P.op("sp", lambda e, h=h, s=s: ld_f(e, h, s), reads=[(tag, "Fl")], writes=[(tag, "sp", h, s)],
                     dma_key=("G", "ldf"))
        for h in range(2):
            for s in range(4):
                P.op("sp", lambda e, h=h, s=s: ld_k(e, h, s), reads=[(tag, "Kl")], writes=[(tag, "KT", h, s)],
                     dma_key=("G", "ldk"))
        rsp = [(tag, "sp", h, s) for h in range(2) for s in range(4)]
        P.op("dve", lambda e: e.tensor_tensor_scan(out=cl[:, :], data0=onesl[:, :], data1=sp[:, :], initial=0.0,
                                                   op0=ALU.mult, op1=ALU.add),
             reads=rsp + [(tag, "onesl")], writes=[(tag, "cl")])
        P.op("pe", lambda e: e.matmul(ps_m[:, 0:2], lhsT=Lm[:, :], rhs=cl[:, segl - 2:segl], start=True, stop=True),
             reads=[(tag, "cl"), (tag, "Lm")], writes=[(tag, "psm")])
        P.op("dve", lambda e: e.tensor_copy(out=offs[:, :], in_=ps_m[:, 1:2]), reads=[(tag, "psm")], writes=[(tag, "offs")])
        P.op("dve", lambda e: e.tensor_scalar(out=cl[:, :], in0=cl[:, :], scalar1=offs[:, 0:1], scalar2=None, op0=ALU.add),
             reads=[(tag, "cl"), (tag, "offs")], writes=[(tag, "cl")])
        P.op("dve", lambda e: e.tensor_copy(out=hi[:, :], in_=cl[:, :]), reads=[(tag, "cl")], writes=[(tag, "hi")])
        P.op("dve", lambda e: e.tensor_tensor(out=r1[:, :], in0=cl[:, :], in1=hi[:, :], op=ALU.subtract),
             reads=[(tag, "cl"), (tag, "hi")], writes=[(tag, "r1")])
        P.op("dve", lambda e: e.tensor_copy(out=mid[:, :], in_=r1[:, :]), reads=[(tag, "r1")], writes=[(tag, "mid")])
        P.op("dve", lambda e: e.tensor_tensor(out=r1[:, :], in0=r1[:, :], in1=mid[:, :], op=ALU.subtract),
             reads=[(tag, "r1"), (tag, "mid")], writes=[(tag, "r1")])
        P.op("dve", lambda e: e.tensor_copy(out=lo[:, :], in_=r1[:, :]), reads=[(tag, "r1")], writes=[(tag, "lo")])
        for src_, dst_, nm in ((hi, nhi, "nhi"), (mid, nmid, "nmid"), (lo, nlo, "nlo")):
            P.op("dve", lambda e, src_=src_, dst_=dst_: e.tensor_scalar(out=dst_[:, :], in0=src_[:, :], scalar1=-1.0, scalar2=None, op0=ALU.mult),
                 reads=[(tag, "hi"), (tag, "mid"), (tag, "lo")], writes=[(tag, nm)])
        cx.chk("attn_cs")
        k = 0
        for h in range(2):
            for r, (tk_, tq_) in enumerate(((hi, onesb), (mid, onesb), (lo, onesb), (onesb, nhi), (onesb, nmid), (onesb, nlo))):
                for dst_nm, t_ in (("AugK", tk_), ("AugQ", tq_)):
                    P.op("sp", lambda e, h=h, r=r, dst_nm=dst_nm, t_=t_: e.dma_start(
                        out=dr[dst_nm][h * 6 + r:h * 6 + r + 1, :].rearrange("o (j t) -> (o j) t", t=segl), in_=t_[h * 64:(h + 1) * 64, :]),
                        reads=[(tag, "hi"), (tag, "mid"), (tag, "lo"), (tag, "nhi"), (tag, "nmid"), (tag, "nlo"), (tag, "onesb")],
                        writes=[(tag, dst_nm, h)], dma_key=("G", "aug"))
                    k += 1
        cx.chk("attn_aug")
        for h in range(2):
            P.op("sp", lambda e, h=h: e.dma_start(out=KT[h][64:70, :], in_=dr["AugK"][h * 6:(h + 1) * 6, :]),
                 reads=[(tag, "AugK", h)], writes=[(tag, "KTa", h)], dma_key=("G", "ldka"))
        cx.chk("attn_ka")
        for h in range(2):
            for s in range(4):
                P.op("sp", lambda e, h=h, s=s: ld_v(e, h, s), reads=[(tag, "Vl")], writes=[(tag, "VT", h, s)],
                     dma_key=("G", "ldv"))

        cx.chk("attn_ld")
        steps = []
        DEPTH = 2
        state = {"qi": -1, "oi": -1}
        qinfo = {}

        def load_q(h, qb):
            state["qi"] += 1
            qs = state["qi"] % NQT
            rq = (tag, "qt", qs)
            Q0 = qb * 512
            s = Q0 // CH_
            c0 = Q0 % CH_

            P.op("sp", lambda e: e.dma_start(out=qt[qs][0:64, :], in_=dr["Ql"][h * 64:(h + 1) * 64, Q0:Q0 + 512]),
                 reads=[(tag, "Ql")], writes=[rq], dma_key=("ldq", qs))
            P.op("sp", lambda e: e.dma_start(out=qt[qs][64:70, :], in_=dr["AugQ"][h * 6:(h + 1) * 6, Q0:Q0 + 512]),
                 reads=[(tag, "AugQ", h)], writes=[(tag, "qta", qs)], dma_key=("ldqa", qs))
            qinfo[(h, qb)] = qs

        def qk(i):
            h, qb, kb, nk = steps[i]
            if kb == 0:
                load_q(h, qb)
            qs = qinfo[(h, qb)]
            sb_i = i % NS
            j = kb - 4 * qb
            a = 128 * j if j > 0 else 0
            diag = j >= 0
            s_src = (kb * 128) // CH_
            rk = [(tag, "KT", h, s_src), (tag, "KTa", h)]
            P.op("pe", lambda e: e.matmul(ps_s[sb_i][:, a:512], lhsT=KT[h][0:70, kb * 128:(kb + 1) * 128],
                                          rhs=qt[qs][0:70, a:512], start=True, stop=not diag),
                 reads=rk + [(tag, "qt", qs), (tag, "qta", qs)], writes=[(tag, "pss", sb_i)])
            if diag and ADBG >= 2:
                P.op("pe", lambda e: e.matmul(ps_s[sb_i][:, a:a + 128], lhsT=cx.ident_b[:, :], rhs=cx.maskb[:, :],
                                              start=False, stop=True),
                     reads=[("c", "ident"), ("c", "maskb")], writes=[(tag, "pss", sb_i)])
            pi = i % NPT
            if ADBG < 3:
                return
            P.op("act", lambda e: e.activation(out=pt[pi][:, a:512], in_=ps_s[sb_i][:, a:512], func=AF.Exp),
                 reads=[(tag, "pss", sb_i)], writes=[(tag, "pt", pi)])

        def pv(i):
            if ADBG < 4:
                return
            h, qb, kb, nk = steps[i]
            j = kb - 4 * qb
            a = 128 * j if j > 0 else 0
            pi = i % NPT
            s_src = (kb * 128) // CH_
            if kb == 0:
                state["oi"] += 1
            ob = state["oi"] % 2
            P.op("pe", lambda e: e.matmul(ps_o[ob][:, a:512], lhsT=VT[h][:, kb, :], rhs=pt[pi][:, a:512],
                                          start=(kb == 0), stop=(kb == nk - 1)),
                 reads=[(tag, "VT", h, s_src), (tag, "VTo", h), (tag, "pt", pi)], writes=[(tag, "pso", ob)])
            if kb == nk - 1 and ADBG >= 5:
                Q0 = qb * 512
                P.op("dve", lambda e: e.reciprocal(out=rec[ob][:, :], in_=ps_o[ob][64:128, :]),
                     reads=[(tag, "pso", ob)], writes=[(tag, "rec", ob)])
                P.op("dve", lambda e: e.tensor_tensor(out=ost[ob][:, :], in0=ps_o[ob][0:64, :], in1=rec[ob][:, :], op=ALU.mult),
                     reads=[(tag, "pso", ob), (tag, "rec", ob)], writes=[(tag, "ost", ob)])
                jc, c0_ = Q0 // CH_, Q0 % CH_
                P.op("pool", lambda e: e.dma_start(out=dr["Oc"][jc * 128 + h * 64:jc * 128 + (h + 1) * 64, c0_:c0_ + 512], in_=ost[ob][:, :]),
                     reads=[(tag, "ost", ob)], writes=[(tag, "Oc")], dma_key=("ost", ob))

        for hh in range(2):
            base = len(steps)
            for qb in range(NQB):
                nk = 4 * qb + 4
                for kb in range(nk):
                    steps.append((hh, qb, kb, nk))
            n = len(steps)
            for i in range(base, n + DEPTH):
                if i < n:
                    qk(i)
                if i - DEPTH >= base:
                    pv(i - DEPTH)
            if hh == 0:
                P.barrier()
    for jc in range(4):
        P.op("pool", lambda e, jc=jc: e.collective_compute(
            "AllGather", ALU.bypass, replica_groups=GROUPS4,
            ins=[dr["Oc"][jc * 128:(jc + 1) * 128, :].opt()], outs=[dr["Og"][(jc + 1) * 512:(jc + 2) * 512, :].opt()]),
            reads=[(tag, "Oc")], writes=[(tag, "Og")], dma_key=("G", "ccO"), inc=1)


def even_post_phase(cx, cfg, X_in, X_out, wo_bf, tag):
    P = cx.P
    dr = cx.dram
    groups = make_groups(cfg.T, 512, 512, 0)
    Xi = X_in.rearrange("(kc p) t -> p kc t", p=128)
    Xo = X_out.rearrange("(kc p) t -> p kc t", p=128)
    Alv = dr["Al"].rearrange("(kc p) t -> p kc t", p=128)
    Pov = dr["Pout"].rearrange("(kc p) t -> p kc t", p=128)

    Og3 = dr["Og"].rearrange("(j r) t -> j r t", r=512)

    def loc_a(e):
        i4 = pid4(e)
        return e.dma_start(out=dr["Al"][:, HALO:], in_=Og3[i4 + 1, :, :])

    def loc_h(e):
        i4 = pid4(e)
        return e.dma_start(out=dr["Al"][:, 0:HALO], in_=Og3[i4, :, cfg.CH - HALO:cfg.CH])
    P.op("sp", loc_a, reads=[(tag, "Og")], writes=[(tag, "Al")], dma_key=("G", "loca"))
    P.op("sp", loc_h, reads=[(tag, "Og")], writes=[(tag, "Alh")], dma_key=("G", "loca"))
    with Phase(cx):
        xt = [cx.sb([128, KC, 512], F32, "xt") for _ in range(2)]
        At = [cx.sb([128, KC, 512], BF16, "At") for _ in range(2)]
        wo = [cx.sb([128, KC, 128], BF16, "wo") for _ in range(3)]
        xo = [cx.sb([128, KC, 512], F32, "xo") for _ in range(2)]
        ps = [cx.ps([128, 512], F32, "ps") for _ in range(2)]
        cnt = {"w": 0, "o": 0}
        for gi, (o0, n, h) in enumerate(groups):
            s = gi % 2
            rx, rA, rxo = (tag, "xt", s), (tag, "At", s), (tag, "xo", s)
            P.op("sp", lambda e, s=s, o0=o0, n=n: e.dma_start(out=xt[s][:, :, :n], in_=Xi[:, :, o0:o0 + n]),
                 reads=[(tag, "Xin")], writes=[rx], dma_key=("xt", s))

            P.op("sp", lambda e, s=s, o0=o0, n=n: e.dma_start(out=At[s][:, 0:4, :n], in_=Alv[:, :, o0:o0 + n]),
                 reads=[(tag, "Al"), (tag, "Alh")], writes=[(tag, "Ata", s)], dma_key=("Ata", s))
            P.op("sp", lambda e, s=s, o0=o0, n=n: e.dma_start(out=At[s][:, 4:8, :n], in_=Pov[:, :, o0:o0 + n]),
                 reads=[(tag, "Pout")], writes=[(tag, "Atp", s)], dma_key=("Atp", s))
            for oc in range(KC):
                ws = cnt["w"] % 3
                cnt["w"] += 1
                rw = (tag, "wo", ws)
                P.op("sp", lambda e, oc=oc, ws=ws: e.dma_start(out=wo[ws][:, :, :].rearrange("p k o -> p (k o)"), in_=wo_bf[oc]),
                     writes=[rw], dma_key=("wo", ws))
                b = cnt["o"] % 2
                cnt["o"] += 1
                rp = (tag, "ps", b)
                for kc in range(KC):
                    P.op("pe", lambda e, kc=kc, ws=ws, b=b, s=s, n=n: e.matmul(
                        ps[b][:, :n], lhsT=wo[ws][:, kc, :], rhs=At[s][:, kc, :n], start=(kc == 0), stop=(kc == KC - 1)),
                        reads=[rw, (tag, "Ata", s), (tag, "Atp", s)], writes=[rp])
                P.op("dve", lambda e, oc=oc, b=b, s=s, n=n: e.tensor_tensor(
                    out=xo[s][:, oc, :n], in0=ps[b][:, :n], in1=xt[s][:, oc, :n], op=ALU.add),
                    reads=[rp, rx], writes=[rxo])
            store_group(cx, tag, xo[s], rxo, Xo, o0, n, gi == 0)


def cast_all(cx, items):
    P = cx.P
    CWD = 4096
    with Phase(cx):
        st_f = [cx.sb([128, CWD], F32, "cst_f") for _ in range(3)]
        st_b = [cx.sb([128, CWD], BF16, "cst_b") for _ in range(3)]
        i = 0
        for src, dst, rows, cols in items:
            for r0 in range(0, rows, 128):
                for c0 in range(0, cols, CWD):
                    cw_ = min(CWD, cols - c0)
                    s = i % 3
                    rf, rb = ("cf", s), ("cb", s)
                    P.op("sp", lambda e, src=src, r0=r0, c0=c0, cw_=cw_, s=s: e.dma_start(
                        out=st_f[s][:, :cw_], in_=src[r0:r0 + 128, c0:c0 + cw_]), writes=[rf], dma_key=rf)
                    if i % 2 == 0:
                        P.op("dve", lambda e, cw_=cw_, s=s: e.tensor_copy(out=st_b[s][:, :cw_], in_=st_f[s][:, :cw_]),
                             reads=[rf], writes=[rb])
                    else:
                        P.op("act", lambda e, cw_=cw_, s=s: e.activation(out=st_b[s][:, :cw_], in_=st_f[s][:, :cw_], func=AF.Copy),
                             reads=[rf], writes=[rb])
                    P.op("pool", lambda e, dst=dst, r0=r0, c0=c0, cw_=cw_, s=s: e.dma_start(
                        out=dst[r0:r0 + 128, c0:c0 + cw_], in_=st_b[s][:, :cw_]),
                        reads=[rb], writes=[("cdst",)], dma_key=("cst", s))
                    i += 1


def build_diag(cx, dww_dram, dg_bf, tag):
    P = cx.P
    with Phase(cx):
        dww, rdw = load_small(cx, tag, "dww", [128, KC * CW31], dww_dram)
        dst = [cx.sb([128, CW31, 128], BF16, "dgst") for _ in range(2)]
        for j in range(KC):
            s = j % 2
            rs = (tag, "dgst", s)
            for k in range(CW31):
                P.op("pool", lambda e, j=j, k=k, s=s: e.tensor_scalar(
                    out=dst[s][:, k, :], in0=cx.ident_b[:, :], scalar1=dww[:, j * CW31 + k:j * CW31 + k + 1], scalar2=0.0,
                    op0=ALU.mult, op1=ALU.add),
                    reads=[rdw, ("c", "ident")], writes=[rs])
            P.op("sp", lambda e, j=j, s=s: e.dma_start(out=dg_bf[j], in_=dst[s][:, :, :].rearrange("p k o -> p (k o)")),
                 reads=[rs], writes=[(tag, "dgbf")], dma_key=("dgst", s))


WSPEC_E = (("wqk", 12 * 128, 1024), ("wf", 128, 64), ("wv", 128, 4096), ("wp", 128, 512), ("wo", 8 * 128, 1024))
WSPEC_O = (("w1", 8 * 128, 2048), ("w2", 8 * 128, 1024))
WSPEC_F = (("wup", 22 * 128, 2048), ("wdn", 8 * 128, 2816))


def build_program(cfg, n_layers=4, do_ffn=True, stop=None):
    nc = bass.Bass("TRN2", target_bir_lowering=False)
    T_, CH_, SEQ_ = cfg.T, cfg.CH, cfg.SEQ

    def din(name, shape, dt=F32):
        return nc.dram_tensor(name, list(shape), dt, kind="ExternalInput").ap()

    def dsc(name, shape, dt):
        return nc.dram_tensor(name, list(shape), dt).ap()

    dr = {}
    dr["x"] = din("x", [D, T_])
    dr["flag"] = din("flag", [128, 1])
    dr["invc"] = din("invc", [128, 4 * 32])
    Y = nc.dram_tensor("y", [D, CH_], F32, kind="ExternalOutput").ap()
    XA = dsc("XA", [D, T_], F32)
    XB = dsc("XB", [D, T_], F32)
    W = {}
    casts = []
    n_even = (n_layers + 1) // 2
    n_odd = n_layers // 2
    for e_ in range(n_even):
        for nm, r, c in WSPEC_E:
            f = din("e%d_%s" % (e_, nm), [r, c])
            b = dsc("e%d_%s_b" % (e_, nm), [r, c], BF16)
            W[("e", e_, nm)] = b
            casts.append((f, b, r, c))
        W[("e", e_, "vec")] = din("e%d_vec" % e_, [128, 16])
        W[("e", e_, "bf")] = din("e%d_bf" % e_, [8, 1])
    for o_ in range(n_odd):
        for nm, r, c in WSPEC_O:
            f = din("o%d_%s" % (o_, nm), [r, c])
            b = dsc("o%d_%s_b" % (o_, nm), [r, c], BF16)
            W[("o", o_, nm)] = b
            casts.append((f, b, r, c))
        W[("o", o_, "vec")] = din("o%d_vec" % o_, [128, 32])
        W[("o", o_, "dww")] = din("o%d_dww" % o_, [128, KC * CW31])
        W[("o", o_, "dg")] = dsc("o%d_dg" % o_, [KC * 128, CW31 * 128], BF16)
    if do_ffn:
        for l in range(n_layers):
            for nm, r, c in WSPEC_F:
                f = din("f%d_%s" % (l, nm), [r, c])
                b = dsc("f%d_%s_b" % (l, nm), [r, c], BF16)
                W[("f", l, nm)] = b
                casts.append((f, b, r, c))
            W[("f", l, "g")] = din("f%d_g" % l, [128, 8])
            W[("f", l, "cw")] = din("f%d_cw" % l, [128, 132])
            W[("f", l, "cb")] = din("f%d_cb" % l, [128, 44])
    dr["Qc"] = dsc("Qc", [512, CH_], BF16)
    dr["Kc"] = dsc("Kc", [512, CH_], BF16)
    dr["Vc"] = dsc("Vc", [CH_, 512], BF16)
    dr["Fc"] = dsc("Fc", [8, CH_], F32)
    dr["Qg"] = dsc("Qg", [4 * 512, CH_], BF16)
    dr["Kg"] = dsc("Kg", [4 * 512, CH_], BF16)
    dr["Vg"] = dsc("Vg", [4 * CH_, 512], BF16)
    dr["Fg"] = dsc("Fg", [4 * 8, CH_], F32)
    dr["Ql"] = dsc("Ql", [128, SEQ_], BF16)
    dr["Kl"] = dsc("Kl", [128, SEQ_], BF16)
    dr["Vl"] = dsc("Vl", [SEQ_, 128], BF16)
    dr["Fl"] = dsc("Fl", [8, CH_], F32)
    dr["Al"] = dsc("Al", [512, T_], BF16)
    dr["AugK"] = dsc("AugK", [12, SEQ_], BF16)
    dr["AugQ"] = dsc("AugQ", [12, SEQ_], BF16)
    dr["Oc"] = dsc("Oc", [4 * 128, CH_], BF16)
    dr["Og"] = dsc("Og", [5 * 512, CH_], BF16)
    dr["Pout"] = dsc("Pout", [512, T_], BF16)

    def slabs(ap, p=128):
        return ap.rearrange("(j p) c -> j p c", p=p)

    with contextlib.ExitStack() as stack:
        cx = Ctx(nc, stack)
        cx.dram = dr
        setup_consts(cx)
        cx.P.barrier()
        class _Stop(Exception):
            pass

        def chk(name):
            if stop == name:
                raise _Stop()
        cx.chk = chk
        try:
            chk("consts")
            cast_all(cx, casts)
            chk("cast")
            for o_ in range(n_odd):
                build_diag(cx, W[("o", o_, "dww")], slabs(W[("o", o_, "dg")]), "dg%d" % o_)
            chk("diag")
            _build_layers(cx, cfg, dr, W, XA, XB, Y, n_layers, do_ffn, slabs, chk)
        except _Stop:
            pass
        cx.P.emit()
    return nc, None


def _build_layers(cx, cfg, dr, W, XA, XB, Y, n_layers, do_ffn, slabs, chk):
    if True:
        cur = dr["x"]
        bufs = [XA, XB]
        bi = 0
        n_sub = n_layers * (2 if do_ffn else 1)
        sub = 0

        def nxt():
            nonlocal bi
            b = bufs[bi]
            bi ^= 1
            return b
        for l in range(n_layers):
            i2 = l // 2
            if l % 2 == 0:
                tag = "E%d" % l
                even_pre_phase(cx, cfg, cur, slabs(W[("e", i2, "wqk")]), W[("e", i2, "wf")], W[("e", i2, "wv")],
                               W[("e", i2, "wp")], W[("e", i2, "vec")], W[("e", i2, "bf")], tag + "a")
                chk("E1")
                attn_phase(cx, cfg, tag + "b")
                cx.P.barrier()
                chk("attn")
                sub += 1
                last = (sub == n_sub)
                pass
                dst = nxt()
                even_post_phase(cx, cfg, cur, dst, slabs(W[("e", i2, "wo")]), tag + "c")
                cur = dst
                chk("E3")
            else:
                tag = "O%d" % l
                sub += 1
                dst = nxt()
                odd_phase(cx, cfg, cur, dst, slabs(W[("o", i2, "w1")]), slabs(W[("o", i2, "w2")]),
                          slabs(W[("o", i2, "dg")]), W[("o", i2, "vec")], tag)
                cur = dst
            if do_ffn:
                sub += 1
                last = (sub == n_sub)
                if last:
                    ffn_phase(cx, cfg, cur, Y, slabs(W[("f", l, "wup")]), slabs(W[("f", l, "wdn")]),
                              W[("f", l, "g")], W[("f", l, "cw")], W[("f", l, "cb")], "F%d" % l, out_off=HALO)
                else:
                    dst = nxt()
                    ffn_phase(cx, cfg, cur, dst, slabs(W[("f", l, "wup")]), slabs(W[("f", l, "wdn")]),
                              W[("f", l, "g")], W[("f", l, "cw")], W[("f", l, "cb")], "F%d" % l)
                    cur = dst
        if not do_ffn:
            cx.P.op("sp", lambda e: e.dma_start(out=Y, in_=cur[:, HALO:]), dma_key=("G", "fin"))


def lay_slabs(w):
    n = w.shape[1] // 128
    a = w.reshape(KC, 128, n, 128).transpose(2, 1, 0, 3)
    return np.ascontiguousarray(a).reshape(n * 128, KC * 128)


def lay_pairs(w, half):
    n = half // 128
    a = w.reshape(KC, 128, 2, n, 128).transpose(3, 1, 2, 0, 4)
    return np.ascontiguousarray(a).reshape(n * 128, 2 * KC * 128)


def lay_kmajor(w):
    c = w.shape[1]
    return np.ascontiguousarray(w.reshape(KC, 128, c).transpose(1, 0, 2)).reshape(128, KC * c)


def lay_wdn(w_down):
    a = w_down.reshape(NJ, 128, KC, 128).transpose(2, 1, 0, 3)
    return np.ascontiguousarray(a).reshape(KC * 128, NJ * 128)


def lay_vec(v, nch):
    return np.ascontiguousarray(v.reshape(nch, 128).T)


def lay_cw(cw):
    k = cw.shape[0]
    n = cw.shape[1] // 128
    return np.ascontiguousarray(cw.reshape(k, n, 128).transpose(2, 1, 0)).reshape(128, n * k)


def host_inputs(cfg, inp, n_layers=4, do_ffn=True):
    f32 = np.float32
    shared = {}
    n_even = (n_layers + 1) // 2
    n_odd = n_layers // 2
    for e_ in range(n_even):
        w_in = np.asarray(inp["even_w_in"][e_], f32)
        q, k, v = w_in[:, 0:512], w_in[:, 512:1024], w_in[:, 1024:1536]
        f, u = w_in[:, 1536:1544], w_in[:, 1544:2056]
        shared["e%d_wqk" % e_] = lay_slabs(np.concatenate([q, k, u], axis=1))
        shared["e%d_wf" % e_] = lay_kmajor(f)
        shared["e%d_wv" % e_] = lay_kmajor(v)
        wp = np.asarray(inp["even_w_pool"][e_], f32)
        shared["e%d_wp" % e_] = np.ascontiguousarray(wp.transpose(1, 0, 2)).reshape(128, 512)
        shared["e%d_wo" % e_] = lay_slabs(np.asarray(inp["even_w_out"][e_], f32))
        vec = np.zeros((128, 16), f32)
        vec[:, 0:8] = lay_vec(np.asarray(inp["even_norm_g"][e_], f32), 8)
        vec[:, 8] = np.tile(np.asarray(inp["even_q_norm_g"][e_], f32), 2)
        vec[:, 9] = np.tile(np.asarray(inp["even_k_norm_g"][e_], f32), 2)
        vec[:, 10:14] = lay_vec(np.asarray(inp["even_pool_scale"][e_], f32), 4)
        shared["e%d_vec" % e_] = vec
        shared["e%d_bf" % e_] = np.asarray(inp["even_b_f"][e_], f32).reshape(8, 1).copy()
    for o_ in range(n_odd):
        shared["o%d_w1" % o_] = lay_pairs(np.asarray(inp["odd_w_pw1"][o_], f32), 1024)
        shared["o%d_w2" % o_] = lay_slabs(np.asarray(inp["odd_w_pw2"][o_], f32))
        vec = np.zeros((128, 32), f32)
        vec[:, 0:8] = lay_vec(np.asarray(inp["odd_norm_g"][o_], f32), 8)
        vec[:, 8:16] = lay_vec(np.asarray(inp["odd_dw_b"][o_], f32), 8)
        vec[:, 16:24] = lay_vec(np.asarray(inp["odd_ln_g"][o_], f32), 8)
        vec[:, 24:32] = lay_vec(np.asarray(inp["odd_ln_b"][o_], f32), 8)
        shared["o%d_vec" % o_] = vec
        shared["o%d_dww" % o_] = lay_cw(np.asarray(inp["odd_dw_w"][o_], f32))
    if do_ffn:
        for l in range(n_layers):
            shared["f%d_wup" % l] = lay_pairs(np.asarray(inp["ffn_w_up"][l], f32), DFF)
            shared["f%d_wdn" % l] = lay_wdn(np.asarray(inp["ffn_w_down"][l], f32))
            shared["f%d_g" % l] = lay_vec(np.asarray(inp["ffn_norm_g"][l], f32), 8)
            shared["f%d_cw" % l] = lay_cw(np.asarray(inp["ffn_conv_w"][l], f32))
            shared["f%d_cb" % l] = lay_vec(np.asarray(inp["ffn_conv_b"][l], f32), 44)
    x = np.asarray(inp["x"], f32)
    maps = []
    for c in range(NCORES):
        b, i = c // 4, c % 4
        xt = np.zeros((D, cfg.T), f32)
        lo = i * cfg.CH - HALO
        if i == 0:
            xt[:, HALO:] = x[b, 0:cfg.CH].T
        else:
            xt[:, :] = x[b, lo:lo + cfg.T].T
        flag = np.full((128, 1), 0.0 if i == 0 else 1.0, f32)
        invc = np.zeros((128, 4 * 32), f32)
        pos = np.arange(1, 33, dtype=f32)
        for g, w in enumerate(POOL_W):
            cntv = np.minimum(pos, float(w)) if i == 0 else np.full(32, float(w), f32)
            invc[:, g * 32:(g + 1) * 32] = (1.0 / cntv)[None, :]
        m = dict(shared)
        m["x"] = xt
        m["flag"] = flag
        m["invc"] = invc
        maps.append(m)
    return maps


_CACHE = {}
N_SPLIT = 1


def _sub_inputs(inputs, l0, nl):
    out = {}
    for k, v in inputs.items():
        if k == "x":
            out[k] = v
        elif k.startswith("even_"):
            out[k] = v[(l0 + 1) // 2:]
        elif k.startswith("odd_"):
            out[k] = v[l0 // 2:]
        else:
            out[k] = v[l0:]
    return out


def kernel(**inputs):
    cfg = Cfg(4096)
    nl = 4 // N_SPLIT
    if "nc" not in _CACHE:
        _CACHE["nc"] = build_program(cfg, n_layers=nl)[0]
    nc = _CACHE["nc"]
    inp = {k: np.asarray(v) for k, v in inputs.items()}
    x = np.asarray(inp["x"], np.float32)
    for part in range(N_SPLIT):
        sub = _sub_inputs(inp, part * nl, nl)
        sub["x"] = x
        maps = host_inputs(cfg, sub, n_layers=nl)
        res = run_bass_kernel_spmd(nc, maps, core_ids=list(range(NCORES)))
        out = np.empty((2, SEQ, D), np.float32)
        for c in range(NCORES):
            b, i = c // 4, c % 4
            out[b, i * cfg.CH:(i + 1) * cfg.CH, :] = res.results[c]["y"].T
        x = out
    return x
```

```python
import contextlib
import numpy as np
import concourse.bass as bass
import concourse.mybir as mybir
from concourse.bass_utils import run_bass_kernel_spmd

F32 = mybir.dt.float32
BF16 = mybir.dt.bfloat16
AF = mybir.ActivationFunctionType
ALU = mybir.AluOpType

D = 1024
KC = 8
DFF = 2816
NJ = 22
NCORES = 8
SEQ = 16384
HALO = 128
CH = 2048
SEG = CH + HALO
T = 2 * SEG
EPS = 1e-6

SAME_ENGINE_SYNC = True
NO_SELF_SYNC = ("pe",)


class Op:
    __slots__ = ("eng", "fn", "deps", "is_dma", "sem", "val", "need_inc", "key", "phase", "inc")

    def __init__(self, eng, fn, is_dma, key=None):
        self.eng = eng
        self.fn = fn
        self.deps = []
        self.is_dma = is_dma
        self.sem = None
        self.val = None
        self.need_inc = False
        self.key = key
        self.inc = 16


class Prog:
    ENGS = ("pe", "act", "dve", "pool", "sp")

    def __init__(self, nc, stack):
        self.nc = nc
        self.stack = stack
        self.ops = {e: [] for e in self.ENGS}
        self.last_w = {}
        self.readers = {}
        self.dma_sems = {}
        self.n_ops = 0
        self.phase = 0

    def op(self, eng, fn, reads=(), writes=(), dma_key=None, inc=16):
        o = Op(eng, fn, dma_key is not None, dma_key)
        o.inc = inc
        o.phase = self.phase
        deps = []
        for r in reads:
            w = self.last_w.get(r)
            if w is not None:
                deps.append(w)
        for r in writes:
            w = self.last_w.get(r)
            if w is not None:
                deps.append(w)
            deps.extend(self.readers.get(r, ()))
        seen = set()
        for d in deps:
            if id(d) in seen or d is o:
                continue
            seen.add(id(d))
            if (not d.is_dma) and d.eng == eng and (eng in NO_SELF_SYNC or not SAME_ENGINE_SYNC):
                continue
            if d.is_dma and o.is_dma and d.key == o.key and isinstance(o.key, tuple) and o.key[0] == "G":
                continue
            o.deps.append(d)
            d.need_inc = True
        for r in reads:
            self.readers.setdefault(r, []).append(o)
        for r in writes:
            self.last_w[r] = o
            self.readers[r] = []
        self.ops[eng].append(o)
        self.n_ops += 1
        return o

    def barrier(self):
        deps = []
        for e in self.ENGS:
            for o in reversed(self.ops[e]):
                if not o.is_dma:
                    deps.append(o)
                    break
        last_dma = {}
        for e in self.ENGS:
            for o in self.ops[e]:
                if o.is_dma:
                    last_dma[o.key] = o
        deps.extend(last_dma.values())
        for e in self.ENGS:
            o = Op(e, lambda en: en.nop(), False)
            o.phase = self.phase
            for d in deps:
                if (not d.is_dma) and d.eng == e:
                    continue
                o.deps.append(d)
                d.need_inc = True
            self.ops[e].append(o)
        self.last_w = {}
        self.readers = {}
        self.phase += 1

    def emit(self, final_waits=()):
        nc = self.nc
        stack = self.stack
        eng_sem = {}
        ecnt = {}
        gtot = {}
        for e in self.ENGS:
            for o in self.ops[e]:
                if o.is_dma:
                    if o.key not in self.dma_sems:
                        self.dma_sems[o.key] = [
                            stack.enter_context(nc.semaphore("d%d" % len(self.dma_sems))), 0]
                    ent = self.dma_sems[o.key]
                    ent[1] += o.inc
                    o.sem, o.val = ent[0], ent[1]
                    if isinstance(o.key, tuple) and o.key[0] == "G":
                        gtot[(o.key, o.phase)] = ent[1]
                elif o.need_inc:
                    k_ = (e, o.phase % 4)
                    if k_ not in eng_sem:
                        eng_sem[k_] = stack.enter_context(nc.semaphore("s_%s_%d" % k_))
                        ecnt[k_] = 0
                    ecnt[k_] += 1
                    o.sem, o.val = eng_sem[k_], ecnt[k_]
        for e in self.ENGS:
            for o in self.ops[e]:
                if o.is_dma and (o.key, o.phase) in gtot:
                    o.val = gtot[(o.key, o.phase)]
        final = list(final_waits)
        block = stack.enter_context(nc.Block())
        handles = {"pe": "tensor", "act": "scalar", "dve": "vector", "pool": "gpsimd", "sp": "sync"}

        def run(ename, e):
            waited = {}
            for o in self.ops[ename]:
                for d in o.deps:
                    k = id(d.sem)
                    if waited.get(k, 0) >= d.val:
                        continue
                    e.wait_ge(d.sem, d.val)
                    waited[k] = d.val
                ins = o.fn(e)
                if o.is_dma:
                    ins.then_inc(o.sem, o.inc)
                elif o.need_inc:
                    ins.then_inc(o.sem, 1)
            if ename == "pool":
                for ent in self.dma_sems.values():
                    e.wait_ge(ent[0], ent[1])

        for ename in self.ENGS:
            dec = getattr(block, handles[ename])

            def body(e, _n=ename):
                run(_n, e)
            dec(body)


_PID = {}


def pid4(e):
    k = id(e)
    if k not in _PID:
        _PID[k] = e.partition_id() % 4
    return _PID[k]


class Ctx:
    def __init__(self, nc, stack):
        self.nc = nc
        self.stack = stack
        self.P = Prog(nc, stack)
        self._n = 0

    def sb(self, shape, dtype, name=None):
        self._n += 1
        return self.stack.enter_context(self.nc.sbuf_tensor("%s_%d" % (name or "sb", self._n), list(shape), dtype))

    def ps(self, shape, dtype=F32, name=None):
        self._n += 1
        return self.stack.enter_context(self.nc.psum_tensor("%s_%d" % (name or "ps", self._n), list(shape), dtype))


class Cfg:
    def __init__(self, CH=4096):
        self.CH = CH
        self.T = CH + HALO
        self.SEQ = 4 * CH
        self.NKB = self.SEQ // 128
        self.NQB = self.SEQ // 512


def make_groups(T_, first_n, step, halo):
    out = [(0, min(first_n, T_), 0)]
    pos = out[0][1]
    while pos < T_:
        n = min(step, T_ - pos)
        out.append((pos, n, halo))
        pos += n
    return out


class Phase:
    def __init__(self, cx):
        self.cx = cx

    def __enter__(self):
        self.st = contextlib.ExitStack()
        self.saved = self.cx.stack
        self.cx.stack = self.st
        return self

    def __exit__(self, *a):
        self.cx.P.barrier()
        self.cx.stack = self.saved
        self.st.close()
        return False


def setup_consts(cx):
    P = cx.P
    cx.ones_f = cx.sb([128, 128], F32, "ones_f")
    cx.ones2_f = cx.sb([128, 128], F32, "ones2_f")
    cx.eps_t = cx.sb([128, 1], F32, "eps")
    cx.one_t = cx.sb([128, 1], F32, "one")
    cx.ident_b = cx.sb([128, 128], BF16, "ident_b")
    cx.maskb = cx.sb([128, 128], BF16, "maskb")
    cx.flag = cx.sb([128, 1], F32, "flag")
    P.op("dve", lambda e: e.memset(cx.ones_f[:, :], 1.0), writes=[("c", "ones_f")])
    P.op("dve", lambda e: e.memset(cx.ones2_f[:, :], 1.0), writes=[("c", "ones2_f")])
    P.op("dve", lambda e: e.memset(cx.ones2_f[0:64, 64:128], 0.0), writes=[("c", "ones2_f")])
    P.op("dve", lambda e: e.memset(cx.ones2_f[64:128, 0:64], 0.0), writes=[("c", "ones2_f")])
    P.op("dve", lambda e: e.memset(cx.eps_t[:, :], EPS), writes=[("c", "eps")])
    P.op("dve", lambda e: e.memset(cx.one_t[:, :], 1.0), writes=[("c", "one")])
    P.op("dve", lambda e: e.memset(cx.ident_b[:, :], 0.0), writes=[("c", "ident")])
    P.op("pool", lambda e: e.affine_select(out=cx.ident_b[:, :], in_=cx.ident_b[:, :], pattern=[[1, 128]],
                                           compare_op=ALU.not_equal, fill=1.0, base=0, channel_multiplier=-1),
         reads=[("c", "ident")], writes=[("c", "ident")])
    P.op("dve", lambda e: e.memset(cx.maskb[:, :], 0.0), writes=[("c", "maskb")])
    P.op("pool", lambda e: e.affine_select(out=cx.maskb[:, :], in_=cx.maskb[:, :], pattern=[[1, 128]],
                                           compare_op=ALU.is_ge, fill=-30000.0, base=0, channel_multiplier=-1),
         reads=[("c", "maskb")], writes=[("c", "maskb")])
    P.op("sp", lambda e: e.dma_start(out=cx.flag[:, :], in_=cx.dram["flag"]), writes=[("c", "flag")], dma_key=("G", "small"))


def emit_rmsnorm(cx, xt, nu, g_t, H, res, sq, red, ps_stat, std, rstd):
    P = cx.P
    P.op("act", lambda e: e.activation(out=sq[:, :, :nu], in_=xt[:, :, :nu], func=AF.Square),
         reads=[res["xt"]], writes=[res["sq"]])
    P.op("pool", lambda e: e.tensor_tensor(out=red[:, :nu], in0=sq[:, 0, :nu], in1=sq[:, 1, :nu], op=ALU.add),
         reads=[res["sq"]], writes=[res["red"]])
    for kc in range(2, KC):
        P.op("pool", lambda e, kc=kc: e.tensor_tensor(out=red[:, :nu], in0=red[:, :nu], in1=sq[:, kc, :nu], op=ALU.add),
             reads=[res["sq"], res["red"]], writes=[res["red"]])
    P.op("pe", lambda e: e.matmul(ps_stat[:, :nu], lhsT=cx.ones_f[:, :], rhs=red[:, :nu], start=True, stop=True),
         reads=[res["red"], ("c", "ones_f")], writes=[res["ps_stat"]])
    P.op("act", lambda e: e.activation(out=std[:, :nu], in_=ps_stat[:, :nu], func=AF.Sqrt, scale=1.0 / D, bias=cx.eps_t[:, 0:1]),
         reads=[res["ps_stat"], ("c", "eps")], writes=[res["std"]])
    P.op("dve", lambda e: e.reciprocal(out=rstd[:, :nu], in_=std[:, :nu]),
         reads=[res["std"]], writes=[res["rstd"]])
    for kc in range(KC):
        P.op("dve", lambda e, kc=kc: e.scalar_tensor_tensor(
            out=H[:, kc, :nu], in0=xt[:, kc, :nu], scalar=g_t[:, kc:kc + 1], in1=rstd[:, :nu],
            op0=ALU.mult, op1=ALU.mult),
            reads=[res["xt"], res["rstd"], res["g"]], writes=[res["H"]])


def load_small(cx, tag, name, shape, src):
    t = cx.sb(shape, F32, name)
    r = (tag, name)
    cx.P.op("sp", lambda e: e.dma_start(out=t[:, :], in_=src), writes=[r], dma_key=("G", "small"))
    return t, r


def store_group(cx, tag, xo, rxo, Xo, o0, n, first):
    P = cx.P
    if first:
        P.op("dve", lambda e: e.tensor_scalar(out=xo[:, :, 0:HALO], in0=xo[:, :, 0:HALO], scalar1=cx.flag[:, 0:1],
                                              scalar2=None, op0=ALU.mult),
             reads=[rxo, ("c", "flag")], writes=[rxo])
    P.op("pool", lambda e: e.dma_start(out=Xo[:, :, o0:o0 + n], in_=xo[:, :, :n]),
         reads=[rxo], writes=[(tag, "Xout")], dma_key=("st", 0))


def ffn_phase(cx, cfg, X_in, X_out, wup_bf, wdn_bf, g_dram, cw_dram, cb_dram, tag, out_off=0):
    P = cx.P
    groups = make_groups(cfg.T, 512, 510, 2)
    Xi = X_in.rearrange("(kc p) t -> p kc t", p=128)
    Xo = X_out.rearrange("(kc p) t -> p kc t", p=128)
    with Phase(cx):
        g_t, rg = load_small(cx, tag, "g", [128, KC], g_dram)
        cw_t, rcw = load_small(cx, tag, "cw", [128, 44 * 3], cw_dram)
        cb_t, rcb = load_small(cx, tag, "cb", [128, 44], cb_dram)
        NX = 2
        xt = [cx.sb([128, KC, 512], F32, "xt") for _ in range(NX)]
        sq = cx.sb([128, KC, 512], F32, "sq")
        red = cx.sb([128, 512], F32, "red")
        std = cx.sb([128, 512], F32, "std")
        rstd = cx.sb([128, 512], F32, "rstd")
        Hs = [cx.sb([128, KC, 512], BF16, "H") for _ in range(2)]
        NW = 5
        wup = [cx.sb([128, 2, KC, 128], BF16, "wup") for _ in range(NW)]
        acc_g = [cx.sb([128, 512], F32, "accg") for _ in range(2)]
        acc_v = [cx.sb([128, 512], F32, "accv") for _ in range(2)]
        sg = [cx.sb([128, 512], F32, "sg") for _ in range(2)]
        Gs = [cx.sb([128, NJ, 512], BF16, "G") for _ in range(2)]
        ND = 3
        wdn = [cx.sb([128, NJ, 128], BF16, "wdn") for _ in range(ND)]
        xo = cx.sb([128, KC, 512], F32, "xo")
        ps_stat = cx.ps([128, 512], F32, "psst")
        ps_g = [cx.ps([128, 512], F32, "psg") for _ in range(2)]
        ps_v = [cx.ps([128, 512], F32, "psv") for _ in range(2)]
        ps_o = [cx.ps([128, 512], F32, "pso") for _ in range(2)]
        cnt = {"w": 0, "d": 0, "a": 0, "o": 0}

        def pre(gi):
            o0, n, h = groups[gi]
            nu = n + h
            s = gi % NX
            rx = (tag, "xt", s)
            P.op("sp", lambda e: e.dma_start(out=xt[s][:, :, :nu], in_=Xi[:, :, o0 - h:o0 + n]),
                 reads=[(tag, "Xin")], writes=[rx], dma_key=("xt", s))
            res = {"xt": rx, "sq": (tag, "sq"), "red": (tag, "red"), "ps_stat": (tag, "psst"), "std": (tag, "std"),
                   "rstd": (tag, "rstd"), "g": rg, "H": (tag, "H", gi % 2)}
            emit_rmsnorm(cx, xt[s], nu, g_t, Hs[gi % 2], res, sq, red, ps_stat, std, rstd)

        def up(gi):
            o0, n, h = groups[gi]
            nu = n + h
            H = Hs[gi % 2]
            G = Gs[gi % 2]
            rH = (tag, "H", gi % 2)
            rG = (tag, "G", gi % 2)
            for j in range(NJ):
                ws = cnt["w"] % NW
                cnt["w"] += 1
                rw = (tag, "wup", ws)
                P.op("sp", lambda e, j=j, ws=ws: e.dma_start(
                    out=wup[ws][:, :, :, :].rearrange("p a k o -> p (a k o)"), in_=wup_bf[j]),
                    writes=[rw], dma_key=("wup", ws))
                a = cnt["a"] % 2
                cnt["a"] += 1
                rpg, rpv = (tag, "psg", a), (tag, "psv", a)
                for gv, psb, rp in ((0, ps_g[a], rpg), (1, ps_v[a], rpv)):
                    for kc in range(KC):
                        P.op("pe", lambda e, gv=gv, kc=kc, psb=psb, ws=ws: e.matmul(
                            psb[:, :nu], lhsT=wup[ws][:, gv, kc, :], rhs=H[:, kc, :nu],
                            start=(kc == 0), stop=(kc == KC - 1)),
                            reads=[rw, rH], writes=[rp])
                ag, av, sgt = acc_g[a], acc_v[a], sg[a]
                rag, rav, rsg = (tag, "accg", a), (tag, "accv", a), (tag, "sg", a)
                pairs = ((ag, rag, ps_g[a], rpg, j), (av, rav, ps_v[a], rpv, NJ + j))
                for (acc, racc, psb, rp, ch) in pairs:
                    w2 = cw_t[:, ch * 3 + 2:ch * 3 + 3]
                    bb = cb_t[:, ch:ch + 1]
                    P.op("act", lambda e, acc=acc, psb=psb, w2=w2, bb=bb: e.activation(
                        out=acc[:, :n], in_=psb[:, h:h + n], func=AF.Identity, scale=w2, bias=bb),
                        reads=[rp, rcw, rcb], writes=[racc])
                for sh in (1, 2):
                    for (acc, racc, psb, rp, ch) in pairs:
                        wk = cw_t[:, ch * 3 + (2 - sh):ch * 3 + (3 - sh)]
                        if h >= sh:
                            P.op("dve", lambda e, acc=acc, psb=psb, wk=wk, sh=sh: e.scalar_tensor_tensor(
                                out=acc[:, :n], in0=psb[:, h - sh:h - sh + n], scalar=wk, in1=acc[:, :n],
                                op0=ALU.mult, op1=ALU.add),
                                reads=[rp, rcw, racc], writes=[racc])
                        else:
                            P.op("dve", lambda e, acc=acc, psb=psb, wk=wk, sh=sh: e.scalar_tensor_tensor(
                                out=acc[:, sh:n], in0=psb[:, 0:n - sh], scalar=wk, in1=acc[:, sh:n],
                                op0=ALU.mult, op1=ALU.add),
                                reads=[rp, rcw, racc], writes=[racc])
                P.op("act", lambda e, ag=ag, sgt=sgt: e.activation(out=sgt[:, :n], in_=ag[:, :n], func=AF.Silu),
                     reads=[rag], writes=[rsg])
                P.op("pool", lambda e, j=j, sgt=sgt, av=av: e.tensor_tensor(
                    out=G[:, j, :n], in0=sgt[:, :n], in1=av[:, :n], op=ALU.mult),
                    reads=[rsg, rav], writes=[rG])

        def down(gi):
            o0, n, h = groups[gi]
            G = Gs[gi % 2]
            rG = (tag, "G", gi % 2)
            s = gi % NX
            rx = (tag, "xt", s)
            rxo = (tag, "xo")
            for oc in range(KC):
                ds_ = cnt["d"] % ND
                cnt["d"] += 1
                rw = (tag, "wdn", ds_)
                P.op("sp", lambda e, oc=oc, ds_=ds_: e.dma_start(
                    out=wdn[ds_][:, :, :].rearrange("p j o -> p (j o)"), in_=wdn_bf[oc]),
                    writes=[rw], dma_key=("wdn", ds_))
                b = cnt["o"] % 2
                cnt["o"] += 1
                rp = (tag, "pso", b)
                for j in range(NJ):
                    P.op("pe", lambda e, j=j, b=b, ds_=ds_: e.matmul(
                        ps_o[b][:, :n], lhsT=wdn[ds_][:, j, :], rhs=G[:, j, :n],
                        start=(j == 0), stop=(j == NJ - 1)),
                        reads=[rw, rG], writes=[rp])
                P.op("dve", lambda e, oc=oc, b=b: e.tensor_tensor(
                    out=xo[:, oc, :n], in0=ps_o[b][:, :n], in1=xt[s][:, oc, h:h + n], op=ALU.add),
                    reads=[rp, rx], writes=[rxo])
            if out_off == 0:
                store_group(cx, tag, xo, rxo, Xo, o0, n, gi == 0)
            else:
                lo = max(o0, out_off)
                if lo < o0 + n:
                    P.op("pool", lambda e: e.dma_start(out=Xo[:, :, lo - out_off:o0 + n - out_off], in_=xo[:, :, lo - o0:n]),
                         reads=[rxo], writes=[(tag, "Xout")], dma_key=("st", 0))

        ng = len(groups)
        pre(0)
        up(0)
        for gi in range(ng):
            if gi + 1 < ng:
                pre(gi + 1)
                up(gi + 1)
            down(gi)


CW31 = 31
HO = 30


def odd_phase(cx, cfg, X_in, X_out, w1_bf, w2_bf, dg_bf, vec_dram, tag):
    P = cx.P
    groups = make_groups(cfg.T, 482, 482, HO)
    Xi = X_in.rearrange("(kc p) t -> p kc t", p=128)
    Xo = X_out.rearrange("(kc p) t -> p kc t", p=128)
    with Phase(cx):
        vec, rvec = load_small(cx, tag, "vec", [128, 32], vec_dram)
        g_t = vec[:, 0:8]
        xt = [cx.sb([128, KC, 512], F32, "xt") for _ in range(2)]
        sqk = [cx.sb([128, 512], F32, "sqk") for _ in range(2)]
        red = cx.sb([128, 512], F32, "red")
        lr1 = cx.sb([128, 512], F32, "lr1")
        lr2 = cx.sb([128, 512], F32, "lr2")
        std = cx.sb([128, 512], F32, "std")
        rstd = cx.sb([128, 512], F32, "rstd")
        mean = cx.sb([128, 512], F32, "mean")
        m2 = cx.sb([128, 512], F32, "m2")
        t1 = [cx.sb([128, 512], F32, "t1") for _ in range(2)]
        Hs = [cx.sb([128, KC, 512], BF16, "H") for _ in range(2)]
        w1 = [cx.sb([128, 2, KC, 128], BF16, "w1") for _ in range(3)]
        dg = [cx.sb([128, CW31, 128], BF16, "dg") for _ in range(2)]
        Us = [cx.sb([128, KC, 512 + HO], BF16, "U") for _ in range(2)]
        sig = [cx.sb([128, 512], F32, "sig") for _ in range(2)]
        Vs = [cx.sb([128, KC, 512], F32, "V") for _ in range(2)]
        Ss = [cx.sb([128, KC, 512], BF16, "S") for _ in range(2)]
        w2 = [cx.sb([128, KC, 128], BF16, "w2") for _ in range(2)]
        xo = cx.sb([128, KC, 512], F32, "xo")
        ps = [cx.ps([128, 512], F32, "ps") for _ in range(8)]
        rps = [(tag, "ps", i) for i in range(8)]
        cnt = {"w1": 0, "ag": 0, "dg": 0, "c": 0, "w2": 0, "o": 0}

        def pre(gi):
            o0, n, h = groups[gi]
            nu = n + h
            s = gi % 2
            rx = (tag, "xt", s)
            P.op("sp", lambda e: e.dma_start(out=xt[s][:, :, :nu], in_=Xi[:, :, o0 - h:o0 + n]),
                 reads=[(tag, "Xin")], writes=[rx], dma_key=("xt", s))
            rH = (tag, "H", s)
            x_ = xt[s]
            P.op("act", lambda e: e.activation(out=red[:, :nu], in_=x_[:, 0, :nu], func=AF.Square),
                 reads=[rx], writes=[(tag, "red")])
            for kc in range(1, KC):
                q = sqk[kc % 2]
                rq = (tag, "sqk", kc % 2)
                P.op("act", lambda e, kc=kc, q=q: e.activation(out=q[:, :nu], in_=x_[:, kc, :nu], func=AF.Square),
                     reads=[rx], writes=[rq])
                P.op("pool", lambda e, q=q: e.tensor_tensor(out=red[:, :nu], in0=red[:, :nu], in1=q[:, :nu], op=ALU.add),
                     reads=[rq, (tag, "red")], writes=[(tag, "red")])
            P.op("pe", lambda e: e.matmul(ps[0][:, :nu], lhsT=cx.ones_f[:, :], rhs=red[:, :nu], start=True, stop=True),
                 reads=[(tag, "red"), ("c", "ones_f")], writes=[rps[0]])
            P.op("act", lambda e: e.activation(out=std[:, :nu], in_=ps[0][:, :nu], func=AF.Sqrt, scale=1.0 / D, bias=cx.eps_t[:, 0:1]),
                 reads=[rps[0], ("c", "eps")], writes=[(tag, "std")])
            P.op("dve", lambda e: e.reciprocal(out=rstd[:, :nu], in_=std[:, :nu]),
                 reads=[(tag, "std")], writes=[(tag, "rstd")])
            for kc in range(KC):
                P.op("dve", lambda e, kc=kc: e.scalar_tensor_tensor(
                    out=Hs[s][:, kc, :nu], in0=x_[:, kc, :nu], scalar=g_t[:, kc:kc + 1], in1=rstd[:, :nu],
                    op0=ALU.mult, op1=ALU.mult),
                    reads=[rx, (tag, "rstd"), rvec], writes=[rH])

        def glu(gi):
            o0, n, h = groups[gi]
            nu = n + h
            s = gi % 2
            H = Hs[s]
            rH = (tag, "H", s)
            U = Us[s]
            rU = (tag, "U", s)
            cs = HO - h
            if h == 0:
                P.op("pool", lambda e: e.memset(U[:, :, 0:HO], 0.0), writes=[rU])
            for j in range(KC):
                ws = cnt["w1"] % 3
                cnt["w1"] += 1
                rw = (tag, "w1", ws)
                P.op("sp", lambda e, j=j, ws=ws: e.dma_start(
                    out=w1[ws][:, :, :, :].rearrange("p a k o -> p (a k o)"), in_=w1_bf[j]),
                    writes=[rw], dma_key=("w1", ws))
                a = cnt["ag"] % 2
                cnt["ag"] += 1
                pa, pg = ps[1 + 2 * a], ps[2 + 2 * a]
                rpa, rpg = rps[1 + 2 * a], rps[2 + 2 * a]
                for gv, psb, rp in ((0, pa, rpa), (1, pg, rpg)):
                    for kc in range(KC):
                        P.op("pe", lambda e, gv=gv, kc=kc, psb=psb, ws=ws: e.matmul(
                            psb[:, :nu], lhsT=w1[ws][:, gv, kc, :], rhs=H[:, kc, :nu],
                            start=(kc == 0), stop=(kc == KC - 1)),
                            reads=[rw, rH], writes=[rp])
                sg_ = sig[a]
                rsg = (tag, "sig", a)
                P.op("act", lambda e, sg_=sg_, pg=pg: e.activation(out=sg_[:, :nu], in_=pg[:, :nu], func=AF.Sigmoid),
                     reads=[rpg], writes=[rsg])
                P.op("dve", lambda e, j=j, sg_=sg_, pa=pa: e.tensor_tensor(
                    out=U[:, j, cs:cs + nu], in0=pa[:, :nu], in1=sg_[:, :nu], op=ALU.mult),
                    reads=[rpa, rsg], writes=[rU])

        def conv(gi):
            o0, n, h = groups[gi]
            s = gi % 2
            U = Us[s]
            rU = (tag, "U", s)
            V = Vs[s]
            rV = (tag, "V", s)
            for j in range(KC):
                d_ = cnt["dg"] % 2
                cnt["dg"] += 1
                rd = (tag, "dg", d_)
                P.op("sp", lambda e, j=j, d_=d_: e.dma_start(
                    out=dg[d_][:, :, :].rearrange("p k o -> p (k o)"), in_=dg_bf[j]),
                    writes=[rd], dma_key=("dg", d_))
                b = cnt["c"] % 2
                cnt["c"] += 1
                pc, rpc = ps[5 + b], rps[5 + b]
                for k in range(CW31):
                    P.op("pe", lambda e, j=j, k=k, d_=d_, pc=pc: e.matmul(
                        pc[:, :n], lhsT=dg[d_][:, k, :], rhs=U[:, j, k:k + n],
                        start=(k == 0), stop=(k == CW31 - 1)),
                        reads=[rd, rU], writes=[rpc])
                P.op("act", lambda e, j=j, pc=pc: e.activation(
                    out=V[:, j, :n], in_=pc[:, :n], func=AF.Identity, bias=vec[:, 8 + j:9 + j], scale=1.0),
                    reads=[rpc, rvec], writes=[rV])

        def ln_a(gi):
            o0, n, h = groups[gi]
            s = gi % 2
            V = Vs[s]
            rV = (tag, "V", s)
            P.op("pool", lambda e: e.tensor_tensor(out=lr1[:, :n], in0=V[:, 0, :n], in1=V[:, 1, :n], op=ALU.add),
                 reads=[rV], writes=[(tag, "lr1")])
            for kc in range(2, KC):
                P.op("pool", lambda e, kc=kc: e.tensor_tensor(out=lr1[:, :n], in0=lr1[:, :n], in1=V[:, kc, :n], op=ALU.add),
                     reads=[rV, (tag, "lr1")], writes=[(tag, "lr1")])
            P.op("act", lambda e: e.activation(out=lr2[:, :n], in_=V[:, 0, :n], func=AF.Square),
                 reads=[rV], writes=[(tag, "lr2")])
            for kc in range(1, KC):
                q = sqk[kc % 2]
                rq = (tag, "sqk", kc % 2)
                P.op("act", lambda e, kc=kc, q=q: e.activation(out=q[:, :n], in_=V[:, kc, :n], func=AF.Square),
                     reads=[rV], writes=[rq])
                P.op("pool", lambda e, q=q: e.tensor_tensor(out=lr2[:, :n], in0=lr2[:, :n], in1=q[:, :n], op=ALU.add),
                     reads=[rq, (tag, "lr2")], writes=[(tag, "lr2")])

        def ln_b(gi):
            o0, n, h = groups[gi]
            s = gi % 2
            V = Vs[s]
            rV = (tag, "V", s)
            S = Ss[s]
            rS = (tag, "S", s)
            P.op("pe", lambda e: e.matmul(ps[0][:, :n], lhsT=cx.ones_f[:, :], rhs=lr1[:, :n], start=True, stop=True),
                 reads=[(tag, "lr1"), ("c", "ones_f")], writes=[rps[0]])
            P.op("pe", lambda e: e.matmul(ps[7][:, :n], lhsT=cx.ones_f[:, :], rhs=lr2[:, :n], start=True, stop=True),
                 reads=[(tag, "lr2"), ("c", "ones_f")], writes=[rps[7]])
            P.op("dve", lambda e: e.tensor_scalar(out=mean[:, :n], in0=ps[0][:, :n], scalar1=1.0 / D, scalar2=None, op0=ALU.mult),
                 reads=[rps[0]], writes=[(tag, "mean")])
            P.op("dve", lambda e: e.tensor_tensor(out=m2[:, :n], in0=mean[:, :n], in1=mean[:, :n], op=ALU.mult),
                 reads=[(tag, "mean")], writes=[(tag, "m2")])
            P.op("dve", lambda e: e.scalar_tensor_tensor(out=m2[:, :n], in0=ps[7][:, :n], scalar=1.0 / D, in1=m2[:, :n],
                                                         op0=ALU.mult, op1=ALU.subtract),
                 reads=[rps[7], (tag, "m2")], writes=[(tag, "m2")])
            P.op("act", lambda e: e.activation(out=std[:, :n], in_=m2[:, :n], func=AF.Sqrt, scale=1.0, bias=cx.eps_t[:, 0:1]),
                 reads=[(tag, "m2"), ("c", "eps")], writes=[(tag, "std")])
            P.op("dve", lambda e: e.reciprocal(out=rstd[:, :n], in_=std[:, :n]),
                 reads=[(tag, "std")], writes=[(tag, "rstd")])
            for kc in range(KC):
                tt = t1[kc % 2]
                rt = (tag, "t1", kc % 2)
                P.op("dve", lambda e, kc=kc, tt=tt: e.tensor_tensor(out=tt[:, :n], in0=V[:, kc, :n], in1=mean[:, :n], op=ALU.subtract),
                     reads=[rV, (tag, "mean")], writes=[rt])
                P.op("dve", lambda e, tt=tt: e.tensor_tensor(out=tt[:, :n], in0=tt[:, :n], in1=rstd[:, :n], op=ALU.mult),
                     reads=[rt, (tag, "rstd")], writes=[rt])
                P.op("act", lambda e, kc=kc, tt=tt: e.activation(
                    out=S[:, kc, :n], in_=tt[:, :n], func=AF.Silu, scale=vec[:, 16 + kc:17 + kc], bias=vec[:, 24 + kc:25 + kc]),
                    reads=[rt, rvec], writes=[rS])

        def pw2(gi):
            o0, n, h = groups[gi]
            s = gi % 2
            S = Ss[s]
            rS = (tag, "S", s)
            rx = (tag, "xt", s)
            rxo = (tag, "xo")
            for oc in range(KC):
                w_ = cnt["w2"] % 2
                cnt["w2"] += 1
                rw = (tag, "w2", w_)
                P.op("sp", lambda e, oc=oc, w_=w_: e.dma_start(
                    out=w2[w_][:, :, :].rearrange("p k o -> p (k o)"), in_=w2_bf[oc]),
                    writes=[rw], dma_key=("w2", w_))
                b = cnt["o"] % 2
                cnt["o"] += 1
                po, rpo = ps[1 + b], rps[1 + b]
                for kc in range(KC):
                    P.op("pe", lambda e, kc=kc, w_=w_, po=po: e.matmul(
                        po[:, :n], lhsT=w2[w_][:, kc, :], rhs=S[:, kc, :n],
                        start=(kc == 0), stop=(kc == KC - 1)),
                        reads=[rw, rS], writes=[rpo])
                P.op("dve", lambda e, oc=oc, po=po: e.tensor_tensor(
                    out=xo[:, oc, :n], in0=po[:, :n], in1=xt[s][:, oc, h:h + n], op=ALU.add),
                    reads=[rpo, rx], writes=[rxo])
            store_group(cx, tag, xo, rxo, Xo, o0, n, gi == 0)

        ng = len(groups)
        pre(0)
        glu(0)
        conv(0)
        for gi in range(ng):
            ln_a(gi)
            if gi + 1 < ng:
                pre(gi + 1)
                glu(gi + 1)
            ln_b(gi)
            if gi + 1 < ng:
                conv(gi + 1)
            pw2(gi)


HP = 15
POOL_W = (2, 4, 8, 16)


def even_pre_phase(cx, cfg, X_in, wqk_bf, wf_bf, wv_bf, wp_bf, vec_dram, bf_dram, tag):
    P = cx.P
    dr = cx.dram
    groups = make_groups(cfg.T, 512, 497, HP)
    Xi = X_in.rearrange("(kc p) t -> p kc t", p=128)
    with Phase(cx):
        vec, rvec = load_small(cx, tag, "vec", [128, 16], vec_dram)
        invc, rinvc = load_small(cx, tag, "invc", [128, 4 * 32], dr["invc"])
        bft = cx.sb([8, 1], F32, "bft")
        nbf = cx.sb([8, 1], F32, "nbf")
        gq8 = cx.sb([128, 1], F32, "gq8")
        P.op("sp", lambda e: e.dma_start(out=bft[:, :], in_=bf_dram), writes=[(tag, "bft")], dma_key=("G", "small"))
        P.op("dve", lambda e: e.tensor_scalar(out=nbf[:, :], in0=bft[:, :], scalar1=-1.0, scalar2=None, op0=ALU.mult),
             reads=[(tag, "bft")], writes=[(tag, "nbf")])
        P.op("dve", lambda e: e.tensor_scalar(out=gq8[:, :], in0=vec[:, 8:9], scalar1=0.125, scalar2=None, op0=ALU.mult),
             reads=[rvec], writes=[(tag, "gq8")])
        g_t = vec[:, 0:8]
        wv = cx.sb([128, KC, 512], BF16, "wv")
        wf = cx.sb([128, KC, 8], BF16, "wf")
        wp = cx.sb([128, 4, 128], BF16, "wp")
        P.op("sp", lambda e: e.dma_start(out=wv[:, :, :].rearrange("p k o -> p (k o)"), in_=wv_bf), writes=[(tag, "wv")], dma_key=("G", "small"))
        P.op("sp", lambda e: e.dma_start(out=wf[:, :, :].rearrange("p k o -> p (k o)"), in_=wf_bf), writes=[(tag, "wf")], dma_key=("G", "small"))
        P.op("sp", lambda e: e.dma_start(out=wp[:, :, :].rearrange("p k o -> p (k o)"), in_=wp_bf), writes=[(tag, "wp")], dma_key=("G", "small"))
        xt = [cx.sb([128, KC, 512], F32, "xt") for _ in range(2)]
        sqk = [cx.sb([128, 512], F32, "sqk") for _ in range(2)]
        red = cx.sb([128, 512], F32, "red")
        std = cx.sb([128, 512], F32, "std")
        rstd = cx.sb([128, 512], F32, "rstd")
        Hs = [cx.sb([128, KC, 512], BF16, "H") for _ in range(2)]
        wq = [cx.sb([128, KC, 128], BF16, "wq") for _ in range(3)]
        qsq = [cx.sb([128, 512], F32, "qsq") for _ in range(2)]
        qstd = [cx.sb([128, 512], F32, "qstd") for _ in range(2)]
        qo = [cx.sb([128, 512], BF16, "qo") for _ in range(2)]
        vo = [cx.sb([128, 512], BF16, "vo") for _ in range(2)]
        fe = cx.sb([8, 512], F32, "fe")
        fo = [cx.sb([8, 512], F32, "fo") for _ in range(2)]
        ub = [cx.sb([128, 512], F32, "ub") for _ in range(2)]
        sa = cx.sb([128, 512], F32, "sa")
        sb_ = cx.sb([128, 512], F32, "sb")
        mx = [cx.sb([128, 512], BF16, "mx") for _ in range(2)]
        po = [cx.sb([128, 512], BF16, "po") for _ in range(2)]
        ps = [cx.ps([128, 512], F32, "ps") for _ in range(8)]
        rps = [(tag, "ps", i) for i in range(8)]
        cnt = {"w": 0, "pj": 0, "q": 0, "v": 0, "u": 0, "f": 0}

        def pre(gi):
            o0, n, h = groups[gi]
            nu = n + h
            s = gi % 2
            rx = (tag, "xt", s)
            x_ = xt[s]
            P.op("sp", lambda e: e.dma_start(out=x_[:, :, :nu], in_=Xi[:, :, o0 - h:o0 + n]),
                 reads=[(tag, "Xin")], writes=[rx], dma_key=("xt", s))
            P.op("act", lambda e: e.activation(out=red[:, :nu], in_=x_[:, 0, :nu], func=AF.Square),
                 reads=[rx], writes=[(tag, "red")])
            for kc in range(1, KC):
                q = sqk[kc % 2]
                rq = (tag, "sqk", kc % 2)
                P.op("act", lambda e, kc=kc, q=q: e.activation(out=q[:, :nu], in_=x_[:, kc, :nu], func=AF.Square),
                     reads=[rx], writes=[rq])
                P.op("pool", lambda e, q=q: e.tensor_tensor(out=red[:, :nu], in0=red[:, :nu], in1=q[:, :nu], op=ALU.add),
                     reads=[rq, (tag, "red")], writes=[(tag, "red")])
            P.op("pe", lambda e: e.matmul(ps[0][:, :nu], lhsT=cx.ones_f[:, :], rhs=red[:, :nu], start=True, stop=True),
                 reads=[(tag, "red"), ("c", "ones_f")], writes=[rps[0]])
            P.op("act", lambda e: e.activation(out=std[:, :nu], in_=ps[0][:, :nu], func=AF.Sqrt, scale=1.0 / D, bias=cx.eps_t[:, 0:1]),
                 reads=[rps[0], ("c", "eps")], writes=[(tag, "std")])
            P.op("dve", lambda e: e.reciprocal(out=rstd[:, :nu], in_=std[:, :nu]),
                 reads=[(tag, "std")], writes=[(tag, "rstd")])
            for kc in range(KC):
                P.op("dve", lambda e, kc=kc: e.scalar_tensor_tensor(
                    out=Hs[s][:, kc, :nu], in0=x_[:, kc, :nu], scalar=g_t[:, kc:kc + 1], in1=rstd[:, :nu],
                    op0=ALU.mult, op1=ALU.mult),
                    reads=[rx, (tag, "rstd"), rvec], writes=[(tag, "H", s)])

        def proj(gi):
            o0, n, h = groups[gi]
            nu = n + h
            s = gi % 2
            H = Hs[s]
            rH = (tag, "H", s)
            t0 = o0 - h
            lo = max(o0, HALO)
            hi = o0 + n
            own = lo < hi
            c_lo, c_hi = lo - t0, hi - t0
            tk = lo - HALO
            for c in range(12):
                if c < 8 and not own:
                    continue
                ws = cnt["w"] % 3
                cnt["w"] += 1
                rw = (tag, "wq", ws)
                P.op("sp", lambda e, c=c, ws=ws: e.dma_start(
                    out=wq[ws][:, :, :].rearrange("p k o -> p (k o)"), in_=wqk_bf[c]),
                    writes=[rw], dma_key=("wq", ws))
                b = cnt["pj"] % 2
                cnt["pj"] += 1
                pj, rpj = ps[1 + b], rps[1 + b]
                for kc in range(KC):
                    P.op("pe", lambda e, kc=kc, ws=ws, pj=pj: e.matmul(
                        pj[:, :nu], lhsT=wq[ws][:, kc, :], rhs=H[:, kc, :nu],
                        start=(kc == 0), stop=(kc == KC - 1)),
                        reads=[rw, rH], writes=[rpj])
                if c < 8:
                    a = cnt["q"] % 2
                    cnt["q"] += 1
                    rqs, rqd, rqo = (tag, "qsq", a), (tag, "qstd", a), (tag, "qo", a)
                    P.op("act", lambda e, a=a, pj=pj: e.activation(out=qsq[a][:, :nu], in_=pj[:, :nu], func=AF.Square),
                         reads=[rpj], writes=[rqs])
                    P.op("pe", lambda e, a=a: e.matmul(ps[3][:, :nu], lhsT=cx.ones2_f[:, :], rhs=qsq[a][:, :nu], start=True, stop=True),
                         reads=[rqs, ("c", "ones2_f")], writes=[rps[3]])
                    P.op("act", lambda e, a=a: e.activation(out=qstd[a][:, :nu], in_=ps[3][:, :nu], func=AF.Sqrt, scale=1.0 / 64, bias=cx.eps_t[:, 0:1]),
                         reads=[rps[3], ("c", "eps")], writes=[rqd])
                    P.op("dve", lambda e, a=a: e.reciprocal(out=qstd[a][:, :nu], in_=qstd[a][:, :nu]),
                         reads=[rqd], writes=[rqd])
                    gsc = gq8[:, 0:1] if c < 4 else vec[:, 9:10]
                    P.op("dve", lambda e, a=a, pj=pj, gsc=gsc: e.scalar_tensor_tensor(
                        out=qo[a][:, :nu], in0=pj[:, :nu], scalar=gsc, in1=qstd[a][:, :nu], op0=ALU.mult, op1=ALU.mult),
                        reads=[rpj, rqd, rvec, (tag, "gq8")], writes=[rqo])
                    dst = dr["Qc"] if c < 4 else dr["Kc"]
                    cc = c % 4
                    P.op("pool", lambda e, a=a, dst=dst, cc=cc: e.dma_start(
                        out=dst[cc * 128:(cc + 1) * 128, tk:tk + (hi - lo)], in_=qo[a][:, c_lo:c_hi]),
                        reads=[rqo], writes=[(tag, "QKc")], dma_key=("qo", a))
                else:
                    g = c - 8
                    a = cnt["u"] % 2
                    cnt["u"] += 1
                    u_ = ub[a]
                    ru = (tag, "ub", a)
                    P.op("act", lambda e, u_=u_, pj=pj: e.activation(out=u_[:, :nu], in_=pj[:, :nu], func=AF.Identity),
                         reads=[rpj], writes=[ru])
                    src, rsrc = u_, ru
                    bufs = [(sa, (tag, "sa")), (sb_, (tag, "sb"))]
                    for st in range(g + 1):
                        sh = 1 << st
                        dstt, rdst = bufs[st % 2]
                        P.op("dve", lambda e, src=src, dstt=dstt, sh=sh: e.tensor_tensor(
                            out=dstt[:, sh:nu], in0=src[:, sh:nu], in1=src[:, 0:nu - sh], op=ALU.add),
                            reads=[rsrc], writes=[rdst])
                        P.op("pool", lambda e, src=src, dstt=dstt, sh=sh: e.tensor_copy(out=dstt[:, 0:sh], in_=src[:, 0:sh]),
                             reads=[rsrc], writes=[rdst])
                        src, rsrc = dstt, rdst
                    w_ = POOL_W[g]
                    m_ = mx[a]
                    rm = (tag, "mx", a)
                    P.op("dve", lambda e, src=src, u_=u_, m_=m_, w_=w_: e.scalar_tensor_tensor(
                        out=m_[:, :n], in0=src[:, h:nu], scalar=1.0 / w_, in1=u_[:, h:nu], op0=ALU.mult, op1=ALU.subtract),
                        reads=[rsrc, ru], writes=[rm])
                    if gi == 0:
                        tt = sqk[0]
                        rt = (tag, "sqk", 0)
                        P.op("dve", lambda e, src=src, g=g, tt=tt: e.tensor_tensor(
                            out=tt[:, 0:32], in0=src[:, HALO:HALO + 32], in1=invc[:, g * 32:(g + 1) * 32], op=ALU.mult),
                            reads=[rsrc, rinvc], writes=[rt])
                        P.op("dve", lambda e, u_=u_, m_=m_, tt=tt: e.tensor_tensor(
                            out=m_[:, HALO:HALO + 32], in0=tt[:, 0:32], in1=u_[:, HALO:HALO + 32], op=ALU.subtract),
                            reads=[rt, ru, rm], writes=[rm])
                    P.op("pe", lambda e, g=g, m_=m_: e.matmul(ps[7][:, :n], lhsT=wp[:, g, :], rhs=m_[:, :n], start=True, stop=True),
                         reads=[rm, (tag, "wp")], writes=[rps[7]])
                    p_ = po[a]
                    rp_ = (tag, "po", a)
                    P.op("act", lambda e, g=g, p_=p_: e.activation(out=p_[:, :n], in_=ps[7][:, :n], func=AF.Identity,
                                                                   scale=vec[:, 10 + g:11 + g]),
                         reads=[rps[7], rvec], writes=[rp_])
                    P.op("pool", lambda e, g=g, p_=p_: e.dma_start(out=dr["Pout"][g * 128:(g + 1) * 128, o0:o0 + n], in_=p_[:, :n]),
                         reads=[rp_], writes=[(tag, "Pout")], dma_key=("po", a))
            if not own:
                return
            for kc in range(KC):
                P.op("pe", lambda e, kc=kc: e.matmul(ps[6][0:8, :nu], lhsT=wf[:, kc, :], rhs=H[:, kc, :nu],
                                                     start=(kc == 0), stop=(kc == KC - 1)),
                     reads=[(tag, "wf"), rH], writes=[rps[6]])
            a = cnt["f"] % 2
            cnt["f"] += 1
            P.op("act", lambda e: e.activation(out=fe[:, :nu], in_=ps[6][0:8, :nu], func=AF.Exp, scale=-1.0, bias=nbf[:, 0:1]),
                 reads=[rps[6], (tag, "nbf")], writes=[(tag, "fe")])
            P.op("act", lambda e, a=a: e.activation(out=fo[a][:, :nu], in_=fe[:, :nu], func=AF.Ln, scale=1.0, bias=cx.one_t[0:8, 0:1]),
                 reads=[(tag, "fe"), ("c", "one")], writes=[(tag, "fo", a)])
            P.op("pool", lambda e, a=a: e.dma_start(out=dr["Fc"][:, tk:tk + (hi - lo)], in_=fo[a][:, c_lo:c_hi]),
                 reads=[(tag, "fo", a)], writes=[(tag, "Fc")], dma_key=("fo", a))
            c0 = c_lo
            while c0 < c_hi:
                m = min(128, c_hi - c0)
                b = cnt["v"] % 2
                cnt["v"] += 1
                pv, rpv = ps[4 + b], rps[4 + b]
                for kc in range(KC):
                    P.op("pe", lambda e, kc=kc, c0=c0, m=m, pv=pv: e.matmul(
                        pv[:m, :], lhsT=H[:, kc, c0:c0 + m], rhs=wv[:, kc, :], start=(kc == 0), stop=(kc == KC - 1)),
                        reads=[rH, (tag, "wv")], writes=[rpv])
                v_ = vo[b]
                rv = (tag, "vo", b)
                P.op("act", lambda e, m=m, pv=pv, v_=v_: e.activation(out=v_[:m, :], in_=pv[:m, :], func=AF.Copy),
                     reads=[rpv], writes=[rv])
                tok = tk + (c0 - c_lo)
                P.op("pool", lambda e, m=m, v_=v_, tok=tok: e.dma_start(out=dr["Vc"][tok:tok + m, :], in_=v_[:m, :]),
                     reads=[rv], writes=[(tag, "Vc")], dma_key=("vo", b))
                c0 += m

        ng = len(groups)
        pre(0)
        for gi in range(ng):
            if gi + 1 < ng:
                pre(gi + 1)
            proj(gi)


GROUPS4 = [[0, 1, 2, 3], [4, 5, 6, 7]]
import os
ADBG = int(os.environ.get("ATT_DBG", "5"))


def attn_phase(cx, cfg, tag):
    P = cx.P
    dr = cx.dram
    CH_, SEQ_, NKB, NQB = cfg.CH, cfg.SEQ, cfg.NKB, cfg.NQB
    segl = SEQ_ // 64
    nbs = CH_ // 128
    q4 = CH_ // 4
    for cc in range(4):
        for nm in ("Q", "K"):
            P.op("pool", lambda e, nm=nm, cc=cc: e.collective_compute(
                "AllGather", ALU.bypass, replica_groups=GROUPS4,
                ins=[dr[nm + "c"][cc * 128:(cc + 1) * 128, :].opt()], outs=[dr[nm + "g"][cc * 512:(cc + 1) * 512, :].opt()]),
                reads=[(tag, nm + "c")], writes=[(tag, nm + "g")], dma_key=("G", "cc" + nm), inc=1)
        P.op("pool", lambda e, cc=cc: e.collective_compute(
            "AllGather", ALU.bypass, replica_groups=GROUPS4,
            ins=[dr["Vc"][cc * q4:(cc + 1) * q4, :].opt()], outs=[dr["Vg"][cc * CH_:(cc + 1) * CH_, :].opt()]),
            reads=[(tag, "Vc")], writes=[(tag, "Vg")], dma_key=("G", "ccV"), inc=1)
    P.op("pool", lambda e: e.collective_compute(
        "AllGather", ALU.bypass, replica_groups=GROUPS4, ins=[dr["Fc"].opt()], outs=[dr["Fg"].opt()]),
        reads=[(tag, "Fc")], writes=[(tag, "Fg")], dma_key=("G", "ccF"), inc=1)
    cx.chk("attn_ag")
    with Phase(cx):
        KT = [cx.sb([128, SEQ_], BF16, "KT") for _ in range(2)]
        VT = [cx.sb([128, NKB, 128], BF16, "VT") for _ in range(2)]
        Lm = cx.sb([128, 128], F32, "Lm")
        sp = cx.sb([128, segl], F32, "sp")
        onesl = cx.sb([128, segl], F32, "onesl")
        cl = cx.sb([128, segl], F32, "cl")
        offs = cx.sb([128, 1], F32, "offs")
        r1 = cx.sb([128, segl], F32, "r1")
        hi = cx.sb([128, segl], BF16, "hi")
        mid = cx.sb([128, segl], BF16, "mid")
        lo = cx.sb([128, segl], BF16, "lo")
        nhi = cx.sb([128, segl], BF16, "nhi")
        nmid = cx.sb([128, segl], BF16, "nmid")
        nlo = cx.sb([128, segl], BF16, "nlo")
        onesb = cx.sb([128, segl], BF16, "onesb")
        zt = cx.sb([128, 4, HALO], BF16, "zt")
        NQT = 3
        qt = [cx.sb([128, 512], BF16, "qt") for _ in range(NQT)]
        NPT = 4
        pt = [cx.sb([128, 512], BF16, "pt") for _ in range(NPT)]
        rec = [cx.sb([64, 512], F32, "rec") for _ in range(2)]
        ost = [cx.sb([64, 512], BF16, "ost") for _ in range(2)]
        NS = 4
        ps_s = [cx.ps([128, 512], F32, "pss") for _ in range(NS)]
        ps_o = [cx.ps([128, 512], F32, "pso") for _ in range(2)]
        ps_m = cx.ps([128, 512], F32, "psm")

        P.op("dve", lambda e: e.memset(Lm[:, :], 1.0), writes=[(tag, "Lm")])
        P.op("pool", lambda e: e.affine_select(out=Lm[:, :], in_=Lm[:, :], pattern=[[1, 128]], compare_op=ALU.is_gt,
                                               fill=0.0, base=0, channel_multiplier=-1),
             reads=[(tag, "Lm")], writes=[(tag, "Lm")])
        P.op("dve", lambda e: e.memset(Lm[0:64, 64:128], 0.0), reads=[(tag, "Lm")], writes=[(tag, "Lm")])
        P.op("dve", lambda e: e.memset(onesl[:, :], 1.0), writes=[(tag, "onesl")])
        P.op("dve", lambda e: e.memset(onesb[:, :], 1.0), writes=[(tag, "onesb")])
        P.op("dve", lambda e: e.memset(zt[:, :, :], 0.0), writes=[(tag, "zt")])
        P.op("sp", lambda e: e.dma_start(out=dr["Og"][0:512, CH_ - HALO:CH_].rearrange("(a p) t -> p a t", p=128), in_=zt[:, :, :]),
             reads=[(tag, "zt")], writes=[(tag, "Ogpad")], dma_key=("G", "small"))
        for h in range(2):
            P.op("dve", lambda e, h=h: e.memset(VT[h][:, :, 64:128], 1.0), writes=[(tag, "VTo", h)])

        def loc_k(e, nm):
            i4 = pid4(e)
            src = dr[nm + "g"].rearrange("(c s r) t -> c s r t", c=4, s=4)[i4, :, :, :]
            return e.dma_start(out=dr[nm + "l"].rearrange("r (s t) -> s r t", s=4), in_=src)

        def loc_v(e, j):
            i4 = pid4(e)
            src = dr["Vg"][j * CH_:(j + 1) * CH_, :].rearrange("(s n) (i c) -> s n i c", s=4, i=4)[:, :, i4, :]
            return e.dma_start(out=dr["Vl"].rearrange("(s j n) c -> j s n c", s=4, j=4)[j], in_=src)

        def loc_f(e):
            i4 = pid4(e)
            src = dr["Fg"].rearrange("(s i h) t -> s i h t", s=4, i=4)[:, i4, :, :]
            return e.dma_start(out=dr["Fl"].rearrange("(s h) t -> s h t", s=4), in_=src)

        P.op("sp", loc_f, reads=[(tag, "Fg")], writes=[(tag, "Fl")], dma_key=("G", "locf"))
        P.op("sp", lambda e: loc_k(e, "K"), reads=[(tag, "Kg")], writes=[(tag, "Kl")], dma_key=("G", "lock"))
        P.op("sp", lambda e: loc_k(e, "Q"), reads=[(tag, "Qg")], writes=[(tag, "Ql")], dma_key=("G", "locq"))
        for j in range(4):
            P.op("sp", lambda e, j=j: loc_v(e, j), reads=[(tag, "Vg")], writes=[(tag, "Vl")], dma_key=("G", "locv"))

        cx.chk("attn_loc")

        def ld_k(e, h, s):
            return e.dma_start(out=KT[h][0:64, s * CH_:(s + 1) * CH_], in_=dr["Kl"][h * 64:(h + 1) * 64, s * CH_:(s + 1) * CH_])

        def ld_v(e, h, s):
            src = dr["Vl"][s * CH_:(s + 1) * CH_, h * 64:(h + 1) * 64].rearrange("(b p) c -> p b c", p=128)
            return e.dma_start(out=VT[h][:, s * nbs:(s + 1) * nbs, 0:64], in_=src)

        def ld_f(e, h, s):
            src = dr["Fl"][s * 2 + h:s * 2 + h + 1, :].rearrange("o (j t) -> (o j) t", t=segl)
            return e.dma_start(out=sp[h * 64 + s * 16:h * 64 + (s + 1) * 16, :], in_=src)

        for h in range(2):
            for s in range(4):
                P.op("sp", lambda e, h=h, s=s: ld_f(e, h, s), reads=[(tag, "Fl")], writes=[(tag, "sp", h, s)],
                     dma_key=("G", "ldf"))
        for h in range(2):
            for s in range(4):
                P.op("sp", lambda e, h=h, s=s: ld_k(e, h, s), reads=[(tag, "Kl")], writes=[(tag, "KT", h, s)],
                     dma_key=("G", "ldk"))
        rsp = [(tag, "sp", h, s) for h in range(2) for s in range(4)]
        P.op("dve", lambda e: e.tensor_tensor_scan(out=cl[:, :], data0=onesl[:, :], data1=sp[:, :], initial=0.0,
                                                   op0=ALU.mult, op1=ALU.add),
             reads=rsp + [(tag, "onesl")], writes=[(tag, "cl")])
        P.op("pe", lambda e: e.matmul(ps_m[:, 0:2], lhsT=Lm[:, :], rhs=cl[:, segl - 2:segl], start=True, stop=True),
             reads=[(tag, "cl"), (tag, "Lm")], writes=[(tag, "psm")])
        P.op("dve", lambda e: e.tensor_copy(out=offs[:, :], in_=ps_m[:, 1:2]), reads=[(tag, "psm")], writes=[(tag, "offs")])
        P.op("dve", lambda e: e.tensor_scalar(out=cl[:, :], in0=cl[:, :], scalar1=offs[:, 0:1], scalar2=None, op0=ALU.add),
             reads=[(tag, "cl"), (tag, "offs")], writes=[(tag, "cl")])
        P.op("dve", lambda e: e.tensor_copy(out=hi[:, :], in_=cl[:, :]), reads=[(tag, "cl")], writes=[(tag, "hi")])
        P.op("dve", lambda e: e.tensor_tensor(out=r1[:, :], in0=cl[:, :], in1=hi[:, :], op=ALU.subtract),
             reads=[(tag, "cl"), (tag, "hi")], writes=[(tag, "r1")])
        P.op("dve", lambda e: e.tensor_copy(out=mid[:, :], in_=r1[:, :]), reads=[(tag, "r1")], writes=[(tag, "mid")])
        P.op("dve", lambda e: e.tensor_tensor(out=r1[:, :], in0=r1[:, :], in1=mid[:, :], op=ALU.subtract),
             reads=[(tag, "r1"), (tag, "mid")], writes=[(tag, "r1")])
        P.op("dve", lambda e: e.tensor_copy(out=lo[:, :], in_=r1[:, :]), reads=[(tag, "r1")], writes=[(tag, "lo")])
        for src_, dst_, nm in ((hi, nhi, "nhi"), (mid, nmid, "nmid"), (lo, nlo, "nlo")):
            P.op("dve", lambda e, src_=src_, dst_=dst_: e.tensor_scalar(out=dst_[:, :], in0=src_[:, :], scalar1=-1.0, scalar2=None, op0=ALU.mult),
                 reads=[(tag, "hi"), (tag, "mid"), (tag, "lo")], writes=[(tag, nm)])
        cx.chk("attn_cs")
        k = 0
        for h in range(2):
            for r, (tk_, tq_) in enumerate(((hi, onesb), (mid, onesb), (lo, onesb), (onesb, nhi), (onesb, nmid), (onesb, nlo))):
                for dst_nm, t_ in (("AugK", tk_), ("AugQ", tq_)):
                    P.op("sp", lambda e, h=h, r=r, dst_nm=dst_nm, t_=t_: e.dma_start(
                        out=dr[dst_nm][h * 6 + r:h * 6 + r + 1, :].rearrange("o (j t) -> (o j) t", t=segl), in_=t_[h * 64:(h + 1) * 64, :]),
                        reads=[(tag, "hi"), (tag, "mid"), (tag, "lo"), (tag, "nhi"), (tag, "nmid"), (tag, "nlo"), (tag, "onesb")],
                        writes=[(tag, dst_nm, h)], dma_key=("G", "aug"))
                    k += 1
        cx.chk("attn_aug")
        for h in range(2):
            P.op("sp", lambda e, h=h: e.dma_start(out=KT[h][64:70, :], in_=dr["AugK"][h * 6:(h + 1) * 6, :]),
                 reads=[(tag, "AugK", h)], writes=[(tag, "KTa", h)], dma_key=("G", "ldka"))
        cx.chk("attn_ka")
        for h in range(2):
            for s in range(4):
                P.op("sp", lambda e, h=h, s=s: ld_v(e, h, s), reads=[(tag, "Vl")], writes=[(tag, "VT", h, s)],
                     dma_key=("G", "ldv"))

        cx.chk("attn_ld")
        steps = []
        DEPTH = 3
        state = {"qi": -1, "oi": -1}
        qinfo = {}

        def load_q(h, qb):
            state["qi"] += 1
            qs = state["qi"] % NQT
            rq = (tag, "qt", qs)
            Q0 = qb * 512
            s = Q0 // CH_
            c0 = Q0 % CH_

            P.op("sp", lambda e: e.dma_start(out=qt[qs][0:64, :], in_=dr["Ql"][h * 64:(h + 1) * 64, Q0:Q0 + 512]),
                 reads=[(tag, "Ql")], writes=[rq], dma_key=("ldq", qs))
            P.op("sp", lambda e: e.dma_start(out=qt[qs][64:70, :], in_=dr["AugQ"][h * 6:(h + 1) * 6, Q0:Q0 + 512]),
                 reads=[(tag, "AugQ", h)], writes=[(tag, "qta", qs)], dma_key=("ldqa", qs))
            qinfo[(h, qb)] = qs

        def qk(i):
            h, qb, kb, nk = steps[i]
            if kb == 0:
                load_q(h, qb)
            qs = qinfo[(h, qb)]
            sb_i = i % NS
            j = kb - 4 * qb
            a = 128 * j if j > 0 else 0
            diag = j >= 0
            s_src = (kb * 128) // CH_
            rk = [(tag, "KT", h, s_src), (tag, "KTa", h)]
            P.op("pe", lambda e: e.matmul(ps_s[sb_i][:, a:512], lhsT=KT[h][0:70, kb * 128:(kb + 1) * 128],
                                          rhs=qt[qs][0:70, a:512], start=True, stop=not diag),
                 reads=rk + [(tag, "qt", qs), (tag, "qta", qs)], writes=[(tag, "pss", sb_i)])
            if diag and ADBG >= 2:
                P.op("pe", lambda e: e.matmul(ps_s[sb_i][:, a:a + 128], lhsT=cx.ident_b[:, :], rhs=cx.maskb[:, :],
                                              start=False, stop=True),
                     reads=[("c", "ident"), ("c", "maskb")], writes=[(tag, "pss", sb_i)])
            pi = i % NPT
            if ADBG < 3:
                return
            P.op("act", lambda e: e.activation(out=pt[pi][:, a:512], in_=ps_s[sb_i][:, a:512], func=AF.Exp),
                 reads=[(tag, "pss", sb_i)], writes=[(tag, "pt", pi)])

        def pv(i):
            if ADBG < 4:
                return
            h, qb, kb, nk = steps[i]
            j = kb - 4 * qb
            a = 128 * j if j > 0 else 0
            pi = i % NPT
            s_src = (kb * 128) // CH_
            if kb == 0:
                state["oi"] += 1
            ob = state["oi"] % 2
            P.op("pe", lambda e: e.matmul(ps_o[ob][:, a:512], lhsT=VT[h][:, kb, :], rhs=pt[pi][:, a:512],
                                          start=(kb == 0), stop=(kb == nk - 1)),
                 reads=[(tag, "VT", h, s_src), (tag, "VTo", h), (tag, "pt", pi)], writes=[(tag, "pso", ob)])
            if kb == nk - 1 and ADBG >= 5:
                Q0 = qb * 512
                P.op("dve", lambda e: e.reciprocal(out=rec[ob][:, :], in_=ps_o[ob][64:128, :]),
                     reads=[(tag, "pso", ob)], writes=[(tag, "rec", ob)])
                P.op("dve", lambda e: e.tensor_tensor(out=ost[ob][:, :], in0=ps_o[ob][0:64, :], in1=rec[ob][:, :], op=ALU.mult),
                     reads=[(tag, "pso", ob), (tag, "rec", ob)], writes=[(tag, "ost", ob)])
                jc, c0_ = Q0 // CH_, Q0 % CH_
                P.op("pool", lambda e: e.dma_start(out=dr["Oc"][jc * 128 + h * 64:jc * 128 + (h + 1) * 64, c0_:c0_ + 512], in_=ost[ob][:, :]),
                     reads=[(tag, "ost", ob)], writes=[(tag, "Oc")], dma_key=("ost", ob))

        for hh in range(2):
            base = len(steps)
            for qb in range(NQB):
                nk = 4 * qb + 4
                for kb in range(nk):
                    steps.append((hh, qb, kb, nk))
            n = len(steps)
            for i in range(base, n + DEPTH):
                if i < n:
                    qk(i)
                if i - DEPTH >= base:
                    pv(i - DEPTH)
            if hh == 0:
                P.barrier()
    for jc in range(4):
        P.op("pool", lambda e, jc=jc: e.collective_compute(
            "AllGather", ALU.bypass, replica_groups=GROUPS4,
            ins=[dr["Oc"][jc * 128:(jc + 1) * 128, :].opt()], outs=[dr["Og"][(jc + 1) * 512:(jc + 2) * 512, :].opt()]),
            reads=[(tag, "Oc")], writes=[(tag, "Og")], dma_key=("G", "ccO"), inc=1)


def even_post_phase(cx, cfg, X_in, X_out, wo_bf, tag):
    P = cx.P
    dr = cx.dram
    groups = make_groups(cfg.T, 512, 512, 0)
    Xi = X_in.rearrange("(kc p) t -> p kc t", p=128)
    Xo = X_out.rearrange("(kc p) t -> p kc t", p=128)
    Alv = dr["Al"].rearrange("(kc p) t -> p kc t", p=128)
    Pov = dr["Pout"].rearrange("(kc p) t -> p kc t", p=128)

    Og3 = dr["Og"].rearrange("(j r) t -> j r t", r=512)

    def loc_a(e):
        i4 = pid4(e)
        return e.dma_start(out=dr["Al"][:, HALO:], in_=Og3[i4 + 1, :, :])

    def loc_h(e):
        i4 = pid4(e)
        return e.dma_start(out=dr["Al"][:, 0:HALO], in_=Og3[i4, :, cfg.CH - HALO:cfg.CH])
    P.op("sp", loc_a, reads=[(tag, "Og")], writes=[(tag, "Al")], dma_key=("G", "loca"))
    P.op("sp", loc_h, reads=[(tag, "Og")], writes=[(tag, "Alh")], dma_key=("G", "loca"))
    with Phase(cx):
        xt = [cx.sb([128, KC, 512], F32, "xt") for _ in range(2)]
        At = [cx.sb([128, KC, 512], BF16, "At") for _ in range(2)]
        wo = [cx.sb([128, KC, 128], BF16, "wo") for _ in range(3)]
        xo = [cx.sb([128, KC, 512], F32, "xo") for _ in range(2)]
        ps = [cx.ps([128, 512], F32, "ps") for _ in range(2)]
        cnt = {"w": 0, "o": 0}
        for gi, (o0, n, h) in enumerate(groups):
            s = gi % 2
            rx, rA, rxo = (tag, "xt", s), (tag, "At", s), (tag, "xo", s)
            P.op("sp", lambda e, s=s, o0=o0, n=n: e.dma_start(out=xt[s][:, :, :n], in_=Xi[:, :, o0:o0 + n]),
                 reads=[(tag, "Xin")], writes=[rx], dma_key=("xt", s))

            P.op("sp", lambda e, s=s, o0=o0, n=n: e.dma_start(out=At[s][:, 0:4, :n], in_=Alv[:, :, o0:o0 + n]),
                 reads=[(tag, "Al"), (tag, "Alh")], writes=[(tag, "Ata", s)], dma_key=("Ata", s))
            P.op("sp", lambda e, s=s, o0=o0, n=n: e.dma_start(out=At[s][:, 4:8, :n], in_=Pov[:, :, o0:o0 + n]),
                 reads=[(tag, "Pout")], writes=[(tag, "Atp", s)], dma_key=("Atp", s))
            for oc in range(KC):
                ws = cnt["w"] % 3
                cnt["w"] += 1
                rw = (tag, "wo", ws)
                P.op("sp", lambda e, oc=oc, ws=ws: e.dma_start(out=wo[ws][:, :, :].rearrange("p k o -> p (k o)"), in_=wo_bf[oc]),
                     writes=[rw], dma_key=("wo", ws))
                b = cnt["o"] % 2
                cnt["o"] += 1
                rp = (tag, "ps", b)
                for kc in range(KC):
                    P.op("pe", lambda e, kc=kc, ws=ws, b=b, s=s, n=n: e.matmul(
                        ps[b][:, :n], lhsT=wo[ws][:, kc, :], rhs=At[s][:, kc, :n], start=(kc == 0), stop=(kc == KC - 1)),
                        reads=[rw, (tag, "Ata", s), (tag, "Atp", s)], writes=[rp])
                P.op("dve", lambda e, oc=oc, b=b, s=s, n=n: e.tensor_tensor(
                    out=xo[s][:, oc, :n], in0=ps[b][:, :n], in1=xt[s][:, oc, :n], op=ALU.add),
                    reads=[rp, rx], writes=[rxo])
            store_group(cx, tag, xo[s], rxo, Xo, o0, n, gi == 0)


def cast_all(cx, items):
    P = cx.P
    CWD = 4096
    with Phase(cx):
        st_f = [cx.sb([128, CWD], F32, "cst_f") for _ in range(3)]
        st_b = [cx.sb([128, CWD], BF16, "cst_b") for _ in range(3)]
        i = 0
        for src, dst, rows, cols in items:
            for r0 in range(0, rows, 128):
                for c0 in range(0, cols, CWD):
                    cw_ = min(CWD, cols - c0)
                    s = i % 3
                    rf, rb = ("cf", s), ("cb", s)
                    P.op("sp", lambda e, src=src, r0=r0, c0=c0, cw_=cw_, s=s: e.dma_start(
                        out=st_f[s][:, :cw_], in_=src[r0:r0 + 128, c0:c0 + cw_]), writes=[rf], dma_key=rf)
                    if i % 2 == 0:
                        P.op("dve", lambda e, cw_=cw_, s=s: e.tensor_copy(out=st_b[s][:, :cw_], in_=st_f[s][:, :cw_]),
                             reads=[rf], writes=[rb])
                    else:
                        P.op("act", lambda e, cw_=cw_, s=s: e.activation(out=st_b[s][:, :cw_], in_=st_f[s][:, :cw_], func=AF.Copy),
                             reads=[rf], writes=[rb])
                    P.op("pool", lambda e, dst=dst, r0=r0, c0=c0, cw_=cw_, s=s: e.dma_start(
                        out=dst[r0:r0 + 128, c0:c0 + cw_], in_=st_b[s][:, :cw_]),
                        reads=[rb], writes=[("cdst",)], dma_key=("cst", s))
                    i += 1


def build_diag(cx, dww_dram, dg_bf, tag):
    P = cx.P
    with Phase(cx):
        dww, rdw = load_small(cx, tag, "dww", [128, KC * CW31], dww_dram)
        dst = [cx.sb([128, CW31, 128], BF16, "dgst") for _ in range(2)]
        for j in range(KC):
            s = j % 2
            rs = (tag, "dgst", s)
            for k in range(CW31):
                P.op("pool", lambda e, j=j, k=k, s=s: e.tensor_scalar(
                    out=dst[s][:, k, :], in0=cx.ident_b[:, :], scalar1=dww[:, j * CW31 + k:j * CW31 + k + 1], scalar2=0.0,
                    op0=ALU.mult, op1=ALU.add),
                    reads=[rdw, ("c", "ident")], writes=[rs])
            P.op("sp", lambda e, j=j, s=s: e.dma_start(out=dg_bf[j], in_=dst[s][:, :, :].rearrange("p k o -> p (k o)")),
                 reads=[rs], writes=[(tag, "dgbf")], dma_key=("dgst", s))


WSPEC_E = (("wqk", 12 * 128, 1024), ("wf", 128, 64), ("wv", 128, 4096), ("wp", 128, 512), ("wo", 8 * 128, 1024))
WSPEC_O = (("w1", 8 * 128, 2048), ("w2", 8 * 128, 1024))
WSPEC_F = (("wup", 22 * 128, 2048), ("wdn", 8 * 128, 2816))


def build_program(cfg, n_layers=4, do_ffn=True, stop=None):
    nc = bass.Bass("TRN2", target_bir_lowering=False)
    T_, CH_, SEQ_ = cfg.T, cfg.CH, cfg.SEQ

    def din(name, shape, dt=F32):
        return nc.dram_tensor(name, list(shape), dt, kind="ExternalInput").ap()

    def dsc(name, shape, dt):
        return nc.dram_tensor(name, list(shape), dt).ap()

    dr = {}
    dr["x"] = din("x", [D, T_])
    dr["flag"] = din("flag", [128, 1])
    dr["invc"] = din("invc", [128, 4 * 32])
    Y = nc.dram_tensor("y", [D, CH_], F32, kind="ExternalOutput").ap()
    XA = dsc("XA", [D, T_], F32)
    XB = dsc("XB", [D, T_], F32)
    W = {}
    casts = []
    n_even = (n_layers + 1) // 2
    n_odd = n_layers // 2
    for e_ in range(n_even):
        for nm, r, c in WSPEC_E:
            f = din("e%d_%s" % (e_, nm), [r, c])
            b = dsc("e%d_%s_b" % (e_, nm), [r, c], BF16)
            W[("e", e_, nm)] = b
            casts.append((f, b, r, c))
        W[("e", e_, "vec")] = din("e%d_vec" % e_, [128, 16])
        W[("e", e_, "bf")] = din("e%d_bf" % e_, [8, 1])
    for o_ in range(n_odd):
        for nm, r, c in WSPEC_O:
            f = din("o%d_%s" % (o_, nm), [r, c])
            b = dsc("o%d_%s_b" % (o_, nm), [r, c], BF16)
            W[("o", o_, nm)] = b
            casts.append((f, b, r, c))
        W[("o", o_, "vec")] = din("o%d_vec" % o_, [128, 32])
        W[("o", o_, "dww")] = din("o%d_dww" % o_, [128, KC * CW31])
        W[("o", o_, "dg")] = dsc("o%d_dg" % o_, [KC * 128, CW31 * 128], BF16)
    if do_ffn:
        for l in range(n_layers):
            for nm, r, c in WSPEC_F:
                f = din("f%d_%s" % (l, nm), [r, c])
                b = dsc("f%d_%s_b" % (l, nm), [r, c], BF16)
                W[("f", l, nm)] = b
                casts.append((f, b, r, c))
            W[("f", l, "g")] = din("f%d_g" % l, [128, 8])
            W[("f", l, "cw")] = din("f%d_cw" % l, [128, 132])
            W[("f", l, "cb")] = din("f%d_cb" % l, [128, 44])
    dr["Qc"] = dsc("Qc", [512, CH_], BF16)
    dr["Kc"] = dsc("Kc", [512, CH_], BF16)
    dr["Vc"] = dsc("Vc", [CH_, 512], BF16)
    dr["Fc"] = dsc("Fc", [8, CH_], F32)
    dr["Qg"] = dsc("Qg", [4 * 512, CH_], BF16)
    dr["Kg"] = dsc("Kg", [4 * 512, CH_], BF16)
    dr["Vg"] = dsc("Vg", [4 * CH_, 512], BF16)
    dr["Fg"] = dsc("Fg", [4 * 8, CH_], F32)
    dr["Ql"] = dsc("Ql", [128, SEQ_], BF16)
    dr["Kl"] = dsc("Kl", [128, SEQ_], BF16)
    dr["Vl"] = dsc("Vl", [SEQ_, 128], BF16)
    dr["Fl"] = dsc("Fl", [8, CH_], F32)
    dr["Al"] = dsc("Al", [512, T_], BF16)
    dr["AugK"] = dsc("AugK", [12, SEQ_], BF16)
    dr["AugQ"] = dsc("AugQ", [12, SEQ_], BF16)
    dr["Oc"] = dsc("Oc", [4 * 128, CH_], BF16)
    dr["Og"] = dsc("Og", [5 * 512, CH_], BF16)
    dr["Pout"] = dsc("Pout", [512, T_], BF16)

    def slabs(ap, p=128):
        return ap.rearrange("(j p) c -> j p c", p=p)

    with contextlib.ExitStack() as stack:
        cx = Ctx(nc, stack)
        cx.dram = dr
        setup_consts(cx)
        cx.P.barrier()
        class _Stop(Exception):
            pass

        def chk(name):
            if stop == name:
                raise _Stop()
        cx.chk = chk
        try:
            chk("consts")
            cast_all(cx, casts)
            chk("cast")
            for o_ in range(n_odd):
                build_diag(cx, W[("o", o_, "dww")], slabs(W[("o", o_, "dg")]), "dg%d" % o_)
            chk("diag")
            _build_layers(cx, cfg, dr, W, XA, XB, Y, n_layers, do_ffn, slabs, chk)
        except _Stop:
            pass
        cx.P.emit()
    return nc, None


def _build_layers(cx, cfg, dr, W, XA, XB, Y, n_layers, do_ffn, slabs, chk):
    if True:
        cur = dr["x"]
        bufs = [XA, XB]
        bi = 0
        n_sub = n_layers * (2 if do_ffn else 1)
        sub = 0

        def nxt():
            nonlocal bi
            b = bufs[bi]
            bi ^= 1
            return b
        for l in range(n_layers):
            i2 = l // 2
            if l % 2 == 0:
                tag = "E%d" % l
                even_pre_phase(cx, cfg, cur, slabs(W[("e", i2, "wqk")]), W[("e", i2, "wf")], W[("e", i2, "wv")],
                               W[("e", i2, "wp")], W[("e", i2, "vec")], W[("e", i2, "bf")], tag + "a")
                chk("E1")
                attn_phase(cx, cfg, tag + "b")
                cx.P.barrier()
                chk("attn")
                sub += 1
                last = (sub == n_sub)
                pass
                dst = nxt()
                even_post_phase(cx, cfg, cur, dst, slabs(W[("e", i2, "wo")]), tag + "c")
                cur = dst
                chk("E3")
            else:
                tag = "O%d" % l
                sub += 1
                dst = nxt()
                odd_phase(cx, cfg, cur, dst, slabs(W[("o", i2, "w1")]), slabs(W[("o", i2, "w2")]),
                          slabs(W[("o", i2, "dg")]), W[("o", i2, "vec")], tag)
                cur = dst
            if do_ffn:
                sub += 1
                last = (sub == n_sub)
                if last:
                    ffn_phase(cx, cfg, cur, Y, slabs(W[("f", l, "wup")]), slabs(W[("f", l, "wdn")]),
                              W[("f", l, "g")], W[("f", l, "cw")], W[("f", l, "cb")], "F%d" % l, out_off=HALO)
                else:
                    dst = nxt()
                    ffn_phase(cx, cfg, cur, dst, slabs(W[("f", l, "wup")]), slabs(W[("f", l, "wdn")]),
                              W[("f", l, "g")], W[("f", l, "cw")], W[("f", l, "cb")], "F%d" % l)
                    cur = dst
        if not do_ffn:
            cx.P.op("sp", lambda e: e.dma_start(out=Y, in_=cur[:, HALO:]), dma_key=("G", "fin"))


def lay_slabs(w):
    n = w.shape[1] // 128
    a = w.reshape(KC, 128, n, 128).transpose(2, 1, 0, 3)
    return np.ascontiguousarray(a).reshape(n * 128, KC * 128)


def lay_pairs(w, half):
    n = half // 128
    a = w.reshape(KC, 128, 2, n, 128).transpose(3, 1, 2, 0, 4)
    return np.ascontiguousarray(a).reshape(n * 128, 2 * KC * 128)


def lay_kmajor(w):
    c = w.shape[1]
    return np.ascontiguousarray(w.reshape(KC, 128, c).transpose(1, 0, 2)).reshape(128, KC * c)


def lay_wdn(w_down):
    a = w_down.reshape(NJ, 128, KC, 128).transpose(2, 1, 0, 3)
    return np.ascontiguousarray(a).reshape(KC * 128, NJ * 128)


def lay_vec(v, nch):
    return np.ascontiguousarray(v.reshape(nch, 128).T)


def lay_cw(cw):
    k = cw.shape[0]
    n = cw.shape[1] // 128
    return np.ascontiguousarray(cw.reshape(k, n, 128).transpose(2, 1, 0)).reshape(128, n * k)


def host_inputs(cfg, inp, n_layers=4, do_ffn=True):
    f32 = np.float32
    shared = {}
    n_even = (n_layers + 1) // 2
    n_odd = n_layers // 2
    for e_ in range(n_even):
        w_in = np.asarray(inp["even_w_in"][e_], f32)
        q, k, v = w_in[:, 0:512], w_in[:, 512:1024], w_in[:, 1024:1536]
        f, u = w_in[:, 1536:1544], w_in[:, 1544:2056]
        shared["e%d_wqk" % e_] = lay_slabs(np.concatenate([q, k, u], axis=1))
        shared["e%d_wf" % e_] = lay_kmajor(f)
        shared["e%d_wv" % e_] = lay_kmajor(v)
        wp = np.asarray(inp["even_w_pool"][e_], f32)
        shared["e%d_wp" % e_] = np.ascontiguousarray(wp.transpose(1, 0, 2)).reshape(128, 512)
        shared["e%d_wo" % e_] = lay_slabs(np.asarray(inp["even_w_out"][e_], f32))
        vec = np.zeros((128, 16), f32)
        vec[:, 0:8] = lay_vec(np.asarray(inp["even_norm_g"][e_], f32), 8)
        vec[:, 8] = np.tile(np.asarray(inp["even_q_norm_g"][e_], f32), 2)
        vec[:, 9] = np.tile(np.asarray(inp["even_k_norm_g"][e_], f32), 2)
        vec[:, 10:14] = lay_vec(np.asarray(inp["even_pool_scale"][e_], f32), 4)
        shared["e%d_vec" % e_] = vec
        shared["e%d_bf" % e_] = np.asarray(inp["even_b_f"][e_], f32).reshape(8, 1).copy()
    for o_ in range(n_odd):
        shared["o%d_w1" % o_] = lay_pairs(np.asarray(inp["odd_w_pw1"][o_], f32), 1024)
        shared["o%d_w2" % o_] = lay_slabs(np.asarray(inp["odd_w_pw2"][o_], f32))
        vec = np.zeros((128, 32), f32)
        vec[:, 0:8] = lay_vec(np.asarray(inp["odd_norm_g"][o_], f32), 8)
        vec[:, 8:16] = lay_vec(np.asarray(inp["odd_dw_b"][o_], f32), 8)
        vec[:, 16:24] = lay_vec(np.asarray(inp["odd_ln_g"][o_], f32), 8)
        vec[:, 24:32] = lay_vec(np.asarray(inp["odd_ln_b"][o_], f32), 8)
        shared["o%d_vec" % o_] = vec
        shared["o%d_dww" % o_] = lay_cw(np.asarray(inp["odd_dw_w"][o_], f32))
    if do_ffn:
        for l in range(n_layers):
            shared["f%d_wup" % l] = lay_pairs(np.asarray(inp["ffn_w_up"][l], f32), DFF)
            shared["f%d_wdn" % l] = lay_wdn(np.asarray(inp["ffn_w_down"][l], f32))
            shared["f%d_g" % l] = lay_vec(np.asarray(inp["ffn_norm_g"][l], f32), 8)
            shared["f%d_cw" % l] = lay_cw(np.asarray(inp["ffn_conv_w"][l], f32))
            shared["f%d_cb" % l] = lay_vec(np.asarray(inp["ffn_conv_b"][l], f32), 44)
    x = np.asarray(inp["x"], f32)
    maps = []
    for c in range(NCORES):
        b, i = c // 4, c % 4
        xt = np.zeros((D, cfg.T), f32)
        lo = i * cfg.CH - HALO
        if i == 0:
            xt[:, HALO:] = x[b, 0:cfg.CH].T
        else:
            xt[:, :] = x[b, lo:lo + cfg.T].T
        flag = np.full((128, 1), 0.0 if i == 0 else 1.0, f32)
        invc = np.zeros((128, 4 * 32), f32)
        pos = np.arange(1, 33, dtype=f32)
        for g, w in enumerate(POOL_W):
            cntv = np.minimum(pos, float(w)) if i == 0 else np.full(32, float(w), f32)
            invc[:, g * 32:(g + 1) * 32] = (1.0 / cntv)[None, :]
        m = dict(shared)
        m["x"] = xt
        m["flag"] = flag
        m["invc"] = invc
        maps.append(m)
    return maps


_CACHE = {}
N_SPLIT = 1


def _sub_inputs(inputs, l0, nl):
    out = {}
    for k, v in inputs.items():
        if k == "x":
            out[k] = v
        elif k.startswith("even_"):
            out[k] = v[(l0 + 1) // 2:]
        elif k.startswith("odd_"):
            out[k] = v[l0 // 2:]
        else:
            out[k] = v[l0:]
    return out


def kernel(**inputs):
    cfg = Cfg(4096)
    nl = 4 // N_SPLIT
    if "nc" not in _CACHE:
        _CACHE["nc"] = build_program(cfg, n_layers=nl)[0]
    nc = _CACHE["nc"]
    inp = {k: np.asarray(v) for k, v in inputs.items()}
    x = np.asarray(inp["x"], np.float32)
    for part in range(N_SPLIT):
        sub = _sub_inputs(inp, part * nl, nl)
        sub["x"] = x
        maps = host_inputs(cfg, sub, n_layers=nl)
        res = run_bass_kernel_spmd(nc, maps, core_ids=list(range(NCORES)))
        out = np.empty((2, SEQ, D), np.float32)
        for c in range(NCORES):
            b, i = c // 4, c % 4
            out[b, i * cfg.CH:(i + 1) * cfg.CH, :] = res.results[c]["y"].T
        x = out
    return x
```

```python
import contextlib
import numpy as np
import concourse.bass as bass
import concourse.mybir as mybir
from concourse.bass_utils import run_bass_kernel_spmd

F32 = mybir.dt.float32
BF16 = mybir.dt.bfloat16
AF = mybir.ActivationFunctionType
ALU = mybir.AluOpType

D = 1024
KC = 8
DFF = 2816
NJ = 22
NCORES = 8
SEQ = 16384
HALO = 128
CH = 2048
SEG = CH + HALO
T = 2 * SEG
EPS = 1e-6

SAME_ENGINE_SYNC = True
NO_SELF_SYNC = ("pe",)


class Op:
    __slots__ = ("eng", "fn", "deps", "is_dma", "sem", "val", "need_inc", "key", "phase", "inc")

    def __init__(self, eng, fn, is_dma, key=None):
        self.eng = eng
        self.fn = fn
        self.deps = []
        self.is_dma = is_dma
        self.sem = None
        self.val = None
        self.need_inc = False
        self.key = key
        self.inc = 16


class Prog:
    ENGS = ("pe", "act", "dve", "pool", "sp")

    def __init__(self, nc, stack):
        self.nc = nc
        self.stack = stack
        self.ops = {e: [] for e in self.ENGS}
        self.last_w = {}
        self.readers = {}
        self.dma_sems = {}
        self.n_ops = 0
        self.phase = 0

    def op(self, eng, fn, reads=(), writes=(), dma_key=None, inc=16):
        o = Op(eng, fn, dma_key is not None, dma_key)
        o.inc = inc
        o.phase = self.phase
        deps = []
        for r in reads:
            w = self.last_w.get(r)
            if w is not None:
                deps.append(w)
        for r in writes:
            w = self.last_w.get(r)
            if w is not None:
                deps.append(w)
            deps.extend(self.readers.get(r, ()))
        seen = set()
        for d in deps:
            if id(d) in seen or d is o:
                continue
            seen.add(id(d))
            if (not d.is_dma) and d.eng == eng and (eng in NO_SELF_SYNC or not SAME_ENGINE_SYNC):
                continue
            if d.is_dma and o.is_dma and d.key == o.key and isinstance(o.key, tuple) and o.key[0] == "G":
                continue
            o.deps.append(d)
            d.need_inc = True
        for r in reads:
            self.readers.setdefault(r, []).append(o)
        for r in writes:
            self.last_w[r] = o
            self.readers[r] = []
        self.ops[eng].append(o)
        self.n_ops += 1
        return o

    def barrier(self):
        deps = []
        for e in self.ENGS:
            for o in reversed(self.ops[e]):
                if not o.is_dma:
                    deps.append(o)
                    break
        last_dma = {}
        for e in self.ENGS:
            for o in self.ops[e]:
                if o.is_dma:
                    last_dma[o.key] = o
        deps.extend(last_dma.values())
        for e in self.ENGS:
            o = Op(e, lambda en: en.nop(), False)
            o.phase = self.phase
            for d in deps:
                if (not d.is_dma) and d.eng == e:
                    continue
                o.deps.append(d)
                d.need_inc = True
            self.ops[e].append(o)
        self.last_w = {}
        self.readers = {}
        self.phase += 1

    def emit(self, final_waits=()):
        nc = self.nc
        stack = self.stack
        eng_sem = {}
        ecnt = {}
        gtot = {}
        for e in self.ENGS:
            for o in self.ops[e]:
                if o.is_dma:
                    if o.key not in self.dma_sems:
                        self.dma_sems[o.key] = [
                            stack.enter_context(nc.semaphore("d%d" % len(self.dma_sems))), 0]
                    ent = self.dma_sems[o.key]
                    ent[1] += o.inc
                    o.sem, o.val = ent[0], ent[1]
                    if isinstance(o.key, tuple) and o.key[0] == "G":
                        gtot[(o.key, o.phase)] = ent[1]
                elif o.need_inc:
                    k_ = (e, o.phase % 4)
                    if k_ not in eng_sem:
                        eng_sem[k_] = stack.enter_context(nc.semaphore("s_%s_%d" % k_))
                        ecnt[k_] = 0
                    ecnt[k_] += 1
                    o.sem, o.val = eng_sem[k_], ecnt[k_]
        for e in self.ENGS:
            for o in self.ops[e]:
                if o.is_dma and (o.key, o.phase) in gtot:
                    o.val = gtot[(o.key, o.phase)]
        final = list(final_waits)
        block = stack.enter_context(nc.Block())
        handles = {"pe": "tensor", "act": "scalar", "dve": "vector", "pool": "gpsimd", "sp": "sync"}

        def run(ename, e):
            waited = {}
            for o in self.ops[ename]:
                for d in o.deps:
                    k = id(d.sem)
                    if waited.get(k, 0) >= d.val:
                        continue
                    e.wait_ge(d.sem, d.val)
                    waited[k] = d.val
                ins = o.fn(e)
                if o.is_dma:
                    ins.then_inc(o.sem, o.inc)
                elif o.need_inc:
                    ins.then_inc(o.sem, 1)
            if ename == "pool":
                for ent in self.dma_sems.values():
                    e.wait_ge(ent[0], ent[1])

        for ename in self.ENGS:
            dec = getattr(block, handles[ename])

            def body(e, _n=ename):
                run(_n, e)
            dec(body)


_PID = {}


def pid4(e):
    k = id(e)
    if k not in _PID:
        _PID[k] = e.partition_id() % 4
    return _PID[k]


class Ctx:
    def __init__(self, nc, stack):
        self.nc = nc
        self.stack = stack
        self.P = Prog(nc, stack)
        self._n = 0

    def sb(self, shape, dtype, name=None):
        self._n += 1
        return self.stack.enter_context(self.nc.sbuf_tensor("%s_%d" % (name or "sb", self._n), list(shape), dtype))

    def ps(self, shape, dtype=F32, name=None):
        self._n += 1
        return self.stack.enter_context(self.nc.psum_tensor("%s_%d" % (name or "ps", self._n), list(shape), dtype))


class Cfg:
    def __init__(self, CH=4096):
        self.CH = CH
        self.T = CH + HALO
        self.SEQ = 4 * CH
        self.NKB = self.SEQ // 128
        self.NQB = self.SEQ // 512


def make_groups(T_, first_n, step, halo):
    out = [(0, min(first_n, T_), 0)]
    pos = out[0][1]
    while pos < T_:
        n = min(step, T_ - pos)
        out.append((pos, n, halo))
        pos += n
    return out


class Phase:
    def __init__(self, cx):
        self.cx = cx

    def __enter__(self):
        self.st = contextlib.ExitStack()
        self.saved = self.cx.stack
        self.cx.stack = self.st
        return self

    def __exit__(self, *a):
        self.cx.P.barrier()
        self.cx.stack = self.saved
        self.st.close()
        return False


def setup_consts(cx):
    P = cx.P
    cx.ones_f = cx.sb([128, 128], F32, "ones_f")
    cx.ones2_f = cx.sb([128, 128], F32, "ones2_f")
    cx.eps_t = cx.sb([128, 1], F32, "eps")
    cx.one_t = cx.sb([128, 1], F32, "one")
    cx.ident_b = cx.sb([128, 128], BF16, "ident_b")
    cx.maskb = cx.sb([128, 128], BF16, "maskb")
    cx.flag = cx.sb([128, 1], F32, "flag")
    P.op("dve", lambda e: e.memset(cx.ones_f[:, :], 1.0), writes=[("c", "ones_f")])
    P.op("dve", lambda e: e.memset(cx.ones2_f[:, :], 1.0), writes=[("c", "ones2_f")])
    P.op("dve", lambda e: e.memset(cx.ones2_f[0:64, 64:128], 0.0), writes=[("c", "ones2_f")])
    P.op("dve", lambda e: e.memset(cx.ones2_f[64:128, 0:64], 0.0), writes=[("c", "ones2_f")])
    P.op("dve", lambda e: e.memset(cx.eps_t[:, :], EPS), writes=[("c", "eps")])
    P.op("dve", lambda e: e.memset(cx.one_t[:, :], 1.0), writes=[("c", "one")])
    P.op("dve", lambda e: e.memset(cx.ident_b[:, :], 0.0), writes=[("c", "ident")])
    P.op("pool", lambda e: e.affine_select(out=cx.ident_b[:, :], in_=cx.ident_b[:, :], pattern=[[1, 128]],
                                           compare_op=ALU.not_equal, fill=1.0, base=0, channel_multiplier=-1),
         reads=[("c", "ident")], writes=[("c", "ident")])
    P.op("dve", lambda e: e.memset(cx.maskb[:, :], 0.0), writes=[("c", "maskb")])
    P.op("pool", lambda e: e.affine_select(out=cx.maskb[:, :], in_=cx.maskb[:, :], pattern=[[1, 128]],
                                           compare_op=ALU.is_ge, fill=-30000.0, base=0, channel_multiplier=-1),
         reads=[("c", "maskb")], writes=[("c", "maskb")])
    P.op("sp", lambda e: e.dma_start(out=cx.flag[:, :], in_=cx.dram["flag"]), writes=[("c", "flag")], dma_key=("G", "small"))


def emit_rmsnorm(cx, xt, nu, g_t, H, res, sq, red, ps_stat, std, rstd):
    P = cx.P
    P.op("act", lambda e: e.activation(out=sq[:, :, :nu], in_=xt[:, :, :nu], func=AF.Square),
         reads=[res["xt"]], writes=[res["sq"]])
    P.op("pool", lambda e: e.tensor_tensor(out=red[:, :nu], in0=sq[:, 0, :nu], in1=sq[:, 1, :nu], op=ALU.add),
         reads=[res["sq"]], writes=[res["red"]])
    for kc in range(2, KC):
        P.op("pool", lambda e, kc=kc: e.tensor_tensor(out=red[:, :nu], in0=red[:, :nu], in1=sq[:, kc, :nu], op=ALU.add),
             reads=[res["sq"], res["red"]], writes=[res["red"]])
    P.op("pe", lambda e: e.matmul(ps_stat[:, :nu], lhsT=cx.ones_f[:, :], rhs=red[:, :nu], start=True, stop=True),
         reads=[res["red"], ("c", "ones_f")], writes=[res["ps_stat"]])
    P.op("act", lambda e: e.activation(out=std[:, :nu], in_=ps_stat[:, :nu], func=AF.Sqrt, scale=1.0 / D, bias=cx.eps_t[:, 0:1]),
         reads=[res["ps_stat"], ("c", "eps")], writes=[res["std"]])
    P.op("dve", lambda e: e.reciprocal(out=rstd[:, :nu], in_=std[:, :nu]),
         reads=[res["std"]], writes=[res["rstd"]])
    for kc in range(KC):
        P.op("dve", lambda e, kc=kc: e.scalar_tensor_tensor(
            out=H[:, kc, :nu], in0=xt[:, kc, :nu], scalar=g_t[:, kc:kc + 1], in1=rstd[:, :nu],
            op0=ALU.mult, op1=ALU.mult),
            reads=[res["xt"], res["rstd"], res["g"]], writes=[res["H"]])


def load_small(cx, tag, name, shape, src):
    t = cx.sb(shape, F32, name)
    r = (tag, name)
    cx.P.op("sp", lambda e: e.dma_start(out=t[:, :], in_=src), writes=[r], dma_key=("G", "small"))
    return t, r


def store_group(cx, tag, xo, rxo, Xo, o0, n, first):
    P = cx.P
    if first:
        P.op("dve", lambda e: e.tensor_scalar(out=xo[:, :, 0:HALO], in0=xo[:, :, 0:HALO], scalar1=cx.flag[:, 0:1],
                                              scalar2=None, op0=ALU.mult),
             reads=[rxo, ("c", "flag")], writes=[rxo])
    P.op("pool", lambda e: e.dma_start(out=Xo[:, :, o0:o0 + n], in_=xo[:, :, :n]),
         reads=[rxo], writes=[(tag, "Xout")], dma_key=("st", 0))


def ffn_phase(cx, cfg, X_in, X_out, wup_bf, wdn_bf, g_dram, cw_dram, cb_dram, tag, out_off=0):
    P = cx.P
    groups = make_groups(cfg.T, 512, 510, 2)
    Xi = X_in.rearrange("(kc p) t -> p kc t", p=128)
    Xo = X_out.rearrange("(kc p) t -> p kc t", p=128)
    with Phase(cx):
        g_t, rg = load_small(cx, tag, "g", [128, KC], g_dram)
        cw_t, rcw = load_small(cx, tag, "cw", [128, 44 * 3], cw_dram)
        cb_t, rcb = load_small(cx, tag, "cb", [128, 44], cb_dram)
        NX = 2
        xt = [cx.sb([128, KC, 512], F32, "xt") for _ in range(NX)]
        sq = cx.sb([128, KC, 512], F32, "sq")
        red = cx.sb([128, 512], F32, "red")
        std = cx.sb([128, 512], F32, "std")
        rstd = cx.sb([128, 512], F32, "rstd")
        Hs = [cx.sb([128, KC, 512], BF16, "H") for _ in range(2)]
        NW = 5
        wup = [cx.sb([128, 2, KC, 128], BF16, "wup") for _ in range(NW)]
        acc_g = [cx.sb([128, 512], F32, "accg") for _ in range(2)]
        acc_v = [cx.sb([128, 512], F32, "accv") for _ in range(2)]
        sg = [cx.sb([128, 512], F32, "sg") for _ in range(2)]
        Gs = [cx.sb([128, NJ, 512], BF16, "G") for _ in range(2)]
        ND = 3
        wdn = [cx.sb([128, NJ, 128], BF16, "wdn") for _ in range(ND)]
        xo = cx.sb([128, KC, 512], F32, "xo")
        ps_stat = cx.ps([128, 512], F32, "psst")
        ps_g = [cx.ps([128, 512], F32, "psg") for _ in range(2)]
        ps_v = [cx.ps([128, 512], F32, "psv") for _ in range(2)]
        ps_o = [cx.ps([128, 512], F32, "pso") for _ in range(2)]
        cnt = {"w": 0, "d": 0, "a": 0, "o": 0}

        def pre(gi):
            o0, n, h = groups[gi]
            nu = n + h
            s = gi % NX
            rx = (tag, "xt", s)
            P.op("sp", lambda e: e.dma_start(out=xt[s][:, :, :nu], in_=Xi[:, :, o0 - h:o0 + n]),
                 reads=[(tag, "Xin")], writes=[rx], dma_key=("xt", s))
            res = {"xt": rx, "sq": (tag, "sq"), "red": (tag, "red"), "ps_stat": (tag, "psst"), "std": (tag, "std"),
                   "rstd": (tag, "rstd"), "g": rg, "H": (tag, "H", gi % 2)}
            emit_rmsnorm(cx, xt[s], nu, g_t, Hs[gi % 2], res, sq, red, ps_stat, std, rstd)

        def up(gi):
            o0, n, h = groups[gi]
            nu = n + h
            H = Hs[gi % 2]
            G = Gs[gi % 2]
            rH = (tag, "H", gi % 2)
            rG = (tag, "G", gi % 2)
            for j in range(NJ):
                ws = cnt["w"] % NW
                cnt["w"] += 1
                rw = (tag, "wup", ws)
                P.op("sp", lambda e, j=j, ws=ws: e.dma_start(
                    out=wup[ws][:, :, :, :].rearrange("p a k o -> p (a k o)"), in_=wup_bf[j]),
                    writes=[rw], dma_key=("wup", ws))
                a = cnt["a"] % 2
                cnt["a"] += 1
                rpg, rpv = (tag, "psg", a), (tag, "psv", a)
                for gv, psb, rp in ((0, ps_g[a], rpg), (1, ps_v[a], rpv)):
                    for kc in range(KC):
                        P.op("pe", lambda e, gv=gv, kc=kc, psb=psb, ws=ws: e.matmul(
                            psb[:, :nu], lhsT=wup[ws][:, gv, kc, :], rhs=H[:, kc, :nu],
                            start=(kc == 0), stop=(kc == KC - 1)),
                            reads=[rw, rH], writes=[rp])
                ag, av, sgt = acc_g[a], acc_v[a], sg[a]
                rag, rav, rsg = (tag, "accg", a), (tag, "accv", a), (tag, "sg", a)
                pairs = ((ag, rag, ps_g[a], rpg, j), (av, rav, ps_v[a], rpv, NJ + j))
                for (acc, racc, psb, rp, ch) in pairs:
                    w2 = cw_t[:, ch * 3 + 2:ch * 3 + 3]
                    bb = cb_t[:, ch:ch + 1]
                    P.op("act", lambda e, acc=acc, psb=psb, w2=w2, bb=bb: e.activation(
                        out=acc[:, :n], in_=psb[:, h:h + n], func=AF.Identity, scale=w2, bias=bb),
                        reads=[rp, rcw, rcb], writes=[racc])
                for sh in (1, 2):
                    for (acc, racc, psb, rp, ch) in pairs:
                        wk = cw_t[:, ch * 3 + (2 - sh):ch * 3 + (3 - sh)]
                        if h >= sh:
                            P.op("dve", lambda e, acc=acc, psb=psb, wk=wk, sh=sh: e.scalar_tensor_tensor(
                                out=acc[:, :n], in0=psb[:, h - sh:h - sh + n], scalar=wk, in1=acc[:, :n],
                                op0=ALU.mult, op1=ALU.add),
                                reads=[rp, rcw, racc], writes=[racc])
                        else:
                            P.op("dve", lambda e, acc=acc, psb=psb, wk=wk, sh=sh: e.scalar_tensor_tensor(
                                out=acc[:, sh:n], in0=psb[:, 0:n - sh], scalar=wk, in1=acc[:, sh:n],
                                op0=ALU.mult, op1=ALU.add),
                                reads=[rp, rcw, racc], writes=[racc])
                P.op("act", lambda e, ag=ag, sgt=sgt: e.activation(out=sgt[:, :n], in_=ag[:, :n], func=AF.Silu),
                     reads=[rag], writes=[rsg])
                P.op("pool", lambda e, j=j, sgt=sgt, av=av: e.tensor_tensor(
                    out=G[:, j, :n], in0=sgt[:, :n], in1=av[:, :n], op=ALU.mult),
                    reads=[rsg, rav], writes=[rG])

        def down(gi):
            o0, n, h = groups[gi]
            G = Gs[gi % 2]
            rG = (tag, "G", gi % 2)
            s = gi % NX
            rx = (tag, "xt", s)
            rxo = (tag, "xo")
            for oc in range(KC):
                ds_ = cnt["d"] % ND
                cnt["d"] += 1
                rw = (tag, "wdn", ds_)
                P.op("sp", lambda e, oc=oc, ds_=ds_: e.dma_start(
                    out=wdn[ds_][:, :, :].rearrange("p j o -> p (j o)"), in_=wdn_bf[oc]),
                    writes=[rw], dma_key=("wdn", ds_))
                b = cnt["o"] % 2
                cnt["o"] += 1
                rp = (tag, "pso", b)
                for j in range(NJ):
                    P.op("pe", lambda e, j=j, b=b, ds_=ds_: e.matmul(
                        ps_o[b][:, :n], lhsT=wdn[ds_][:, j, :], rhs=G[:, j, :n],
                        start=(j == 0), stop=(j == NJ - 1)),
                        reads=[rw, rG], writes=[rp])
                P.op("dve", lambda e, oc=oc, b=b: e.tensor_tensor(
                    out=xo[:, oc, :n], in0=ps_o[b][:, :n], in1=xt[s][:, oc, h:h + n], op=ALU.add),
                    reads=[rp, rx], writes=[rxo])
            if out_off == 0:
                store_group(cx, tag, xo, rxo, Xo, o0, n, gi == 0)
            else:
                lo = max(o0, out_off)
                if lo < o0 + n:
                    P.op("pool", lambda e: e.dma_start(out=Xo[:, :, lo - out_off:o0 + n - out_off], in_=xo[:, :, lo - o0:n]),
                         reads=[rxo], writes=[(tag, "Xout")], dma_key=("st", 0))

        ng = len(groups)
        pre(0)
        up(0)
        for gi in range(ng):
            if gi + 1 < ng:
                pre(gi + 1)
                up(gi + 1)
            down(gi)


CW31 = 31
HO = 30


def odd_phase(cx, cfg, X_in, X_out, w1_bf, w2_bf, dg_bf, vec_dram, tag):
    P = cx.P
    groups = make_groups(cfg.T, 482, 482, HO)
    Xi = X_in.rearrange("(kc p) t -> p kc t", p=128)
    Xo = X_out.rearrange("(kc p) t -> p kc t", p=128)
    with Phase(cx):
        vec, rvec = load_small(cx, tag, "vec", [128, 32], vec_dram)
        g_t = vec[:, 0:8]
        xt = [cx.sb([128, KC, 512], F32, "xt") for _ in range(2)]
        sqk = [cx.sb([128, 512], F32, "sqk") for _ in range(2)]
        red = cx.sb([128, 512], F32, "red")
        lr1 = cx.sb([128, 512], F32, "lr1")
        lr2 = cx.sb([128, 512], F32, "lr2")
        std = cx.sb([128, 512], F32, "std")
        rstd = cx.sb([128, 512], F32, "rstd")
        mean = cx.sb([128, 512], F32, "mean")
        m2 = cx.sb([128, 512], F32, "m2")
        t1 = [cx.sb([128, 512], F32, "t1") for _ in range(2)]
        Hs = [cx.sb([128, KC, 512], BF16, "H") for _ in range(2)]
        w1 = [cx.sb([128, 2, KC, 128], BF16, "w1") for _ in range(3)]
        dg = [cx.sb([128, CW31, 128], BF16, "dg") for _ in range(2)]
        Us = [cx.sb([128, KC, 512 + HO], BF16, "U") for _ in range(2)]
        sig = [cx.sb([128, 512], F32, "sig") for _ in range(2)]
        Vs = [cx.sb([128, KC, 512], F32, "V") for _ in range(2)]
        Ss = [cx.sb([128, KC, 512], BF16, "S") for _ in range(2)]
        w2 = [cx.sb([128, KC, 128], BF16, "w2") for _ in range(2)]
        xo = cx.sb([128, KC, 512], F32, "xo")
        ps = [cx.ps([128, 512], F32, "ps") for _ in range(8)]
        rps = [(tag, "ps", i) for i in range(8)]
        cnt = {"w1": 0, "ag": 0, "dg": 0, "c": 0, "w2": 0, "o": 0}

        def pre(gi):
            o0, n, h = groups[gi]
            nu = n + h
            s = gi % 2
            rx = (tag, "xt", s)
            P.op("sp", lambda e: e.dma_start(out=xt[s][:, :, :nu], in_=Xi[:, :, o0 - h:o0 + n]),
                 reads=[(tag, "Xin")], writes=[rx], dma_key=("xt", s))
            rH = (tag, "H", s)
            x_ = xt[s]
            P.op("act", lambda e: e.activation(out=red[:, :nu], in_=x_[:, 0, :nu], func=AF.Square),
                 reads=[rx], writes=[(tag, "red")])
            for kc in range(1, KC):
                q = sqk[kc % 2]
                rq = (tag, "sqk", kc % 2)
                P.op("act", lambda e, kc=kc, q=q: e.activation(out=q[:, :nu], in_=x_[:, kc, :nu], func=AF.Square),
                     reads=[rx], writes=[rq])
                P.op("pool", lambda e, q=q: e.tensor_tensor(out=red[:, :nu], in0=red[:, :nu], in1=q[:, :nu], op=ALU.add),
                     reads=[rq, (tag, "red")], writes=[(tag, "red")])
            P.op("pe", lambda e: e.matmul(ps[0][:, :nu], lhsT=cx.ones_f[:, :], rhs=red[:, :nu], start=True, stop=True),
                 reads=[(tag, "red"), ("c", "ones_f")], writes=[rps[0]])
            P.op("act", lambda e: e.activation(out=std[:, :nu], in_=ps[0][:, :nu], func=AF.Sqrt, scale=1.0 / D, bias=cx.eps_t[:, 0:1]),
                 reads=[rps[0], ("c", "eps")], writes=[(tag, "std")])
            P.op("dve", lambda e: e.reciprocal(out=rstd[:, :nu], in_=std[:, :nu]),
                 reads=[(tag, "std")], writes=[(tag, "rstd")])
            for kc in range(KC):
                P.op("dve", lambda e, kc=kc: e.scalar_tensor_tensor(
                    out=Hs[s][:, kc, :nu], in0=x_[:, kc, :nu], scalar=g_t[:, kc:kc + 1], in1=rstd[:, :nu],
                    op0=ALU.mult, op1=ALU.mult),
                    reads=[rx, (tag, "rstd"), rvec], writes=[rH])

        def glu(gi):
            o0, n, h = groups[gi]
            nu = n + h
            s = gi % 2
            H = Hs[s]
            rH = (tag, "H", s)
            U = Us[s]
            rU = (tag, "U", s)
            cs = HO - h
            if h == 0:
                P.op("pool", lambda e: e.memset(U[:, :, 0:HO], 0.0), writes=[rU])
            for j in range(KC):
                ws = cnt["w1"] % 3
                cnt["w1"] += 1
                rw = (tag, "w1", ws)
                P.op("sp", lambda e, j=j, ws=ws: e.dma_start(
                    out=w1[ws][:, :, :, :].rearrange("p a k o -> p (a k o)"), in_=w1_bf[j]),
                    writes=[rw], dma_key=("w1", ws))
                a = cnt["ag"] % 2
                cnt["ag"] += 1
                pa, pg = ps[1 + 2 * a], ps[2 + 2 * a]
                rpa, rpg = rps[1 + 2 * a], rps[2 + 2 * a]
                for gv, psb, rp in ((0, pa, rpa), (1, pg, rpg)):
                    for kc in range(KC):
                        P.op("pe", lambda e, gv=gv, kc=kc, psb=psb, ws=ws: e.matmul(
                            psb[:, :nu], lhsT=w1[ws][:, gv, kc, :], rhs=H[:, kc, :nu],
                            start=(kc == 0), stop=(kc == KC - 1)),
                            reads=[rw, rH], writes=[rp])
                sg_ = sig[a]
                rsg = (tag, "sig", a)
                P.op("act", lambda e, sg_=sg_, pg=pg: e.activation(out=sg_[:, :nu], in_=pg[:, :nu], func=AF.Sigmoid),
                     reads=[rpg], writes=[rsg])
                P.op("dve", lambda e, j=j, sg_=sg_, pa=pa: e.tensor_tensor(
                    out=U[:, j, cs:cs + nu], in0=pa[:, :nu], in1=sg_[:, :nu], op=ALU.mult),
                    reads=[rpa, rsg], writes=[rU])

        def conv(gi):
            o0, n, h = groups[gi]
            s = gi % 2
            U = Us[s]
            rU = (tag, "U", s)
            V = Vs[s]
            rV = (tag, "V", s)
            for j in range(KC):
                d_ = cnt["dg"] % 2
                cnt["dg"] += 1
                rd = (tag, "dg", d_)
                P.op("sp", lambda e, j=j, d_=d_: e.dma_start(
                    out=dg[d_][:, :, :].rearrange("p k o -> p (k o)"), in_=dg_bf[j]),
                    writes=[rd], dma_key=("dg", d_))
                b = cnt["c"] % 2
                cnt["c"] += 1
                pc, rpc = ps[5 + b], rps[5 + b]
                for k in range(CW31):
                    P.op("pe", lambda e, j=j, k=k, d_=d_, pc=pc: e.matmul(
                        pc[:, :n], lhsT=dg[d_][:, k, :], rhs=U[:, j, k:k + n],
                        start=(k == 0), stop=(k == CW31 - 1)),
                        reads=[rd, rU], writes=[rpc])
                P.op("act", lambda e, j=j, pc=pc: e.activation(
                    out=V[:, j, :n], in_=pc[:, :n], func=AF.Identity, bias=vec[:, 8 + j:9 + j], scale=1.0),
                    reads=[rpc, rvec], writes=[rV])

        def ln_a(gi):
            o0, n, h = groups[gi]
            s = gi % 2
            V = Vs[s]
            rV = (tag, "V", s)
            P.op("pool", lambda e: e.tensor_tensor(out=lr1[:, :n], in0=V[:, 0, :n], in1=V[:, 1, :n], op=ALU.add),
                 reads=[rV], writes=[(tag, "lr1")])
            for kc in range(2, KC):
                P.op("pool", lambda e, kc=kc: e.tensor_tensor(out=lr1[:, :n], in0=lr1[:, :n], in1=V[:, kc, :n], op=ALU.add),
                     reads=[rV, (tag, "lr1")], writes=[(tag, "lr1")])
            P.op("act", lambda e: e.activation(out=lr2[:, :n], in_=V[:, 0, :n], func=AF.Square),
                 reads=[rV], writes=[(tag, "lr2")])
            for kc in range(1, KC):
                q = sqk[kc % 2]
                rq = (tag, "sqk", kc % 2)
                P.op("act", lambda e, kc=kc, q=q: e.activation(out=q[:, :n], in_=V[:, kc, :n], func=AF.Square),
                     reads=[rV], writes=[rq])
                P.op("pool", lambda e, q=q: e.tensor_tensor(out=lr2[:, :n], in0=lr2[:, :n], in1=q[:, :n], op=ALU.add),
                     reads=[rq, (tag, "lr2")], writes=[(tag, "lr2")])

        def ln_b(gi):
            o0, n, h = groups[gi]
            s = gi % 2
            V = Vs[s]
            rV = (tag, "V", s)
            S = Ss[s]
            rS = (tag, "S", s)
            P.op("pe", lambda e: e.matmul(ps[0][:, :n], lhsT=cx.ones_f[:, :], rhs=lr1[:, :n], start=True, stop=True),
                 reads=[(tag, "lr1"), ("c", "ones_f")], writes=[rps[0]])
            P.op("pe", lambda e: e.matmul(ps[7][:, :n], lhsT=cx.ones_f[:, :], rhs=lr2[:, :n], start=True, stop=True),
                 reads=[(tag, "lr2"), ("c", "ones_f")], writes=[rps[7]])
            P.op("dve", lambda e: e.tensor_scalar(out=mean[:, :n], in0=ps[0][:, :n], scalar1=1.0 / D, scalar2=None, op0=ALU.mult),
                 reads=[rps[0]], writes=[(tag, "mean")])
            P.op("dve", lambda e: e.tensor_tensor(out=m2[:, :n], in0=mean[:, :n], in1=mean[:, :n], op=ALU.mult),
                 reads=[(tag, "mean")], writes=[(tag, "m2")])
            P.op("dve", lambda e: e.scalar_tensor_tensor(out=m2[:, :n], in0=ps[7][:, :n], scalar=1.0 / D, in1=m2[:, :n],
                                                         op0=ALU.mult, op1=ALU.subtract),
                 reads=[rps[7], (tag, "m2")], writes=[(tag, "m2")])
            P.op("act", lambda e: e.activation(out=std[:, :n], in_=m2[:, :n], func=AF.Sqrt, scale=1.0, bias=cx.eps_t[:, 0:1]),
                 reads=[(tag, "m2"), ("c", "eps")], writes=[(tag, "std")])
            P.op("dve", lambda e: e.reciprocal(out=rstd[:, :n], in_=std[:, :n]),
                 reads=[(tag, "std")], writes=[(tag, "rstd")])
            for kc in range(KC):
                tt = t1[kc % 2]
                rt = (tag, "t1", kc % 2)
                P.op("dve", lambda e, kc=kc, tt=tt: e.tensor_tensor(out=tt[:, :n], in0=V[:, kc, :n], in1=mean[:, :n], op=ALU.subtract),
                     reads=[rV, (tag, "mean")], writes=[rt])
                P.op("dve", lambda e, tt=tt: e.tensor_tensor(out=tt[:, :n], in0=tt[:, :n], in1=rstd[:, :n], op=ALU.mult),
                     reads=[rt, (tag, "rstd")], writes=[rt])
                P.op("act", lambda e, kc=kc, tt=tt: e.activation(
                    out=S[:, kc, :n], in_=tt[:, :n], func=AF.Silu, scale=vec[:, 16 + kc:17 + kc], bias=vec[:, 24 + kc:25 + kc]),
                    reads=[rt, rvec], writes=[rS])

        def pw2(gi):
            o0, n, h = groups[gi]
            s = gi % 2
            S = Ss[s]
            rS = (tag, "S", s)
            rx = (tag, "xt", s)
            rxo = (tag, "xo")
            for oc in range(KC):
                w_ = cnt["w2"] % 2
                cnt["w2"] += 1
                rw = (tag, "w2", w_)
                P.op("sp", lambda e, oc=oc, w_=w_: e.dma_start(
                    out=w2[w_][:, :, :].rearrange("p k o -> p (k o)"), in_=w2_bf[oc]),
                    writes=[rw], dma_key=("w2", w_))
                b = cnt["o"] % 2
                cnt["o"] += 1
                po, rpo = ps[1 + b], rps[1 + b]
                for kc in range(KC):
                    P.op("pe", lambda e, kc=kc, w_=w_, po=po: e.matmul(
                        po[:, :n], lhsT=w2[w_][:, kc, :], rhs=S[:, kc, :n],
                        start=(kc == 0), stop=(kc == KC - 1)),
                        reads=[rw, rS], writes=[rpo])
                P.op("dve", lambda e, oc=oc, po=po: e.tensor_tensor(
                    out=xo[:, oc, :n], in0=po[:, :n], in1=xt[s][:, oc, h:h + n], op=ALU.add),
                    reads=[rpo, rx], writes=[rxo])
            store_group(cx, tag, xo, rxo, Xo, o0, n, gi == 0)

        ng = len(groups)
        pre(0)
        glu(0)
        conv(0)
        for gi in range(ng):
            ln_a(gi)
            if gi + 1 < ng:
                pre(gi + 1)
                glu(gi + 1)
            ln_b(gi)
            if gi + 1 < ng:
                conv(gi + 1)
            pw2(gi)


HP = 15
POOL_W = (2, 4, 8, 16)


def even_pre_phase(cx, cfg, X_in, wqk_bf, wf_bf, wv_bf, wp_bf, vec_dram, bf_dram, tag):
    P = cx.P
    dr = cx.dram
    groups = make_groups(cfg.T, 512, 497, HP)
    Xi = X_in.rearrange("(kc p) t -> p kc t", p=128)
    with Phase(cx):
        vec, rvec = load_small(cx, tag, "vec", [128, 16], vec_dram)
        invc, rinvc = load_small(cx, tag, "invc", [128, 4 * 32], dr["invc"])
        bft = cx.sb([8, 1], F32, "bft")
        nbf = cx.sb([8, 1], F32, "nbf")
        gq8 = cx.sb([128, 1], F32, "gq8")
        P.op("sp", lambda e: e.dma_start(out=bft[:, :], in_=bf_dram), writes=[(tag, "bft")], dma_key=("G", "small"))
        P.op("dve", lambda e: e.tensor_scalar(out=nbf[:, :], in0=bft[:, :], scalar1=-1.0, scalar2=None, op0=ALU.mult),
             reads=[(tag, "bft")], writes=[(tag, "nbf")])
        P.op("dve", lambda e: e.tensor_scalar(out=gq8[:, :], in0=vec[:, 8:9], scalar1=0.125, scalar2=None, op0=ALU.mult),
             reads=[rvec], writes=[(tag, "gq8")])
        g_t = vec[:, 0:8]
        wv = cx.sb([128, KC, 512], BF16, "wv")
        wf = cx.sb([128, KC, 8], BF16, "wf")
        wp = cx.sb([128, 4, 128], BF16, "wp")
        P.op("sp", lambda e: e.dma_start(out=wv[:, :, :].rearrange("p k o -> p (k o)"), in_=wv_bf), writes=[(tag, "wv")], dma_key=("G", "small"))
        P.op("sp", lambda e: e.dma_start(out=wf[:, :, :].rearrange("p k o -> p (k o)"), in_=wf_bf), writes=[(tag, "wf")], dma_key=("G", "small"))
        P.op("sp", lambda e: e.dma_start(out=wp[:, :, :].rearrange("p k o -> p (k o)"), in_=wp_bf), writes=[(tag, "wp")], dma_key=("G", "small"))
        xt = [cx.sb([128, KC, 512], F32, "xt") for _ in range(2)]
        sqk = [cx.sb([128, 512], F32, "sqk") for _ in range(2)]
        red = cx.sb([128, 512], F32, "red")
        std = cx.sb([128, 512], F32, "std")
        rstd = cx.sb([128, 512], F32, "rstd")
        Hs = [cx.sb([128, KC, 512], BF16, "H") for _ in range(2)]
        wq = [cx.sb([128, KC, 128], BF16, "wq") for _ in range(3)]
        qsq = [cx.sb([128, 512], F32, "qsq") for _ in range(2)]
        qstd = [cx.sb([128, 512], F32, "qstd") for _ in range(2)]
        qo = [cx.sb([128, 512], BF16, "qo") for _ in range(2)]
        vo = [cx.sb([128, 512], BF16, "vo") for _ in range(2)]
        fe = cx.sb([8, 512], F32, "fe")
        fo = [cx.sb([8, 512], F32, "fo") for _ in range(2)]
        ub = [cx.sb([128, 512], F32, "ub") for _ in range(2)]
        sa = cx.sb([128, 512], F32, "sa")
        sb_ = cx.sb([128, 512], F32, "sb")
        mx = [cx.sb([128, 512], BF16, "mx") for _ in range(2)]
        po = [cx.sb([128, 512], BF16, "po") for _ in range(2)]
        ps = [cx.ps([128, 512], F32, "ps") for _ in range(8)]
        rps = [(tag, "ps", i) for i in range(8)]
        cnt = {"w": 0, "pj": 0, "q": 0, "v": 0, "u": 0, "f": 0}

        def pre(gi):
            o0, n, h = groups[gi]
            nu = n + h
            s = gi % 2
            rx = (tag, "xt", s)
            x_ = xt[s]
            P.op("sp", lambda e: e.dma_start(out=x_[:, :, :nu], in_=Xi[:, :, o0 - h:o0 + n]),
                 reads=[(tag, "Xin")], writes=[rx], dma_key=("xt", s))
            P.op("act", lambda e: e.activation(out=red[:, :nu], in_=x_[:, 0, :nu], func=AF.Square),
                 reads=[rx], writes=[(tag, "red")])
            for kc in range(1, KC):
                q = sqk[kc % 2]
                rq = (tag, "sqk", kc % 2)
                P.op("act", lambda e, kc=kc, q=q: e.activation(out=q[:, :nu], in_=x_[:, kc, :nu], func=AF.Square),
                     reads=[rx], writes=[rq])
                P.op("pool", lambda e, q=q: e.tensor_tensor(out=red[:, :nu], in0=red[:, :nu], in1=q[:, :nu], op=ALU.add),
                     reads=[rq, (tag, "red")], writes=[(tag, "red")])
            P.op("pe", lambda e: e.matmul(ps[0][:, :nu], lhsT=cx.ones_f[:, :], rhs=red[:, :nu], start=True, stop=True),
                 reads=[(tag, "red"), ("c", "ones_f")], writes=[rps[0]])
            P.op("act", lambda e: e.activation(out=std[:, :nu], in_=ps[0][:, :nu], func=AF.Sqrt, scale=1.0 / D, bias=cx.eps_t[:, 0:1]),
                 reads=[rps[0], ("c", "eps")], writes=[(tag, "std")])
            P.op("dve", lambda e: e.reciprocal(out=rstd[:, :nu], in_=std[:, :nu]),
                 reads=[(tag, "std")], writes=[(tag, "rstd")])
            for kc in range(KC):
                P.op("dve", lambda e, kc=kc: e.scalar_tensor_tensor(
                    out=Hs[s][:, kc, :nu], in0=x_[:, kc, :nu], scalar=g_t[:, kc:kc + 1], in1=rstd[:, :nu],
                    op0=ALU.mult, op1=ALU.mult),
                    reads=[rx, (tag, "rstd"), rvec], writes=[(tag, "H", s)])

        def proj(gi):
            o0, n, h = groups[gi]
            nu = n + h
            s = gi % 2
            H = Hs[s]
            rH = (tag, "H", s)
            t0 = o0 - h
            lo = max(o0, HALO)
            hi = o0 + n
            own = lo < hi
            c_lo, c_hi = lo - t0, hi - t0
            tk = lo - HALO
            for c in range(12):
                if c < 8 and not own:
                    continue
                ws = cnt["w"] % 3
                cnt["w"] += 1
                rw = (tag, "wq", ws)
                P.op("sp", lambda e, c=c, ws=ws: e.dma_start(
                    out=wq[ws][:, :, :].rearrange("p k o -> p (k o)"), in_=wqk_bf[c]),
                    writes=[rw], dma_key=("wq", ws))
                b = cnt["pj"] % 2
                cnt["pj"] += 1
                pj, rpj = ps[1 + b], rps[1 + b]
                for kc in range(KC):
                    P.op("pe", lambda e, kc=kc, ws=ws, pj=pj: e.matmul(
                        pj[:, :nu], lhsT=wq[ws][:, kc, :], rhs=H[:, kc, :nu],
                        start=(kc == 0), stop=(kc == KC - 1)),
                        reads=[rw, rH], writes=[rpj])
                if c < 8:
                    a = cnt["q"] % 2
                    cnt["q"] += 1
                    rqs, rqd, rqo = (tag, "qsq", a), (tag, "qstd", a), (tag, "qo", a)
                    P.op("act", lambda e, a=a, pj=pj: e.activation(out=qsq[a][:, :nu], in_=pj[:, :nu], func=AF.Square),
                         reads=[rpj], writes=[rqs])
                    P.op("pe", lambda e, a=a: e.matmul(ps[3][:, :nu], lhsT=cx.ones2_f[:, :], rhs=qsq[a][:, :nu], start=True, stop=True),
                         reads=[rqs, ("c", "ones2_f")], writes=[rps[3]])
                    P.op("act", lambda e, a=a: e.activation(out=qstd[a][:, :nu], in_=ps[3][:, :nu], func=AF.Sqrt, scale=1.0 / 64, bias=cx.eps_t[:, 0:1]),
                         reads=[rps[3], ("c", "eps")], writes=[rqd])
                    P.op("dve", lambda e, a=a: e.reciprocal(out=qstd[a][:, :nu], in_=qstd[a][:, :nu]),
                         reads=[rqd], writes=[rqd])
                    gsc = gq8[:, 0:1] if c < 4 else vec[:, 9:10]
                    P.op("dve", lambda e, a=a, pj=pj, gsc=gsc: e.scalar_tensor_tensor(
                        out=qo[a][:, :nu], in0=pj[:, :nu], scalar=gsc, in1=qstd[a][:, :nu], op0=ALU.mult, op1=ALU.mult),
                        reads=[rpj, rqd, rvec, (tag, "gq8")], writes=[rqo])
                    dst = dr["Qc"] if c < 4 else dr["Kc"]
                    cc = c % 4
                    P.op("pool", lambda e, a=a, dst=dst, cc=cc: e.dma_start(
                        out=dst[cc * 128:(cc + 1) * 128, tk:tk + (hi - lo)], in_=qo[a][:, c_lo:c_hi]),
                        reads=[rqo], writes=[(tag, "QKc")], dma_key=("qo", a))
                else:
                    g = c - 8
                    a = cnt["u"] % 2
                    cnt["u"] += 1
                    u_ = ub[a]
                    ru = (tag, "ub", a)
                    P.op("act", lambda e, u_=u_, pj=pj: e.activation(out=u_[:, :nu], in_=pj[:, :nu], func=AF.Identity),
                         reads=[rpj], writes=[ru])
                    src, rsrc = u_, ru
                    bufs = [(sa, (tag, "sa")), (sb_, (tag, "sb"))]
                    for st in range(g + 1):
                        sh = 1 << st
                        dstt, rdst = bufs[st % 2]
                        P.op("dve", lambda e, src=src, dstt=dstt, sh=sh: e.tensor_tensor(
                            out=dstt[:, sh:nu], in0=src[:, sh:nu], in1=src[:, 0:nu - sh], op=ALU.add),
                            reads=[rsrc], writes=[rdst])
                        P.op("pool", lambda e, src=src, dstt=dstt, sh=sh: e.tensor_copy(out=dstt[:, 0:sh], in_=src[:, 0:sh]),
                             reads=[rsrc], writes=[rdst])
                        src, rsrc = dstt, rdst
                    w_ = POOL_W[g]
                    m_ = mx[a]
                    rm = (tag, "mx", a)
                    P.op("dve", lambda e, src=src, u_=u_, m_=m_, w_=w_: e.scalar_tensor_tensor(
                        out=m_[:, :n], in0=src[:, h:nu], scalar=1.0 / w_, in1=u_[:, h:nu], op0=ALU.mult, op1=ALU.subtract),
                        reads=[rsrc, ru], writes=[rm])
                    if gi == 0:
                        tt = sqk[0]
                        rt = (tag, "sqk", 0)
                        P.op("dve", lambda e, src=src, g=g, tt=tt: e.tensor_tensor(
                            out=tt[:, 0:32], in0=src[:, HALO:HALO + 32], in1=invc[:, g * 32:(g + 1) * 32], op=ALU.mult),
                            reads=[rsrc, rinvc], writes=[rt])
                        P.op("dve", lambda e, u_=u_, m_=m_, tt=tt: e.tensor_tensor(
                            out=m_[:, HALO:HALO + 32], in0=tt[:, 0:32], in1=u_[:, HALO:HALO + 32], op=ALU.subtract),
                            reads=[rt, ru, rm], writes=[rm])
                    P.op("pe", lambda e, g=g, m_=m_: e.matmul(ps[7][:, :n], lhsT=wp[:, g, :], rhs=m_[:, :n], start=True, stop=True),
                         reads=[rm, (tag, "wp")], writes=[rps[7]])
                    p_ = po[a]
                    rp_ = (tag, "po", a)
                    P.op("act", lambda e, g=g, p_=p_: e.activation(out=p_[:, :n], in_=ps[7][:, :n], func=AF.Identity,
                                                                   scale=vec[:, 10 + g:11 + g]),
                         reads=[rps[7], rvec], writes=[rp_])
                    P.op("pool", lambda e, g=g, p_=p_: e.dma_start(out=dr["Pout"][g * 128:(g + 1) * 128, o0:o0 + n], in_=p_[:, :n]),
                         reads=[rp_], writes=[(tag, "Pout")], dma_key=("po", a))
            if not own:
                return
            for kc in range(KC):
                P.op("pe", lambda e, kc=kc: e.matmul(ps[6][0:8, :nu], lhsT=wf[:, kc, :], rhs=H[:, kc, :nu],
                                                     start=(kc == 0), stop=(kc == KC - 1)),
                     reads=[(tag, "wf"), rH], writes=[rps[6]])
            a = cnt["f"] % 2
            cnt["f"] += 1
            P.op("act", lambda e: e.activation(out=fe[:, :nu], in_=ps[6][0:8, :nu], func=AF.Exp, scale=-1.0, bias=nbf[:, 0:1]),
                 reads=[rps[6], (tag, "nbf")], writes=[(tag, "fe")])
            P.op("act", lambda e, a=a: e.activation(out=fo[a][:, :nu], in_=fe[:, :nu], func=AF.Ln, scale=1.0, bias=cx.one_t[0:8, 0:1]),
                 reads=[(tag, "fe"), ("c", "one")], writes=[(tag, "fo", a)])
            P.op("pool", lambda e, a=a: e.dma_start(out=dr["Fc"][:, tk:tk + (hi - lo)], in_=fo[a][:, c_lo:c_hi]),
                 reads=[(tag, "fo", a)], writes=[(tag, "Fc")], dma_key=("fo", a))
            c0 = c_lo
            while c0 < c_hi:
                m = min(128, c_hi - c0)
                b = cnt["v"] % 2
                cnt["v"] += 1
                pv, rpv = ps[4 + b], rps[4 + b]
                for kc in range(KC):
                    P.op("pe", lambda e, kc=kc, c0=c0, m=m, pv=pv: e.matmul(
                        pv[:m, :], lhsT=H[:, kc, c0:c0 + m], rhs=wv[:, kc, :], start=(kc == 0), stop=(kc == KC - 1)),
                        reads=[rH, (tag, "wv")], writes=[rpv])
                v_ = vo[b]
                rv = (tag, "vo", b)
                P.op("act", lambda e, m=m, pv=pv, v_=v_: e.activation(out=v_[:m, :], in_=pv[:m, :], func=AF.Copy),
                     reads=[rpv], writes=[rv])
                tok = tk + (c0 - c_lo)
                P.op("pool", lambda e, m=m, v_=v_, tok=tok: e.dma_start(out=dr["Vc"][tok:tok + m, :], in_=v_[:m, :]),
                     reads=[rv], writes=[(tag, "Vc")], dma_key=("vo", b))
                c0 += m

        ng = len(groups)
        pre(0)
        for gi in range(ng):
            if gi + 1 < ng:
                pre(gi + 1)
            proj(gi)


GROUPS4 = [[0, 1, 2, 3], [4, 5, 6, 7]]
BG_CAST = True
import os
ADBG = int(os.environ.get("ATT_DBG", "5"))


def attn_phase(cx, cfg, tag, bg_casts=None):
    P = cx.P
    dr = cx.dram
    CH_, SEQ_, NKB, NQB = cfg.CH, cfg.SEQ, cfg.NKB, cfg.NQB
    segl = SEQ_ // 64
    nbs = CH_ // 128
    q4 = CH_ // 4
    for cc in range(4):
        for nm in ("Q", "K"):
            P.op("pool", lambda e, nm=nm, cc=cc: e.collective_compute(
                "AllGather", ALU.bypass, replica_groups=GROUPS4,
                ins=[dr[nm + "c"][cc * 128:(cc + 1) * 128, :].opt()], outs=[dr[nm + "g"][cc * 512:(cc + 1) * 512, :].opt()]),
                reads=[(tag, nm + "c")], writes=[(tag, nm + "g")], dma_key=("G", "cc" + nm), inc=1)
        P.op("pool", lambda e, cc=cc: e.collective_compute(
            "AllGather", ALU.bypass, replica_groups=GROUPS4,
            ins=[dr["Vc"][cc * q4:(cc + 1) * q4, :].opt()], outs=[dr["Vg"][cc * CH_:(cc + 1) * CH_, :].opt()]),
            reads=[(tag, "Vc")], writes=[(tag, "Vg")], dma_key=("G", "ccV"), inc=1)
    P.op("pool", lambda e: e.collective_compute(
        "AllGather", ALU.bypass, replica_groups=GROUPS4, ins=[dr["Fc"].opt()], outs=[dr["Fg"].opt()]),
        reads=[(tag, "Fc")], writes=[(tag, "Fg")], dma_key=("G", "ccF"), inc=1)
    cx.chk("attn_ag")
    with Phase(cx):
        KT = [cx.sb([128, SEQ_], BF16, "KT") for _ in range(2)]
        VT = [cx.sb([128, NKB, 128], BF16, "VT") for _ in range(2)]
        Lm = cx.sb([128, 128], F32, "Lm")
        sp = cx.sb([128, segl], F32, "sp")
        onesl = cx.sb([128, segl], F32, "onesl")
        cl = cx.sb([128, segl], F32, "cl")
        offs = cx.sb([128, 1], F32, "offs")
        r1 = cx.sb([128, segl], F32, "r1")
        hi = cx.sb([128, segl], BF16, "hi")
        mid = cx.sb([128, segl], BF16, "mid")
        lo = cx.sb([128, segl], BF16, "lo")
        nhi = cx.sb([128, segl], BF16, "nhi")
        nmid = cx.sb([128, segl], BF16, "nmid")
        nlo = cx.sb([128, segl], BF16, "nlo")
        onesb = cx.sb([128, segl], BF16, "onesb")
        zt = cx.sb([128, 4, HALO], BF16, "zt")
        NQT = 3
        qt = [cx.sb([128, 512], BF16, "qt") for _ in range(NQT)]
        NPT = 4
        pt = [cx.sb([128, 512], BF16, "pt") for _ in range(NPT)]
        rec = [cx.sb([64, 512], F32, "rec") for _ in range(2)]
        ost = [cx.sb([64, 512], BF16, "ost") for _ in range(2)]
        NS = 4
        ps_s = [cx.ps([128, 512], F32, "pss") for _ in range(NS)]
        ps_o = [cx.ps([128, 512], F32, "pso") for _ in range(2)]
        ps_m = cx.ps([128, 512], F32, "psm")

        P.op("dve", lambda e: e.memset(Lm[:, :], 1.0), writes=[(tag, "Lm")])
        P.op("pool", lambda e: e.affine_select(out=Lm[:, :], in_=Lm[:, :], pattern=[[1, 128]], compare_op=ALU.is_gt,
                                               fill=0.0, base=0, channel_multiplier=-1),
             reads=[(tag, "Lm")], writes=[(tag, "Lm")])
        P.op("dve", lambda e: e.memset(Lm[0:64, 64:128], 0.0), reads=[(tag, "Lm")], writes=[(tag, "Lm")])
        P.op("dve", lambda e: e.memset(onesl[:, :], 1.0), writes=[(tag, "onesl")])
        P.op("dve", lambda e: e.memset(onesb[:, :], 1.0), writes=[(tag, "onesb")])
        P.op("dve", lambda e: e.memset(zt[:, :, :], 0.0), writes=[(tag, "zt")])
        P.op("sp", lambda e: e.dma_start(out=dr["Og"][0:512, CH_ - HALO:CH_].rearrange("(a p) t -> p a t", p=128), in_=zt[:, :, :]),
             reads=[(tag, "zt")], writes=[(tag, "Ogpad")], dma_key=("G", "small"))
        for h in range(2):
            P.op("dve", lambda e, h=h: e.memset(VT[h][:, :, 64:128], 1.0), writes=[(tag, "VTo", h)])

        def loc_k(e, nm):
            i4 = pid4(e)
            src = dr[nm + "g"].rearrange("(c s r) t -> c s r t", c=4, s=4)[i4, :, :, :]
            return e.dma_start(out=dr[nm + "l"].rearrange("r (s t) -> s r t", s=4), in_=src)

        def loc_v(e, j):
            i4 = pid4(e)
            src = dr["Vg"][j * CH_:(j + 1) * CH_, :].rearrange("(s n) (i c) -> s n i c", s=4, i=4)[:, :, i4, :]
            return e.dma_start(out=dr["Vl"].rearrange("(s j n) c -> j s n c", s=4, j=4)[j], in_=src)

        def loc_f(e):
            i4 = pid4(e)
            src = dr["Fg"].rearrange("(s i h) t -> s i h t", s=4, i=4)[:, i4, :, :]
            return e.dma_start(out=dr["Fl"].rearrange("(s h) t -> s h t", s=4), in_=src)

        P.op("sp", loc_f, reads=[(tag, "Fg")], writes=[(tag, "Fl")], dma_key=("G", "locf"))
        P.op("sp", lambda e: loc_k(e, "K"), reads=[(tag, "Kg")], writes=[(tag, "Kl")], dma_key=("G", "lock"))
        P.op("sp", lambda e: loc_k(e, "Q"), reads=[(tag, "Qg")], writes=[(tag, "Ql")], dma_key=("G", "locq"))
        for j in range(4):
            P.op("sp", lambda e, j=j: loc_v(e, j), reads=[(tag, "Vg")], writes=[(tag, "Vl")], dma_key=("G", "locv"))

        cx.chk("attn_loc")

        def ld_k(e, h, s):
            return e.dma_start(out=KT[h][0:64, s * CH_:(s + 1) * CH_], in_=dr["Kl"][h * 64:(h + 1) * 64, s * CH_:(s + 1) * CH_])

        def ld_v(e, h, s):
            src = dr["Vl"][s * CH_:(s + 1) * CH_, h * 64:(h + 1) * 64].rearrange("(b p) c -> p b c", p=128)
            return e.dma_start(out=VT[h][:, s * nbs:(s + 1) * nbs, 0:64], in_=src)

        def ld_f(e, h, s):
            src = dr["Fl"][s * 2 + h:s * 2 + h + 1, :].rearrange("o (j t) -> (o j) t", t=segl)
            return e.dma_start(out=sp[h * 64 + s * 16:h * 64 + (s + 1) * 16, :], in_=src)

        for h in range(2):
            for s in range(4):
                P.op("sp", lambda e, h=h, s=s: ld_f(e, h, s), reads=[(tag, "Fl")], writes=[(tag, "sp", h, s)],
                     dma_key=("G", "ldf"))
        for h in range(2):
            for s in range(4):
                P.op("sp", lambda e, h=h, s=s: ld_k(e, h, s), reads=[(tag, "Kl")], writes=[(tag, "KT", h, s)],
                     dma_key=("G", "ldk"))
        rsp = [(tag, "sp", h, s) for h in range(2) for s in range(4)]
        P.op("dve", lambda e: e.tensor_tensor_scan(out=cl[:, :], data0=onesl[:, :], data1=sp[:, :], initial=0.0,
                                                   op0=ALU.mult, op1=ALU.add),
             reads=rsp + [(tag, "onesl")], writes=[(tag, "cl")])
        P.op("pe", lambda e: e.matmul(ps_m[:, 0:2], lhsT=Lm[:, :], rhs=cl[:, segl - 2:segl], start=True, stop=True),
             reads=[(tag, "cl"), (tag, "Lm")], writes=[(tag, "psm")])
        P.op("dve", lambda e: e.tensor_copy(out=offs[:, :], in_=ps_m[:, 1:2]), reads=[(tag, "psm")], writes=[(tag, "offs")])
        P.op("dve", lambda e: e.tensor_scalar(out=cl[:, :], in0=cl[:, :], scalar1=offs[:, 0:1], scalar2=None, op0=ALU.add),
             reads=[(tag, "cl"), (tag, "offs")], writes=[(tag, "cl")])
        P.op("dve", lambda e: e.tensor_copy(out=hi[:, :], in_=cl[:, :]), reads=[(tag, "cl")], writes=[(tag, "hi")])
        P.op("dve", lambda e: e.tensor_tensor(out=r1[:, :], in0=cl[:, :], in1=hi[:, :], op=ALU.subtract),
             reads=[(tag, "cl"), (tag, "hi")], writes=[(tag, "r1")])
        P.op("dve", lambda e: e.tensor_copy(out=mid[:, :], in_=r1[:, :]), reads=[(tag, "r1")], writes=[(tag, "mid")])
        P.op("dve", lambda e: e.tensor_tensor(out=r1[:, :], in0=r1[:, :], in1=mid[:, :], op=ALU.subtract),
             reads=[(tag, "r1"), (tag, "mid")], writes=[(tag, "r1")])
        P.op("dve", lambda e: e.tensor_copy(out=lo[:, :], in_=r1[:, :]), reads=[(tag, "r1")], writes=[(tag, "lo")])
        for src_, dst_, nm in ((hi, nhi, "nhi"), (mid, nmid, "nmid"), (lo, nlo, "nlo")):
            P.op("dve", lambda e, src_=src_, dst_=dst_: e.tensor_scalar(out=dst_[:, :], in0=src_[:, :], scalar1=-1.0, scalar2=None, op0=ALU.mult),
                 reads=[(tag, "hi"), (tag, "mid"), (tag, "lo")], writes=[(tag, nm)])
        cx.chk("attn_cs")
        k = 0
        for h in range(2):
            for r, (tk_, tq_) in enumerate(((hi, onesb), (mid, onesb), (lo, onesb), (onesb, nhi), (onesb, nmid), (onesb, nlo))):
                for dst_nm, t_ in (("AugK", tk_), ("AugQ", tq_)):
                    P.op("sp", lambda e, h=h, r=r, dst_nm=dst_nm, t_=t_: e.dma_start(
                        out=dr[dst_nm][h * 6 + r:h * 6 + r + 1, :].rearrange("o (j t) -> (o j) t", t=segl), in_=t_[h * 64:(h + 1) * 64, :]),
                        reads=[(tag, "hi"), (tag, "mid"), (tag, "lo"), (tag, "nhi"), (tag, "nmid"), (tag, "nlo"), (tag, "onesb")],
                        writes=[(tag, dst_nm, h)], dma_key=("G", "aug"))
                    k += 1
        cx.chk("attn_aug")
        for h in range(2):
            P.op("sp", lambda e, h=h: e.dma_start(out=KT[h][64:70, :], in_=dr["AugK"][h * 6:(h + 1) * 6, :]),
                 reads=[(tag, "AugK", h)], writes=[(tag, "KTa", h)], dma_key=("G", "ldka"))
        cx.chk("attn_ka")
        for h in range(2):
            for s in range(4):
                P.op("sp", lambda e, h=h, s=s: ld_v(e, h, s), reads=[(tag, "Vl")], writes=[(tag, "VT", h, s)],
                     dma_key=("G", "ldv"))

        cx.chk("attn_ld")
        bg = None
        if bg_casts:
            bcf = [cx.sb([128, 1024], F32, "bcf") for _ in range(3)]
            bcb = [cx.sb([128, 1024], BF16, "bcb") for _ in range(3)]
            bg = cast_iter(cx, bg_casts, bcf, bcb, 1024)
        steps = []
        DEPTH = 3
        state = {"qi": -1, "oi": -1}
        qinfo = {}

        def load_q(h, qb):
            state["qi"] += 1
            qs = state["qi"] % NQT
            rq = (tag, "qt", qs)
            Q0 = qb * 512
            s = Q0 // CH_
            c0 = Q0 % CH_

            P.op("sp", lambda e: e.dma_start(out=qt[qs][0:64, :], in_=dr["Ql"][h * 64:(h + 1) * 64, Q0:Q0 + 512]),
                 reads=[(tag, "Ql")], writes=[rq], dma_key=("ldq", qs))
            P.op("sp", lambda e: e.dma_start(out=qt[qs][64:70, :], in_=dr["AugQ"][h * 6:(h + 1) * 6, Q0:Q0 + 512]),
                 reads=[(tag, "AugQ", h)], writes=[(tag, "qta", qs)], dma_key=("ldqa", qs))
            qinfo[(h, qb)] = qs

        def qk(i):
            h, qb, kb, nk = steps[i]
            if kb == 0:
                load_q(h, qb)
            qs = qinfo[(h, qb)]
            sb_i = i % NS
            j = kb - 4 * qb
            a = 128 * j if j > 0 else 0
            diag = j >= 0
            s_src = (kb * 128) // CH_
            rk = [(tag, "KT", h, s_src), (tag, "KTa", h)]
            P.op("pe", lambda e: e.matmul(ps_s[sb_i][:, a:512], lhsT=KT[h][0:70, kb * 128:(kb + 1) * 128],
                                          rhs=qt[qs][0:70, a:512], start=True, stop=not diag),
                 reads=rk + [(tag, "qt", qs), (tag, "qta", qs)], writes=[(tag, "pss", sb_i)])
            if diag and ADBG >= 2:
                P.op("pe", lambda e: e.matmul(ps_s[sb_i][:, a:a + 128], lhsT=cx.ident_b[:, :], rhs=cx.maskb[:, :],
                                              start=False, stop=True),
                     reads=[("c", "ident"), ("c", "maskb")], writes=[(tag, "pss", sb_i)])
            pi = i % NPT
            if ADBG < 3:
                return
            P.op("act", lambda e: e.activation(out=pt[pi][:, a:512], in_=ps_s[sb_i][:, a:512], func=AF.Exp),
                 reads=[(tag, "pss", sb_i)], writes=[(tag, "pt", pi)])

        def pv(i):
            if ADBG < 4:
                return
            h, qb, kb, nk = steps[i]
            j = kb - 4 * qb
            a = 128 * j if j > 0 else 0
            pi = i % NPT
            s_src = (kb * 128) // CH_
            if kb == 0:
                state["oi"] += 1
            ob = state["oi"] % 2
            P.op("pe", lambda e: e.matmul(ps_o[ob][:, a:512], lhsT=VT[h][:, kb, :], rhs=pt[pi][:, a:512],
                                          start=(kb == 0), stop=(kb == nk - 1)),
                 reads=[(tag, "VT", h, s_src), (tag, "VTo", h), (tag, "pt", pi)], writes=[(tag, "pso", ob)])
            if kb == nk - 1 and ADBG >= 5:
                Q0 = qb * 512
                P.op("dve", lambda e: e.reciprocal(out=rec[ob][:, :], in_=ps_o[ob][64:128, :]),
                     reads=[(tag, "pso", ob)], writes=[(tag, "rec", ob)])
                P.op("dve", lambda e: e.tensor_tensor(out=ost[ob][:, :], in0=ps_o[ob][0:64, :], in1=rec[ob][:, :], op=ALU.mult),
                     reads=[(tag, "pso", ob), (tag, "rec", ob)], writes=[(tag, "ost", ob)])
                jc, c0_ = Q0 // CH_, Q0 % CH_
                P.op("pool", lambda e: e.dma_start(out=dr["Oc"][jc * 128 + h * 64:jc * 128 + (h + 1) * 64, c0_:c0_ + 512], in_=ost[ob][:, :]),
                     reads=[(tag, "ost", ob)], writes=[(tag, "Oc")], dma_key=("ost", ob))

        for hh in range(2):
            base = len(steps)
            for qb in range(NQB):
                nk = 4 * qb + 4
                for kb in range(nk):
                    steps.append((hh, qb, kb, nk))
            n = len(steps)
            for i in range(base, n + DEPTH):
                if i < n:
                    qk(i)
                if i - DEPTH >= base:
                    pv(i - DEPTH)
                if bg is not None and i % 8 == 0:
                    if next(bg, None) is None:
                        bg = None
            if hh == 0:
                P.barrier()
        while bg is not None:
            if next(bg, None) is None:
                bg = None
    for jc in range(4):
        P.op("pool", lambda e, jc=jc: e.collective_compute(
            "AllGather", ALU.bypass, replica_groups=GROUPS4,
            ins=[dr["Oc"][jc * 128:(jc + 1) * 128, :].opt()], outs=[dr["Og"][(jc + 1) * 512:(jc + 2) * 512, :].opt()]),
            reads=[(tag, "Oc")], writes=[(tag, "Og")], dma_key=("G", "ccO"), inc=1)


def even_post_phase(cx, cfg, X_in, X_out, wo_bf, tag):
    P = cx.P
    dr = cx.dram
    groups = make_groups(cfg.T, 512, 512, 0)
    Xi = X_in.rearrange("(kc p) t -> p kc t", p=128)
    Xo = X_out.rearrange("(kc p) t -> p kc t", p=128)
    Alv = dr["Al"].rearrange("(kc p) t -> p kc t", p=128)
    Pov = dr["Pout"].rearrange("(kc p) t -> p kc t", p=128)

    Og3 = dr["Og"].rearrange("(j r) t -> j r t", r=512)

    def loc_a(e):
        i4 = pid4(e)
        return e.dma_start(out=dr["Al"][:, HALO:], in_=Og3[i4 + 1, :, :])

    def loc_h(e):
        i4 = pid4(e)
        return e.dma_start(out=dr["Al"][:, 0:HALO], in_=Og3[i4, :, cfg.CH - HALO:cfg.CH])
    P.op("sp", loc_a, reads=[(tag, "Og")], writes=[(tag, "Al")], dma_key=("G", "loca"))
    P.op("sp", loc_h, reads=[(tag, "Og")], writes=[(tag, "Alh")], dma_key=("G", "loca"))
    with Phase(cx):
        xt = [cx.sb([128, KC, 512], F32, "xt") for _ in range(2)]
        At = [cx.sb([128, KC, 512], BF16, "At") for _ in range(2)]
        wo = [cx.sb([128, KC, 128], BF16, "wo") for _ in range(3)]
        xo = [cx.sb([128, KC, 512], F32, "xo") for _ in range(2)]
        ps = [cx.ps([128, 512], F32, "ps") for _ in range(2)]
        cnt = {"w": 0, "o": 0}
        for gi, (o0, n, h) in enumerate(groups):
            s = gi % 2
            rx, rA, rxo = (tag, "xt", s), (tag, "At", s), (tag, "xo", s)
            P.op("sp", lambda e, s=s, o0=o0, n=n: e.dma_start(out=xt[s][:, :, :n], in_=Xi[:, :, o0:o0 + n]),
                 reads=[(tag, "Xin")], writes=[rx], dma_key=("xt", s))

            P.op("sp", lambda e, s=s, o0=o0, n=n: e.dma_start(out=At[s][:, 0:4, :n], in_=Alv[:, :, o0:o0 + n]),
                 reads=[(tag, "Al"), (tag, "Alh")], writes=[(tag, "Ata", s)], dma_key=("Ata", s))
            P.op("sp", lambda e, s=s, o0=o0, n=n: e.dma_start(out=At[s][:, 4:8, :n], in_=Pov[:, :, o0:o0 + n]),
                 reads=[(tag, "Pout")], writes=[(tag, "Atp", s)], dma_key=("Atp", s))
            for oc in range(KC):
                ws = cnt["w"] % 3
                cnt["w"] += 1
                rw = (tag, "wo", ws)
                P.op("sp", lambda e, oc=oc, ws=ws: e.dma_start(out=wo[ws][:, :, :].rearrange("p k o -> p (k o)"), in_=wo_bf[oc]),
                     writes=[rw], dma_key=("wo", ws))
                b = cnt["o"] % 2
                cnt["o"] += 1
                rp = (tag, "ps", b)
                for kc in range(KC):
                    P.op("pe", lambda e, kc=kc, ws=ws, b=b, s=s, n=n: e.matmul(
                        ps[b][:, :n], lhsT=wo[ws][:, kc, :], rhs=At[s][:, kc, :n], start=(kc == 0), stop=(kc == KC - 1)),
                        reads=[rw, (tag, "Ata", s), (tag, "Atp", s)], writes=[rp])
                P.op("dve", lambda e, oc=oc, b=b, s=s, n=n: e.tensor_tensor(
                    out=xo[s][:, oc, :n], in0=ps[b][:, :n], in1=xt[s][:, oc, :n], op=ALU.add),
                    reads=[rp, rx], writes=[rxo])
            store_group(cx, tag, xo[s], rxo, Xo, o0, n, gi == 0)


def cast_all(cx, items):
    P = cx.P
    CWD = 4096
    with Phase(cx):
        st_f = [cx.sb([128, CWD], F32, "cst_f") for _ in range(3)]
        st_b = [cx.sb([128, CWD], BF16, "cst_b") for _ in range(3)]
        i = 0
        for src, dst, rows, cols in items:
            for r0 in range(0, rows, 128):
                for c0 in range(0, cols, CWD):
                    cw_ = min(CWD, cols - c0)
                    s = i % 3
                    rf, rb = ("cf", s), ("cb", s)
                    P.op("sp", lambda e, src=src, r0=r0, c0=c0, cw_=cw_, s=s: e.dma_start(
                        out=st_f[s][:, :cw_], in_=src[r0:r0 + 128, c0:c0 + cw_]), writes=[rf], dma_key=rf)
                    if i % 2 == 0:
                        P.op("dve", lambda e, cw_=cw_, s=s: e.tensor_copy(out=st_b[s][:, :cw_], in_=st_f[s][:, :cw_]),
                             reads=[rf], writes=[rb])
                    else:
                        P.op("act", lambda e, cw_=cw_, s=s: e.activation(out=st_b[s][:, :cw_], in_=st_f[s][:, :cw_], func=AF.Copy),
                             reads=[rf], writes=[rb])
                    P.op("pool", lambda e, dst=dst, r0=r0, c0=c0, cw_=cw_, s=s: e.dma_start(
                        out=dst[r0:r0 + 128, c0:c0 + cw_], in_=st_b[s][:, :cw_]),
                        reads=[rb], writes=[("cdst",)], dma_key=("cst", s))
                    i += 1


def cast_iter(cx, items, st_f, st_b, cwd):
    P = cx.P
    i = 0
    nb = len(st_f)
    for src, dst, rows, cols in items:
        for r0 in range(0, rows, 128):
            for c0 in range(0, cols, cwd):
                cw_ = min(cwd, cols - c0)
                s = i % nb
                rf, rb = ("bcf", s), ("bcb", s)
                P.op("sp", lambda e, src=src, r0=r0, c0=c0, cw_=cw_, s=s: e.dma_start(
                    out=st_f[s][:, :cw_], in_=src[r0:r0 + 128, c0:c0 + cw_]), writes=[rf], dma_key=("cf", s))
                P.op("dve", lambda e, cw_=cw_, s=s: e.tensor_copy(out=st_b[s][:, :cw_], in_=st_f[s][:, :cw_]),
                     reads=[rf], writes=[rb])
                P.op("pool", lambda e, dst=dst, r0=r0, c0=c0, cw_=cw_, s=s: e.dma_start(
                    out=dst[r0:r0 + 128, c0:c0 + cw_], in_=st_b[s][:, :cw_]),
                    reads=[rb], writes=[("bcdst", i)], dma_key=("cst", s))
                i += 1
                yield i


def build_diag(cx, dww_dram, dg_bf, tag):
    P = cx.P
    with Phase(cx):
        dww, rdw = load_small(cx, tag, "dww", [128, KC * CW31], dww_dram)
        dst = [cx.sb([128, CW31, 128], BF16, "dgst") for _ in range(2)]
        for j in range(KC):
            s = j % 2
            rs = (tag, "dgst", s)
            for k in range(CW31):
                P.op("pool", lambda e, j=j, k=k, s=s: e.tensor_scalar(
                    out=dst[s][:, k, :], in0=cx.ident_b[:, :], scalar1=dww[:, j * CW31 + k:j * CW31 + k + 1], scalar2=0.0,
                    op0=ALU.mult, op1=ALU.add),
                    reads=[rdw, ("c", "ident")], writes=[rs])
            P.op("sp", lambda e, j=j, s=s: e.dma_start(out=dg_bf[j], in_=dst[s][:, :, :].rearrange("p k o -> p (k o)")),
                 reads=[rs], writes=[(tag, "dgbf")], dma_key=("dgst", s))


WSPEC_E = (("wqk", 12 * 128, 1024), ("wf", 128, 64), ("wv", 128, 4096), ("wp", 128, 512), ("wo", 8 * 128, 1024))
WSPEC_O = (("w1", 8 * 128, 2048), ("w2", 8 * 128, 1024))
WSPEC_F = (("wup", 22 * 128, 2048), ("wdn", 8 * 128, 2816))


def build_program(cfg, n_layers=4, do_ffn=True, stop=None):
    nc = bass.Bass("TRN2", target_bir_lowering=False)
    T_, CH_, SEQ_ = cfg.T, cfg.CH, cfg.SEQ

    def din(name, shape, dt=F32):
        return nc.dram_tensor(name, list(shape), dt, kind="ExternalInput").ap()

    def dsc(name, shape, dt):
        return nc.dram_tensor(name, list(shape), dt).ap()

    dr = {}
    dr["x"] = din("x", [D, T_])
    dr["flag"] = din("flag", [128, 1])
    dr["invc"] = din("invc", [128, 4 * 32])
    Y = nc.dram_tensor("y", [D, CH_], F32, kind="ExternalOutput").ap()
    XA = dsc("XA", [D, T_], F32)
    XB = dsc("XB", [D, T_], F32)
    W = {}
    casts = []
    n_even = (n_layers + 1) // 2
    n_odd = n_layers // 2
    for e_ in range(n_even):
        for nm, r, c in WSPEC_E:
            f = din("e%d_%s" % (e_, nm), [r, c])
            b = dsc("e%d_%s_b" % (e_, nm), [r, c], BF16)
            W[("e", e_, nm)] = b
            casts.append((f, b, r, c, "e%d" % e_))
        W[("e", e_, "vec")] = din("e%d_vec" % e_, [128, 16])
        W[("e", e_, "bf")] = din("e%d_bf" % e_, [8, 1])
    for o_ in range(n_odd):
        for nm, r, c in WSPEC_O:
            f = din("o%d_%s" % (o_, nm), [r, c])
            b = dsc("o%d_%s_b" % (o_, nm), [r, c], BF16)
            W[("o", o_, nm)] = b
            casts.append((f, b, r, c, "o"))
        W[("o", o_, "vec")] = din("o%d_vec" % o_, [128, 32])
        W[("o", o_, "dww")] = din("o%d_dww" % o_, [128, KC * CW31])
        W[("o", o_, "dg")] = dsc("o%d_dg" % o_, [KC * 128, CW31 * 128], BF16)
    if do_ffn:
        for l in range(n_layers):
            for nm, r, c in WSPEC_F:
                f = din("f%d_%s" % (l, nm), [r, c])
                b = dsc("f%d_%s_b" % (l, nm), [r, c], BF16)
                W[("f", l, nm)] = b
                casts.append((f, b, r, c, "f"))
            W[("f", l, "g")] = din("f%d_g" % l, [128, 8])
            W[("f", l, "cw")] = din("f%d_cw" % l, [128, 132])
            W[("f", l, "cb")] = din("f%d_cb" % l, [128, 44])
    dr["Qc"] = dsc("Qc", [512, CH_], BF16)
    dr["Kc"] = dsc("Kc", [512, CH_], BF16)
    dr["Vc"] = dsc("Vc", [CH_, 512], BF16)
    dr["Fc"] = dsc("Fc", [8, CH_], F32)
    dr["Qg"] = dsc("Qg", [4 * 512, CH_], BF16)
    dr["Kg"] = dsc("Kg", [4 * 512, CH_], BF16)
    dr["Vg"] = dsc("Vg", [4 * CH_, 512], BF16)
    dr["Fg"] = dsc("Fg", [4 * 8, CH_], F32)
    dr["Ql"] = dsc("Ql", [128, SEQ_], BF16)
    dr["Kl"] = dsc("Kl", [128, SEQ_], BF16)
    dr["Vl"] = dsc("Vl", [SEQ_, 128], BF16)
    dr["Fl"] = dsc("Fl", [8, CH_], F32)
    dr["Al"] = dsc("Al", [512, T_], BF16)
    dr["AugK"] = dsc("AugK", [12, SEQ_], BF16)
    dr["AugQ"] = dsc("AugQ", [12, SEQ_], BF16)
    dr["Oc"] = dsc("Oc", [4 * 128, CH_], BF16)
    dr["Og"] = dsc("Og", [5 * 512, CH_], BF16)
    dr["Pout"] = dsc("Pout", [512, T_], BF16)

    def slabs(ap, p=128):
        return ap.rearrange("(j p) c -> j p c", p=p)

    with contextlib.ExitStack() as stack:
        cx = Ctx(nc, stack)
        cx.dram = dr
        setup_consts(cx)
        cx.P.barrier()
        class _Stop(Exception):
            pass

        def chk(name):
            if stop == name:
                raise _Stop()
        cx.chk = chk
        try:
            chk("consts")
            first = [c for c in casts if c[4] == "e0"]
            rest = [c[:4] for c in casts if c[4] != "e0"]
            cast_all(cx, [c[:4] for c in first] + ([] if n_layers > 0 and BG_CAST else rest))
            cx.bg_casts = rest if BG_CAST else None
            chk("cast")
            for o_ in range(n_odd):
                build_diag(cx, W[("o", o_, "dww")], slabs(W[("o", o_, "dg")]), "dg%d" % o_)
            chk("diag")
            _build_layers(cx, cfg, dr, W, XA, XB, Y, n_layers, do_ffn, slabs, chk)
        except _Stop:
            pass
        cx.P.emit()
    return nc, None


def _build_layers(cx, cfg, dr, W, XA, XB, Y, n_layers, do_ffn, slabs, chk):
    if True:
        cur = dr["x"]
        bufs = [XA, XB]
        bi = 0
        n_sub = n_layers * (2 if do_ffn else 1)
        sub = 0

        def nxt():
            nonlocal bi
            b = bufs[bi]
            bi ^= 1
            return b
        for l in range(n_layers):
            i2 = l // 2
            if l % 2 == 0:
                tag = "E%d" % l
                even_pre_phase(cx, cfg, cur, slabs(W[("e", i2, "wqk")]), W[("e", i2, "wf")], W[("e", i2, "wv")],
                               W[("e", i2, "wp")], W[("e", i2, "vec")], W[("e", i2, "bf")], tag + "a")
                chk("E1")
                attn_phase(cx, cfg, tag + "b", bg_casts=cx.bg_casts if l == 0 else None)
                cx.P.barrier()
                chk("attn")
                sub += 1
                last = (sub == n_sub)
                pass
                dst = nxt()
                even_post_phase(cx, cfg, cur, dst, slabs(W[("e", i2, "wo")]), tag + "c")
                cur = dst
                chk("E3")
            else:
                tag = "O%d" % l
                sub += 1
                dst = nxt()
                odd_phase(cx, cfg, cur, dst, slabs(W[("o", i2, "w1")]), slabs(W[("o", i2, "w2")]),
                          slabs(W[("o", i2, "dg")]), W[("o", i2, "vec")], tag)
                cur = dst
            if do_ffn:
                sub += 1
                last = (sub == n_sub)
                if last:
                    ffn_phase(cx, cfg, cur, Y, slabs(W[("f", l, "wup")]), slabs(W[("f", l, "wdn")]),
                              W[("f", l, "g")], W[("f", l, "cw")], W[("f", l, "cb")], "F%d" % l, out_off=HALO)
                else:
                    dst = nxt()
                    ffn_phase(cx, cfg, cur, dst, slabs(W[("f", l, "wup")]), slabs(W[("f", l, "wdn")]),
                              W[("f", l, "g")], W[("f", l, "cw")], W[("f", l, "cb")], "F%d" % l)
                    cur = dst
        if not do_ffn:
            cx.P.op("sp", lambda e: e.dma_start(out=Y, in_=cur[:, HALO:]), dma_key=("G", "fin"))


def lay_slabs(w):
    n = w.shape[1] // 128
    a = w.reshape(KC, 128, n, 128).transpose(2, 1, 0, 3)
    return np.ascontiguousarray(a).reshape(n * 128, KC * 128)


def lay_pairs(w, half):
    n = half // 128
    a = w.reshape(KC, 128, 2, n, 128).transpose(3, 1, 2, 0, 4)
    return np.ascontiguousarray(a).reshape(n * 128, 2 * KC * 128)


def lay_kmajor(w):
    c = w.shape[1]
    return np.ascontiguousarray(w.reshape(KC, 128, c).transpose(1, 0, 2)).reshape(128, KC * c)


def lay_wdn(w_down):
    a = w_down.reshape(NJ, 128, KC, 128).transpose(2, 1, 0, 3)
    return np.ascontiguousarray(a).reshape(KC * 128, NJ * 128)


def lay_vec(v, nch):
    return np.ascontiguousarray(v.reshape(nch, 128).T)


def lay_cw(cw):
    k = cw.shape[0]
    n = cw.shape[1] // 128
    return np.ascontiguousarray(cw.reshape(k, n, 128).transpose(2, 1, 0)).reshape(128, n * k)


def host_inputs(cfg, inp, n_layers=4, do_ffn=True):
    f32 = np.float32
    shared = {}
    n_even = (n_layers + 1) // 2
    n_odd = n_layers // 2
    for e_ in range(n_even):
        w_in = np.asarray(inp["even_w_in"][e_], f32)
        q, k, v = w_in[:, 0:512], w_in[:, 512:1024], w_in[:, 1024:1536]
        f, u = w_in[:, 1536:1544], w_in[:, 1544:2056]
        shared["e%d_wqk" % e_] = lay_slabs(np.concatenate([q, k, u], axis=1))
        shared["e%d_wf" % e_] = lay_kmajor(f)
        shared["e%d_wv" % e_] = lay_kmajor(v)
        wp = np.asarray(inp["even_w_pool"][e_], f32)
        shared["e%d_wp" % e_] = np.ascontiguousarray(wp.transpose(1, 0, 2)).reshape(128, 512)
        shared["e%d_wo" % e_] = lay_slabs(np.asarray(inp["even_w_out"][e_], f32))
        vec = np.zeros((128, 16), f32)
        vec[:, 0:8] = lay_vec(np.asarray(inp["even_norm_g"][e_], f32), 8)
        vec[:, 8] = np.tile(np.asarray(inp["even_q_norm_g"][e_], f32), 2)
        vec[:, 9] = np.tile(np.asarray(inp["even_k_norm_g"][e_], f32), 2)
        vec[:, 10:14] = lay_vec(np.asarray(inp["even_pool_scale"][e_], f32), 4)
        shared["e%d_vec" % e_] = vec
        shared["e%d_bf" % e_] = np.asarray(inp["even_b_f"][e_], f32).reshape(8, 1).copy()
    for o_ in range(n_odd):
        shared["o%d_w1" % o_] = lay_pairs(np.asarray(inp["odd_w_pw1"][o_], f32), 1024)
        shared["o%d_w2" % o_] = lay_slabs(np.asarray(inp["odd_w_pw2"][o_], f32))
        vec = np.zeros((128, 32), f32)
        vec[:, 0:8] = lay_vec(np.asarray(inp["odd_norm_g"][o_], f32), 8)
        vec[:, 8:16] = lay_vec(np.asarray(inp["odd_dw_b"][o_], f32), 8)
        vec[:, 16:24] = lay_vec(np.asarray(inp["odd_ln_g"][o_], f32), 8)
        vec[:, 24:32] = lay_vec(np.asarray(inp["odd_ln_b"][o_], f32), 8)
        shared["o%d_vec" % o_] = vec
        shared["o%d_dww" % o_] = lay_cw(np.asarray(inp["odd_dw_w"][o_], f32))
    if do_ffn:
        for l in range(n_layers):
            shared["f%d_wup" % l] = lay_pairs(np.asarray(inp["ffn_w_up"][l], f32), DFF)
            shared["f%d_wdn" % l] = lay_wdn(np.asarray(inp["ffn_w_down"][l], f32))
            shared["f%d_g" % l] = lay_vec(np.asarray(inp["ffn_norm_g"][l], f32), 8)
            shared["f%d_cw" % l] = lay_cw(np.asarray(inp["ffn_conv_w"][l], f32))
            shared["f%d_cb" % l] = lay_vec(np.asarray(inp["ffn_conv_b"][l], f32), 44)
    x = np.asarray(inp["x"], f32)
    maps = []
    for c in range(NCORES):
        b, i = c // 4, c % 4
        xt = np.zeros((D, cfg.T), f32)
        lo = i * cfg.CH - HALO
        if i == 0:
            xt[:, HALO:] = x[b, 0:cfg.CH].T
        else:
            xt[:, :] = x[b, lo:lo + cfg.T].T
        flag = np.full((128, 1), 0.0 if i == 0 else 1.0, f32)
        invc = np.zeros((128, 4 * 32), f32)
        pos = np.arange(1, 33, dtype=f32)
        for g, w in enumerate(POOL_W):
            cntv = np.minimum(pos, float(w)) if i == 0 else np.full(32, float(w), f32)
            invc[:, g * 32:(g + 1) * 32] = (1.0 / cntv)[None, :]
        m = dict(shared)
        m["x"] = xt
        m["flag"] = flag
        m["invc"] = invc
        maps.append(m)
    return maps


_CACHE = {}
N_SPLIT = 1


def _sub_inputs(inputs, l0, nl):
    out = {}
    for k, v in inputs.items():
        if k == "x":
            out[k] = v
        elif k.startswith("even_"):
            out[k] = v[(l0 + 1) // 2:]
        elif k.startswith("odd_"):
            out[k] = v[l0 // 2:]
        else:
            out[k] = v[l0:]
    return out


def kernel(**inputs):
    cfg = Cfg(4096)
    nl = 4 // N_SPLIT
    if "nc" not in _CACHE:
        _CACHE["nc"] = build_program(cfg, n_layers=nl)[0]
    nc = _CACHE["nc"]
    inp = {k: np.asarray(v) for k, v in inputs.items()}
    x = np.asarray(inp["x"], np.float32)
    for part in range(N_SPLIT):
        sub = _sub_inputs(inp, part * nl, nl)
        sub["x"] = x
        maps = host_inputs(cfg, sub, n_layers=nl)
        res = run_bass_kernel_spmd(nc, maps, core_ids=list(range(NCORES)))
        out = np.empty((2, SEQ, D), np.float32)
        for c in range(NCORES):
            b, i = c // 4, c % 4
            out[b, i * cfg.CH:(i + 1) * cfg.CH, :] = res.results[c]["y"].T
        x = out
    return x
```

```python
import contextlib
import numpy as np
import concourse.bass as bass
import concourse.mybir as mybir
from concourse.bass_utils import run_bass_kernel_spmd

F32 = mybir.dt.float32
BF16 = mybir.dt.bfloat16
AF = mybir.ActivationFunctionType
ALU = mybir.AluOpType

D = 1024
KC = 8
DFF = 2816
NJ = 22
NCORES = 8
SEQ = 16384
HALO = 128
CH = 2048
SEG = CH + HALO
T = 2 * SEG
EPS = 1e-6

SAME_ENGINE_SYNC = True
NO_SELF_SYNC = ("pe",)


class Op:
    __slots__ = ("eng", "fn", "deps", "is_dma", "sem", "val", "need_inc", "key", "phase", "inc")

    def __init__(self, eng, fn, is_dma, key=None):
        self.eng = eng
        self.fn = fn
        self.deps = []
        self.is_dma = is_dma
        self.sem = None
        self.val = None
        self.need_inc = False
        self.key = key
        self.inc = 16


class Prog:
    ENGS = ("pe", "act", "dve", "pool", "sp")

    def __init__(self, nc, stack):
        self.nc = nc
        self.stack = stack
        self.ops = {e: [] for e in self.ENGS}
        self.last_w = {}
        self.readers = {}
        self.dma_sems = {}
        self.n_ops = 0
        self.phase = 0

    def op(self, eng, fn, reads=(), writes=(), dma_key=None, inc=16):
        o = Op(eng, fn, dma_key is not None, dma_key)
        o.inc = inc
        o.phase = self.phase
        deps = []
        for r in reads:
            w = self.last_w.get(r)
            if w is not None:
                deps.append(w)
        for r in writes:
            w = self.last_w.get(r)
            if w is not None:
                deps.append(w)
            deps.extend(self.readers.get(r, ()))
        seen = set()
        for d in deps:
            if id(d) in seen or d is o:
                continue
            seen.add(id(d))
            if (not d.is_dma) and d.eng == eng and (eng in NO_SELF_SYNC or not SAME_ENGINE_SYNC):
                continue
            if d.is_dma and o.is_dma and d.key == o.key and isinstance(o.key, tuple) and o.key[0] == "G":
                continue
            o.deps.append(d)
            d.need_inc = True
        for r in reads:
            self.readers.setdefault(r, []).append(o)
        for r in writes:
            self.last_w[r] = o
            self.readers[r] = []
        self.ops[eng].append(o)
        self.n_ops += 1
        return o

    def barrier(self):
        deps = []
        for e in self.ENGS:
            for o in reversed(self.ops[e]):
                if not o.is_dma:
                    deps.append(o)
                    break
        last_dma = {}
        for e in self.ENGS:
            for o in self.ops[e]:
                if o.is_dma:
                    last_dma[o.key] = o
        deps.extend(last_dma.values())
        for e in self.ENGS:
            o = Op(e, lambda en: en.nop(), False)
            o.phase = self.phase
            for d in deps:
                if (not d.is_dma) and d.eng == e:
                    continue
                o.deps.append(d)
                d.need_inc = True
            self.ops[e].append(o)
        self.last_w = {}
        self.readers = {}
        self.phase += 1

    def emit(self, final_waits=()):
        nc = self.nc
        stack = self.stack
        eng_sem = {}
        ecnt = {}
        gtot = {}
        for e in self.ENGS:
            for o in self.ops[e]:
                if o.is_dma:
                    if o.key not in self.dma_sems:
                        self.dma_sems[o.key] = [
                            stack.enter_context(nc.semaphore("d%d" % len(self.dma_sems))), 0]
                    ent = self.dma_sems[o.key]
                    ent[1] += o.inc
                    o.sem, o.val = ent[0], ent[1]
                    if isinstance(o.key, tuple) and o.key[0] == "G":
                        gtot[(o.key, o.phase)] = ent[1]
                elif o.need_inc:
                    k_ = (e, o.phase % 4)
                    if k_ not in eng_sem:
                        eng_sem[k_] = stack.enter_context(nc.semaphore("s_%s_%d" % k_))
                        ecnt[k_] = 0
                    ecnt[k_] += 1
                    o.sem, o.val = eng_sem[k_], ecnt[k_]
        for e in self.ENGS:
            for o in self.ops[e]:
                if o.is_dma and (o.key, o.phase) in gtot:
                    o.val = gtot[(o.key, o.phase)]
        final = list(final_waits)
        block = stack.enter_context(nc.Block())
        handles = {"pe": "tensor", "act": "scalar", "dve": "vector", "pool": "gpsimd", "sp": "sync"}

        def run(ename, e):
            waited = {}
            for o in self.ops[ename]:
                for d in o.deps:
                    k = id(d.sem)
                    if waited.get(k, 0) >= d.val:
                        continue
                    e.wait_ge(d.sem, d.val)
                    waited[k] = d.val
                ins = o.fn(e)
                if o.is_dma:
                    ins.then_inc(o.sem, o.inc)
                elif o.need_inc:
                    ins.then_inc(o.sem, 1)
            if ename == "pool":
                for ent in self.dma_sems.values():
                    e.wait_ge(ent[0], ent[1])

        for ename in self.ENGS:
            dec = getattr(block, handles[ename])

            def body(e, _n=ename):
                run(_n, e)
            dec(body)


_PID = {}


def pid4(e):
    k = id(e)
    if k not in _PID:
        _PID[k] = e.partition_id() % 4
    return _PID[k]


class Ctx:
    def __init__(self, nc, stack):
        self.nc = nc
        self.stack = stack
        self.P = Prog(nc, stack)
        self._n = 0

    def sb(self, shape, dtype, name=None):
        self._n += 1
        return self.stack.enter_context(self.nc.sbuf_tensor("%s_%d" % (name or "sb", self._n), list(shape), dtype))

    def ps(self, shape, dtype=F32, name=None):
        self._n += 1
        return self.stack.enter_context(self.nc.psum_tensor("%s_%d" % (name or "ps", self._n), list(shape), dtype))


class Cfg:
    def __init__(self, CH=4096):
        self.CH = CH
        self.T = CH + HALO
        self.SEQ = 4 * CH
        self.NKB = self.SEQ // 128
        self.NQB = self.SEQ // 512


def make_groups(T_, first_n, step, halo):
    out = [(0, min(first_n, T_), 0)]
    pos = out[0][1]
    while pos < T_:
        n = min(step, T_ - pos)
        out.append((pos, n, halo))
        pos += n
    return out


class Phase:
    def __init__(self, cx):
        self.cx = cx

    def __enter__(self):
        self.st = contextlib.ExitStack()
        self.saved = self.cx.stack
        self.cx.stack = self.st
        return self

    def __exit__(self, *a):
        self.cx.P.barrier()
        self.cx.stack = self.saved
        self.st.close()
        return False


def setup_consts(cx):
    P = cx.P
    cx.ones_f = cx.sb([128, 128], F32, "ones_f")
    cx.ones2_f = cx.sb([128, 128], F32, "ones2_f")
    cx.eps_t = cx.sb([128, 1], F32, "eps")
    cx.one_t = cx.sb([128, 1], F32, "one")
    cx.ident_b = cx.sb([128, 128], BF16, "ident_b")
    cx.maskb = cx.sb([128, 128], BF16, "maskb")
    cx.flag = cx.sb([128, 1], F32, "flag")
    P.op("dve", lambda e: e.memset(cx.ones_f[:, :], 1.0), writes=[("c", "ones_f")])
    P.op("dve", lambda e: e.memset(cx.ones2_f[:, :], 1.0), writes=[("c", "ones2_f")])
    P.op("dve", lambda e: e.memset(cx.ones2_f[0:64, 64:128], 0.0), writes=[("c", "ones2_f")])
    P.op("dve", lambda e: e.memset(cx.ones2_f[64:128, 0:64], 0.0), writes=[("c", "ones2_f")])
    P.op("dve", lambda e: e.memset(cx.eps_t[:, :], EPS), writes=[("c", "eps")])
    P.op("dve", lambda e: e.memset(cx.one_t[:, :], 1.0), writes=[("c", "one")])
    P.op("dve", lambda e: e.memset(cx.ident_b[:, :], 0.0), writes=[("c", "ident")])
    P.op("pool", lambda e: e.affine_select(out=cx.ident_b[:, :], in_=cx.ident_b[:, :], pattern=[[1, 128]],
                                           compare_op=ALU.not_equal, fill=1.0, base=0, channel_multiplier=-1),
         reads=[("c", "ident")], writes=[("c", "ident")])
    P.op("dve", lambda e: e.memset(cx.maskb[:, :], 0.0), writes=[("c", "maskb")])
    P.op("pool", lambda e: e.affine_select(out=cx.maskb[:, :], in_=cx.maskb[:, :], pattern=[[1, 128]],
                                           compare_op=ALU.is_ge, fill=-30000.0, base=0, channel_multiplier=-1),
         reads=[("c", "maskb")], writes=[("c", "maskb")])
    P.op("sp", lambda e: e.dma_start(out=cx.flag[:, :], in_=cx.dram["flag"]), writes=[("c", "flag")], dma_key=("G", "small"))


def emit_rmsnorm(cx, xt, nu, g_t, H, res, sq, red, ps_stat, std, rstd):
    P = cx.P
    P.op("act", lambda e: e.activation(out=sq[:, :, :nu], in_=xt[:, :, :nu], func=AF.Square),
         reads=[res["xt"]], writes=[res["sq"]])
    P.op("pool", lambda e: e.tensor_tensor(out=red[:, :nu], in0=sq[:, 0, :nu], in1=sq[:, 1, :nu], op=ALU.add),
         reads=[res["sq"]], writes=[res["red"]])
    for kc in range(2, KC):
        P.op("pool", lambda e, kc=kc: e.tensor_tensor(out=red[:, :nu], in0=red[:, :nu], in1=sq[:, kc, :nu], op=ALU.add),
             reads=[res["sq"], res["red"]], writes=[res["red"]])
    P.op("pe", lambda e: e.matmul(ps_stat[:, :nu], lhsT=cx.ones_f[:, :], rhs=red[:, :nu], start=True, stop=True),
         reads=[res["red"], ("c", "ones_f")], writes=[res["ps_stat"]])
    P.op("act", lambda e: e.activation(out=std[:, :nu], in_=ps_stat[:, :nu], func=AF.Sqrt, scale=1.0 / D, bias=cx.eps_t[:, 0:1]),
         reads=[res["ps_stat"], ("c", "eps")], writes=[res["std"]])
    P.op("dve", lambda e: e.reciprocal(out=rstd[:, :nu], in_=std[:, :nu]),
         reads=[res["std"]], writes=[res["rstd"]])
    for kc in range(KC):
        P.op("dve", lambda e, kc=kc: e.scalar_tensor_tensor(
            out=H[:, kc, :nu], in0=xt[:, kc, :nu], scalar=g_t[:, kc:kc + 1], in1=rstd[:, :nu],
            op0=ALU.mult, op1=ALU.mult),
            reads=[res["xt"], res["rstd"], res["g"]], writes=[res["H"]])


def load_small(cx, tag, name, shape, src):
    t = cx.sb(shape, F32, name)
    r = (tag, name)
    cx.P.op("sp", lambda e: e.dma_start(out=t[:, :], in_=src), writes=[r], dma_key=("G", "small"))
    return t, r


def store_group(cx, tag, xo, rxo, Xo, o0, n, first):
    P = cx.P
    if first:
        P.op("dve", lambda e: e.tensor_scalar(out=xo[:, :, 0:HALO], in0=xo[:, :, 0:HALO], scalar1=cx.flag[:, 0:1],
                                              scalar2=None, op0=ALU.mult),
             reads=[rxo, ("c", "flag")], writes=[rxo])
    P.op("pool", lambda e: e.dma_start(out=Xo[:, :, o0:o0 + n], in_=xo[:, :, :n]),
         reads=[rxo], writes=[(tag, "Xout")], dma_key=("st", 0))


def ffn_phase(cx, cfg, X_in, X_out, wup_bf, wdn_bf, g_dram, cw_dram, cb_dram, tag, out_off=0):
    P = cx.P
    groups = make_groups(cfg.T, 512, 510, 2)
    Xi = X_in.rearrange("(kc p) t -> p kc t", p=128)
    Xo = X_out.rearrange("(kc p) t -> p kc t", p=128)
    with Phase(cx):
        g_t, rg = load_small(cx, tag, "g", [128, KC], g_dram)
        cw_t, rcw = load_small(cx, tag, "cw", [128, 44 * 3], cw_dram)
        cb_t, rcb = load_small(cx, tag, "cb", [128, 44], cb_dram)
        NX = 2
        xt = [cx.sb([128, KC, 512], F32, "xt") for _ in range(NX)]
        sq = cx.sb([128, KC, 512], F32, "sq")
        red = cx.sb([128, 512], F32, "red")
        std = cx.sb([128, 512], F32, "std")
        rstd = cx.sb([128, 512], F32, "rstd")
        Hs = [cx.sb([128, KC, 512], BF16, "H") for _ in range(2)]
        NW = 5
        wup = [cx.sb([128, 2, KC, 128], BF16, "wup") for _ in range(NW)]
        acc_g = [cx.sb([128, 512], F32, "accg") for _ in range(2)]
        acc_v = [cx.sb([128, 512], F32, "accv") for _ in range(2)]
        sg = [cx.sb([128, 512], F32, "sg") for _ in range(2)]
        Gs = [cx.sb([128, NJ, 512], BF16, "G") for _ in range(2)]
        ND = 3
        wdn = [cx.sb([128, NJ, 128], BF16, "wdn") for _ in range(ND)]
        xo = cx.sb([128, KC, 512], F32, "xo")
        ps_stat = cx.ps([128, 512], F32, "psst")
        ps_g = [cx.ps([128, 512], F32, "psg") for _ in range(2)]
        ps_v = [cx.ps([128, 512], F32, "psv") for _ in range(2)]
        ps_o = [cx.ps([128, 512], F32, "pso") for _ in range(2)]
        cnt = {"w": 0, "d": 0, "a": 0, "o": 0}

        def pre(gi):
            o0, n, h = groups[gi]
            nu = n + h
            s = gi % NX
            rx = (tag, "xt", s)
            P.op("sp", lambda e: e.dma_start(out=xt[s][:, :, :nu], in_=Xi[:, :, o0 - h:o0 + n]),
                 reads=[(tag, "Xin")], writes=[rx], dma_key=("xt", s))
            res = {"xt": rx, "sq": (tag, "sq"), "red": (tag, "red"), "ps_stat": (tag, "psst"), "std": (tag, "std"),
                   "rstd": (tag, "rstd"), "g": rg, "H": (tag, "H", gi % 2)}
            emit_rmsnorm(cx, xt[s], nu, g_t, Hs[gi % 2], res, sq, red, ps_stat, std, rstd)

        def up(gi):
            o0, n, h = groups[gi]
            nu = n + h
            H = Hs[gi % 2]
            G = Gs[gi % 2]
            rH = (tag, "H", gi % 2)
            rG = (tag, "G", gi % 2)
            for j in range(NJ):
                ws = cnt["w"] % NW
                cnt["w"] += 1
                rw = (tag, "wup", ws)
                P.op("sp", lambda e, j=j, ws=ws: e.dma_start(
                    out=wup[ws][:, :, :, :].rearrange("p a k o -> p (a k o)"), in_=wup_bf[j]),
                    writes=[rw], dma_key=("wup", ws))
                a = cnt["a"] % 2
                cnt["a"] += 1
                rpg, rpv = (tag, "psg", a), (tag, "psv", a)
                for gv, psb, rp in ((0, ps_g[a], rpg), (1, ps_v[a], rpv)):
                    for kc in range(KC):
                        P.op("pe", lambda e, gv=gv, kc=kc, psb=psb, ws=ws: e.matmul(
                            psb[:, :nu], lhsT=wup[ws][:, gv, kc, :], rhs=H[:, kc, :nu],
                            start=(kc == 0), stop=(kc == KC - 1)),
                            reads=[rw, rH], writes=[rp])
                ag, av, sgt = acc_g[a], acc_v[a], sg[a]
                rag, rav, rsg = (tag, "accg", a), (tag, "accv", a), (tag, "sg", a)
                pairs = ((ag, rag, ps_g[a], rpg, j), (av, rav, ps_v[a], rpv, NJ + j))
                for (acc, racc, psb, rp, ch) in pairs:
                    w2 = cw_t[:, ch * 3 + 2:ch * 3 + 3]
                    bb = cb_t[:, ch:ch + 1]
                    P.op("act", lambda e, acc=acc, psb=psb, w2=w2, bb=bb: e.activation(
                        out=acc[:, :n], in_=psb[:, h:h + n], func=AF.Identity, scale=w2, bias=bb),
                        reads=[rp, rcw, rcb], writes=[racc])
                for sh in (1, 2):
                    for (acc, racc, psb, rp, ch) in pairs:
                        wk = cw_t[:, ch * 3 + (2 - sh):ch * 3 + (3 - sh)]
                        if h >= sh:
                            P.op("dve", lambda e, acc=acc, psb=psb, wk=wk, sh=sh: e.scalar_tensor_tensor(
                                out=acc[:, :n], in0=psb[:, h - sh:h - sh + n], scalar=wk, in1=acc[:, :n],
                                op0=ALU.mult, op1=ALU.add),
                                reads=[rp, rcw, racc], writes=[racc])
                        else:
                            P.op("dve", lambda e, acc=acc, psb=psb, wk=wk, sh=sh: e.scalar_tensor_tensor(
                                out=acc[:, sh:n], in0=psb[:, 0:n - sh], scalar=wk, in1=acc[:, sh:n],
                                op0=ALU.mult, op1=ALU.add),
                                reads=[rp, rcw, racc], writes=[racc])
                P.op("act", lambda e, ag=ag, sgt=sgt: e.activation(out=sgt[:, :n], in_=ag[:, :n], func=AF.Silu),
                     reads=[rag], writes=[rsg])
                P.op("pool", lambda e, j=j, sgt=sgt, av=av: e.tensor_tensor(
                    out=G[:, j, :n], in0=sgt[:, :n], in1=av[:, :n], op=ALU.mult),
                    reads=[rsg, rav], writes=[rG])

        def down(gi):
            o0, n, h = groups[gi]
            G = Gs[gi % 2]
            rG = (tag, "G", gi % 2)
            s = gi % NX
            rx = (tag, "xt", s)
            rxo = (tag, "xo")
            for oc in range(KC):
                ds_ = cnt["d"] % ND
                cnt["d"] += 1
                rw = (tag, "wdn", ds_)
                P.op("sp", lambda e, oc=oc, ds_=ds_: e.dma_start(
                    out=wdn[ds_][:, :, :].rearrange("p j o -> p (j o)"), in_=wdn_bf[oc]),
                    writes=[rw], dma_key=("wdn", ds_))
                b = cnt["o"] % 2
                cnt["o"] += 1
                rp = (tag, "pso", b)
                for j in range(NJ):
                    P.op("pe", lambda e, j=j, b=b, ds_=ds_: e.matmul(
                        ps_o[b][:, :n], lhsT=wdn[ds_][:, j, :], rhs=G[:, j, :n],
                        start=(j == 0), stop=(j == NJ - 1)),
                        reads=[rw, rG], writes=[rp])
                P.op("dve", lambda e, oc=oc, b=b: e.tensor_tensor(
                    out=xo[:, oc, :n], in0=ps_o[b][:, :n], in1=xt[s][:, oc, h:h + n], op=ALU.add),
                    reads=[rp, rx], writes=[rxo])
            if out_off == 0:
                store_group(cx, tag, xo, rxo, Xo, o0, n, gi == 0)
            else:
                lo = max(o0, out_off)
                if lo < o0 + n:
                    P.op("pool", lambda e: e.dma_start(out=Xo[:, :, lo - out_off:o0 + n - out_off], in_=xo[:, :, lo - o0:n]),
                         reads=[rxo], writes=[(tag, "Xout")], dma_key=("st", 0))

        ng = len(groups)
        pre(0)
        up(0)
        for gi in range(ng):
            if gi + 1 < ng:
                pre(gi + 1)
                up(gi + 1)
            down(gi)


CW31 = 31
HO = 30


def odd_phase(cx, cfg, X_in, X_out, w1_bf, w2_bf, dg_bf, vec_dram, tag):
    P = cx.P
    groups = make_groups(cfg.T, 482, 482, HO)
    Xi = X_in.rearrange("(kc p) t -> p kc t", p=128)
    Xo = X_out.rearrange("(kc p) t -> p kc t", p=128)
    with Phase(cx):
        vec, rvec = load_small(cx, tag, "vec", [128, 32], vec_dram)
        g_t = vec[:, 0:8]
        xt = [cx.sb([128, KC, 512], F32, "xt") for _ in range(2)]
        sqk = [cx.sb([128, 512], F32, "sqk") for _ in range(2)]
        red = cx.sb([128, 512], F32, "red")
        lr1 = cx.sb([128, 512], F32, "lr1")
        lr2 = cx.sb([128, 512], F32, "lr2")
        std = cx.sb([128, 512], F32, "std")
        rstd = cx.sb([128, 512], F32, "rstd")
        mean = cx.sb([128, 512], F32, "mean")
        m2 = cx.sb([128, 512], F32, "m2")
        t1 = [cx.sb([128, 512], F32, "t1") for _ in range(2)]
        Hs = [cx.sb([128, KC, 512], BF16, "H") for _ in range(2)]
        w1 = [cx.sb([128, 2, KC, 128], BF16, "w1") for _ in range(3)]
        dg = [cx.sb([128, CW31, 128], BF16, "dg") for _ in range(2)]
        Us = [cx.sb([128, KC, 512 + HO], BF16, "U") for _ in range(2)]
        sig = [cx.sb([128, 512], F32, "sig") for _ in range(2)]
        Vs = [cx.sb([128, KC, 512], F32, "V") for _ in range(2)]
        Ss = [cx.sb([128, KC, 512], BF16, "S") for _ in range(2)]
        w2 = [cx.sb([128, KC, 128], BF16, "w2") for _ in range(2)]
        xo = cx.sb([128, KC, 512], F32, "xo")
        ps = [cx.ps([128, 512], F32, "ps") for _ in range(8)]
        rps = [(tag, "ps", i) for i in range(8)]
        cnt = {"w1": 0, "ag": 0, "dg": 0, "c": 0, "w2": 0, "o": 0}

        def pre(gi):
            o0, n, h = groups[gi]
            nu = n + h
            s = gi % 2
            rx = (tag, "xt", s)
            P.op("sp", lambda e: e.dma_start(out=xt[s][:, :, :nu], in_=Xi[:, :, o0 - h:o0 + n]),
                 reads=[(tag, "Xin")], writes=[rx], dma_key=("xt", s))
            rH = (tag, "H", s)
            x_ = xt[s]
            P.op("act", lambda e: e.activation(out=red[:, :nu], in_=x_[:, 0, :nu], func=AF.Square),
                 reads=[rx], writes=[(tag, "red")])
            for kc in range(1, KC):
                q = sqk[kc % 2]
                rq = (tag, "sqk", kc % 2)
                P.op("act", lambda e, kc=kc, q=q: e.activation(out=q[:, :nu], in_=x_[:, kc, :nu], func=AF.Square),
                     reads=[rx], writes=[rq])
                P.op("pool", lambda e, q=q: e.tensor_tensor(out=red[:, :nu], in0=red[:, :nu], in1=q[:, :nu], op=ALU.add),
                     reads=[rq, (tag, "red")], writes=[(tag, "red")])
            P.op("pe", lambda e: e.matmul(ps[0][:, :nu], lhsT=cx.ones_f[:, :], rhs=red[:, :nu], start=True, stop=True),
                 reads=[(tag, "red"), ("c", "ones_f")], writes=[rps[0]])
            P.op("act", lambda e: e.activation(out=std[:, :nu], in_=ps[0][:, :nu], func=AF.Sqrt, scale=1.0 / D, bias=cx.eps_t[:, 0:1]),
                 reads=[rps[0], ("c", "eps")], writes=[(tag, "std")])
            P.op("dve", lambda e: e.reciprocal(out=rstd[:, :nu], in_=std[:, :nu]),
                 reads=[(tag, "std")], writes=[(tag, "rstd")])
            for kc in range(KC):
                P.op("dve", lambda e, kc=kc: e.scalar_tensor_tensor(
                    out=Hs[s][:, kc, :nu], in0=x_[:, kc, :nu], scalar=g_t[:, kc:kc + 1], in1=rstd[:, :nu],
                    op0=ALU.mult, op1=ALU.mult),
                    reads=[rx, (tag, "rstd"), rvec], writes=[rH])

        def glu(gi):
            o0, n, h = groups[gi]
            nu = n + h
            s = gi % 2
            H = Hs[s]
            rH = (tag, "H", s)
            U = Us[s]
            rU = (tag, "U", s)
            cs = HO - h
            if h == 0:
                P.op("pool", lambda e: e.memset(U[:, :, 0:HO], 0.0), writes=[rU])
            for j in range(KC):
                ws = cnt["w1"] % 3
                cnt["w1"] += 1
                rw = (tag, "w1", ws)
                P.op("sp", lambda e, j=j, ws=ws: e.dma_start(
                    out=w1[ws][:, :, :, :].rearrange("p a k o -> p (a k o)"), in_=w1_bf[j]),
                    writes=[rw], dma_key=("w1", ws))
                a = cnt["ag"] % 2
                cnt["ag"] += 1
                pa, pg = ps[1 + 2 * a], ps[2 + 2 * a]
                rpa, rpg = rps[1 + 2 * a], rps[2 + 2 * a]
                for gv, psb, rp in ((0, pa, rpa), (1, pg, rpg)):
                    for kc in range(KC):
                        P.op("pe", lambda e, gv=gv, kc=kc, psb=psb, ws=ws: e.matmul(
                            psb[:, :nu], lhsT=w1[ws][:, gv, kc, :], rhs=H[:, kc, :nu],
                            start=(kc == 0), stop=(kc == KC - 1)),
                            reads=[rw, rH], writes=[rp])
                sg_ = sig[a]
                rsg = (tag, "sig", a)
                P.op("act", lambda e, sg_=sg_, pg=pg: e.activation(out=sg_[:, :nu], in_=pg[:, :nu], func=AF.Sigmoid),
                     reads=[rpg], writes=[rsg])
                P.op("dve", lambda e, j=j, sg_=sg_, pa=pa: e.tensor_tensor(
                    out=U[:, j, cs:cs + nu], in0=pa[:, :nu], in1=sg_[:, :nu], op=ALU.mult),
                    reads=[rpa, rsg], writes=[rU])

        def conv(gi):
            o0, n, h = groups[gi]
            s = gi % 2
            U = Us[s]
            rU = (tag, "U", s)
            V = Vs[s]
            rV = (tag, "V", s)
            for j in range(KC):
                d_ = cnt["dg"] % 2
                cnt["dg"] += 1
                rd = (tag, "dg", d_)
                P.op("sp", lambda e, j=j, d_=d_: e.dma_start(
                    out=dg[d_][:, :, :].rearrange("p k o -> p (k o)"), in_=dg_bf[j]),
                    writes=[rd], dma_key=("dg", d_))
                b = cnt["c"] % 2
                cnt["c"] += 1
                pc, rpc = ps[5 + b], rps[5 + b]
                for k in range(CW31):
                    P.op("pe", lambda e, j=j, k=k, d_=d_, pc=pc: e.matmul(
                        pc[:, :n], lhsT=dg[d_][:, k, :], rhs=U[:, j, k:k + n],
                        start=(k == 0), stop=(k == CW31 - 1)),
                        reads=[rd, rU], writes=[rpc])
                P.op("act", lambda e, j=j, pc=pc: e.activation(
                    out=V[:, j, :n], in_=pc[:, :n], func=AF.Identity, bias=vec[:, 8 + j:9 + j], scale=1.0),
                    reads=[rpc, rvec], writes=[rV])

        def ln_a(gi):
            o0, n, h = groups[gi]
            s = gi % 2
            V = Vs[s]
            rV = (tag, "V", s)
            P.op("pool", lambda e: e.tensor_tensor(out=lr1[:, :n], in0=V[:, 0, :n], in1=V[:, 1, :n], op=ALU.add),
                 reads=[rV], writes=[(tag, "lr1")])
            for kc in range(2, KC):
                P.op("pool", lambda e, kc=kc: e.tensor_tensor(out=lr1[:, :n], in0=lr1[:, :n], in1=V[:, kc, :n], op=ALU.add),
                     reads=[rV, (tag, "lr1")], writes=[(tag, "lr1")])
            P.op("act", lambda e: e.activation(out=lr2[:, :n], in_=V[:, 0, :n], func=AF.Square),
                 reads=[rV], writes=[(tag, "lr2")])
            for kc in range(1, KC):
                q = sqk[kc % 2]
                rq = (tag, "sqk", kc % 2)
                P.op("act", lambda e, kc=kc, q=q: e.activation(out=q[:, :n], in_=V[:, kc, :n], func=AF.Square),
                     reads=[rV], writes=[rq])
                P.op("pool", lambda e, q=q: e.tensor_tensor(out=lr2[:, :n], in0=lr2[:, :n], in1=q[:, :n], op=ALU.add),
                     reads=[rq, (tag, "lr2")], writes=[(tag, "lr2")])

        def ln_b(gi):
            o0, n, h = groups[gi]
            s = gi % 2
            V = Vs[s]
            rV = (tag, "V", s)
            S = Ss[s]
            rS = (tag, "S", s)
            P.op("pe", lambda e: e.matmul(ps[0][:, :n], lhsT=cx.ones_f[:, :], rhs=lr1[:, :n], start=True, stop=True),
                 reads=[(tag, "lr1"), ("c", "ones_f")], writes=[rps[0]])
            P.op("pe", lambda e: e.matmul(ps[7][:, :n], lhsT=cx.ones_f[:, :], rhs=lr2[:, :n], start=True, stop=True),
                 reads=[(tag, "lr2"), ("c", "ones_f")], writes=[rps[7]])
            P.op("dve", lambda e: e.tensor_scalar(out=mean[:, :n], in0=ps[0][:, :n], scalar1=1.0 / D, scalar2=None, op0=ALU.mult),
                 reads=[rps[0]], writes=[(tag, "mean")])
            P.op("dve", lambda e: e.tensor_tensor(out=m2[:, :n], in0=mean[:, :n], in1=mean[:, :n], op=ALU.mult),
                 reads=[(tag, "mean")], writes=[(tag, "m2")])
            P.op("dve", lambda e: e.scalar_tensor_tensor(out=m2[:, :n], in0=ps[7][:, :n], scalar=1.0 / D, in1=m2[:, :n],
                                                         op0=ALU.mult, op1=ALU.subtract),
                 reads=[rps[7], (tag, "m2")], writes=[(tag, "m2")])
            P.op("act", lambda e: e.activation(out=std[:, :n], in_=m2[:, :n], func=AF.Sqrt, scale=1.0, bias=cx.eps_t[:, 0:1]),
                 reads=[(tag, "m2"), ("c", "eps")], writes=[(tag, "std")])
            P.op("dve", lambda e: e.reciprocal(out=rstd[:, :n], in_=std[:, :n]),
                 reads=[(tag, "std")], writes=[(tag, "rstd")])
            for kc in range(KC):
                tt = t1[kc % 2]
                rt = (tag, "t1", kc % 2)
                P.op("dve", lambda e, kc=kc, tt=tt: e.tensor_tensor(out=tt[:, :n], in0=V[:, kc, :n], in1=mean[:, :n], op=ALU.subtract),
                     reads=[rV, (tag, "mean")], writes=[rt])
                P.op("dve", lambda e, tt=tt: e.tensor_tensor(out=tt[:, :n], in0=tt[:, :n], in1=rstd[:, :n], op=ALU.mult),
                     reads=[rt, (tag, "rstd")], writes=[rt])
                P.op("act", lambda e, kc=kc, tt=tt: e.activation(
                    out=S[:, kc, :n], in_=tt[:, :n], func=AF.Silu, scale=vec[:, 16 + kc:17 + kc], bias=vec[:, 24 + kc:25 + kc]),
                    reads=[rt, rvec], writes=[rS])

        def pw2(gi):
            o0, n, h = groups[gi]
            s = gi % 2
            S = Ss[s]
            rS = (tag, "S", s)
            rx = (tag, "xt", s)
            rxo = (tag, "xo")
            for oc in range(KC):
                w_ = cnt["w2"] % 2
                cnt["w2"] += 1
                rw = (tag, "w2", w_)
                P.op("sp", lambda e, oc=oc, w_=w_: e.dma_start(
                    out=w2[w_][:, :, :].rearrange("p k o -> p (k o)"), in_=w2_bf[oc]),
                    writes=[rw], dma_key=("w2", w_))
                b = cnt["o"] % 2
                cnt["o"] += 1
                po, rpo = ps[1 + b], rps[1 + b]
                for kc in range(KC):
                    P.op("pe", lambda e, kc=kc, w_=w_, po=po: e.matmul(
                        po[:, :n], lhsT=w2[w_][:, kc, :], rhs=S[:, kc, :n],
                        start=(kc == 0), stop=(kc == KC - 1)),
                        reads=[rw, rS], writes=[rpo])
                P.op("dve", lambda e, oc=oc, po=po: e.tensor_tensor(
                    out=xo[:, oc, :n], in0=po[:, :n], in1=xt[s][:, oc, h:h + n], op=ALU.add),
                    reads=[rpo, rx], writes=[rxo])
            store_group(cx, tag, xo, rxo, Xo, o0, n, gi == 0)

        ng = len(groups)
        pre(0)
        glu(0)
        conv(0)
        for gi in range(ng):
            ln_a(gi)
            if gi + 1 < ng:
                pre(gi + 1)
                glu(gi + 1)
            ln_b(gi)
            if gi + 1 < ng:
                conv(gi + 1)
            pw2(gi)


HP = 15
POOL_W = (2, 4, 8, 16)


def even_pre_phase(cx, cfg, X_in, wqk_bf, wf_bf, wv_bf, wp_bf, vec_dram, bf_dram, tag):
    P = cx.P
    dr = cx.dram
    groups = make_groups(cfg.T, 512, 497, HP)
    Xi = X_in.rearrange("(kc p) t -> p kc t", p=128)
    with Phase(cx):
        vec, rvec = load_small(cx, tag, "vec", [128, 16], vec_dram)
        invc, rinvc = load_small(cx, tag, "invc", [128, 4 * 32], dr["invc"])
        bft = cx.sb([8, 1], F32, "bft")
        nbf = cx.sb([8, 1], F32, "nbf")
        gq8 = cx.sb([128, 1], F32, "gq8")
        P.op("sp", lambda e: e.dma_start(out=bft[:, :], in_=bf_dram), writes=[(tag, "bft")], dma_key=("G", "small"))
        P.op("dve", lambda e: e.tensor_scalar(out=nbf[:, :], in0=bft[:, :], scalar1=-1.0, scalar2=None, op0=ALU.mult),
             reads=[(tag, "bft")], writes=[(tag, "nbf")])
        P.op("dve", lambda e: e.tensor_scalar(out=gq8[:, :], in0=vec[:, 8:9], scalar1=0.125, scalar2=None, op0=ALU.mult),
             reads=[rvec], writes=[(tag, "gq8")])
        g_t = vec[:, 0:8]
        wv = cx.sb([128, KC, 512], BF16, "wv")
        wf = cx.sb([128, KC, 8], BF16, "wf")
        wp = cx.sb([128, 4, 128], BF16, "wp")
        P.op("sp", lambda e: e.dma_start(out=wv[:, :, :].rearrange("p k o -> p (k o)"), in_=wv_bf), writes=[(tag, "wv")], dma_key=("G", "small"))
        P.op("sp", lambda e: e.dma_start(out=wf[:, :, :].rearrange("p k o -> p (k o)"), in_=wf_bf), writes=[(tag, "wf")], dma_key=("G", "small"))
        P.op("sp", lambda e: e.dma_start(out=wp[:, :, :].rearrange("p k o -> p (k o)"), in_=wp_bf), writes=[(tag, "wp")], dma_key=("G", "small"))
        xt = [cx.sb([128, KC, 512], F32, "xt") for _ in range(2)]
        sqk = [cx.sb([128, 512], F32, "sqk") for _ in range(2)]
        red = cx.sb([128, 512], F32, "red")
        std = cx.sb([128, 512], F32, "std")
        rstd = cx.sb([128, 512], F32, "rstd")
        Hs = [cx.sb([128, KC, 512], BF16, "H") for _ in range(2)]
        wq = [cx.sb([128, KC, 128], BF16, "wq") for _ in range(3)]
        qsq = [cx.sb([128, 512], F32, "qsq") for _ in range(2)]
        qstd = [cx.sb([128, 512], F32, "qstd") for _ in range(2)]
        qo = [cx.sb([128, 512], BF16, "qo") for _ in range(2)]
        vo = [cx.sb([128, 512], BF16, "vo") for _ in range(2)]
        fe = cx.sb([8, 512], F32, "fe")
        fo = [cx.sb([8, 512], F32, "fo") for _ in range(2)]
        ub = [cx.sb([128, 512], F32, "ub") for _ in range(2)]
        sa = cx.sb([128, 512], F32, "sa")
        sb_ = cx.sb([128, 512], F32, "sb")
        mx = [cx.sb([128, 512], BF16, "mx") for _ in range(2)]
        po = [cx.sb([128, 512], BF16, "po") for _ in range(2)]
        ps = [cx.ps([128, 512], F32, "ps") for _ in range(8)]
        rps = [(tag, "ps", i) for i in range(8)]
        cnt = {"w": 0, "pj": 0, "q": 0, "v": 0, "u": 0, "f": 0}

        def pre(gi):
            o0, n, h = groups[gi]
            nu = n + h
            s = gi % 2
            rx = (tag, "xt", s)
            x_ = xt[s]
            P.op("sp", lambda e: e.dma_start(out=x_[:, :, :nu], in_=Xi[:, :, o0 - h:o0 + n]),
                 reads=[(tag, "Xin")], writes=[rx], dma_key=("xt", s))
            P.op("act", lambda e: e.activation(out=red[:, :nu], in_=x_[:, 0, :nu], func=AF.Square),
                 reads=[rx], writes=[(tag, "red")])
            for kc in range(1, KC):
                q = sqk[kc % 2]
                rq = (tag, "sqk", kc % 2)
                P.op("act", lambda e, kc=kc, q=q: e.activation(out=q[:, :nu], in_=x_[:, kc, :nu], func=AF.Square),
                     reads=[rx], writes=[rq])
                P.op("pool", lambda e, q=q: e.tensor_tensor(out=red[:, :nu], in0=red[:, :nu], in1=q[:, :nu], op=ALU.add),
                     reads=[rq, (tag, "red")], writes=[(tag, "red")])
            P.op("pe", lambda e: e.matmul(ps[0][:, :nu], lhsT=cx.ones_f[:, :], rhs=red[:, :nu], start=True, stop=True),
                 reads=[(tag, "red"), ("c", "ones_f")], writes=[rps[0]])
            P.op("act", lambda e: e.activation(out=std[:, :nu], in_=ps[0][:, :nu], func=AF.Sqrt, scale=1.0 / D, bias=cx.eps_t[:, 0:1]),
                 reads=[rps[0], ("c", "eps")], writes=[(tag, "std")])
            P.op("dve", lambda e: e.reciprocal(out=rstd[:, :nu], in_=std[:, :nu]),
                 reads=[(tag, "std")], writes=[(tag, "rstd")])
            for kc in range(KC):
                P.op("dve", lambda e, kc=kc: e.scalar_tensor_tensor(
                    out=Hs[s][:, kc, :nu], in0=x_[:, kc, :nu], scalar=g_t[:, kc:kc + 1], in1=rstd[:, :nu],
                    op0=ALU.mult, op1=ALU.mult),
                    reads=[rx, (tag, "rstd"), rvec], writes=[(tag, "H", s)])

        def proj(gi):
            o0, n, h = groups[gi]
            nu = n + h
            s = gi % 2
            H = Hs[s]
            rH = (tag, "H", s)
            t0 = o0 - h
            lo = max(o0, HALO)
            hi = o0 + n
            own = lo < hi
            c_lo, c_hi = lo - t0, hi - t0
            tk = lo - HALO
            for c in range(12):
                if c < 8 and not own:
                    continue
                ws = cnt["w"] % 3
                cnt["w"] += 1
                rw = (tag, "wq", ws)
                P.op("sp", lambda e, c=c, ws=ws: e.dma_start(
                    out=wq[ws][:, :, :].rearrange("p k o -> p (k o)"), in_=wqk_bf[c]),
                    writes=[rw], dma_key=("wq", ws))
                b = cnt["pj"] % 2
                cnt["pj"] += 1
                pj, rpj = ps[1 + b], rps[1 + b]
                for kc in range(KC):
                    P.op("pe", lambda e, kc=kc, ws=ws, pj=pj: e.matmul(
                        pj[:, :nu], lhsT=wq[ws][:, kc, :], rhs=H[:, kc, :nu],
                        start=(kc == 0), stop=(kc == KC - 1)),
                        reads=[rw, rH], writes=[rpj])
                if c < 8:
                    a = cnt["q"] % 2
                    cnt["q"] += 1
                    rqs, rqd, rqo = (tag, "qsq", a), (tag, "qstd", a), (tag, "qo", a)
                    P.op("act", lambda e, a=a, pj=pj: e.activation(out=qsq[a][:, :nu], in_=pj[:, :nu], func=AF.Square),
                         reads=[rpj], writes=[rqs])
                    P.op("pe", lambda e, a=a: e.matmul(ps[3][:, :nu], lhsT=cx.ones2_f[:, :], rhs=qsq[a][:, :nu], start=True, stop=True),
                         reads=[rqs, ("c", "ones2_f")], writes=[rps[3]])
                    P.op("act", lambda e, a=a: e.activation(out=qstd[a][:, :nu], in_=ps[3][:, :nu], func=AF.Sqrt, scale=1.0 / 64, bias=cx.eps_t[:, 0:1]),
                         reads=[rps[3], ("c", "eps")], writes=[rqd])
                    P.op("dve", lambda e, a=a: e.reciprocal(out=qstd[a][:, :nu], in_=qstd[a][:, :nu]),
                         reads=[rqd], writes=[rqd])
                    gsc = gq8[:, 0:1] if c < 4 else vec[:, 9:10]
                    P.op("dve", lambda e, a=a, pj=pj, gsc=gsc: e.scalar_tensor_tensor(
                        out=qo[a][:, :nu], in0=pj[:, :nu], scalar=gsc, in1=qstd[a][:, :nu], op0=ALU.mult, op1=ALU.mult),
                        reads=[rpj, rqd, rvec, (tag, "gq8")], writes=[rqo])
                    dst = dr["Qc"] if c < 4 else dr["Kc"]
                    cc = c % 4
                    P.op("pool", lambda e, a=a, dst=dst, cc=cc: e.dma_start(
                        out=dst[cc * 128:(cc + 1) * 128, tk:tk + (hi - lo)], in_=qo[a][:, c_lo:c_hi]),
                        reads=[rqo], writes=[(tag, "QKc")], dma_key=("qo", a))
                else:
                    g = c - 8
                    a = cnt["u"] % 2
                    cnt["u"] += 1
                    u_ = ub[a]
                    ru = (tag, "ub", a)
                    P.op("act", lambda e, u_=u_, pj=pj: e.activation(out=u_[:, :nu], in_=pj[:, :nu], func=AF.Identity),
                         reads=[rpj], writes=[ru])
                    src, rsrc = u_, ru
                    bufs = [(sa, (tag, "sa")), (sb_, (tag, "sb"))]
                    for st in range(g + 1):
                        sh = 1 << st
                        dstt, rdst = bufs[st % 2]
                        P.op("dve", lambda e, src=src, dstt=dstt, sh=sh: e.tensor_tensor(
                            out=dstt[:, sh:nu], in0=src[:, sh:nu], in1=src[:, 0:nu - sh], op=ALU.add),
                            reads=[rsrc], writes=[rdst])
                        P.op("pool", lambda e, src=src, dstt=dstt, sh=sh: e.tensor_copy(out=dstt[:, 0:sh], in_=src[:, 0:sh]),
                             reads=[rsrc], writes=[rdst])
                        src, rsrc = dstt, rdst
                    w_ = POOL_W[g]
                    m_ = mx[a]
                    rm = (tag, "mx", a)
                    P.op("dve", lambda e, src=src, u_=u_, m_=m_, w_=w_: e.scalar_tensor_tensor(
                        out=m_[:, :n], in0=src[:, h:nu], scalar=1.0 / w_, in1=u_[:, h:nu], op0=ALU.mult, op1=ALU.subtract),
                        reads=[rsrc, ru], writes=[rm])
                    if gi == 0:
                        tt = sqk[0]
                        rt = (tag, "sqk", 0)
                        P.op("dve", lambda e, src=src, g=g, tt=tt: e.tensor_tensor(
                            out=tt[:, 0:32], in0=src[:, HALO:HALO + 32], in1=invc[:, g * 32:(g + 1) * 32], op=ALU.mult),
                            reads=[rsrc, rinvc], writes=[rt])
                        P.op("dve", lambda e, u_=u_, m_=m_, tt=tt: e.tensor_tensor(
                            out=m_[:, HALO:HALO + 32], in0=tt[:, 0:32], in1=u_[:, HALO:HALO + 32], op=ALU.subtract),
                            reads=[rt, ru, rm], writes=[rm])
                    P.op("pe", lambda e, g=g, m_=m_: e.matmul(ps[7][:, :n], lhsT=wp[:, g, :], rhs=m_[:, :n], start=True, stop=True),
                         reads=[rm, (tag, "wp")], writes=[rps[7]])
                    p_ = po[a]
                    rp_ = (tag, "po", a)
                    P.op("act", lambda e, g=g, p_=p_: e.activation(out=p_[:, :n], in_=ps[7][:, :n], func=AF.Identity,
                                                                   scale=vec[:, 10 + g:11 + g]),
                         reads=[rps[7], rvec], writes=[rp_])
                    P.op("pool", lambda e, g=g, p_=p_: e.dma_start(out=dr["Pout"][g * 128:(g + 1) * 128, o0:o0 + n], in_=p_[:, :n]),
                         reads=[rp_], writes=[(tag, "Pout")], dma_key=("po", a))
            if not own:
                return
            for kc in range(KC):
                P.op("pe", lambda e, kc=kc: e.matmul(ps[6][0:8, :nu], lhsT=wf[:, kc, :], rhs=H[:, kc, :nu],
                                                     start=(kc == 0), stop=(kc == KC - 1)),
                     reads=[(tag, "wf"), rH], writes=[rps[6]])
            a = cnt["f"] % 2
            cnt["f"] += 1
            P.op("act", lambda e: e.activation(out=fe[:, :nu], in_=ps[6][0:8, :nu], func=AF.Exp, scale=-1.0, bias=nbf[:, 0:1]),
                 reads=[rps[6], (tag, "nbf")], writes=[(tag, "fe")])
            P.op("act", lambda e, a=a: e.activation(out=fo[a][:, :nu], in_=fe[:, :nu], func=AF.Ln, scale=1.0, bias=cx.one_t[0:8, 0:1]),
                 reads=[(tag, "fe"), ("c", "one")], writes=[(tag, "fo", a)])
            P.op("pool", lambda e, a=a: e.dma_start(out=dr["Fc"][:, tk:tk + (hi - lo)], in_=fo[a][:, c_lo:c_hi]),
                 reads=[(tag, "fo", a)], writes=[(tag, "Fc")], dma_key=("fo", a))
            c0 = c_lo
            while c0 < c_hi:
                m = min(128, c_hi - c0)
                b = cnt["v"] % 2
                cnt["v"] += 1
                pv, rpv = ps[4 + b], rps[4 + b]
                for kc in range(KC):
                    P.op("pe", lambda e, kc=kc, c0=c0, m=m, pv=pv: e.matmul(
                        pv[:m, :], lhsT=H[:, kc, c0:c0 + m], rhs=wv[:, kc, :], start=(kc == 0), stop=(kc == KC - 1)),
                        reads=[rH, (tag, "wv")], writes=[rpv])
                v_ = vo[b]
                rv = (tag, "vo", b)
                P.op("act", lambda e, m=m, pv=pv, v_=v_: e.activation(out=v_[:m, :], in_=pv[:m, :], func=AF.Copy),
                     reads=[rpv], writes=[rv])
                tok = tk + (c0 - c_lo)
                P.op("pool", lambda e, m=m, v_=v_, tok=tok: e.dma_start(out=dr["Vc"][tok:tok + m, :], in_=v_[:m, :]),
                     reads=[rv], writes=[(tag, "Vc")], dma_key=("vo", b))
                c0 += m

        ng = len(groups)
        pre(0)
        for gi in range(ng):
            if gi + 1 < ng:
                pre(gi + 1)
            proj(gi)


GROUPS4 = [[0, 1, 2, 3], [4, 5, 6, 7]]
BG_CAST = True
import os
ADBG = int(os.environ.get("ATT_DBG", "5"))


def attn_phase(cx, cfg, tag, bg_casts=None):
    P = cx.P
    dr = cx.dram
    CH_, SEQ_, NKB, NQB = cfg.CH, cfg.SEQ, cfg.NKB, cfg.NQB
    segl = SEQ_ // 64
    nbs = CH_ // 128
    q4 = CH_ // 4
    for cc in range(4):
        for nm in ("Q", "K"):
            P.op("pool", lambda e, nm=nm, cc=cc: e.collective_compute(
                "AllGather", ALU.bypass, replica_groups=GROUPS4,
                ins=[dr[nm + "c"][cc * 128:(cc + 1) * 128, :].opt()], outs=[dr[nm + "g"][cc * 512:(cc + 1) * 512, :].opt()]),
                reads=[(tag, nm + "c")], writes=[(tag, nm + "g")], dma_key=("G", "cc" + nm), inc=1)
        P.op("pool", lambda e, cc=cc: e.collective_compute(
            "AllGather", ALU.bypass, replica_groups=GROUPS4,
            ins=[dr["Vc"][cc * q4:(cc + 1) * q4, :].opt()], outs=[dr["Vg"][cc * CH_:(cc + 1) * CH_, :].opt()]),
            reads=[(tag, "Vc")], writes=[(tag, "Vg")], dma_key=("G", "ccV"), inc=1)
    P.op("pool", lambda e: e.collective_compute(
        "AllGather", ALU.bypass, replica_groups=GROUPS4, ins=[dr["Fc"].opt()], outs=[dr["Fg"].opt()]),
        reads=[(tag, "Fc")], writes=[(tag, "Fg")], dma_key=("G", "ccF"), inc=1)
    cx.chk("attn_ag")
    with Phase(cx):
        KT = [cx.sb([128, SEQ_], BF16, "KT") for _ in range(2)]
        VT = [cx.sb([128, NKB, 128], BF16, "VT") for _ in range(2)]
        Lm = cx.sb([128, 128], F32, "Lm")
        sp = cx.sb([128, segl], F32, "sp")
        onesl = cx.sb([128, segl], F32, "onesl")
        cl = cx.sb([128, segl], F32, "cl")
        offs = cx.sb([128, 1], F32, "offs")
        r1 = cx.sb([128, segl], F32, "r1")
        hi = cx.sb([128, segl], BF16, "hi")
        mid = cx.sb([128, segl], BF16, "mid")
        lo = cx.sb([128, segl], BF16, "lo")
        nhi = cx.sb([128, segl], BF16, "nhi")
        nmid = cx.sb([128, segl], BF16, "nmid")
        nlo = cx.sb([128, segl], BF16, "nlo")
        onesb = cx.sb([128, segl], BF16, "onesb")
        zt = cx.sb([128, 4, HALO], BF16, "zt")
        NQT = 3
        qt = [cx.sb([128, 512], BF16, "qt") for _ in range(NQT)]
        NPT = 6
        pt = [cx.sb([128, 512], BF16, "pt") for _ in range(NPT)]
        rec = [cx.sb([64, 512], F32, "rec") for _ in range(2)]
        ost = [cx.sb([64, 512], BF16, "ost") for _ in range(2)]
        NS = 5
        ps_s = [cx.ps([128, 512], F32, "pss") for _ in range(NS)]
        ps_o = [cx.ps([128, 512], F32, "pso") for _ in range(2)]
        ps_m = cx.ps([128, 512], F32, "psm")

        P.op("dve", lambda e: e.memset(Lm[:, :], 1.0), writes=[(tag, "Lm")])
        P.op("pool", lambda e: e.affine_select(out=Lm[:, :], in_=Lm[:, :], pattern=[[1, 128]], compare_op=ALU.is_gt,
                                               fill=0.0, base=0, channel_multiplier=-1),
             reads=[(tag, "Lm")], writes=[(tag, "Lm")])
        P.op("dve", lambda e: e.memset(Lm[0:64, 64:128], 0.0), reads=[(tag, "Lm")], writes=[(tag, "Lm")])
        P.op("dve", lambda e: e.memset(onesl[:, :], 1.0), writes=[(tag, "onesl")])
        P.op("dve", lambda e: e.memset(onesb[:, :], 1.0), writes=[(tag, "onesb")])
        P.op("dve", lambda e: e.memset(zt[:, :, :], 0.0), writes=[(tag, "zt")])
        P.op("sp", lambda e: e.dma_start(out=dr["Og"][0:512, CH_ - HALO:CH_].rearrange("(a p) t -> p a t", p=128), in_=zt[:, :, :]),
             reads=[(tag, "zt")], writes=[(tag, "Ogpad")], dma_key=("G", "small"))
        for h in range(2):
            P.op("dve", lambda e, h=h: e.memset(VT[h][:, :, 64:128], 1.0), writes=[(tag, "VTo", h)])

        def loc_k(e, nm):
            i4 = pid4(e)
            src = dr[nm + "g"].rearrange("(c s r) t -> c s r t", c=4, s=4)[i4, :, :, :]
            return e.dma_start(out=dr[nm + "l"].rearrange("r (s t) -> s r t", s=4), in_=src)

        def loc_v(e, j):
            i4 = pid4(e)
            src = dr["Vg"][j * CH_:(j + 1) * CH_, :].rearrange("(s n) (i c) -> s n i c", s=4, i=4)[:, :, i4, :]
            return e.dma_start(out=dr["Vl"].rearrange("(s j n) c -> j s n c", s=4, j=4)[j], in_=src)

        def loc_f(e):
            i4 = pid4(e)
            src = dr["Fg"].rearrange("(s i h) t -> s i h t", s=4, i=4)[:, i4, :, :]
            return e.dma_start(out=dr["Fl"].rearrange("(s h) t -> s h t", s=4), in_=src)

        P.op("sp", loc_f, reads=[(tag, "Fg")], writes=[(tag, "Fl")], dma_key=("G", "locf"))
        P.op("sp", lambda e: loc_k(e, "K"), reads=[(tag, "Kg")], writes=[(tag, "Kl")], dma_key=("G", "lock"))
        P.op("sp", lambda e: loc_k(e, "Q"), reads=[(tag, "Qg")], writes=[(tag, "Ql")], dma_key=("G", "locq"))
        for j in range(4):
            P.op("sp", lambda e, j=j: loc_v(e, j), reads=[(tag, "Vg")], writes=[(tag, "Vl")], dma_key=("G", "locv"))

        cx.chk("attn_loc")

        def ld_k(e, h, s):
            return e.dma_start(out=KT[h][0:64, s * CH_:(s + 1) * CH_], in_=dr["Kl"][h * 64:(h + 1) * 64, s * CH_:(s + 1) * CH_])

        def ld_v(e, h, s):
            src = dr["Vl"][s * CH_:(s + 1) * CH_, h * 64:(h + 1) * 64].rearrange("(b p) c -> p b c", p=128)
            return e.dma_start(out=VT[h][:, s * nbs:(s + 1) * nbs, 0:64], in_=src)

        def ld_f(e, h, s):
            src = dr["Fl"][s * 2 + h:s * 2 + h + 1, :].rearrange("o (j t) -> (o j) t", t=segl)
            return e.dma_start(out=sp[h * 64 + s * 16:h * 64 + (s + 1) * 16, :], in_=src)

        for h in range(2):
            for s in range(4):
                P.op("sp", lambda e, h=h, s=s: ld_f(e, h, s), reads=[(tag, "Fl")], writes=[(tag, "sp", h, s)],
                     dma_key=("G", "ldf"))
        for h in range(2):
            for s in range(4):
                P.op("sp", lambda e, h=h, s=s: ld_k(e, h, s), reads=[(tag, "Kl")], writes=[(tag, "KT", h, s)],
                     dma_key=("G", "ldk"))
        rsp = [(tag, "sp", h, s) for h in range(2) for s in range(4)]
        P.op("dve", lambda e: e.tensor_tensor_scan(out=cl[:, :], data0=onesl[:, :], data1=sp[:, :], initial=0.0,
                                                   op0=ALU.mult, op1=ALU.add),
             reads=rsp + [(tag, "onesl")], writes=[(tag, "cl")])
        P.op("pe", lambda e: e.matmul(ps_m[:, 0:2], lhsT=Lm[:, :], rhs=cl[:, segl - 2:segl], start=True, stop=True),
             reads=[(tag, "cl"), (tag, "Lm")], writes=[(tag, "psm")])
        P.op("dve", lambda e: e.tensor_copy(out=offs[:, :], in_=ps_m[:, 1:2]), reads=[(tag, "psm")], writes=[(tag, "offs")])
        P.op("dve", lambda e: e.tensor_scalar(out=cl[:, :], in0=cl[:, :], scalar1=offs[:, 0:1], scalar2=None, op0=ALU.add),
             reads=[(tag, "cl"), (tag, "offs")], writes=[(tag, "cl")])
        P.op("dve", lambda e: e.tensor_copy(out=hi[:, :], in_=cl[:, :]), reads=[(tag, "cl")], writes=[(tag, "hi")])
        P.op("dve", lambda e: e.tensor_tensor(out=r1[:, :], in0=cl[:, :], in1=hi[:, :], op=ALU.subtract),
             reads=[(tag, "cl"), (tag, "hi")], writes=[(tag, "r1")])
        P.op("dve", lambda e: e.tensor_copy(out=mid[:, :], in_=r1[:, :]), reads=[(tag, "r1")], writes=[(tag, "mid")])
        P.op("dve", lambda e: e.tensor_tensor(out=r1[:, :], in0=r1[:, :], in1=mid[:, :], op=ALU.subtract),
             reads=[(tag, "r1"), (tag, "mid")], writes=[(tag, "r1")])
        P.op("dve", lambda e: e.tensor_copy(out=lo[:, :], in_=r1[:, :]), reads=[(tag, "r1")], writes=[(tag, "lo")])
        for src_, dst_, nm in ((hi, nhi, "nhi"), (mid, nmid, "nmid"), (lo, nlo, "nlo")):
            P.op("dve", lambda e, src_=src_, dst_=dst_: e.tensor_scalar(out=dst_[:, :], in0=src_[:, :], scalar1=-1.0, scalar2=None, op0=ALU.mult),
                 reads=[(tag, "hi"), (tag, "mid"), (tag, "lo")], writes=[(tag, nm)])
        cx.chk("attn_cs")
        k = 0
        for h in range(2):
            for r, (tk_, tq_) in enumerate(((hi, onesb), (mid, onesb), (lo, onesb), (onesb, nhi), (onesb, nmid), (onesb, nlo))):
                for dst_nm, t_ in (("AugK", tk_), ("AugQ", tq_)):
                    P.op("sp", lambda e, h=h, r=r, dst_nm=dst_nm, t_=t_: e.dma_start(
                        out=dr[dst_nm][h * 6 + r:h * 6 + r + 1, :].rearrange("o (j t) -> (o j) t", t=segl), in_=t_[h * 64:(h + 1) * 64, :]),
                        reads=[(tag, "hi"), (tag, "mid"), (tag, "lo"), (tag, "nhi"), (tag, "nmid"), (tag, "nlo"), (tag, "onesb")],
                        writes=[(tag, dst_nm, h)], dma_key=("G", "aug"))
                    k += 1
        cx.chk("attn_aug")
        for h in range(2):
            P.op("sp", lambda e, h=h: e.dma_start(out=KT[h][64:70, :], in_=dr["AugK"][h * 6:(h + 1) * 6, :]),
                 reads=[(tag, "AugK", h)], writes=[(tag, "KTa", h)], dma_key=("G", "ldka"))
        cx.chk("attn_ka")
        for h in range(2):
            for s in range(4):
                P.op("sp", lambda e, h=h, s=s: ld_v(e, h, s), reads=[(tag, "Vl")], writes=[(tag, "VT", h, s)],
                     dma_key=("G", "ldv"))

        cx.chk("attn_ld")
        bg = None
        if bg_casts:
            bcf = [cx.sb([128, 1024], F32, "bcf") for _ in range(3)]
            bcb = [cx.sb([128, 1024], BF16, "bcb") for _ in range(3)]
            bg = cast_iter(cx, bg_casts, bcf, bcb, 1024)
        steps = []
        DEPTH = 4
        state = {"qi": -1, "oi": -1}
        qinfo = {}

        def load_q(h, qb):
            state["qi"] += 1
            qs = state["qi"] % NQT
            rq = (tag, "qt", qs)
            Q0 = qb * 512
            s = Q0 // CH_
            c0 = Q0 % CH_

            P.op("sp", lambda e: e.dma_start(out=qt[qs][0:64, :], in_=dr["Ql"][h * 64:(h + 1) * 64, Q0:Q0 + 512]),
                 reads=[(tag, "Ql")], writes=[rq], dma_key=("ldq", qs))
            P.op("sp", lambda e: e.dma_start(out=qt[qs][64:70, :], in_=dr["AugQ"][h * 6:(h + 1) * 6, Q0:Q0 + 512]),
                 reads=[(tag, "AugQ", h)], writes=[(tag, "qta", qs)], dma_key=("ldqa", qs))
            qinfo[(h, qb)] = qs

        def qk(i):
            h, qb, kb, nk = steps[i]
            if kb == 0:
                load_q(h, qb)
            qs = qinfo[(h, qb)]
            sb_i = i % NS
            j = kb - 4 * qb
            a = 128 * j if j > 0 else 0
            diag = j >= 0
            s_src = (kb * 128) // CH_
            rk = [(tag, "KT", h, s_src), (tag, "KTa", h)]
            P.op("pe", lambda e: e.matmul(ps_s[sb_i][:, a:512], lhsT=KT[h][0:70, kb * 128:(kb + 1) * 128],
                                          rhs=qt[qs][0:70, a:512], start=True, stop=not diag),
                 reads=rk + [(tag, "qt", qs), (tag, "qta", qs)], writes=[(tag, "pss", sb_i)])
            if diag and ADBG >= 2:
                P.op("pe", lambda e: e.matmul(ps_s[sb_i][:, a:a + 128], lhsT=cx.ident_b[:, :], rhs=cx.maskb[:, :],
                                              start=False, stop=True),
                     reads=[("c", "ident"), ("c", "maskb")], writes=[(tag, "pss", sb_i)])
            pi = i % NPT
            if ADBG < 3:
                return
            P.op("act", lambda e: e.activation(out=pt[pi][:, a:512], in_=ps_s[sb_i][:, a:512], func=AF.Exp),
                 reads=[(tag, "pss", sb_i)], writes=[(tag, "pt", pi)])

        def pv(i):
            if ADBG < 4:
                return
            h, qb, kb, nk = steps[i]
            j = kb - 4 * qb
            a = 128 * j if j > 0 else 0
            pi = i % NPT
            s_src = (kb * 128) // CH_
            if kb == 0:
                state["oi"] += 1
            ob = state["oi"] % 2
            P.op("pe", lambda e: e.matmul(ps_o[ob][:, a:512], lhsT=VT[h][:, kb, :], rhs=pt[pi][:, a:512],
                                          start=(kb == 0), stop=(kb == nk - 1)),
                 reads=[(tag, "VT", h, s_src), (tag, "VTo", h), (tag, "pt", pi)], writes=[(tag, "pso", ob)])
            if kb == nk - 1 and ADBG >= 5:
                Q0 = qb * 512
                P.op("dve", lambda e: e.reciprocal(out=rec[ob][:, :], in_=ps_o[ob][64:128, :]),
                     reads=[(tag, "pso", ob)], writes=[(tag, "rec", ob)])
                P.op("dve", lambda e: e.tensor_tensor(out=ost[ob][:, :], in0=ps_o[ob][0:64, :], in1=rec[ob][:, :], op=ALU.mult),
                     reads=[(tag, "pso", ob), (tag, "rec", ob)], writes=[(tag, "ost", ob)])
                jc, c0_ = Q0 // CH_, Q0 % CH_
                P.op("pool", lambda e: e.dma_start(out=dr["Oc"][jc * 128 + h * 64:jc * 128 + (h + 1) * 64, c0_:c0_ + 512], in_=ost[ob][:, :]),
                     reads=[(tag, "ost", ob)], writes=[(tag, "Oc")], dma_key=("ost", ob))

        for hh in range(2):
            base = len(steps)
            for qb in range(NQB):
                nk = 4 * qb + 4
                for kb in range(nk):
                    steps.append((hh, qb, kb, nk))
            n = len(steps)
            for i in range(base, n + DEPTH):
                if i < n:
                    qk(i)
                if i - DEPTH >= base:
                    pv(i - DEPTH)
                if bg is not None and i % 8 == 0:
                    if next(bg, None) is None:
                        bg = None
            if hh == 0:
                P.barrier()
        while bg is not None:
            if next(bg, None) is None:
                bg = None
    for jc in range(4):
        P.op("pool", lambda e, jc=jc: e.collective_compute(
            "AllGather", ALU.bypass, replica_groups=GROUPS4,
            ins=[dr["Oc"][jc * 128:(jc + 1) * 128, :].opt()], outs=[dr["Og"][(jc + 1) * 512:(jc + 2) * 512, :].opt()]),
            reads=[(tag, "Oc")], writes=[(tag, "Og")], dma_key=("G", "ccO"), inc=1)


def even_post_phase(cx, cfg, X_in, X_out, wo_bf, tag):
    P = cx.P
    dr = cx.dram
    groups = make_groups(cfg.T, 512, 512, 0)
    Xi = X_in.rearrange("(kc p) t -> p kc t", p=128)
    Xo = X_out.rearrange("(kc p) t -> p kc t", p=128)
    Alv = dr["Al"].rearrange("(kc p) t -> p kc t", p=128)
    Pov = dr["Pout"].rearrange("(kc p) t -> p kc t", p=128)

    Og3 = dr["Og"].rearrange("(j r) t -> j r t", r=512)

    def loc_a(e):
        i4 = pid4(e)
        return e.dma_start(out=dr["Al"][:, HALO:], in_=Og3[i4 + 1, :, :])

    def loc_h(e):
        i4 = pid4(e)
        return e.dma_start(out=dr["Al"][:, 0:HALO], in_=Og3[i4, :, cfg.CH - HALO:cfg.CH])
    P.op("sp", loc_a, reads=[(tag, "Og")], writes=[(tag, "Al")], dma_key=("G", "loca"))
    P.op("sp", loc_h, reads=[(tag, "Og")], writes=[(tag, "Alh")], dma_key=("G", "loca"))
    with Phase(cx):
        xt = [cx.sb([128, KC, 512], F32, "xt") for _ in range(2)]
        At = [cx.sb([128, KC, 512], BF16, "At") for _ in range(2)]
        wo = [cx.sb([128, KC, 128], BF16, "wo") for _ in range(3)]
        xo = [cx.sb([128, KC, 512], F32, "xo") for _ in range(2)]
        ps = [cx.ps([128, 512], F32, "ps") for _ in range(2)]
        cnt = {"w": 0, "o": 0}
        for gi, (o0, n, h) in enumerate(groups):
            s = gi % 2
            rx, rA, rxo = (tag, "xt", s), (tag, "At", s), (tag, "xo", s)
            P.op("sp", lambda e, s=s, o0=o0, n=n: e.dma_start(out=xt[s][:, :, :n], in_=Xi[:, :, o0:o0 + n]),
                 reads=[(tag, "Xin")], writes=[rx], dma_key=("xt", s))

            P.op("sp", lambda e, s=s, o0=o0, n=n: e.dma_start(out=At[s][:, 0:4, :n], in_=Alv[:, :, o0:o0 + n]),
                 reads=[(tag, "Al"), (tag, "Alh")], writes=[(tag, "Ata", s)], dma_key=("Ata", s))
            P.op("sp", lambda e, s=s, o0=o0, n=n: e.dma_start(out=At[s][:, 4:8, :n], in_=Pov[:, :, o0:o0 + n]),
                 reads=[(tag, "Pout")], writes=[(tag, "Atp", s)], dma_key=("Atp", s))
            for oc in range(KC):
                ws = cnt["w"] % 3
                cnt["w"] += 1
                rw = (tag, "wo", ws)
                P.op("sp", lambda e, oc=oc, ws=ws: e.dma_start(out=wo[ws][:, :, :].rearrange("p k o -> p (k o)"), in_=wo_bf[oc]),
                     writes=[rw], dma_key=("wo", ws))
                b = cnt["o"] % 2
                cnt["o"] += 1
                rp = (tag, "ps", b)
                for kc in range(KC):
                    P.op("pe", lambda e, kc=kc, ws=ws, b=b, s=s, n=n: e.matmul(
                        ps[b][:, :n], lhsT=wo[ws][:, kc, :], rhs=At[s][:, kc, :n], start=(kc == 0), stop=(kc == KC - 1)),
                        reads=[rw, (tag, "Ata", s), (tag, "Atp", s)], writes=[rp])
                P.op("dve", lambda e, oc=oc, b=b, s=s, n=n: e.tensor_tensor(
                    out=xo[s][:, oc, :n], in0=ps[b][:, :n], in1=xt[s][:, oc, :n], op=ALU.add),
                    reads=[rp, rx], writes=[rxo])
            store_group(cx, tag, xo[s], rxo, Xo, o0, n, gi == 0)


def cast_all(cx, items):
    P = cx.P
    CWD = 4096
    with Phase(cx):
        st_f = [cx.sb([128, CWD], F32, "cst_f") for _ in range(3)]
        st_b = [cx.sb([128, CWD], BF16, "cst_b") for _ in range(3)]
        i = 0
        for src, dst, rows, cols in items:
            for r0 in range(0, rows, 128):
                for c0 in range(0, cols, CWD):
                    cw_ = min(CWD, cols - c0)
                    s = i % 3
                    rf, rb = ("cf", s), ("cb", s)
                    P.op("sp", lambda e, src=src, r0=r0, c0=c0, cw_=cw_, s=s: e.dma_start(
                        out=st_f[s][:, :cw_], in_=src[r0:r0 + 128, c0:c0 + cw_]), writes=[rf], dma_key=rf)
                    if i % 2 == 0:
                        P.op("dve", lambda e, cw_=cw_, s=s: e.tensor_copy(out=st_b[s][:, :cw_], in_=st_f[s][:, :cw_]),
                             reads=[rf], writes=[rb])
                    else:
                        P.op("act", lambda e, cw_=cw_, s=s: e.activation(out=st_b[s][:, :cw_], in_=st_f[s][:, :cw_], func=AF.Copy),
                             reads=[rf], writes=[rb])
                    P.op("pool", lambda e, dst=dst, r0=r0, c0=c0, cw_=cw_, s=s: e.dma_start(
                        out=dst[r0:r0 + 128, c0:c0 + cw_], in_=st_b[s][:, :cw_]),
                        reads=[rb], writes=[("cdst",)], dma_key=("cst", s))
                    i += 1


def cast_iter(cx, items, st_f, st_b, cwd):
    P = cx.P
    i = 0
    nb = len(st_f)
    for src, dst, rows, cols in items:
        for r0 in range(0, rows, 128):
            for c0 in range(0, cols, cwd):
                cw_ = min(cwd, cols - c0)
                s = i % nb
                rf, rb = ("bcf", s), ("bcb", s)
                P.op("sp", lambda e, src=src, r0=r0, c0=c0, cw_=cw_, s=s: e.dma_start(
                    out=st_f[s][:, :cw_], in_=src[r0:r0 + 128, c0:c0 + cw_]), writes=[rf], dma_key=("cf", s))
                P.op("dve", lambda e, cw_=cw_, s=s: e.tensor_copy(out=st_b[s][:, :cw_], in_=st_f[s][:, :cw_]),
                     reads=[rf], writes=[rb])
                P.op("pool", lambda e, dst=dst, r0=r0, c0=c0, cw_=cw_, s=s: e.dma_start(
                    out=dst[r0:r0 + 128, c0:c0 + cw_], in_=st_b[s][:, :cw_]),
                    reads=[rb], writes=[("bcdst", i)], dma_key=("cst", s))
                i += 1
                yield i


def build_diag(cx, dww_dram, dg_bf, tag):
    P = cx.P
    with Phase(cx):
        dww, rdw = load_small(cx, tag, "dww", [128, KC * CW31], dww_dram)
        dst = [cx.sb([128, CW31, 128], BF16, "dgst") for _ in range(2)]
        for j in range(KC):
            s = j % 2
            rs = (tag, "dgst", s)
            for k in range(CW31):
                P.op("pool", lambda e, j=j, k=k, s=s: e.tensor_scalar(
                    out=dst[s][:, k, :], in0=cx.ident_b[:, :], scalar1=dww[:, j * CW31 + k:j * CW31 + k + 1], scalar2=0.0,
                    op0=ALU.mult, op1=ALU.add),
                    reads=[rdw, ("c", "ident")], writes=[rs])
            P.op("sp", lambda e, j=j, s=s: e.dma_start(out=dg_bf[j], in_=dst[s][:, :, :].rearrange("p k o -> p (k o)")),
                 reads=[rs], writes=[(tag, "dgbf")], dma_key=("dgst", s))


WSPEC_E = (("wqk", 12 * 128, 1024), ("wf", 128, 64), ("wv", 128, 4096), ("wp", 128, 512), ("wo", 8 * 128, 1024))
WSPEC_O = (("w1", 8 * 128, 2048), ("w2", 8 * 128, 1024))
WSPEC_F = (("wup", 22 * 128, 2048), ("wdn", 8 * 128, 2816))


def build_program(cfg, n_layers=4, do_ffn=True, stop=None):
    nc = bass.Bass("TRN2", target_bir_lowering=False)
    T_, CH_, SEQ_ = cfg.T, cfg.CH, cfg.SEQ

    def din(name, shape, dt=F32):
        return nc.dram_tensor(name, list(shape), dt, kind="ExternalInput").ap()

    def dsc(name, shape, dt):
        return nc.dram_tensor(name, list(shape), dt).ap()

    dr = {}
    dr["x"] = din("x", [D, T_])
    dr["flag"] = din("flag", [128, 1])
    dr["invc"] = din("invc", [128, 4 * 32])
    Y = nc.dram_tensor("y", [D, CH_], F32, kind="ExternalOutput").ap()
    XA = dsc("XA", [D, T_], F32)
    XB = dsc("XB", [D, T_], F32)
    W = {}
    casts = []
    n_even = (n_layers + 1) // 2
    n_odd = n_layers // 2
    for e_ in range(n_even):
        for nm, r, c in WSPEC_E:
            f = din("e%d_%s" % (e_, nm), [r, c])
            b = dsc("e%d_%s_b" % (e_, nm), [r, c], BF16)
            W[("e", e_, nm)] = b
            casts.append((f, b, r, c, "e%d" % e_))
        W[("e", e_, "vec")] = din("e%d_vec" % e_, [128, 16])
        W[("e", e_, "bf")] = din("e%d_bf" % e_, [8, 1])
    for o_ in range(n_odd):
        for nm, r, c in WSPEC_O:
            f = din("o%d_%s" % (o_, nm), [r, c])
            b = dsc("o%d_%s_b" % (o_, nm), [r, c], BF16)
            W[("o", o_, nm)] = b
            casts.append((f, b, r, c, "o"))
        W[("o", o_, "vec")] = din("o%d_vec" % o_, [128, 32])
        W[("o", o_, "dww")] = din("o%d_dww" % o_, [128, KC * CW31])
        W[("o", o_, "dg")] = dsc("o%d_dg" % o_, [KC * 128, CW31 * 128], BF16)
    if do_ffn:
        for l in range(n_layers):
            for nm, r, c in WSPEC_F:
                f = din("f%d_%s" % (l, nm), [r, c])
                b = dsc("f%d_%s_b" % (l, nm), [r, c], BF16)
                W[("f", l, nm)] = b
                casts.append((f, b, r, c, "f"))
            W[("f", l, "g")] = din("f%d_g" % l, [128, 8])
            W[("f", l, "cw")] = din("f%d_cw" % l, [128, 132])
            W[("f", l, "cb")] = din("f%d_cb" % l, [128, 44])
    dr["Qc"] = dsc("Qc", [512, CH_], BF16)
    dr["Kc"] = dsc("Kc", [512, CH_], BF16)
    dr["Vc"] = dsc("Vc", [CH_, 512], BF16)
    dr["Fc"] = dsc("Fc", [8, CH_], F32)
    dr["Qg"] = dsc("Qg", [4 * 512, CH_], BF16)
    dr["Kg"] = dsc("Kg", [4 * 512, CH_], BF16)
    dr["Vg"] = dsc("Vg", [4 * CH_, 512], BF16)
    dr["Fg"] = dsc("Fg", [4 * 8, CH_], F32)
    dr["Ql"] = dsc("Ql", [128, SEQ_], BF16)
    dr["Kl"] = dsc("Kl", [128, SEQ_], BF16)
    dr["Vl"] = dsc("Vl", [SEQ_, 128], BF16)
    dr["Fl"] = dsc("Fl", [8, CH_], F32)
    dr["Al"] = dsc("Al", [512, T_], BF16)
    dr["AugK"] = dsc("AugK", [12, SEQ_], BF16)
    dr["AugQ"] = dsc("AugQ", [12, SEQ_], BF16)
    dr["Oc"] = dsc("Oc", [4 * 128, CH_], BF16)
    dr["Og"] = dsc("Og", [5 * 512, CH_], BF16)
    dr["Pout"] = dsc("Pout", [512, T_], BF16)

    def slabs(ap, p=128):
        return ap.rearrange("(j p) c -> j p c", p=p)

    with contextlib.ExitStack() as stack:
        cx = Ctx(nc, stack)
        cx.dram = dr
        setup_consts(cx)
        cx.P.barrier()
        class _Stop(Exception):
            pass

        def chk(name):
            if stop == name:
                raise _Stop()
        cx.chk = chk
        try:
            chk("consts")
            first = [c for c in casts if c[4] == "e0"]
            rest = [c[:4] for c in casts if c[4] != "e0"]
            cast_all(cx, [c[:4] for c in first] + ([] if n_layers > 0 and BG_CAST else rest))
            cx.bg_casts = rest if BG_CAST else None
            chk("cast")
            for o_ in range(n_odd):
                build_diag(cx, W[("o", o_, "dww")], slabs(W[("o", o_, "dg")]), "dg%d" % o_)
            chk("diag")
            _build_layers(cx, cfg, dr, W, XA, XB, Y, n_layers, do_ffn, slabs, chk)
        except _Stop:
            pass
        cx.P.emit()
    return nc, None


def _build_layers(cx, cfg, dr, W, XA, XB, Y, n_layers, do_ffn, slabs, chk):
    if True:
        cur = dr["x"]
        bufs = [XA, XB]
        bi = 0
        n_sub = n_layers * (2 if do_ffn else 1)
        sub = 0

        def nxt():
            nonlocal bi
            b = bufs[bi]
            bi ^= 1
            return b
        for l in range(n_layers):
            i2 = l // 2
            if l % 2 == 0:
                tag = "E%d" % l
                even_pre_phase(cx, cfg, cur, slabs(W[("e", i2, "wqk")]), W[("e", i2, "wf")], W[("e", i2, "wv")],
                               W[("e", i2, "wp")], W[("e", i2, "vec")], W[("e", i2, "bf")], tag + "a")
                chk("E1")
                attn_phase(cx, cfg, tag + "b", bg_casts=cx.bg_casts if l == 0 else None)
                cx.P.barrier()
                chk("attn")
                sub += 1
                last = (sub == n_sub)
                pass
                dst = nxt()
                even_post_phase(cx, cfg, cur, dst, slabs(W[("e", i2, "wo")]), tag + "c")
                cur = dst
                chk("E3")
            else:
                tag = "O%d" % l
                sub += 1
                dst = nxt()
                odd_phase(cx, cfg, cur, dst, slabs(W[("o", i2, "w1")]), slabs(W[("o", i2, "w2")]),
                          slabs(W[("o", i2, "dg")]), W[("o", i2, "vec")], tag)
                cur = dst
            if do_ffn:
                sub += 1
                last = (sub == n_sub)
                if last:
                    ffn_phase(cx, cfg, cur, Y, slabs(W[("f", l, "wup")]), slabs(W[("f", l, "wdn")]),
                              W[("f", l, "g")], W[("f", l, "cw")], W[("f", l, "cb")], "F%d" % l, out_off=HALO)
                else:
                    dst = nxt()
                    ffn_phase(cx, cfg, cur, dst, slabs(W[("f", l, "wup")]), slabs(W[("f", l, "wdn")]),
                              W[("f", l, "g")], W[("f", l, "cw")], W[("f", l, "cb")], "F%d" % l)
                    cur = dst
        if not do_ffn:
            cx.P.op("sp", lambda e: e.dma_start(out=Y, in_=cur[:, HALO:]), dma_key=("G", "fin"))


def lay_slabs(w):
    n = w.shape[1] // 128
    a = w.reshape(KC, 128, n, 128).transpose(2, 1, 0, 3)
    return np.ascontiguousarray(a).reshape(n * 128, KC * 128)


def lay_pairs(w, half):
    n = half // 128
    a = w.reshape(KC, 128, 2, n, 128).transpose(3, 1, 2, 0, 4)
    return np.ascontiguousarray(a).reshape(n * 128, 2 * KC * 128)


def lay_kmajor(w):
    c = w.shape[1]
    return np.ascontiguousarray(w.reshape(KC, 128, c).transpose(1, 0, 2)).reshape(128, KC * c)


def lay_wdn(w_down):
    a = w_down.reshape(NJ, 128, KC, 128).transpose(2, 1, 0, 3)
    return np.ascontiguousarray(a).reshape(KC * 128, NJ * 128)


def lay_vec(v, nch):
    return np.ascontiguousarray(v.reshape(nch, 128).T)


def lay_cw(cw):
    k = cw.shape[0]
    n = cw.shape[1] // 128
    return np.ascontiguousarray(cw.reshape(k, n, 128).transpose(2, 1, 0)).reshape(128, n * k)


def host_inputs(cfg, inp, n_layers=4, do_ffn=True):
    f32 = np.float32
    shared = {}
    n_even = (n_layers + 1) // 2
    n_odd = n_layers // 2
    for e_ in range(n_even):
        w_in = np.asarray(inp["even_w_in"][e_], f32)
        q, k, v = w_in[:, 0:512], w_in[:, 512:1024], w_in[:, 1024:1536]
        f, u = w_in[:, 1536:1544], w_in[:, 1544:2056]
        shared["e%d_wqk" % e_] = lay_slabs(np.concatenate([q, k, u], axis=1))
        shared["e%d_wf" % e_] = lay_kmajor(f)
        shared["e%d_wv" % e_] = lay_kmajor(v)
        wp = np.asarray(inp["even_w_pool"][e_], f32)
        shared["e%d_wp" % e_] = np.ascontiguousarray(wp.transpose(1, 0, 2)).reshape(128, 512)
        shared["e%d_wo" % e_] = lay_slabs(np.asarray(inp["even_w_out"][e_], f32))
        vec = np.zeros((128, 16), f32)
        vec[:, 0:8] = lay_vec(np.asarray(inp["even_norm_g"][e_], f32), 8)
        vec[:, 8] = np.tile(np.asarray(inp["even_q_norm_g"][e_], f32), 2)
        vec[:, 9] = np.tile(np.asarray(inp["even_k_norm_g"][e_], f32), 2)
        vec[:, 10:14] = lay_vec(np.asarray(inp["even_pool_scale"][e_], f32), 4)
        shared["e%d_vec" % e_] = vec
        shared["e%d_bf" % e_] = np.asarray(inp["even_b_f"][e_], f32).reshape(8, 1).copy()
    for o_ in range(n_odd):
        shared["o%d_w1" % o_] = lay_pairs(np.asarray(inp["odd_w_pw1"][o_], f32), 1024)
        shared["o%d_w2" % o_] = lay_slabs(np.asarray(inp["odd_w_pw2"][o_], f32))
        vec = np.zeros((128, 32), f32)
        vec[:, 0:8] = lay_vec(np.asarray(inp["odd_norm_g"][o_], f32), 8)
        vec[:, 8:16] = lay_vec(np.asarray(inp["odd_dw_b"][o_], f32), 8)
        vec[:, 16:24] = lay_vec(np.asarray(inp["odd_ln_g"][o_], f32), 8)
        vec[:, 24:32] = lay_vec(np.asarray(inp["odd_ln_b"][o_], f32), 8)
        shared["o%d_vec" % o_] = vec
        shared["o%d_dww" % o_] = lay_cw(np.asarray(inp["odd_dw_w"][o_], f32))
    if do_ffn:
        for l in range(n_layers):
            shared["f%d_wup" % l] = lay_pairs(np.asarray(inp["ffn_w_up"][l], f32), DFF)
            shared["f%d_wdn" % l] = lay_wdn(np.asarray(inp["ffn_w_down"][l], f32))
            shared["f%d_g" % l] = lay_vec(np.asarray(inp["ffn_norm_g"][l], f32), 8)
            shared["f%d_cw" % l] = lay_cw(np.asarray(inp["ffn_conv_w"][l], f32))
            shared["f%d_cb" % l] = lay_vec(np.asarray(inp["ffn_conv_b"][l], f32), 44)
    x = np.asarray(inp["x"], f32)
    maps = []
    for c in range(NCORES):
        b, i = c // 4, c % 4
        xt = np.zeros((D, cfg.T), f32)
        lo = i * cfg.CH - HALO
        if i == 0:
            xt[:, HALO:] = x[b, 0:cfg.CH].T
        else:
            xt[:, :] = x[b, lo:lo + cfg.T].T
        flag = np.full((128, 1), 0.0 if i == 0 else 1.0, f32)
        invc = np.zeros((128, 4 * 32), f32)
        pos = np.arange(1, 33, dtype=f32)
        for g, w in enumerate(POOL_W):
            cntv = np.minimum(pos, float(w)) if i == 0 else np.full(32, float(w), f32)
            invc[:, g * 32:(g + 1) * 32] = (1.0 / cntv)[None, :]
        m = dict(shared)
        m["x"] = xt
        m["flag"] = flag
        m["invc"] = invc
        maps.append(m)
    return maps


_CACHE = {}
N_SPLIT = 1


def _sub_inputs(inputs, l0, nl):
    out = {}
    for k, v in inputs.items():
        if k == "x":
            out[k] = v
        elif k.startswith("even_"):
            out[k] = v[(l0 + 1) // 2:]
        elif k.startswith("odd_"):
            out[k] = v[l0 // 2:]
        else:
            out[k] = v[l0:]
    return out


def kernel(**inputs):
    cfg = Cfg(4096)
    nl = 4 // N_SPLIT
    if "nc" not in _CACHE:
        _CACHE["nc"] = build_program(cfg, n_layers=nl)[0]
    nc = _CACHE["nc"]
    inp = {k: np.asarray(v) for k, v in inputs.items()}
    x = np.asarray(inp["x"], np.float32)
    for part in range(N_SPLIT):
        sub = _sub_inputs(inp, part * nl, nl)
        sub["x"] = x
        maps = host_inputs(cfg, sub, n_layers=nl)
        res = run_bass_kernel_spmd(nc, maps, core_ids=list(range(NCORES)))
        out = np.empty((2, SEQ, D), np.float32)
        for c in range(NCORES):
            b, i = c // 4, c % 4
            out[b, i * cfg.CH:(i + 1) * cfg.CH, :] = res.results[c]["y"].T
        x = out
    return x
```

```python
import contextlib
import numpy as np
import concourse.bass as bass
import concourse.mybir as mybir
from concourse.bass_utils import run_bass_kernel_spmd

F32 = mybir.dt.float32
BF16 = mybir.dt.bfloat16
AF = mybir.ActivationFunctionType
ALU = mybir.AluOpType

D = 1024
KC = 8
DFF = 2816
NJ = 22
NCORES = 8
SEQ = 16384
HALO = 128
CH = 2048
SEG = CH + HALO
T = 2 * SEG
EPS = 1e-6

SAME_ENGINE_SYNC = True
NO_SELF_SYNC = ("pe",)


class Op:
    __slots__ = ("eng", "fn", "deps", "is_dma", "sem", "val", "need_inc", "key", "phase", "inc")

    def __init__(self, eng, fn, is_dma, key=None):
        self.eng = eng
        self.fn = fn
        self.deps = []
        self.is_dma = is_dma
        self.sem = None
        self.val = None
        self.need_inc = False
        self.key = key
        self.inc = 16


class Prog:
    ENGS = ("pe", "act", "dve", "pool", "sp")

    def __init__(self, nc, stack):
        self.nc = nc
        self.stack = stack
        self.ops = {e: [] for e in self.ENGS}
        self.last_w = {}
        self.readers = {}
        self.dma_sems = {}
        self.n_ops = 0
        self.phase = 0

    def op(self, eng, fn, reads=(), writes=(), dma_key=None, inc=16):
        o = Op(eng, fn, dma_key is not None, dma_key)
        o.inc = inc
        o.phase = self.phase
        deps = []
        for r in reads:
            w = self.last_w.get(r)
            if w is not None:
                deps.append(w)
        for r in writes:
            w = self.last_w.get(r)
            if w is not None:
                deps.append(w)
            deps.extend(self.readers.get(r, ()))
        seen = set()
        for d in deps:
            if id(d) in seen or d is o:
                continue
            seen.add(id(d))
            if (not d.is_dma) and d.eng == eng and (eng in NO_SELF_SYNC or not SAME_ENGINE_SYNC):
                continue
            if d.is_dma and o.is_dma and d.key == o.key and isinstance(o.key, tuple) and o.key[0] == "G":
                continue
            o.deps.append(d)
            d.need_inc = True
        for r in reads:
            self.readers.setdefault(r, []).append(o)
        for r in writes:
            self.last_w[r] = o
            self.readers[r] = []
        self.ops[eng].append(o)
        self.n_ops += 1
        return o

    def barrier(self):
        deps = []
        for e in self.ENGS:
            for o in reversed(self.ops[e]):
                if not o.is_dma:
                    deps.append(o)
                    break
        last_dma = {}
        for e in self.ENGS:
            for o in self.ops[e]:
                if o.is_dma:
                    last_dma[o.key] = o
        deps.extend(last_dma.values())
        for e in self.ENGS:
            o = Op(e, lambda en: en.nop(), False)
            o.phase = self.phase
            for d in deps:
                if (not d.is_dma) and d.eng == e:
                    continue
                o.deps.append(d)
                d.need_inc = True
            self.ops[e].append(o)
        self.last_w = {}
        self.readers = {}
        self.phase += 1

    def emit(self, final_waits=()):
        nc = self.nc
        stack = self.stack
        eng_sem = {}
        ecnt = {}
        gtot = {}
        for e in self.ENGS:
            for o in self.ops[e]:
                if o.is_dma:
                    if o.key not in self.dma_sems:
                        self.dma_sems[o.key] = [
                            stack.enter_context(nc.semaphore("d%d" % len(self.dma_sems))), 0]
                    ent = self.dma_sems[o.key]
                    ent[1] += o.inc
                    o.sem, o.val = ent[0], ent[1]
                    if isinstance(o.key, tuple) and o.key[0] == "G":
                        gtot[(o.key, o.phase)] = ent[1]
                elif o.need_inc:
                    k_ = (e, o.phase % 4)
                    if k_ not in eng_sem:
                        eng_sem[k_] = stack.enter_context(nc.semaphore("s_%s_%d" % k_))
                        ecnt[k_] = 0
                    ecnt[k_] += 1
                    o.sem, o.val = eng_sem[k_], ecnt[k_]
        for e in self.ENGS:
            for o in self.ops[e]:
                if o.is_dma and (o.key, o.phase) in gtot:
                    o.val = gtot[(o.key, o.phase)]
        final = list(final_waits)
        block = stack.enter_context(nc.Block())
        handles = {"pe": "tensor", "act": "scalar", "dve": "vector", "pool": "gpsimd", "sp": "sync"}

        def run(ename, e):
            waited = {}
            for o in self.ops[ename]:
                for d in o.deps:
                    k = id(d.sem)
                    if waited.get(k, 0) >= d.val:
                        continue
                    e.wait_ge(d.sem, d.val)
                    waited[k] = d.val
                ins = o.fn(e)
                if o.is_dma:
                    ins.then_inc(o.sem, o.inc)
                elif o.need_inc:
                    ins.then_inc(o.sem, 1)
            if ename == "pool":
                for ent in self.dma_sems.values():
                    e.wait_ge(ent[0], ent[1])

        for ename in self.ENGS:
            dec = getattr(block, handles[ename])

            def body(e, _n=ename):
                run(_n, e)
            dec(body)


_PID = {}


def pid4(e):
    k = id(e)
    if k not in _PID:
        _PID[k] = e.partition_id() % 4
    return _PID[k]


class Ctx:
    def __init__(self, nc, stack):
        self.nc = nc
        self.stack = stack
        self.P = Prog(nc, stack)
        self._n = 0

    def sb(self, shape, dtype, name=None):
        self._n += 1
        return self.stack.enter_context(self.nc.sbuf_tensor("%s_%d" % (name or "sb", self._n), list(shape), dtype))

    def ps(self, shape, dtype=F32, name=None):
        self._n += 1
        return self.stack.enter_context(self.nc.psum_tensor("%s_%d" % (name or "ps", self._n), list(shape), dtype))


class Cfg:
    def __init__(self, CH=4096):
        self.CH = CH
        self.T = CH + HALO
        self.SEQ = 4 * CH
        self.NKB = self.SEQ // 128
        self.NQB = self.SEQ // 512


def make_groups(T_, first_n, step, halo):
    out = [(0, min(first_n, T_), 0)]
    pos = out[0][1]
    while pos < T_:
        n = min(step, T_ - pos)
        out.append((pos, n, halo))
        pos += n
    return out


class Phase:
    def __init__(self, cx):
        self.cx = cx

    def __enter__(self):
        self.st = contextlib.ExitStack()
        self.saved = self.cx.stack
        self.cx.stack = self.st
        return self

    def __exit__(self, *a):
        self.cx.P.barrier()
        self.cx.stack = self.saved
        self.st.close()
        return False


def setup_consts(cx):
    P = cx.P
    cx.ones_f = cx.sb([128, 128], F32, "ones_f")
    cx.ones2_f = cx.sb([128, 128], F32, "ones2_f")
    cx.eps_t = cx.sb([128, 1], F32, "eps")
    cx.one_t = cx.sb([128, 1], F32, "one")
    cx.ident_b = cx.sb([128, 128], BF16, "ident_b")
    cx.maskb = cx.sb([128, 128], BF16, "maskb")
    cx.flag = cx.sb([128, 1], F32, "flag")
    P.op("dve", lambda e: e.memset(cx.ones_f[:, :], 1.0), writes=[("c", "ones_f")])
    P.op("dve", lambda e: e.memset(cx.ones2_f[:, :], 1.0), writes=[("c", "ones2_f")])
    P.op("dve", lambda e: e.memset(cx.ones2_f[0:64, 64:128], 0.0), writes=[("c", "ones2_f")])
    P.op("dve", lambda e: e.memset(cx.ones2_f[64:128, 0:64], 0.0), writes=[("c", "ones2_f")])
    P.op("dve", lambda e: e.memset(cx.eps_t[:, :], EPS), writes=[("c", "eps")])
    P.op("dve", lambda e: e.memset(cx.one_t[:, :], 1.0), writes=[("c", "one")])
    P.op("dve", lambda e: e.memset(cx.ident_b[:, :], 0.0), writes=[("c", "ident")])
    P.op("pool", lambda e: e.affine_select(out=cx.ident_b[:, :], in_=cx.ident_b[:, :], pattern=[[1, 128]],
                                           compare_op=ALU.not_equal, fill=1.0, base=0, channel_multiplier=-1),
         reads=[("c", "ident")], writes=[("c", "ident")])
    P.op("dve", lambda e: e.memset(cx.maskb[:, :], 0.0), writes=[("c", "maskb")])
    P.op("pool", lambda e: e.affine_select(out=cx.maskb[:, :], in_=cx.maskb[:, :], pattern=[[1, 128]],
                                           compare_op=ALU.is_ge, fill=-30000.0, base=0, channel_multiplier=-1),
         reads=[("c", "maskb")], writes=[("c", "maskb")])
    P.op("sp", lambda e: e.dma_start(out=cx.flag[:, :], in_=cx.dram["flag"]), writes=[("c", "flag")], dma_key=("G", "small"))


def emit_rmsnorm(cx, xt, nu, g_t, H, res, sq, red, ps_stat, std, rstd):
    P = cx.P
    P.op("act", lambda e: e.activation(out=sq[:, :, :nu], in_=xt[:, :, :nu], func=AF.Square),
         reads=[res["xt"]], writes=[res["sq"]])
    P.op("pool", lambda e: e.tensor_tensor(out=red[:, :nu], in0=sq[:, 0, :nu], in1=sq[:, 1, :nu], op=ALU.add),
         reads=[res["sq"]], writes=[res["red"]])
    for kc in range(2, KC):
        P.op("pool", lambda e, kc=kc: e.tensor_tensor(out=red[:, :nu], in0=red[:, :nu], in1=sq[:, kc, :nu], op=ALU.add),
             reads=[res["sq"], res["red"]], writes=[res["red"]])
    P.op("pe", lambda e: e.matmul(ps_stat[:, :nu], lhsT=cx.ones_f[:, :], rhs=red[:, :nu], start=True, stop=True),
         reads=[res["red"], ("c", "ones_f")], writes=[res["ps_stat"]])
    P.op("act", lambda e: e.activation(out=std[:, :nu], in_=ps_stat[:, :nu], func=AF.Sqrt, scale=1.0 / D, bias=cx.eps_t[:, 0:1]),
         reads=[res["ps_stat"], ("c", "eps")], writes=[res["std"]])
    P.op("dve", lambda e: e.reciprocal(out=rstd[:, :nu], in_=std[:, :nu]),
         reads=[res["std"]], writes=[res["rstd"]])
    for kc in range(KC):
        P.op("dve", lambda e, kc=kc: e.scalar_tensor_tensor(
            out=H[:, kc, :nu], in0=xt[:, kc, :nu], scalar=g_t[:, kc:kc + 1], in1=rstd[:, :nu],
            op0=ALU.mult, op1=ALU.mult),
            reads=[res["xt"], res["rstd"], res["g"]], writes=[res["H"]])


def load_small(cx, tag, name, shape, src):
    t = cx.sb(shape, F32, name)
    r = (tag, name)
    cx.P.op("sp", lambda e: e.dma_start(out=t[:, :], in_=src), writes=[r], dma_key=("G", "small"))
    return t, r


def store_group(cx, tag, xo, rxo, Xo, o0, n, first):
    P = cx.P
    if first:
        P.op("dve", lambda e: e.tensor_scalar(out=xo[:, :, 0:HALO], in0=xo[:, :, 0:HALO], scalar1=cx.flag[:, 0:1],
                                              scalar2=None, op0=ALU.mult),
             reads=[rxo, ("c", "flag")], writes=[rxo])
    P.op("pool", lambda e: e.dma_start(out=Xo[:, :, o0:o0 + n], in_=xo[:, :, :n]),
         reads=[rxo], writes=[(tag, "Xout")], dma_key=("st", 0))


def ffn_phase(cx, cfg, X_in, X_out, wup_bf, wdn_bf, g_dram, cw_dram, cb_dram, tag, out_off=0):
    P = cx.P
    groups = make_groups(cfg.T, 512, 510, 2)
    Xi = X_in.rearrange("(kc p) t -> p kc t", p=128)
    Xo = X_out.rearrange("(kc p) t -> p kc t", p=128)
    with Phase(cx):
        g_t, rg = load_small(cx, tag, "g", [128, KC], g_dram)
        cw_t, rcw = load_small(cx, tag, "cw", [128, 44 * 3], cw_dram)
        cb_t, rcb = load_small(cx, tag, "cb", [128, 44], cb_dram)
        NX = 2
        xt = [cx.sb([128, KC, 512], F32, "xt") for _ in range(NX)]
        sq = cx.sb([128, KC, 512], F32, "sq")
        red = cx.sb([128, 512], F32, "red")
        std = cx.sb([128, 512], F32, "std")
        rstd = cx.sb([128, 512], F32, "rstd")
        Hs = [cx.sb([128, KC, 512], BF16, "H") for _ in range(2)]
        NW = 5
        wup = [cx.sb([128, 2, KC, 128], BF16, "wup") for _ in range(NW)]
        acc_g = [cx.sb([128, 512], F32, "accg") for _ in range(2)]
        acc_v = [cx.sb([128, 512], F32, "accv") for _ in range(2)]
        sg = [cx.sb([128, 512], F32, "sg") for _ in range(2)]
        Gs = [cx.sb([128, NJ, 512], BF16, "G") for _ in range(2)]
        ND = 3
        wdn = [cx.sb([128, NJ, 128], BF16, "wdn") for _ in range(ND)]
        xo = cx.sb([128, KC, 512], F32, "xo")
        ps_stat = cx.ps([128, 512], F32, "psst")
        ps_g = [cx.ps([128, 512], F32, "psg") for _ in range(2)]
        ps_v = [cx.ps([128, 512], F32, "psv") for _ in range(2)]
        ps_o = [cx.ps([128, 512], F32, "pso") for _ in range(2)]
        cnt = {"w": 0, "d": 0, "a": 0, "o": 0}

        def pre(gi):
            o0, n, h = groups[gi]
            nu = n + h
            s = gi % NX
            rx = (tag, "xt", s)
            P.op("sp", lambda e: e.dma_start(out=xt[s][:, :, :nu], in_=Xi[:, :, o0 - h:o0 + n]),
                 reads=[(tag, "Xin")], writes=[rx], dma_key=("xt", s))
            res = {"xt": rx, "sq": (tag, "sq"), "red": (tag, "red"), "ps_stat": (tag, "psst"), "std": (tag, "std"),
                   "rstd": (tag, "rstd"), "g": rg, "H": (tag, "H", gi % 2)}
            emit_rmsnorm(cx, xt[s], nu, g_t, Hs[gi % 2], res, sq, red, ps_stat, std, rstd)

        def up(gi):
            o0, n, h = groups[gi]
            nu = n + h
            H = Hs[gi % 2]
            G = Gs[gi % 2]
            rH = (tag, "H", gi % 2)
            rG = (tag, "G", gi % 2)
            pend = []
            for j in range(NJ + 1):
                if j == NJ:
                    while pend:
                        pend.pop(0)()
                    break
                ws = cnt["w"] % NW
                cnt["w"] += 1
                rw = (tag, "wup", ws)
                P.op("sp", lambda e, j=j, ws=ws: e.dma_start(
                    out=wup[ws][:, :, :, :].rearrange("p a k o -> p (a k o)"), in_=wup_bf[j]),
                    writes=[rw], dma_key=("wup", ws))
                a = cnt["a"] % 2
                cnt["a"] += 1
                rpg, rpv = (tag, "psg", a), (tag, "psv", a)
                for gv, psb, rp in ((0, ps_g[a], rpg), (1, ps_v[a], rpv)):
                    for kc in range(KC):
                        P.op("pe", lambda e, gv=gv, kc=kc, psb=psb, ws=ws: e.matmul(
                            psb[:, :nu], lhsT=wup[ws][:, gv, kc, :], rhs=H[:, kc, :nu],
                            start=(kc == 0), stop=(kc == KC - 1)),
                            reads=[rw, rH], writes=[rp])
                ag, av, sgt = acc_g[a], acc_v[a], sg[a]
                rag, rav, rsg = (tag, "accg", a), (tag, "accv", a), (tag, "sg", a)
                pairs = ((ag, rag, ps_g[a], rpg, j), (av, rav, ps_v[a], rpv, NJ + j))
                for (acc, racc, psb, rp, ch) in pairs:
                    w2 = cw_t[:, ch * 3 + 2:ch * 3 + 3]
                    bb = cb_t[:, ch:ch + 1]
                    P.op("act", lambda e, acc=acc, psb=psb, w2=w2, bb=bb: e.activation(
                        out=acc[:, :n], in_=psb[:, h:h + n], func=AF.Identity, scale=w2, bias=bb),
                        reads=[rp, rcw, rcb], writes=[racc])
                if pend:
                    pend.pop(0)()
                for sh in (1, 2):
                    for (acc, racc, psb, rp, ch) in pairs:
                        wk = cw_t[:, ch * 3 + (2 - sh):ch * 3 + (3 - sh)]
                        if h >= sh:
                            P.op("dve", lambda e, acc=acc, psb=psb, wk=wk, sh=sh: e.scalar_tensor_tensor(
                                out=acc[:, :n], in0=psb[:, h - sh:h - sh + n], scalar=wk, in1=acc[:, :n],
                                op0=ALU.mult, op1=ALU.add),
                                reads=[rp, rcw, racc], writes=[racc])
                        else:
                            P.op("dve", lambda e, acc=acc, psb=psb, wk=wk, sh=sh: e.scalar_tensor_tensor(
                                out=acc[:, sh:n], in0=psb[:, 0:n - sh], scalar=wk, in1=acc[:, sh:n],
                                op0=ALU.mult, op1=ALU.add),
                                reads=[rp, rcw, racc], writes=[racc])
                def fin(j=j, ag=ag, av=av, sgt=sgt, rag=rag, rav=rav, rsg=rsg):
                    P.op("act", lambda e: e.activation(out=sgt[:, :n], in_=ag[:, :n], func=AF.Silu),
                         reads=[rag], writes=[rsg])
                    P.op("pool", lambda e: e.tensor_tensor(
                        out=G[:, j, :n], in0=sgt[:, :n], in1=av[:, :n], op=ALU.mult),
                        reads=[rsg, rav], writes=[rG])
                pend.append(fin)

        def down(gi):
            o0, n, h = groups[gi]
            G = Gs[gi % 2]
            rG = (tag, "G", gi % 2)
            s = gi % NX
            rx = (tag, "xt", s)
            rxo = (tag, "xo")
            for oc in range(KC):
                ds_ = cnt["d"] % ND
                cnt["d"] += 1
                rw = (tag, "wdn", ds_)
                P.op("sp", lambda e, oc=oc, ds_=ds_: e.dma_start(
                    out=wdn[ds_][:, :, :].rearrange("p j o -> p (j o)"), in_=wdn_bf[oc]),
                    writes=[rw], dma_key=("wdn", ds_))
                b = cnt["o"] % 2
                cnt["o"] += 1
                rp = (tag, "pso", b)
                for j in range(NJ):
                    P.op("pe", lambda e, j=j, b=b, ds_=ds_: e.matmul(
                        ps_o[b][:, :n], lhsT=wdn[ds_][:, j, :], rhs=G[:, j, :n],
                        start=(j == 0), stop=(j == NJ - 1)),
                        reads=[rw, rG], writes=[rp])
                P.op("dve", lambda e, oc=oc, b=b: e.tensor_tensor(
                    out=xo[:, oc, :n], in0=ps_o[b][:, :n], in1=xt[s][:, oc, h:h + n], op=ALU.add),
                    reads=[rp, rx], writes=[rxo])
            if out_off == 0:
                store_group(cx, tag, xo, rxo, Xo, o0, n, gi == 0)
            else:
                lo = max(o0, out_off)
                if lo < o0 + n:
                    P.op("pool", lambda e: e.dma_start(out=Xo[:, :, lo - out_off:o0 + n - out_off], in_=xo[:, :, lo - o0:n]),
                         reads=[rxo], writes=[(tag, "Xout")], dma_key=("st", 0))

        ng = len(groups)
        pre(0)
        up(0)
        for gi in range(ng):
            if gi + 1 < ng:
                pre(gi + 1)
                up(gi + 1)
            down(gi)


CW31 = 31
HO = 30


def odd_phase(cx, cfg, X_in, X_out, w1_bf, w2_bf, dg_bf, vec_dram, tag):
    P = cx.P
    groups = make_groups(cfg.T, 482, 482, HO)
    Xi = X_in.rearrange("(kc p) t -> p kc t", p=128)
    Xo = X_out.rearrange("(kc p) t -> p kc t", p=128)
    with Phase(cx):
        vec, rvec = load_small(cx, tag, "vec", [128, 32], vec_dram)
        g_t = vec[:, 0:8]
        xt = [cx.sb([128, KC, 512], F32, "xt") for _ in range(2)]
        sqk = [cx.sb([128, 512], F32, "sqk") for _ in range(2)]
        red = cx.sb([128, 512], F32, "red")
        lr1 = cx.sb([128, 512], F32, "lr1")
        lr2 = cx.sb([128, 512], F32, "lr2")
        std = cx.sb([128, 512], F32, "std")
        rstd = cx.sb([128, 512], F32, "rstd")
        mean = cx.sb([128, 512], F32, "mean")
        m2 = cx.sb([128, 512], F32, "m2")
        t1 = [cx.sb([128, 512], F32, "t1") for _ in range(2)]
        Hs = [cx.sb([128, KC, 512], BF16, "H") for _ in range(2)]
        w1 = [cx.sb([128, 2, KC, 128], BF16, "w1") for _ in range(3)]
        dg = [cx.sb([128, CW31, 128], BF16, "dg") for _ in range(2)]
        Us = [cx.sb([128, KC, 512 + HO], BF16, "U") for _ in range(2)]
        sig = [cx.sb([128, 512], F32, "sig") for _ in range(2)]
        Vs = [cx.sb([128, KC, 512], F32, "V") for _ in range(2)]
        Ss = [cx.sb([128, KC, 512], BF16, "S") for _ in range(2)]
        w2 = [cx.sb([128, KC, 128], BF16, "w2") for _ in range(2)]
        xo = cx.sb([128, KC, 512], F32, "xo")
        ps = [cx.ps([128, 512], F32, "ps") for _ in range(8)]
        rps = [(tag, "ps", i) for i in range(8)]
        cnt = {"w1": 0, "ag": 0, "dg": 0, "c": 0, "w2": 0, "o": 0}

        def pre(gi):
            o0, n, h = groups[gi]
            nu = n + h
            s = gi % 2
            rx = (tag, "xt", s)
            P.op("sp", lambda e: e.dma_start(out=xt[s][:, :, :nu], in_=Xi[:, :, o0 - h:o0 + n]),
                 reads=[(tag, "Xin")], writes=[rx], dma_key=("xt", s))
            rH = (tag, "H", s)
            x_ = xt[s]
            P.op("act", lambda e: e.activation(out=red[:, :nu], in_=x_[:, 0, :nu], func=AF.Square),
                 reads=[rx], writes=[(tag, "red")])
            for kc in range(1, KC):
                q = sqk[kc % 2]
                rq = (tag, "sqk", kc % 2)
                P.op("act", lambda e, kc=kc, q=q: e.activation(out=q[:, :nu], in_=x_[:, kc, :nu], func=AF.Square),
                     reads=[rx], writes=[rq])
                P.op("pool", lambda e, q=q: e.tensor_tensor(out=red[:, :nu], in0=red[:, :nu], in1=q[:, :nu], op=ALU.add),
                     reads=[rq, (tag, "red")], writes=[(tag, "red")])
            P.op("pe", lambda e: e.matmul(ps[0][:, :nu], lhsT=cx.ones_f[:, :], rhs=red[:, :nu], start=True, stop=True),
                 reads=[(tag, "red"), ("c", "ones_f")], writes=[rps[0]])
            P.op("act", lambda e: e.activation(out=std[:, :nu], in_=ps[0][:, :nu], func=AF.Sqrt, scale=1.0 / D, bias=cx.eps_t[:, 0:1]),
                 reads=[rps[0], ("c", "eps")], writes=[(tag, "std")])
            P.op("dve", lambda e: e.reciprocal(out=rstd[:, :nu], in_=std[:, :nu]),
                 reads=[(tag, "std")], writes=[(tag, "rstd")])
            for kc in range(KC):
                P.op("dve", lambda e, kc=kc: e.scalar_tensor_tensor(
                    out=Hs[s][:, kc, :nu], in0=x_[:, kc, :nu], scalar=g_t[:, kc:kc + 1], in1=rstd[:, :nu],
                    op0=ALU.mult, op1=ALU.mult),
                    reads=[rx, (tag, "rstd"), rvec], writes=[rH])

        def glu(gi):
            o0, n, h = groups[gi]
            nu = n + h
            s = gi % 2
            H = Hs[s]
            rH = (tag, "H", s)
            U = Us[s]
            rU = (tag, "U", s)
            cs = HO - h
            if h == 0:
                P.op("pool", lambda e: e.memset(U[:, :, 0:HO], 0.0), writes=[rU])
            for j in range(KC):
                ws = cnt["w1"] % 3
                cnt["w1"] += 1
                rw = (tag, "w1", ws)
                P.op("sp", lambda e, j=j, ws=ws: e.dma_start(
                    out=w1[ws][:, :, :, :].rearrange("p a k o -> p (a k o)"), in_=w1_bf[j]),
                    writes=[rw], dma_key=("w1", ws))
                a = cnt["ag"] % 2
                cnt["ag"] += 1
                pa, pg = ps[1 + 2 * a], ps[2 + 2 * a]
                rpa, rpg = rps[1 + 2 * a], rps[2 + 2 * a]
                for gv, psb, rp in ((0, pa, rpa), (1, pg, rpg)):
                    for kc in range(KC):
                        P.op("pe", lambda e, gv=gv, kc=kc, psb=psb, ws=ws: e.matmul(
                            psb[:, :nu], lhsT=w1[ws][:, gv, kc, :], rhs=H[:, kc, :nu],
                            start=(kc == 0), stop=(kc == KC - 1)),
                            reads=[rw, rH], writes=[rp])
                sg_ = sig[a]
                rsg = (tag, "sig", a)
                P.op("act", lambda e, sg_=sg_, pg=pg: e.activation(out=sg_[:, :nu], in_=pg[:, :nu], func=AF.Sigmoid),
                     reads=[rpg], writes=[rsg])
                P.op("dve", lambda e, j=j, sg_=sg_, pa=pa: e.tensor_tensor(
                    out=U[:, j, cs:cs + nu], in0=pa[:, :nu], in1=sg_[:, :nu], op=ALU.mult),
                    reads=[rpa, rsg], writes=[rU])

        def conv(gi):
            o0, n, h = groups[gi]
            s = gi % 2
            U = Us[s]
            rU = (tag, "U", s)
            V = Vs[s]
            rV = (tag, "V", s)
            for j in range(KC):
                d_ = cnt["dg"] % 2
                cnt["dg"] += 1
                rd = (tag, "dg", d_)
                P.op("sp", lambda e, j=j, d_=d_: e.dma_start(
                    out=dg[d_][:, :, :].rearrange("p k o -> p (k o)"), in_=dg_bf[j]),
                    writes=[rd], dma_key=("dg", d_))
                b = cnt["c"] % 2
                cnt["c"] += 1
                pc, rpc = ps[5 + b], rps[5 + b]
                for k in range(CW31):
                    P.op("pe", lambda e, j=j, k=k, d_=d_, pc=pc: e.matmul(
                        pc[:, :n], lhsT=dg[d_][:, k, :], rhs=U[:, j, k:k + n],
                        start=(k == 0), stop=(k == CW31 - 1)),
                        reads=[rd, rU], writes=[rpc])
                P.op("act", lambda e, j=j, pc=pc: e.activation(
                    out=V[:, j, :n], in_=pc[:, :n], func=AF.Identity, bias=vec[:, 8 + j:9 + j], scale=1.0),
                    reads=[rpc, rvec], writes=[rV])

        def ln_a(gi):
            o0, n, h = groups[gi]
            s = gi % 2
            V = Vs[s]
            rV = (tag, "V", s)
            P.op("pool", lambda e: e.tensor_tensor(out=lr1[:, :n], in0=V[:, 0, :n], in1=V[:, 1, :n], op=ALU.add),
                 reads=[rV], writes=[(tag, "lr1")])
            for kc in range(2, KC):
                P.op("pool", lambda e, kc=kc: e.tensor_tensor(out=lr1[:, :n], in0=lr1[:, :n], in1=V[:, kc, :n], op=ALU.add),
                     reads=[rV, (tag, "lr1")], writes=[(tag, "lr1")])
            P.op("act", lambda e: e.activation(out=lr2[:, :n], in_=V[:, 0, :n], func=AF.Square),
                 reads=[rV], writes=[(tag, "lr2")])
            for kc in range(1, KC):
                q = sqk[kc % 2]
                rq = (tag, "sqk", kc % 2)
                P.op("act", lambda e, kc=kc, q=q: e.activation(out=q[:, :n], in_=V[:, kc, :n], func=AF.Square),
                     reads=[rV], writes=[rq])
                P.op("pool", lambda e, q=q: e.tensor_tensor(out=lr2[:, :n], in0=lr2[:, :n], in1=q[:, :n], op=ALU.add),
                     reads=[rq, (tag, "lr2")], writes=[(tag, "lr2")])

        def ln_b(gi):
            o0, n, h = groups[gi]
            s = gi % 2
            V = Vs[s]
            rV = (tag, "V", s)
            S = Ss[s]
            rS = (tag, "S", s)
            P.op("pe", lambda e: e.matmul(ps[0][:, :n], lhsT=cx.ones_f[:, :], rhs=lr1[:, :n], start=True, stop=True),
                 reads=[(tag, "lr1"), ("c", "ones_f")], writes=[rps[0]])
            P.op("pe", lambda e: e.matmul(ps[7][:, :n], lhsT=cx.ones_f[:, :], rhs=lr2[:, :n], start=True, stop=True),
                 reads=[(tag, "lr2"), ("c", "ones_f")], writes=[rps[7]])
            P.op("dve", lambda e: e.tensor_scalar(out=mean[:, :n], in0=ps[0][:, :n], scalar1=1.0 / D, scalar2=None, op0=ALU.mult),
                 reads=[rps[0]], writes=[(tag, "mean")])
            P.op("dve", lambda e: e.tensor_tensor(out=m2[:, :n], in0=mean[:, :n], in1=mean[:, :n], op=ALU.mult),
                 reads=[(tag, "mean")], writes=[(tag, "m2")])
            P.op("dve", lambda e: e.scalar_tensor_tensor(out=m2[:, :n], in0=ps[7][:, :n], scalar=1.0 / D, in1=m2[:, :n],
                                                         op0=ALU.mult, op1=ALU.subtract),
                 reads=[rps[7], (tag, "m2")], writes=[(tag, "m2")])
            P.op("act", lambda e: e.activation(out=std[:, :n], in_=m2[:, :n], func=AF.Sqrt, scale=1.0, bias=cx.eps_t[:, 0:1]),
                 reads=[(tag, "m2"), ("c", "eps")], writes=[(tag, "std")])
            P.op("dve", lambda e: e.reciprocal(out=rstd[:, :n], in_=std[:, :n]),
                 reads=[(tag, "std")], writes=[(tag, "rstd")])
            for kc in range(KC):
                tt = t1[kc % 2]
                rt = (tag, "t1", kc % 2)
                P.op("dve", lambda e, kc=kc, tt=tt: e.tensor_tensor(out=tt[:, :n], in0=V[:, kc, :n], in1=mean[:, :n], op=ALU.subtract),
                     reads=[rV, (tag, "mean")], writes=[rt])
                P.op("dve", lambda e, tt=tt: e.tensor_tensor(out=tt[:, :n], in0=tt[:, :n], in1=rstd[:, :n], op=ALU.mult),
                     reads=[rt, (tag, "rstd")], writes=[rt])
                P.op("act", lambda e, kc=kc, tt=tt: e.activation(
                    out=S[:, kc, :n], in_=tt[:, :n], func=AF.Silu, scale=vec[:, 16 + kc:17 + kc], bias=vec[:, 24 + kc:25 + kc]),
                    reads=[rt, rvec], writes=[rS])

        def pw2(gi):
            o0, n, h = groups[gi]
            s = gi % 2
            S = Ss[s]
            rS = (tag, "S", s)
            rx = (tag, "xt", s)
            rxo = (tag, "xo")
            for oc in range(KC):
                w_ = cnt["w2"] % 2
                cnt["w2"] += 1
                rw = (tag, "w2", w_)
                P.op("sp", lambda e, oc=oc, w_=w_: e.dma_start(
                    out=w2[w_][:, :, :].rearrange("p k o -> p (k o)"), in_=w2_bf[oc]),
                    writes=[rw], dma_key=("w2", w_))
                b = cnt["o"] % 2
                cnt["o"] += 1
                po, rpo = ps[1 + b], rps[1 + b]
                for kc in range(KC):
                    P.op("pe", lambda e, kc=kc, w_=w_, po=po: e.matmul(
                        po[:, :n], lhsT=w2[w_][:, kc, :], rhs=S[:, kc, :n],
                        start=(kc == 0), stop=(kc == KC - 1)),
                        reads=[rw, rS], writes=[rpo])
                P.op("dve", lambda e, oc=oc, po=po: e.tensor_tensor(
                    out=xo[:, oc, :n], in0=po[:, :n], in1=xt[s][:, oc, h:h + n], op=ALU.add),
                    reads=[rpo, rx], writes=[rxo])
            store_group(cx, tag, xo, rxo, Xo, o0, n, gi == 0)

        ng = len(groups)
        pre(0)
        glu(0)
        conv(0)
        for gi in range(ng):
            ln_a(gi)
            if gi + 1 < ng:
                pre(gi + 1)
                glu(gi + 1)
            ln_b(gi)
            if gi + 1 < ng:
                conv(gi + 1)
            pw2(gi)


HP = 15
POOL_W = (2, 4, 8, 16)


def even_pre_phase(cx, cfg, X_in, wqk_bf, wf_bf, wv_bf, wp_bf, vec_dram, bf_dram, tag):
    P = cx.P
    dr = cx.dram
    groups = make_groups(cfg.T, 512, 497, HP)
    Xi = X_in.rearrange("(kc p) t -> p kc t", p=128)
    with Phase(cx):
        vec, rvec = load_small(cx, tag, "vec", [128, 16], vec_dram)
        invc, rinvc = load_small(cx, tag, "invc", [128, 4 * 32], dr["invc"])
        bft = cx.sb([8, 1], F32, "bft")
        nbf = cx.sb([8, 1], F32, "nbf")
        gq8 = cx.sb([128, 1], F32, "gq8")
        P.op("sp", lambda e: e.dma_start(out=bft[:, :], in_=bf_dram), writes=[(tag, "bft")], dma_key=("G", "small"))
        P.op("dve", lambda e: e.tensor_scalar(out=nbf[:, :], in0=bft[:, :], scalar1=-1.0, scalar2=None, op0=ALU.mult),
             reads=[(tag, "bft")], writes=[(tag, "nbf")])
        P.op("dve", lambda e: e.tensor_scalar(out=gq8[:, :], in0=vec[:, 8:9], scalar1=0.125, scalar2=None, op0=ALU.mult),
             reads=[rvec], writes=[(tag, "gq8")])
        g_t = vec[:, 0:8]
        wv = cx.sb([128, KC, 512], BF16, "wv")
        wf = cx.sb([128, KC, 8], BF16, "wf")
        wp = cx.sb([128, 4, 128], BF16, "wp")
        P.op("sp", lambda e: e.dma_start(out=wv[:, :, :].rearrange("p k o -> p (k o)"), in_=wv_bf), writes=[(tag, "wv")], dma_key=("G", "small"))
        P.op("sp", lambda e: e.dma_start(out=wf[:, :, :].rearrange("p k o -> p (k o)"), in_=wf_bf), writes=[(tag, "wf")], dma_key=("G", "small"))
        P.op("sp", lambda e: e.dma_start(out=wp[:, :, :].rearrange("p k o -> p (k o)"), in_=wp_bf), writes=[(tag, "wp")], dma_key=("G", "small"))
        xt = [cx.sb([128, KC, 512], F32, "xt") for _ in range(2)]
        sqk = [cx.sb([128, 512], F32, "sqk") for _ in range(2)]
        red = cx.sb([128, 512], F32, "red")
        std = cx.sb([128, 512], F32, "std")
        rstd = cx.sb([128, 512], F32, "rstd")
        Hs = [cx.sb([128, KC, 512], BF16, "H") for _ in range(2)]
        wq = [cx.sb([128, KC, 128], BF16, "wq") for _ in range(3)]
        qsq = [cx.sb([128, 512], F32, "qsq") for _ in range(2)]
        qstd = [cx.sb([128, 512], F32, "qstd") for _ in range(2)]
        qo = [cx.sb([128, 512], BF16, "qo") for _ in range(2)]
        vo = [cx.sb([128, 512], BF16, "vo") for _ in range(2)]
        fe = cx.sb([8, 512], F32, "fe")
        fo = [cx.sb([8, 512], F32, "fo") for _ in range(2)]
        ub = [cx.sb([128, 512], F32, "ub") for _ in range(2)]
        sa = cx.sb([128, 512], F32, "sa")
        sb_ = cx.sb([128, 512], F32, "sb")
        mx = [cx.sb([128, 512], BF16, "mx") for _ in range(2)]
        po = [cx.sb([128, 512], BF16, "po") for _ in range(2)]
        ps = [cx.ps([128, 512], F32, "ps") for _ in range(8)]
        rps = [(tag, "ps", i) for i in range(8)]
        cnt = {"w": 0, "pj": 0, "q": 0, "v": 0, "u": 0, "f": 0}

        def pre(gi):
            o0, n, h = groups[gi]
            nu = n + h
            s = gi % 2
            rx = (tag, "xt", s)
            x_ = xt[s]
            P.op("sp", lambda e: e.dma_start(out=x_[:, :, :nu], in_=Xi[:, :, o0 - h:o0 + n]),
                 reads=[(tag, "Xin")], writes=[rx], dma_key=("xt", s))
            P.op("act", lambda e: e.activation(out=red[:, :nu], in_=x_[:, 0, :nu], func=AF.Square),
                 reads=[rx], writes=[(tag, "red")])
            for kc in range(1, KC):
                q = sqk[kc % 2]
                rq = (tag, "sqk", kc % 2)
                P.op("act", lambda e, kc=kc, q=q: e.activation(out=q[:, :nu], in_=x_[:, kc, :nu], func=AF.Square),
                     reads=[rx], writes=[rq])
                P.op("pool", lambda e, q=q: e.tensor_tensor(out=red[:, :nu], in0=red[:, :nu], in1=q[:, :nu], op=ALU.add),
                     reads=[rq, (tag, "red")], writes=[(tag, "red")])
            P.op("pe", lambda e: e.matmul(ps[0][:, :nu], lhsT=cx.ones_f[:, :], rhs=red[:, :nu], start=True, stop=True),
                 reads=[(tag, "red"), ("c", "ones_f")], writes=[rps[0]])
            P.op("act", lambda e: e.activation(out=std[:, :nu], in_=ps[0][:, :nu], func=AF.Sqrt, scale=1.0 / D, bias=cx.eps_t[:, 0:1]),
                 reads=[rps[0], ("c", "eps")], writes=[(tag, "std")])
            P.op("dve", lambda e: e.reciprocal(out=rstd[:, :nu], in_=std[:, :nu]),
                 reads=[(tag, "std")], writes=[(tag, "rstd")])
            for kc in range(KC):
                P.op("dve", lambda e, kc=kc: e.scalar_tensor_tensor(
                    out=Hs[s][:, kc, :nu], in0=x_[:, kc, :nu], scalar=g_t[:, kc:kc + 1], in1=rstd[:, :nu],
                    op0=ALU.mult, op1=ALU.mult),
                    reads=[rx, (tag, "rstd"), rvec], writes=[(tag, "H", s)])

        def proj(gi):
            o0, n, h = groups[gi]
            nu = n + h
            s = gi % 2
            H = Hs[s]
            rH = (tag, "H", s)
            t0 = o0 - h
            lo = max(o0, HALO)
            hi = o0 + n
            own = lo < hi
            c_lo, c_hi = lo - t0, hi - t0
            tk = lo - HALO
            for c in range(12):
                if c < 8 and not own:
                    continue
                ws = cnt["w"] % 3
                cnt["w"] += 1
                rw = (tag, "wq", ws)
                P.op("sp", lambda e, c=c, ws=ws: e.dma_start(
                    out=wq[ws][:, :, :].rearrange("p k o -> p (k o)"), in_=wqk_bf[c]),
                    writes=[rw], dma_key=("wq", ws))
                b = cnt["pj"] % 2
                cnt["pj"] += 1
                pj, rpj = ps[1 + b], rps[1 + b]
                for kc in range(KC):
                    P.op("pe", lambda e, kc=kc, ws=ws, pj=pj: e.matmul(
                        pj[:, :nu], lhsT=wq[ws][:, kc, :], rhs=H[:, kc, :nu],
                        start=(kc == 0), stop=(kc == KC - 1)),
                        reads=[rw, rH], writes=[rpj])
                if c < 8:
                    a = cnt["q"] % 2
                    cnt["q"] += 1
                    rqs, rqd, rqo = (tag, "qsq", a), (tag, "qstd", a), (tag, "qo", a)
                    P.op("act", lambda e, a=a, pj=pj: e.activation(out=qsq[a][:, :nu], in_=pj[:, :nu], func=AF.Square),
                         reads=[rpj], writes=[rqs])
                    P.op("pe", lambda e, a=a: e.matmul(ps[3][:, :nu], lhsT=cx.ones2_f[:, :], rhs=qsq[a][:, :nu], start=True, stop=True),
                         reads=[rqs, ("c", "ones2_f")], writes=[rps[3]])
                    P.op("act", lambda e, a=a: e.activation(out=qstd[a][:, :nu], in_=ps[3][:, :nu], func=AF.Sqrt, scale=1.0 / 64, bias=cx.eps_t[:, 0:1]),
                         reads=[rps[3], ("c", "eps")], writes=[rqd])
                    P.op("dve", lambda e, a=a: e.reciprocal(out=qstd[a][:, :nu], in_=qstd[a][:, :nu]),
                         reads=[rqd], writes=[rqd])
                    gsc = gq8[:, 0:1] if c < 4 else vec[:, 9:10]
                    P.op("dve", lambda e, a=a, pj=pj, gsc=gsc: e.scalar_tensor_tensor(
                        out=qo[a][:, :nu], in0=pj[:, :nu], scalar=gsc, in1=qstd[a][:, :nu], op0=ALU.mult, op1=ALU.mult),
                        reads=[rpj, rqd, rvec, (tag, "gq8")], writes=[rqo])
                    dst = dr["Qc"] if c < 4 else dr["Kc"]
                    cc = c % 4
                    P.op("pool", lambda e, a=a, dst=dst, cc=cc: e.dma_start(
                        out=dst[cc * 128:(cc + 1) * 128, tk:tk + (hi - lo)], in_=qo[a][:, c_lo:c_hi]),
                        reads=[rqo], writes=[(tag, "QKc")], dma_key=("qo", a))
                else:
                    g = c - 8
                    a = cnt["u"] % 2
                    cnt["u"] += 1
                    u_ = ub[a]
                    ru = (tag, "ub", a)
                    P.op("act", lambda e, u_=u_, pj=pj: e.activation(out=u_[:, :nu], in_=pj[:, :nu], func=AF.Identity),
                         reads=[rpj], writes=[ru])
                    src, rsrc = u_, ru
                    bufs = [(sa, (tag, "sa")), (sb_, (tag, "sb"))]
                    for st in range(g + 1):
                        sh = 1 << st
                        dstt, rdst = bufs[st % 2]
                        P.op("dve", lambda e, src=src, dstt=dstt, sh=sh: e.tensor_tensor(
                            out=dstt[:, sh:nu], in0=src[:, sh:nu], in1=src[:, 0:nu - sh], op=ALU.add),
                            reads=[rsrc], writes=[rdst])
                        P.op("pool", lambda e, src=src, dstt=dstt, sh=sh: e.tensor_copy(out=dstt[:, 0:sh], in_=src[:, 0:sh]),
                             reads=[rsrc], writes=[rdst])
                        src, rsrc = dstt, rdst
                    w_ = POOL_W[g]
                    m_ = mx[a]
                    rm = (tag, "mx", a)
                    P.op("dve", lambda e, src=src, u_=u_, m_=m_, w_=w_: e.scalar_tensor_tensor(
                        out=m_[:, :n], in0=src[:, h:nu], scalar=1.0 / w_, in1=u_[:, h:nu], op0=ALU.mult, op1=ALU.subtract),
                        reads=[rsrc, ru], writes=[rm])
                    if gi == 0:
                        tt = sqk[0]
                        rt = (tag, "sqk", 0)
                        P.op("dve", lambda e, src=src, g=g, tt=tt: e.tensor_tensor(
                            out=tt[:, 0:32], in0=src[:, HALO:HALO + 32], in1=invc[:, g * 32:(g + 1) * 32], op=ALU.mult),
                            reads=[rsrc, rinvc], writes=[rt])
                        P.op("dve", lambda e, u_=u_, m_=m_, tt=tt: e.tensor_tensor(
                            out=m_[:, HALO:HALO + 32], in0=tt[:, 0:32], in1=u_[:, HALO:HALO + 32], op=ALU.subtract),
                            reads=[rt, ru, rm], writes=[rm])
                    P.op("pe", lambda e, g=g, m_=m_: e.matmul(ps[7][:, :n], lhsT=wp[:, g, :], rhs=m_[:, :n], start=True, stop=True),
                         reads=[rm, (tag, "wp")], writes=[rps[7]])
                    p_ = po[a]
                    rp_ = (tag, "po", a)
                    P.op("act", lambda e, g=g, p_=p_: e.activation(out=p_[:, :n], in_=ps[7][:, :n], func=AF.Identity,
                                                                   scale=vec[:, 10 + g:11 + g]),
                         reads=[rps[7], rvec], writes=[rp_])
                    P.op("pool", lambda e, g=g, p_=p_: e.dma_start(out=dr["Pout"][g * 128:(g + 1) * 128, o0:o0 + n], in_=p_[:, :n]),
                         reads=[rp_], writes=[(tag, "Pout")], dma_key=("po", a))
            if not own:
                return
            for kc in range(KC):
                P.op("pe", lambda e, kc=kc: e.matmul(ps[6][0:8, :nu], lhsT=wf[:, kc, :], rhs=H[:, kc, :nu],
                                                     start=(kc == 0), stop=(kc == KC - 1)),
                     reads=[(tag, "wf"), rH], writes=[rps[6]])
            a = cnt["f"] % 2
            cnt["f"] += 1
            P.op("act", lambda e: e.activation(out=fe[:, :nu], in_=ps[6][0:8, :nu], func=AF.Exp, scale=-1.0, bias=nbf[:, 0:1]),
                 reads=[rps[6], (tag, "nbf")], writes=[(tag, "fe")])
            P.op("act", lambda e, a=a: e.activation(out=fo[a][:, :nu], in_=fe[:, :nu], func=AF.Ln, scale=1.0, bias=cx.one_t[0:8, 0:1]),
                 reads=[(tag, "fe"), ("c", "one")], writes=[(tag, "fo", a)])
            P.op("pool", lambda e, a=a: e.dma_start(out=dr["Fc"][:, tk:tk + (hi - lo)], in_=fo[a][:, c_lo:c_hi]),
                 reads=[(tag, "fo", a)], writes=[(tag, "Fc")], dma_key=("fo", a))
            c0 = c_lo
            while c0 < c_hi:
                m = min(128, c_hi - c0)
                b = cnt["v"] % 2
                cnt["v"] += 1
                pv, rpv = ps[4 + b], rps[4 + b]
                for kc in range(KC):
                    P.op("pe", lambda e, kc=kc, c0=c0, m=m, pv=pv: e.matmul(
                        pv[:m, :], lhsT=H[:, kc, c0:c0 + m], rhs=wv[:, kc, :], start=(kc == 0), stop=(kc == KC - 1)),
                        reads=[rH, (tag, "wv")], writes=[rpv])
                v_ = vo[b]
                rv = (tag, "vo", b)
                P.op("act", lambda e, m=m, pv=pv, v_=v_: e.activation(out=v_[:m, :], in_=pv[:m, :], func=AF.Copy),
                     reads=[rpv], writes=[rv])
                tok = tk + (c0 - c_lo)
                P.op("pool", lambda e, m=m, v_=v_, tok=tok: e.dma_start(out=dr["Vc"][tok:tok + m, :], in_=v_[:m, :]),
                     reads=[rv], writes=[(tag, "Vc")], dma_key=("vo", b))
                c0 += m

        ng = len(groups)
        pre(0)
        for gi in range(ng):
            if gi + 1 < ng:
                pre(gi + 1)
            proj(gi)


GROUPS4 = [[0, 1, 2, 3], [4, 5, 6, 7]]
BG_CAST = True
import os
ADBG = int(os.environ.get("ATT_DBG", "5"))


def attn_phase(cx, cfg, tag, bg_casts=None):
    P = cx.P
    dr = cx.dram
    CH_, SEQ_, NKB, NQB = cfg.CH, cfg.SEQ, cfg.NKB, cfg.NQB
    segl = SEQ_ // 64
    nbs = CH_ // 128
    q4 = CH_ // 4
    for cc in range(4):
        for nm in ("Q", "K"):
            P.op("pool", lambda e, nm=nm, cc=cc: e.collective_compute(
                "AllGather", ALU.bypass, replica_groups=GROUPS4,
                ins=[dr[nm + "c"][cc * 128:(cc + 1) * 128, :].opt()], outs=[dr[nm + "g"][cc * 512:(cc + 1) * 512, :].opt()]),
                reads=[(tag, nm + "c")], writes=[(tag, nm + "g")], dma_key=("G", "cc" + nm), inc=1)
        P.op("pool", lambda e, cc=cc: e.collective_compute(
            "AllGather", ALU.bypass, replica_groups=GROUPS4,
            ins=[dr["Vc"][cc * q4:(cc + 1) * q4, :].opt()], outs=[dr["Vg"][cc * CH_:(cc + 1) * CH_, :].opt()]),
            reads=[(tag, "Vc")], writes=[(tag, "Vg")], dma_key=("G", "ccV"), inc=1)
    P.op("pool", lambda e: e.collective_compute(
        "AllGather", ALU.bypass, replica_groups=GROUPS4, ins=[dr["Fc"].opt()], outs=[dr["Fg"].opt()]),
        reads=[(tag, "Fc")], writes=[(tag, "Fg")], dma_key=("G", "ccF"), inc=1)
    cx.chk("attn_ag")
    with Phase(cx):
        KT = [cx.sb([128, SEQ_], BF16, "KT") for _ in range(2)]
        VT = [cx.sb([128, NKB, 128], BF16, "VT") for _ in range(2)]
        Lm = cx.sb([128, 128], F32, "Lm")
        sp = cx.sb([128, segl], F32, "sp")
        onesl = cx.sb([128, segl], F32, "onesl")
        cl = cx.sb([128, segl], F32, "cl")
        offs = cx.sb([128, 1], F32, "offs")
        r1 = cx.sb([128, segl], F32, "r1")
        hi = cx.sb([128, segl], BF16, "hi")
        mid = cx.sb([128, segl], BF16, "mid")
        lo = cx.sb([128, segl], BF16, "lo")
        nhi = cx.sb([128, segl], BF16, "nhi")
        nmid = cx.sb([128, segl], BF16, "nmid")
        nlo = cx.sb([128, segl], BF16, "nlo")
        onesb = cx.sb([128, segl], BF16, "onesb")
        zt = cx.sb([128, 4, HALO], BF16, "zt")
        NQT = 3
        qt = [cx.sb([128, 512], BF16, "qt") for _ in range(NQT)]
        NPT = 4
        pt = [cx.sb([128, 512], BF16, "pt") for _ in range(NPT)]
        rec = [cx.sb([64, 512], F32, "rec") for _ in range(2)]
        ost = [cx.sb([64, 512], BF16, "ost") for _ in range(2)]
        NS = 4
        ps_s = [cx.ps([128, 512], F32, "pss") for _ in range(NS)]
        ps_o = [cx.ps([128, 512], F32, "pso") for _ in range(2)]
        ps_m = cx.ps([128, 512], F32, "psm")

        P.op("dve", lambda e: e.memset(Lm[:, :], 1.0), writes=[(tag, "Lm")])
        P.op("pool", lambda e: e.affine_select(out=Lm[:, :], in_=Lm[:, :], pattern=[[1, 128]], compare_op=ALU.is_gt,
                                               fill=0.0, base=0, channel_multiplier=-1),
             reads=[(tag, "Lm")], writes=[(tag, "Lm")])
        P.op("dve", lambda e: e.memset(Lm[0:64, 64:128], 0.0), reads=[(tag, "Lm")], writes=[(tag, "Lm")])
        P.op("dve", lambda e: e.memset(onesl[:, :], 1.0), writes=[(tag, "onesl")])
        P.op("dve", lambda e: e.memset(onesb[:, :], 1.0), writes=[(tag, "onesb")])
        P.op("dve", lambda e: e.memset(zt[:, :, :], 0.0), writes=[(tag, "zt")])
        P.op("sp", lambda e: e.dma_start(out=dr["Og"][0:512, CH_ - HALO:CH_].rearrange("(a p) t -> p a t", p=128), in_=zt[:, :, :]),
             reads=[(tag, "zt")], writes=[(tag, "Ogpad")], dma_key=("G", "small"))
        for h in range(2):
            P.op("dve", lambda e, h=h: e.memset(VT[h][:, :, 64:128], 1.0), writes=[(tag, "VTo", h)])

        def loc_k(e, nm):
            i4 = pid4(e)
            src = dr[nm + "g"].rearrange("(c s r) t -> c s r t", c=4, s=4)[i4, :, :, :]
            return e.dma_start(out=dr[nm + "l"].rearrange("r (s t) -> s r t", s=4), in_=src)

        def loc_v(e, j):
            i4 = pid4(e)
            src = dr["Vg"][j * CH_:(j + 1) * CH_, :].rearrange("(s n) (i c) -> s n i c", s=4, i=4)[:, :, i4, :]
            return e.dma_start(out=dr["Vl"].rearrange("(s j n) c -> j s n c", s=4, j=4)[j], in_=src)

        def loc_f(e):
            i4 = pid4(e)
            src = dr["Fg"].rearrange("(s i h) t -> s i h t", s=4, i=4)[:, i4, :, :]
            return e.dma_start(out=dr["Fl"].rearrange("(s h) t -> s h t", s=4), in_=src)

        P.op("sp", loc_f, reads=[(tag, "Fg")], writes=[(tag, "Fl")], dma_key=("G", "locf"))
        P.op("sp", lambda e: loc_k(e, "K"), reads=[(tag, "Kg")], writes=[(tag, "Kl")], dma_key=("G", "lock"))
        P.op("sp", lambda e: loc_k(e, "Q"), reads=[(tag, "Qg")], writes=[(tag, "Ql")], dma_key=("G", "locq"))
        for j in range(4):
            P.op("sp", lambda e, j=j: loc_v(e, j), reads=[(tag, "Vg")], writes=[(tag, "Vl")], dma_key=("G", "locv"))

        cx.chk("attn_loc")

        def ld_k(e, h, s):
            return e.dma_start(out=KT[h][0:64, s * CH_:(s + 1) * CH_], in_=dr["Kl"][h * 64:(h + 1) * 64, s * CH_:(s + 1) * CH_])

        def ld_v(e, h, s):
            src = dr["Vl"][s * CH_:(s + 1) * CH_, h * 64:(h + 1) * 64].rearrange("(b p) c -> p b c", p=128)
            return e.dma_start(out=VT[h][:, s * nbs:(s + 1) * nbs, 0:64], in_=src)

        def ld_f(e, h, s):
            src = dr["Fl"][s * 2 + h:s * 2 + h + 1, :].rearrange("o (j t) -> (o j) t", t=segl)
            return e.dma_start(out=sp[h * 64 + s * 16:h * 64 + (s + 1) * 16, :], in_=src)

        for h in range(2):
            for s in range(4):
                P.op("sp", lambda e, h=h, s=s: ld_f(e, h, s), reads=[(tag, "Fl")], writes=[(tag, "sp", h, s)],
                     dma_key=("G", "ldf"))
        for h in range(2):
            for s in range(4):
                P.op("sp", lambda e, h=h, s=s: ld_k(e, h, s), reads=[(tag, "Kl")], writes=[(tag, "KT", h, s)],
                     dma_key=("G", "ldk"))
        rsp = [(tag, "sp", h, s) for h in range(2) for s in range(4)]
        P.op("dve", lambda e: e.tensor_tensor_scan(out=cl[:, :], data0=onesl[:, :], data1=sp[:, :], initial=0.0,
                                                   op0=ALU.mult, op1=ALU.add),
             reads=rsp + [(tag, "onesl")], writes=[(tag, "cl")])
        P.op("pe", lambda e: e.matmul(ps_m[:, 0:2], lhsT=Lm[:, :], rhs=cl[:, segl - 2:segl], start=True, stop=True),
             reads=[(tag, "cl"), (tag, "Lm")], writes=[(tag, "psm")])
        P.op("dve", lambda e: e.tensor_copy(out=offs[:, :], in_=ps_m[:, 1:2]), reads=[(tag, "psm")], writes=[(tag, "offs")])
        P.op("dve", lambda e: e.tensor_scalar(out=cl[:, :], in0=cl[:, :], scalar1=offs[:, 0:1], scalar2=None, op0=ALU.add),
             reads=[(tag, "cl"), (tag, "offs")], writes=[(tag, "cl")])
        P.op("dve", lambda e: e.tensor_copy(out=hi[:, :], in_=cl[:, :]), reads=[(tag, "cl")], writes=[(tag, "hi")])
        P.op("dve", lambda e: e.tensor_tensor(out=r1[:, :], in0=cl[:, :], in1=hi[:, :], op=ALU.subtract),
             reads=[(tag, "cl"), (tag, "hi")], writes=[(tag, "r1")])
        P.op("dve", lambda e: e.tensor_copy(out=mid[:, :], in_=r1[:, :]), reads=[(tag, "r1")], writes=[(tag, "mid")])
        P.op("dve", lambda e: e.tensor_tensor(out=r1[:, :], in0=r1[:, :], in1=mid[:, :], op=ALU.subtract),
             reads=[(tag, "r1"), (tag, "mid")], writes=[(tag, "r1")])
        P.op("dve", lambda e: e.tensor_copy(out=lo[:, :], in_=r1[:, :]), reads=[(tag, "r1")], writes=[(tag, "lo")])
        for src_, dst_, nm in ((hi, nhi, "nhi"), (mid, nmid, "nmid"), (lo, nlo, "nlo")):
            P.op("dve", lambda e, src_=src_, dst_=dst_: e.tensor_scalar(out=dst_[:, :], in0=src_[:, :], scalar1=-1.0, scalar2=None, op0=ALU.mult),
                 reads=[(tag, "hi"), (tag, "mid"), (tag, "lo")], writes=[(tag, nm)])
        cx.chk("attn_cs")
        k = 0
        for h in range(2):
            for r, (tk_, tq_) in enumerate(((hi, onesb), (mid, onesb), (lo, onesb), (onesb, nhi), (onesb, nmid), (onesb, nlo))):
                for dst_nm, t_ in (("AugK", tk_), ("AugQ", tq_)):
                    P.op("sp", lambda e, h=h, r=r, dst_nm=dst_nm, t_=t_: e.dma_start(
                        out=dr[dst_nm][h * 6 + r:h * 6 + r + 1, :].rearrange("o (j t) -> (o j) t", t=segl), in_=t_[h * 64:(h + 1) * 64, :]),
                        reads=[(tag, "hi"), (tag, "mid"), (tag, "lo"), (tag, "nhi"), (tag, "nmid"), (tag, "nlo"), (tag, "onesb")],
                        writes=[(tag, dst_nm, h)], dma_key=("G", "aug"))
                    k += 1
        cx.chk("attn_aug")
        for h in range(2):
            P.op("sp", lambda e, h=h: e.dma_start(out=KT[h][64:70, :], in_=dr["AugK"][h * 6:(h + 1) * 6, :]),
                 reads=[(tag, "AugK", h)], writes=[(tag, "KTa", h)], dma_key=("G", "ldka"))
        cx.chk("attn_ka")
        for h in range(2):
            for s in range(4):
                P.op("sp", lambda e, h=h, s=s: ld_v(e, h, s), reads=[(tag, "Vl")], writes=[(tag, "VT", h, s)],
                     dma_key=("G", "ldv"))

        cx.chk("attn_ld")
        bg = None
        if bg_casts:
            bcf = [cx.sb([128, 1024], F32, "bcf") for _ in range(3)]
            bcb = [cx.sb([128, 1024], BF16, "bcb") for _ in range(3)]
            bg = cast_iter(cx, bg_casts, bcf, bcb, 1024)
        steps = []
        DEPTH = 3
        state = {"qi": -1, "oi": -1}
        qinfo = {}

        def load_q(h, qb):
            state["qi"] += 1
            qs = state["qi"] % NQT
            rq = (tag, "qt", qs)
            Q0 = qb * 512
            s = Q0 // CH_
            c0 = Q0 % CH_

            P.op("sp", lambda e: e.dma_start(out=qt[qs][0:64, :], in_=dr["Ql"][h * 64:(h + 1) * 64, Q0:Q0 + 512]),
                 reads=[(tag, "Ql")], writes=[rq], dma_key=("ldq", qs))
            P.op("sp", lambda e: e.dma_start(out=qt[qs][64:70, :], in_=dr["AugQ"][h * 6:(h + 1) * 6, Q0:Q0 + 512]),
                 reads=[(tag, "AugQ", h)], writes=[(tag, "qta", qs)], dma_key=("ldqa", qs))
            qinfo[(h, qb)] = qs

        def qk(i):
            h, qb, kb, nk = steps[i]
            if kb == 0:
                load_q(h, qb)
            qs = qinfo[(h, qb)]
            sb_i = i % NS
            j = kb - 4 * qb
            a = 128 * j if j > 0 else 0
            diag = j >= 0
            s_src = (kb * 128) // CH_
            rk = [(tag, "KT", h, s_src), (tag, "KTa", h)]
            P.op("pe", lambda e: e.matmul(ps_s[sb_i][:, a:512], lhsT=KT[h][0:70, kb * 128:(kb + 1) * 128],
                                          rhs=qt[qs][0:70, a:512], start=True, stop=not diag),
                 reads=rk + [(tag, "qt", qs), (tag, "qta", qs)], writes=[(tag, "pss", sb_i)])
            if diag and ADBG >= 2:
                P.op("pe", lambda e: e.matmul(ps_s[sb_i][:, a:a + 128], lhsT=cx.ident_b[:, :], rhs=cx.maskb[:, :],
                                              start=False, stop=True),
                     reads=[("c", "ident"), ("c", "maskb")], writes=[(tag, "pss", sb_i)])
            pi = i % NPT
            if ADBG < 3:
                return
            P.op("act", lambda e: e.activation(out=pt[pi][:, a:512], in_=ps_s[sb_i][:, a:512], func=AF.Exp),
                 reads=[(tag, "pss", sb_i)], writes=[(tag, "pt", pi)])

        def pv(i):
            if ADBG < 4:
                return
            h, qb, kb, nk = steps[i]
            j = kb - 4 * qb
            a = 128 * j if j > 0 else 0
            pi = i % NPT
            s_src = (kb * 128) // CH_
            if kb == 0:
                state["oi"] += 1
            ob = state["oi"] % 2
            P.op("pe", lambda e: e.matmul(ps_o[ob][:, a:512], lhsT=VT[h][:, kb, :], rhs=pt[pi][:, a:512],
                                          start=(kb == 0), stop=(kb == nk - 1)),
                 reads=[(tag, "VT", h, s_src), (tag, "VTo", h), (tag, "pt", pi)], writes=[(tag, "pso", ob)])
            if kb == nk - 1 and ADBG >= 5:
                Q0 = qb * 512
                P.op("dve", lambda e: e.reciprocal(out=rec[ob][:, :], in_=ps_o[ob][64:128, :]),
                     reads=[(tag, "pso", ob)], writes=[(tag, "rec", ob)])
                P.op("dve", lambda e: e.tensor_tensor(out=ost[ob][:, :], in0=ps_o[ob][0:64, :], in1=rec[ob][:, :], op=ALU.mult),
                     reads=[(tag, "pso", ob), (tag, "rec", ob)], writes=[(tag, "ost", ob)])
                jc, c0_ = Q0 // CH_, Q0 % CH_
                P.op("pool", lambda e: e.dma_start(out=dr["Oc"][jc * 128 + h * 64:jc * 128 + (h + 1) * 64, c0_:c0_ + 512], in_=ost[ob][:, :]),
                     reads=[(tag, "ost", ob)], writes=[(tag, "Oc")], dma_key=("ost", ob))

        for hh in range(2):
            base = len(steps)
            for qb in range(NQB):
                nk = 4 * qb + 4
                for kb in range(nk):
                    steps.append((hh, qb, kb, nk))
            n = len(steps)
            for i in range(base, n + DEPTH):
                if i < n:
                    qk(i)
                if i - DEPTH >= base:
                    pv(i - DEPTH)
                if bg is not None and i % 8 == 0:
                    if next(bg, None) is None:
                        bg = None
            if hh == 0:
                P.barrier()
        while bg is not None:
            if next(bg, None) is None:
                bg = None
    for jc in range(4):
        P.op("pool", lambda e, jc=jc: e.collective_compute(
            "AllGather", ALU.bypass, replica_groups=GROUPS4,
            ins=[dr["Oc"][jc * 128:(jc + 1) * 128, :].opt()], outs=[dr["Og"][(jc + 1) * 512:(jc + 2) * 512, :].opt()]),
            reads=[(tag, "Oc")], writes=[(tag, "Og")], dma_key=("G", "ccO"), inc=1)


def even_post_phase(cx, cfg, X_in, X_out, wo_bf, tag):
    P = cx.P
    dr = cx.dram
    groups = make_groups(cfg.T, 512, 512, 0)
    Xi = X_in.rearrange("(kc p) t -> p kc t", p=128)
    Xo = X_out.rearrange("(kc p) t -> p kc t", p=128)
    Alv = dr["Al"].rearrange("(kc p) t -> p kc t", p=128)
    Pov = dr["Pout"].rearrange("(kc p) t -> p kc t", p=128)

    Og3 = dr["Og"].rearrange("(j r) t -> j r t", r=512)

    def loc_a(e):
        i4 = pid4(e)
        return e.dma_start(out=dr["Al"][:, HALO:], in_=Og3[i4 + 1, :, :])

    def loc_h(e):
        i4 = pid4(e)
        return e.dma_start(out=dr["Al"][:, 0:HALO], in_=Og3[i4, :, cfg.CH - HALO:cfg.CH])
    P.op("sp", loc_a, reads=[(tag, "Og")], writes=[(tag, "Al")], dma_key=("G", "loca"))
    P.op("sp", loc_h, reads=[(tag, "Og")], writes=[(tag, "Alh")], dma_key=("G", "loca"))
    with Phase(cx):
        xt = [cx.sb([128, KC, 512], F32, "xt") for _ in range(2)]
        At = [cx.sb([128, KC, 512], BF16, "At") for _ in range(2)]
        wo = [cx.sb([128, KC, 128], BF16, "wo") for _ in range(3)]
        xo = [cx.sb([128, KC, 512], F32, "xo") for _ in range(2)]
        ps = [cx.ps([128, 512], F32, "ps") for _ in range(2)]
        cnt = {"w": 0, "o": 0}
        for gi, (o0, n, h) in enumerate(groups):
            s = gi % 2
            rx, rA, rxo = (tag, "xt", s), (tag, "At", s), (tag, "xo", s)
            P.op("sp", lambda e, s=s, o0=o0, n=n: e.dma_start(out=xt[s][:, :, :n], in_=Xi[:, :, o0:o0 + n]),
                 reads=[(tag, "Xin")], writes=[rx], dma_key=("xt", s))

            P.op("sp", lambda e, s=s, o0=o0, n=n: e.dma_start(out=At[s][:, 0:4, :n], in_=Alv[:, :, o0:o0 + n]),
                 reads=[(tag, "Al"), (tag, "Alh")], writes=[(tag, "Ata", s)], dma_key=("Ata", s))
            P.op("sp", lambda e, s=s, o0=o0, n=n: e.dma_start(out=At[s][:, 4:8, :n], in_=Pov[:, :, o0:o0 + n]),
                 reads=[(tag, "Pout")], writes=[(tag, "Atp", s)], dma_key=("Atp", s))
            for oc in range(KC):
                ws = cnt["w"] % 3
                cnt["w"] += 1
                rw = (tag, "wo", ws)
                P.op("sp", lambda e, oc=oc, ws=ws: e.dma_start(out=wo[ws][:, :, :].rearrange("p k o -> p (k o)"), in_=wo_bf[oc]),
                     writes=[rw], dma_key=("wo", ws))
                b = cnt["o"] % 2
                cnt["o"] += 1
                rp = (tag, "ps", b)
                for kc in range(KC):
                    P.op("pe", lambda e, kc=kc, ws=ws, b=b, s=s, n=n: e.matmul(
                        ps[b][:, :n], lhsT=wo[ws][:, kc, :], rhs=At[s][:, kc, :n], start=(kc == 0), stop=(kc == KC - 1)),
                        reads=[rw, (tag, "Ata", s), (tag, "Atp", s)], writes=[rp])
                P.op("dve", lambda e, oc=oc, b=b, s=s, n=n: e.tensor_tensor(
                    out=xo[s][:, oc, :n], in0=ps[b][:, :n], in1=xt[s][:, oc, :n], op=ALU.add),
                    reads=[rp, rx], writes=[rxo])
            store_group(cx, tag, xo[s], rxo, Xo, o0, n, gi == 0)


def cast_all(cx, items):
    P = cx.P
    CWD = 4096
    with Phase(cx):
        st_f = [cx.sb([128, CWD], F32, "cst_f") for _ in range(3)]
        st_b = [cx.sb([128, CWD], BF16, "cst_b") for _ in range(3)]
        i = 0
        for src, dst, rows, cols in items:
            for r0 in range(0, rows, 128):
                for c0 in range(0, cols, CWD):
                    cw_ = min(CWD, cols - c0)
                    s = i % 3
                    rf, rb = ("cf", s), ("cb", s)
                    P.op("sp", lambda e, src=src, r0=r0, c0=c0, cw_=cw_, s=s: e.dma_start(
                        out=st_f[s][:, :cw_], in_=src[r0:r0 + 128, c0:c0 + cw_]), writes=[rf], dma_key=rf)
                    if i % 2 == 0:
                        P.op("dve", lambda e, cw_=cw_, s=s: e.tensor_copy(out=st_b[s][:, :cw_], in_=st_f[s][:, :cw_]),
                             reads=[rf], writes=[rb])
                    else:
                        P.op("act", lambda e, cw_=cw_, s=s: e.activation(out=st_b[s][:, :cw_], in_=st_f[s][:, :cw_], func=AF.Copy),
                             reads=[rf], writes=[rb])
                    P.op("pool", lambda e, dst=dst, r0=r0, c0=c0, cw_=cw_, s=s: e.dma_start(
                        out=dst[r0:r0 + 128, c0:c0 + cw_], in_=st_b[s][:, :cw_]),
                        reads=[rb], writes=[("cdst",)], dma_key=("cst", s))
                    i += 1


def cast_iter(cx, items, st_f, st_b, cwd):
    P = cx.P
    i = 0
    nb = len(st_f)
    for src, dst, rows, cols in items:
        for r0 in range(0, rows, 128):
            for c0 in range(0, cols, cwd):
                cw_ = min(cwd, cols - c0)
                s = i % nb
                rf, rb = ("bcf", s), ("bcb", s)
                P.op("sp", lambda e, src=src, r0=r0, c0=c0, cw_=cw_, s=s: e.dma_start(
                    out=st_f[s][:, :cw_], in_=src[r0:r0 + 128, c0:c0 + cw_]), writes=[rf], dma_key=("cf", s))
                P.op("dve", lambda e, cw_=cw_, s=s: e.tensor_copy(out=st_b[s][:, :cw_], in_=st_f[s][:, :cw_]),
                     reads=[rf], writes=[rb])
                P.op("pool", lambda e, dst=dst, r0=r0, c0=c0, cw_=cw_, s=s: e.dma_start(
                    out=dst[r0:r0 + 128, c0:c0 + cw_], in_=st_b[s][:, :cw_]),
                    reads=[rb], writes=[("bcdst", i)], dma_key=("cst", s))
                i += 1
                yield i


def build_diag(cx, dww_dram, dg_bf, tag):
    P = cx.P
    with Phase(cx):
        dww, rdw = load_small(cx, tag, "dww", [128, KC * CW31], dww_dram)
        dst = [cx.sb([128, CW31, 128], BF16, "dgst") for _ in range(2)]
        for j in range(KC):
            s = j % 2
            rs = (tag, "dgst", s)
            for k in range(CW31):
                P.op("pool", lambda e, j=j, k=k, s=s: e.tensor_scalar(
                    out=dst[s][:, k, :], in0=cx.ident_b[:, :], scalar1=dww[:, j * CW31 + k:j * CW31 + k + 1], scalar2=0.0,
                    op0=ALU.mult, op1=ALU.add),
                    reads=[rdw, ("c", "ident")], writes=[rs])
            P.op("sp", lambda e, j=j, s=s: e.dma_start(out=dg_bf[j], in_=dst[s][:, :, :].rearrange("p k o -> p (k o)")),
                 reads=[rs], writes=[(tag, "dgbf")], dma_key=("dgst", s))


WSPEC_E = (("wqk", 12 * 128, 1024), ("wf", 128, 64), ("wv", 128, 4096), ("wp", 128, 512), ("wo", 8 * 128, 1024))
WSPEC_O = (("w1", 8 * 128, 2048), ("w2", 8 * 128, 1024))
WSPEC_F = (("wup", 22 * 128, 2048), ("wdn", 8 * 128, 2816))


def build_program(cfg, n_layers=4, do_ffn=True, stop=None):
    nc = bass.Bass("TRN2", target_bir_lowering=False)
    T_, CH_, SEQ_ = cfg.T, cfg.CH, cfg.SEQ

    def din(name, shape, dt=F32):
        return nc.dram_tensor(name, list(shape), dt, kind="ExternalInput").ap()

    def dsc(name, shape, dt):
        return nc.dram_tensor(name, list(shape), dt).ap()

    dr = {}
    dr["x"] = din("x", [D, T_])
    dr["flag"] = din("flag", [128, 1])
    dr["invc"] = din("invc", [128, 4 * 32])
    Y = nc.dram_tensor("y", [D, CH_], F32, kind="ExternalOutput").ap()
    XA = dsc("XA", [D, T_], F32)
    XB = dsc("XB", [D, T_], F32)
    W = {}
    casts = []
    n_even = (n_layers + 1) // 2
    n_odd = n_layers // 2
    for e_ in range(n_even):
        for nm, r, c in WSPEC_E:
            f = din("e%d_%s" % (e_, nm), [r, c])
            b = dsc("e%d_%s_b" % (e_, nm), [r, c], BF16)
            W[("e", e_, nm)] = b
            casts.append((f, b, r, c, "e%d" % e_))
        W[("e", e_, "vec")] = din("e%d_vec" % e_, [128, 16])
        W[("e", e_, "bf")] = din("e%d_bf" % e_, [8, 1])
    for o_ in range(n_odd):
        for nm, r, c in WSPEC_O:
            f = din("o%d_%s" % (o_, nm), [r, c])
            b = dsc("o%d_%s_b" % (o_, nm), [r, c], BF16)
            W[("o", o_, nm)] = b
            casts.append((f, b, r, c, "o"))
        W[("o", o_, "vec")] = din("o%d_vec" % o_, [128, 32])
        W[("o", o_, "dww")] = din("o%d_dww" % o_, [128, KC * CW31])
        W[("o", o_, "dg")] = dsc("o%d_dg" % o_, [KC * 128, CW31 * 128], BF16)
    if do_ffn:
        for l in range(n_layers):
            for nm, r, c in WSPEC_F:
                f = din("f%d_%s" % (l, nm), [r, c])
                b = dsc("f%d_%s_b" % (l, nm), [r, c], BF16)
                W[("f", l, nm)] = b
                casts.append((f, b, r, c, "f"))
            W[("f", l, "g")] = din("f%d_g" % l, [128, 8])
            W[("f", l, "cw")] = din("f%d_cw" % l, [128, 132])
            W[("f", l, "cb")] = din("f%d_cb" % l, [128, 44])
    dr["Qc"] = dsc("Qc", [512, CH_], BF16)
    dr["Kc"] = dsc("Kc", [512, CH_], BF16)
    dr["Vc"] = dsc("Vc", [CH_, 512], BF16)
    dr["Fc"] = dsc("Fc", [8, CH_], F32)
    dr["Qg"] = dsc("Qg", [4 * 512, CH_], BF16)
    dr["Kg"] = dsc("Kg", [4 * 512, CH_], BF16)
    dr["Vg"] = dsc("Vg", [4 * CH_, 512], BF16)
    dr["Fg"] = dsc("Fg", [4 * 8, CH_], F32)
    dr["Ql"] = dsc("Ql", [128, SEQ_], BF16)
    dr["Kl"] = dsc("Kl", [128, SEQ_], BF16)
    dr["Vl"] = dsc("Vl", [SEQ_, 128], BF16)
    dr["Fl"] = dsc("Fl", [8, CH_], F32)
    dr["Al"] = dsc("Al", [512, T_], BF16)
    dr["AugK"] = dsc("AugK", [12, SEQ_], BF16)
    dr["AugQ"] = dsc("AugQ", [12, SEQ_], BF16)
    dr["Oc"] = dsc("Oc", [4 * 128, CH_], BF16)
    dr["Og"] = dsc("Og", [5 * 512, CH_], BF16)
    dr["Pout"] = dsc("Pout", [512, T_], BF16)

    def slabs(ap, p=128):
        return ap.rearrange("(j p) c -> j p c", p=p)

    with contextlib.ExitStack() as stack:
        cx = Ctx(nc, stack)
        cx.dram = dr
        setup_consts(cx)
        cx.P.barrier()
        class _Stop(Exception):
            pass

        def chk(name):
            if stop == name:
                raise _Stop()
        cx.chk = chk
        try:
            chk("consts")
            first = [c for c in casts if c[4] == "e0"]
            rest = [c[:4] for c in casts if c[4] != "e0"]
            cast_all(cx, [c[:4] for c in first] + ([] if n_layers > 0 and BG_CAST else rest))
            cx.bg_casts = rest if BG_CAST else None
            chk("cast")
            for o_ in range(n_odd):
                build_diag(cx, W[("o", o_, "dww")], slabs(W[("o", o_, "dg")]), "dg%d" % o_)
            chk("diag")
            _build_layers(cx, cfg, dr, W, XA, XB, Y, n_layers, do_ffn, slabs, chk)
        except _Stop:
            pass
        cx.P.emit()
    return nc, None


def _build_layers(cx, cfg, dr, W, XA, XB, Y, n_layers, do_ffn, slabs, chk):
    if True:
        cur = dr["x"]
        bufs = [XA, XB]
        bi = 0
        n_sub = n_layers * (2 if do_ffn else 1)
        sub = 0

        def nxt():
            nonlocal bi
            b = bufs[bi]
            bi ^= 1
            return b
        for l in range(n_layers):
            i2 = l // 2
            if l % 2 == 0:
                tag = "E%d" % l
                even_pre_phase(cx, cfg, cur, slabs(W[("e", i2, "wqk")]), W[("e", i2, "wf")], W[("e", i2, "wv")],
                               W[("e", i2, "wp")], W[("e", i2, "vec")], W[("e", i2, "bf")], tag + "a")
                chk("E1")
                attn_phase(cx, cfg, tag + "b", bg_casts=cx.bg_casts if l == 0 else None)
                cx.P.barrier()
                chk("attn")
                sub += 1
                last = (sub == n_sub)
                pass
                dst = nxt()
                even_post_phase(cx, cfg, cur, dst, slabs(W[("e", i2, "wo")]), tag + "c")
                cur = dst
                chk("E3")
            else:
                tag = "O%d" % l
                sub += 1
                dst = nxt()
                odd_phase(cx, cfg, cur, dst, slabs(W[("o", i2, "w1")]), slabs(W[("o", i2, "w2")]),
                          slabs(W[("o", i2, "dg")]), W[("o", i2, "vec")], tag)
                cur = dst
            if do_ffn:
                sub += 1
                last = (sub == n_sub)
                if last:
                    ffn_phase(cx, cfg, cur, Y, slabs(W[("f", l, "wup")]), slabs(W[("f", l, "wdn")]),
                              W[("f", l, "g")], W[("f", l, "cw")], W[("f", l, "cb")], "F%d" % l, out_off=HALO)
                else:
                    dst = nxt()
                    ffn_phase(cx, cfg, cur, dst, slabs(W[("f", l, "wup")]), slabs(W[("f", l, "wdn")]),
                              W[("f", l, "g")], W[("f", l, "cw")], W[("f", l, "cb")], "F%d" % l)
                    cur = dst
        if not do_ffn:
            cx.P.op("sp", lambda e: e.dma_start(out=Y, in_=cur[:, HALO:]), dma_key=("G", "fin"))


def lay_slabs(w):
    n = w.shape[1] // 128
    a = w.reshape(KC, 128, n, 128).transpose(2, 1, 0, 3)
    return np.ascontiguousarray(a).reshape(n * 128, KC * 128)


def lay_pairs(w, half):
    n = half // 128
    a = w.reshape(KC, 128, 2, n, 128).transpose(3, 1, 2, 0, 4)
    return np.ascontiguousarray(a).reshape(n * 128, 2 * KC * 128)


def lay_kmajor(w):
    c = w.shape[1]
    return np.ascontiguousarray(w.reshape(KC, 128, c).transpose(1, 0, 2)).reshape(128, KC * c)


def lay_wdn(w_down):
    a = w_down.reshape(NJ, 128, KC, 128).transpose(2, 1, 0, 3)
    return np.ascontiguousarray(a).reshape(KC * 128, NJ * 128)


def lay_vec(v, nch):
    return np.ascontiguousarray(v.reshape(nch, 128).T)


def lay_cw(cw):
    k = cw.shape[0]
    n = cw.shape[1] // 128
    return np.ascontiguousarray(cw.reshape(k, n, 128).transpose(2, 1, 0)).reshape(128, n * k)


def host_inputs(cfg, inp, n_layers=4, do_ffn=True):
    f32 = np.float32
    shared = {}
    n_even = (n_layers + 1) // 2
    n_odd = n_layers // 2
    for e_ in range(n_even):
        w_in = np.asarray(inp["even_w_in"][e_], f32)
        q, k, v = w_in[:, 0:512], w_in[:, 512:1024], w_in[:, 1024:1536]
        f, u = w_in[:, 1536:1544], w_in[:, 1544:2056]
        shared["e%d_wqk" % e_] = lay_slabs(np.concatenate([q, k, u], axis=1))
        shared["e%d_wf" % e_] = lay_kmajor(f)
        shared["e%d_wv" % e_] = lay_kmajor(v)
        wp = np.asarray(inp["even_w_pool"][e_], f32)
        shared["e%d_wp" % e_] = np.ascontiguousarray(wp.transpose(1, 0, 2)).reshape(128, 512)
        shared["e%d_wo" % e_] = lay_slabs(np.asarray(inp["even_w_out"][e_], f32))
        vec = np.zeros((128, 16), f32)
        vec[:, 0:8] = lay_vec(np.asarray(inp["even_norm_g"][e_], f32), 8)
        vec[:, 8] = np.tile(np.asarray(inp["even_q_norm_g"][e_], f32), 2)
        vec[:, 9] = np.tile(np.asarray(inp["even_k_norm_g"][e_], f32), 2)
        vec[:, 10:14] = lay_vec(np.asarray(inp["even_pool_scale"][e_], f32), 4)
        shared["e%d_vec" % e_] = vec
        shared["e%d_bf" % e_] = np.asarray(inp["even_b_f"][e_], f32).reshape(8, 1).copy()
    for o_ in range(n_odd):
        shared["o%d_w1" % o_] = lay_pairs(np.asarray(inp["odd_w_pw1"][o_], f32), 1024)
        shared["o%d_w2" % o_] = lay_slabs(np.asarray(inp["odd_w_pw2"][o_], f32))
        vec = np.zeros((128, 32), f32)
        vec[:, 0:8] = lay_vec(np.asarray(inp["odd_norm_g"][o_], f32), 8)
        vec[:, 8:16] = lay_vec(np.asarray(inp["odd_dw_b"][o_], f32), 8)
        vec[:, 16:24] = lay_vec(np.asarray(inp["odd_ln_g"][o_], f32), 8)
        vec[:, 24:32] = lay_vec(np.asarray(inp["odd_ln_b"][o_], f32), 8)
        shared["o%d_vec" % o_] = vec
        shared["o%d_dww" % o_] = lay_cw(np.asarray(inp["odd_dw_w"][o_], f32))
    if do_ffn:
        for l in range(n_layers):
            shared["f%d_wup" % l] = lay_pairs(np.asarray(inp["ffn_w_up"][l], f32), DFF)
            shared["f%d_wdn" % l] = lay_wdn(np.asarray(inp["ffn_w_down"][l], f32))
            shared["f%d_g" % l] = lay_vec(np.asarray(inp["ffn_norm_g"][l], f32), 8)
            shared["f%d_cw" % l] = lay_cw(np.asarray(inp["ffn_conv_w"][l], f32))
            shared["f%d_cb" % l] = lay_vec(np.asarray(inp["ffn_conv_b"][l], f32), 44)
    x = np.asarray(inp["x"], f32)
    maps = []
    for c in range(NCORES):
        b, i = c // 4, c % 4
        xt = np.zeros((D, cfg.T), f32)
        lo = i * cfg.CH - HALO
        if i == 0:
            xt[:, HALO:] = x[b, 0:cfg.CH].T
        else:
            xt[:, :] = x[b, lo:lo + cfg.T].T
        flag = np.full((128, 1), 0.0 if i == 0 else 1.0, f32)
        invc = np.zeros((128, 4 * 32), f32)
        pos = np.arange(1, 33, dtype=f32)
        for g, w in enumerate(POOL_W):
            cntv = np.minimum(pos, float(w)) if i == 0 else np.full(32, float(w), f32)
            invc[:, g * 32:(g + 1) * 32] = (1.0 / cntv)[None, :]
        m = dict(shared)
        m["x"] = xt
        m["flag"] = flag
        m["invc"] = invc
        maps.append(m)
    return maps


_CACHE = {}
N_SPLIT = 1


def _sub_inputs(inputs, l0, nl):
    out = {}
    for k, v in inputs.items():
        if k == "x":
            out[k] = v
        elif k.startswith("even_"):
            out[k] = v[(l0 + 1) // 2:]
        elif k.startswith("odd_"):
            out[k] = v[l0 // 2:]
        else:
            out[k] = v[l0:]
    return out


def kernel(**inputs):
    cfg = Cfg(4096)
    nl = 4 // N_SPLIT
    if "nc" not in _CACHE:
        _CACHE["nc"] = build_program(cfg, n_layers=nl)[0]
    nc = _CACHE["nc"]
    inp = {k: np.asarray(v) for k, v in inputs.items()}
    x = np.asarray(inp["x"], np.float32)
    for part in range(N_SPLIT):
        sub = _sub_inputs(inp, part * nl, nl)
        sub["x"] = x
        maps = host_inputs(cfg, sub, n_layers=nl)
        res = run_bass_kernel_spmd(nc, maps, core_ids=list(range(NCORES)))
        out = np.empty((2, SEQ, D), np.float32)
        for c in range(NCORES):
            b, i = c // 4, c % 4
            out[b, i * cfg.CH:(i + 1) * cfg.CH, :] = res.results[c]["y"].T
        x = out
    return x
```
